# Optimizing a Trainium2 kernel written in Bass

```python
import math
import jax
import jax.numpy as jnp
from jax import lax
import numpy as np

D_MODEL = 1024
BATCH = 2
SEQ = 8192
DEPTH = 4
DEC_BATCH = 32
DEC_SEQ = 4
PAST_LEN = 8192
PAGE_SIZE = 128

N_MIXERS = 3
N_LAYERS_A = (DEPTH + 2) // 3
N_LAYERS_B = (DEPTH + 1) // 3
N_LAYERS_C = DEPTH // 3

A_HEADS = 16
A_HEAD_DIM = D_MODEL // A_HEADS
A_KV_GROUPS = 4
A_GROUP_SIZE = A_HEADS // A_KV_GROUPS
A_WIDTH = A_HEADS * A_HEAD_DIM
A_KV_WIDTH = 2 * A_KV_GROUPS * A_HEAD_DIM
A_IN = 2 * A_WIDTH + 3 * A_KV_WIDTH + 3 * A_HEADS
CMP_STRIDE = 16
CMP_LEN = 2 * CMP_STRIDE
SLC_BLOCK = 64
CMP_PER_SLC = SLC_BLOCK // CMP_STRIDE
N_SELECT = 16
WINDOW = 512
Q_BLOCK = 128

DN_HEADS = 8
DN_HEAD_DIM = D_MODEL // DN_HEADS
DN_WIDTH = DN_HEADS * DN_HEAD_DIM
DN_CONV = 4
DN_CHUNK = 64
DN_IN = 4 * DN_WIDTH + 2 * DN_HEADS

CONF_WIDTH = D_MODEL
CONF_KERNEL = 31
CONF_IN = 3 * CONF_WIDTH

NORM_EPS = 1e-6
NEG_INF = -1e30
FORCE_BONUS = 1e4
TINY = 1e-30

kernel_name = 'hybrid_nsa_gdn_conformer_step'


def rms_norm(x, g):
    xf = x.astype(jnp.float32)
    y = xf * lax.rsqrt(jnp.mean(xf * xf, axis=-1, keepdims=True) + NORM_EPS)
    return (y * g.astype(jnp.float32)).astype(x.dtype)


def layer_norm(x, g, b):
    xf = x.astype(jnp.float32)
    mu = jnp.mean(xf, axis=-1, keepdims=True)
    var = jnp.mean(jnp.square(xf - mu), axis=-1, keepdims=True)
    y = (xf - mu) * lax.rsqrt(var + NORM_EPS)
    return (y * g.astype(jnp.float32) + b.astype(jnp.float32)).astype(x.dtype)


def l2_norm(x):
    return x * lax.rsqrt(jnp.sum(x * x, axis=-1, keepdims=True) + NORM_EPS)


def alibi_slopes(n_heads):
    return jnp.exp2(-8.0 * jnp.arange(1, n_heads + 1, dtype=jnp.float32) / n_heads)


def masked_probs(logits, valid):
    logits = jnp.where(valid, logits, NEG_INF)
    m = jnp.max(logits, axis=-1, keepdims=True)
    p = jnp.where(valid, jnp.exp(logits - m), 0.0)
    return p / (jnp.sum(p, axis=-1, keepdims=True) + TINY)


def causal_dwconv(x_ext, w):
    return lax.conv_general_dilated(x_ext, w[:, None, :].astype(x_ext.dtype), window_strides=(1,),
                                    padding='VALID', dimension_numbers=('NWC', 'WIO', 'NWC'),
                                    feature_group_count=x_ext.shape[-1])


def gather_pages(pool, page_table):
    rows = pool[page_table]
    return rows.reshape((page_table.shape[0], -1) + pool.shape[2:])


def key_norm_rows(kv, g):
    return jnp.stack([rms_norm(kv[:, :, 0], g), kv[:, :, 1]], axis=2)


def compress_kv(rows, pos_emb, w_cmp, k_gain):
    B, L = rows.shape[0], rows.shape[1]
    nh = L // CMP_STRIDE
    halves = rows[:, :nh * CMP_STRIDE].reshape(B, nh, CMP_STRIDE, 2, A_KV_GROUPS, A_HEAD_DIM)
    w = w_cmp.reshape(2, 2, CMP_STRIDE, A_HEAD_DIM, A_HEAD_DIM)
    pe = pos_emb.reshape(2, 2, CMP_STRIDE, A_HEAD_DIM)
    proj = jnp.einsum('bnlcgd,chlde->bnhcge', halves, w)
    bias = jnp.einsum('chld,chlde->ce', pe, w)
    comp = proj[:, :-1, 0] + proj[:, 1:, 1] + bias[:, None, :]
    kc = rms_norm(comp[:, :, 0], k_gain)
    vc = comp[:, :, 1]
    cpos = jnp.arange(nh - 1, dtype=jnp.int32) * CMP_STRIDE + (CMP_LEN - 1)
    return kc, vc, cpos


def nsa_mixer(h, w_in, q_gain, k_gain, cmp_pos, cmp_w, gate_b, w_out, slopes, q_pos0,
              past_cmp, past_slc, win_buf):
    B, T, _ = h.shape
    G, R, dh = A_KV_GROUPS, A_GROUP_SIZE, A_HEAD_DIM
    p = h @ w_in
    q, kv_c, kv_s, kv_w, gl, z = jnp.split(
        p, [A_WIDTH, A_WIDTH + A_KV_WIDTH, A_WIDTH + 2 * A_KV_WIDTH, A_WIDTH + 3 * A_KV_WIDTH,
            A_WIDTH + 3 * A_KV_WIDTH + 3 * A_HEADS], axis=-1)
    q = rms_norm(q.reshape(B, T, A_HEADS, dh), q_gain) * (dh ** -0.5)
    q = q.reshape(B, T, G, R, dh)
    kv_c = kv_c.reshape(B, T, 2, G, dh)
    kv_s = key_norm_rows(kv_s.reshape(B, T, 2, G, dh), k_gain[1])
    kv_w = key_norm_rows(kv_w.reshape(B, T, 2, G, dh), k_gain[2])
    if past_cmp is None:
        full_c, full_s, full_w = kv_c, kv_s, kv_w
        w_off = 0
        new_win = kv_w[:, T - min(WINDOW, T):]
    else:
        full_c = jnp.concatenate([past_cmp.astype(kv_c.dtype), kv_c], axis=1)
        full_s = jnp.concatenate([past_slc.astype(kv_s.dtype), kv_s], axis=1)
        full_w = jnp.concatenate([win_buf.astype(kv_w.dtype), kv_w], axis=1)
        w_buf = win_buf.shape[1]
        w_off = q_pos0 - w_buf
        new_win = full_w[:, full_w.shape[1] - w_buf:]

    kc, vc, cpos = compress_kv(full_c, cmp_pos, cmp_w, k_gain[0])
    nc = kc.shape[1]
    L = full_s.shape[1]
    n_sel = -(-L // SLC_BLOCK)
    s_pad = jnp.pad(full_s, ((0, 0), (0, n_sel * SLC_BLOCK - L), (0, 0), (0, 0), (0, 0)))
    s_blk = s_pad.reshape(B, n_sel, SLC_BLOCK, 2, G, dh).transpose(3, 0, 4, 1, 2, 5)
    ks_blk, vs_blk = s_blk[0], s_blk[1]
    n_top = min(N_SELECT, n_sel)
    blk_ids = jnp.arange(n_sel, dtype=jnp.int32)

    QB = min(Q_BLOCK, T)
    nqb = -(-T // QB)
    Tp = nqb * QB
    w_pad = jnp.pad(full_w, ((0, 0), (WINDOW, QB), (0, 0), (0, 0), (0, 0)))
    q_blocks = jnp.pad(q, ((0, 0), (0, Tp - T), (0, 0), (0, 0), (0, 0)))
    q_blocks = q_blocks.reshape(B, nqb, QB, G, R, dh).transpose(1, 0, 2, 3, 4, 5)
    starts = q_pos0 + jnp.arange(nqb, dtype=jnp.int32) * QB
    sl = slopes.reshape(G, R)[None, :, :, None, None]
    gather = jax.vmap(jax.vmap(lambda blocks, ix: blocks[ix]))

    def attend_block(args):
        qb, s = args
        t = s + jnp.arange(QB, dtype=jnp.int32)
        dist_c = (t[:, None] - cpos[None, :]).astype(jnp.float32)
        lc = jnp.einsum('bqgrd,bngd->bgrqn', qb, kc).astype(jnp.float32) - sl * dist_c
        pc = masked_probs(lc, cpos[None, :] <= t[:, None])
        o_c = jnp.einsum('bgrqn,bngd->bqgrd', pc.astype(vc.dtype), vc)
        imp = jnp.pad(pc.sum(axis=2), ((0, 0), (0, 0), (0, 0), (0, n_sel * CMP_PER_SLC - nc)))
        imp = imp.reshape(B, G, QB, n_sel, CMP_PER_SLC).sum(-1)
        jt = (t // SLC_BLOCK)[:, None]
        forced = (blk_ids == 0) | (blk_ids == jt) | (blk_ids == jt - 1)
        score = jnp.where(blk_ids > jt, NEG_INF, imp + jnp.where(forced, FORCE_BONUS, 0.0))
        _, idx = lax.top_k(score, n_top)
        ksel = gather(ks_blk, idx).reshape(B, G, QB, n_top * SLC_BLOCK, dh)
        vsel = gather(vs_blk, idx).reshape(B, G, QB, n_top * SLC_BLOCK, dh)
        kpos = (idx[..., None] * SLC_BLOCK + jnp.arange(SLC_BLOCK, dtype=jnp.int32)).reshape(B, G, QB, n_top * SLC_BLOCK)
        dist_s = (t[None, None, None, :, None] - kpos[:, :, None]).astype(jnp.float32)
        ls = jnp.einsum('bqgrd,bgqmd->bgrqm', qb, ksel).astype(jnp.float32) - sl * dist_s
        ps = masked_probs(ls, (kpos <= t[None, None, :, None])[:, :, None])
        o_s = jnp.einsum('bgrqm,bgqmd->bqgrd', ps.astype(vsel.dtype), vsel)
        band = lax.dynamic_slice_in_dim(w_pad, s - w_off, WINDOW + QB, axis=1)
        wpos = s - WINDOW + jnp.arange(WINDOW + QB, dtype=jnp.int32)
        dist_w = t[:, None] - wpos[None, :]
        valid_w = (dist_w >= 0) & (dist_w < WINDOW) & (wpos[None, :] >= w_off)
        lw = jnp.einsum('bqgrd,bkgd->bgrqk', qb, band[:, :, 0]).astype(jnp.float32) - sl * dist_w.astype(jnp.float32)
        pw = masked_probs(lw, valid_w)
        o_w = jnp.einsum('bgrqk,bkgd->bqgrd', pw.astype(band.dtype), band[:, :, 1])
        return o_c, o_s, o_w

    o_c, o_s, o_w = lax.map(attend_block, (q_blocks, starts))
    unblock = lambda o: o.transpose(1, 0, 2, 3, 4, 5).reshape(B, Tp, G, R, dh)[:, :T]
    gates = jax.nn.sigmoid(gl + gate_b).reshape(B, T, 3, G, R, 1)
    o = gates[:, :, 0] * unblock(o_c) + gates[:, :, 1] * unblock(o_s) + gates[:, :, 2] * unblock(o_w)
    out = (o.reshape(B, T, A_WIDTH) * jax.nn.silu(z)) @ w_out
    return out, kv_c, kv_s, new_win


def chunk_gated_delta(q, k, v, g, beta, s0):
    B, T, H, dk = q.shape
    dv = v.shape[-1]
    C = min(DN_CHUNK, T)
    n = -(-T // C)
    pad = n * C - T

    def blocks(a):
        a = jnp.pad(a, [(0, 0), (0, pad)] + [(0, 0)] * (a.ndim - 2))
        a = a.reshape((B, n, C) + a.shape[2:])
        return jnp.moveaxis(jnp.moveaxis(a, 1, 0), 3, 2)

    qc, kc, vc, gc, bc = blocks(q), blocks(k), blocks(v), blocks(g), blocks(beta)
    Gc = jnp.cumsum(gc, axis=-1)
    pos = jnp.arange(C)
    tril = pos[:, None] >= pos[None, :]
    stril = pos[:, None] > pos[None, :]
    diff = Gc[..., :, None] - Gc[..., None, :]
    decay = jnp.where(tril, jnp.exp(jnp.where(tril, diff, 0.0)), 0.0)
    kb = kc * bc[..., None]
    a_mat = jnp.where(stril, jnp.einsum('nbhid,nbhjd->nbhij', kb, kc) * decay, 0.0)
    eye = jnp.eye(C, dtype=jnp.float32)
    t_mat = lax.linalg.triangular_solve(eye + a_mat, jnp.broadcast_to(eye, a_mat.shape),
                                        left_side=True, lower=True, unit_diagonal=True)
    u = t_mat @ (vc * bc[..., None])
    w = t_mat @ (kb * jnp.exp(Gc)[..., None])
    qk = jnp.where(tril, jnp.einsum('nbhid,nbhjd->nbhij', qc, kc) * decay, 0.0)

    def step(s, xs):
        q_i, k_i, u_i, w_i, qk_i, g_i = xs
        v_new = u_i - w_i @ s
        o_i = (q_i * jnp.exp(g_i)[..., None]) @ s + qk_i @ v_new
        g_last = g_i[..., -1:]
        s = s * jnp.exp(g_last)[..., None] + jnp.einsum('bhck,bhcv->bhkv', k_i * jnp.exp(g_last - g_i)[..., None], v_new)
        return s, o_i

    s_final, o = lax.scan(step, s0, (qc, kc, u, w, qk, Gc))
    o = jnp.moveaxis(jnp.moveaxis(o, 2, 3), 0, 1).reshape(B, n * C, H, dv)[:, :T]
    return o, s_final


def gdn_mixer(h, w_in, conv_w, a_log, dt_bias, o_gain, w_out, conv_buf, s0):
    B, T, _ = h.shape
    p = h @ w_in
    qkv, z, b_logit, a_logit = jnp.split(p, [3 * DN_WIDTH, 4 * DN_WIDTH, 4 * DN_WIDTH + DN_HEADS], axis=-1)
    x_ext = jnp.concatenate([conv_buf.astype(qkv.dtype), qkv], axis=1)
    qkv = jax.nn.silu(causal_dwconv(x_ext, conv_w)).reshape(B, T, 3, DN_HEADS, DN_HEAD_DIM).astype(jnp.float32)
    q = l2_norm(qkv[:, :, 0]) * (DN_HEAD_DIM ** -0.5)
    k = l2_norm(qkv[:, :, 1])
    v = qkv[:, :, 2]
    beta = jax.nn.sigmoid(b_logit.astype(jnp.float32))
    g = -jnp.exp(a_log.astype(jnp.float32)) * jax.nn.softplus(a_logit.astype(jnp.float32) + dt_bias.astype(jnp.float32))
    o, s_new = chunk_gated_delta(q, k, v, g, beta, s0.astype(jnp.float32))
    o = rms_norm(o, o_gain).astype(h.dtype).reshape(B, T, DN_WIDTH)
    out = (o * jax.nn.silu(z)) @ w_out
    return out, s_new, x_ext[:, x_ext.shape[1] - (DN_CONV - 1):]


def conformer_mixer(h, w_in, conv_w, conv_b, ln_g, ln_b, w_out, conv_buf):
    p = h @ w_in
    a, b, z = jnp.split(p, 3, axis=-1)
    u = a * jax.nn.sigmoid(b)
    x_ext = jnp.concatenate([conv_buf.astype(u.dtype), u], axis=1)
    c = causal_dwconv(x_ext, conv_w) + conv_b
    c = jax.nn.silu(layer_norm(c, ln_g, ln_b))
    out = (c * jax.nn.silu(z)) @ w_out
    return out, x_ext[:, x_ext.shape[1] - (CONF_KERNEL - 1):]


def setup_inputs(seed: int = 0) -> dict:
    key = jax.random.key(seed)
    keys = iter(jax.random.split(key, 40))

    def normal(shape, scale):
        return jax.random.normal(next(keys), shape, jnp.float32) * scale

    def gain(shape):
        return 1.0 + normal(shape, 0.05)

    n_pages = PAST_LEN // PAGE_SIZE
    n_pool = (DEC_BATCH * n_pages * 5) // 4
    w_buf = min(WINDOW, PAST_LEN)
    page_table = jax.random.permutation(next(keys), n_pool)[:DEC_BATCH * n_pages].reshape(DEC_BATCH, n_pages).astype(jnp.int32)
    dt = jnp.exp(jax.random.uniform(next(keys), (N_LAYERS_B, DN_HEADS), jnp.float32, math.log(1e-3), math.log(1e-1)))
    pool_shape = (N_LAYERS_A, n_pool, PAGE_SIZE, 2, A_KV_GROUPS, A_HEAD_DIM)
    return {
        'x_prompt': normal((BATCH, SEQ, D_MODEL), 1.0),
        'x_sample': normal((DEC_BATCH, DEC_SEQ, D_MODEL), 1.0),
        'cache_cmp_kv': normal(pool_shape, 1.0),
        'cache_slc_kv': normal(pool_shape, 1.0),
        'cache_win_kv': normal((N_LAYERS_A, DEC_BATCH, w_buf, 2, A_KV_GROUPS, A_HEAD_DIM), 1.0),
        'state_delta': normal((N_LAYERS_B, DEC_BATCH, DN_HEADS, DN_HEAD_DIM, DN_HEAD_DIM), 0.1),
        'state_delta_conv': normal((N_LAYERS_B, DEC_BATCH, DN_CONV - 1, 3 * DN_WIDTH), 1.0),
        'state_conv': normal((N_LAYERS_C, DEC_BATCH, CONF_KERNEL - 1, CONF_WIDTH), 1.0),
        'page_table': page_table,
        'norm_g': gain((DEPTH, D_MODEL)),
        'a_w_in': normal((N_LAYERS_A, D_MODEL, A_IN), D_MODEL ** -0.5),
        'a_q_gain': gain((N_LAYERS_A, A_HEAD_DIM)),
        'a_k_gain': gain((N_LAYERS_A, 3, A_HEAD_DIM)),
        'a_cmp_pos': normal((N_LAYERS_A, 2, CMP_LEN, A_HEAD_DIM), 0.1),
        'a_cmp_w': normal((N_LAYERS_A, 2, CMP_LEN * A_HEAD_DIM, A_HEAD_DIM), (CMP_LEN * A_HEAD_DIM) ** -0.5),
        'a_gate_b': normal((N_LAYERS_A, 3 * A_HEADS), 0.1),
        'a_w_out': normal((N_LAYERS_A, A_WIDTH, D_MODEL), A_WIDTH ** -0.5),
        'b_w_in': normal((N_LAYERS_B, D_MODEL, DN_IN), D_MODEL ** -0.5),
        'b_conv_w': normal((N_LAYERS_B, DN_CONV, 3 * DN_WIDTH), DN_CONV ** -0.5),
        'b_a_log': jnp.log(jax.random.uniform(next(keys), (N_LAYERS_B, DN_HEADS), jnp.float32, 1.0, 16.0)),
        'b_dt_bias': jnp.log(jnp.expm1(dt)),
        'b_o_gain': gain((N_LAYERS_B, DN_HEAD_DIM)),
        'b_w_out': normal((N_LAYERS_B, DN_WIDTH, D_MODEL), DN_WIDTH ** -0.5),
        'c_w_in': normal((N_LAYERS_C, D_MODEL, CONF_IN), D_MODEL ** -0.5),
        'c_conv_w': normal((N_LAYERS_C, CONF_KERNEL, CONF_WIDTH), CONF_KERNEL ** -0.5),
        'c_conv_b': normal((N_LAYERS_C, CONF_WIDTH), 0.02),
        'c_ln_g': gain((N_LAYERS_C, CONF_WIDTH)),
        'c_ln_b': normal((N_LAYERS_C, CONF_WIDTH), 0.02),
        'c_w_out': normal((N_LAYERS_C, CONF_WIDTH, D_MODEL), CONF_WIDTH ** -0.5),
    }


def reference(x_prompt, x_sample, cache_cmp_kv, cache_slc_kv, cache_win_kv, state_delta, state_delta_conv,
              state_conv, page_table, norm_g, a_w_in, a_q_gain, a_k_gain, a_cmp_pos, a_cmp_w, a_gate_b, a_w_out,
              b_w_in, b_conv_w, b_a_log, b_dt_bias, b_o_gain, b_w_out, c_w_in, c_conv_w, c_conv_b, c_ln_g, c_ln_b,
              c_w_out):
    slopes = alibi_slopes(A_HEADS)
    yp, ys = x_prompt, x_sample
    bp = x_prompt.shape[0]
    cmp_p, cmp_s, slc_p, slc_s, win_p, win_s = [], [], [], [], [], []
    dst_p, dst_s, dcv_p, dcv_s, ccv_p, ccv_s = [], [], [], [], [], []
    for i in range(DEPTH):
        j = i // N_MIXERS
        kind = i % N_MIXERS
        hp = rms_norm(yp, norm_g[i])
        hs = rms_norm(ys, norm_g[i])
        if kind == 0:
            wa = (a_w_in[j], a_q_gain[j], a_k_gain[j], a_cmp_pos[j], a_cmp_w[j], a_gate_b[j], a_w_out[j], slopes)
            op, c_rows_p, s_rows_p, w_rows_p = nsa_mixer(hp, *wa, 0, None, None, None)
            os_, c_rows_s, s_rows_s, w_rows_s = nsa_mixer(
                hs, *wa, PAST_LEN, gather_pages(cache_cmp_kv[j], page_table),
                gather_pages(cache_slc_kv[j], page_table), cache_win_kv[j])
            cmp_p.append(c_rows_p)
            cmp_s.append(c_rows_s)
            slc_p.append(s_rows_p)
            slc_s.append(s_rows_s)
            win_p.append(w_rows_p)
            win_s.append(w_rows_s)
        elif kind == 1:
            wb = (b_w_in[j], b_conv_w[j], b_a_log[j], b_dt_bias[j], b_o_gain[j], b_w_out[j])
            op, st_p, cv_p = gdn_mixer(hp, *wb, jnp.zeros((bp, DN_CONV - 1, 3 * DN_WIDTH), hp.dtype),
                                       jnp.zeros((bp, DN_HEADS, DN_HEAD_DIM, DN_HEAD_DIM), jnp.float32))
            os_, st_s, cv_s = gdn_mixer(hs, *wb, state_delta_conv[j], state_delta[j])
            dst_p.append(st_p)
            dst_s.append(st_s)
            dcv_p.append(cv_p)
            dcv_s.append(cv_s)
        else:
            wc = (c_w_in[j], c_conv_w[j], c_conv_b[j], c_ln_g[j], c_ln_b[j], c_w_out[j])
            op, cb_p = conformer_mixer(hp, *wc, jnp.zeros((bp, CONF_KERNEL - 1, CONF_WIDTH), hp.dtype))
            os_, cb_s = conformer_mixer(hs, *wc, state_conv[j])
            ccv_p.append(cb_p)
            ccv_s.append(cb_s)
        yp = yp + op
        ys = ys + os_
    cmp_kv_prompt, cmp_kv_sample = jnp.stack(cmp_p), jnp.stack(cmp_s)
    slc_kv_prompt, slc_kv_sample = jnp.stack(slc_p), jnp.stack(slc_s)
    win_kv_prompt, win_kv_sample = jnp.stack(win_p), jnp.stack(win_s)
    delta_state_prompt, delta_state_sample = jnp.stack(dst_p), jnp.stack(dst_s)
    delta_conv_prompt, delta_conv_sample = jnp.stack(dcv_p), jnp.stack(dcv_s)
    conv_prompt, conv_sample = jnp.stack(ccv_p), jnp.stack(ccv_s)
    return (yp, ys, cmp_kv_prompt, cmp_kv_sample, slc_kv_prompt, slc_kv_sample, win_kv_prompt, win_kv_sample,
            delta_state_prompt, delta_state_sample, delta_conv_prompt, delta_conv_sample, conv_prompt, conv_sample)
```

```python
import numpy as np
from contextlib import ExitStack
import concourse.bass as bass
import concourse.mybir as mybir
from concourse.bass_utils import run_bass_kernel_spmd

F32 = mybir.dt.float32
BF16 = mybir.dt.bfloat16
I32 = mybir.dt.int32
ALU = mybir.AluOpType
AF = mybir.ActivationFunctionType
AX = mybir.AxisListType

D = 1024
T_P = 8192
N_CORES = 8
EPS = 1e-6
A_IN = 3632
DN_IN = 4112
CONF_IN = 3072


class KB:
    def __init__(self, nc, n_dma_sems=32):
        self.nc = nc
        self.eng = {'pe': nc.tensor, 'act': nc.scalar, 'dve': nc.vector, 'pool': nc.gpsimd, 'sp': nc.sync}
        self.sem = {k: nc.alloc_semaphore('sem_' + k) for k in self.eng}
        self.cnt = {k: 0 for k in self.eng}
        self.dsem = [nc.alloc_semaphore('dsem%d' % i) for i in range(n_dma_sems)]
        self.dval = [0] * n_dma_sems
        self.dq = {'sp': list(range(0, 16)), 'pool': list(range(16, 28)), 'act': list(range(28, 32))}
        self.dnext = {'sp': 0, 'pool': 0, 'act': 0}
        self.waited = {k: {} for k in self.eng}
        self.res_w = {}
        self.res_wacc = {}
        self.res_r = {}
        self.psum_keys = set()
        self.n_ins = 0

    def _semobj(self, sk):
        return self.sem[sk] if isinstance(sk, str) else self.dsem[sk]

    def _need(self, e, evs):
        for sk, v in evs.items():
            if sk == e and e == 'pe':
                continue
            if self.waited[e].get(sk, 0) >= v:
                continue
            self.eng[e].wait_ge(self._semobj(sk), v)
            self.waited[e][sk] = v

    @staticmethod
    def _nk(x):
        if isinstance(x, (str, int)):
            return x
        if isinstance(x, tuple):
            return tuple(KB._nk(y) for y in x)
        return getattr(x, 'name', None) or id(x)

    def _deps(self, e, reads, writes, acc=False):
        reads = [self._nk(r) for r in reads]
        writes = [self._nk(w) for w in writes]
        evs = {}

        def add(d):
            for sk, v in d.items():
                if evs.get(sk, 0) < v:
                    evs[sk] = v
        for r in reads:
            add(self.res_w.get(r, {}))
            add(self.res_wacc.get(r, {}))
        for w in writes:
            add(self.res_w.get(w, {}))
            add(self.res_r.get(w, {}))
            if not acc:
                add(self.res_wacc.get(w, {}))
        self._need(e, evs)

    def _commit(self, ev, reads, writes, acc=False):
        reads = [self._nk(r) for r in reads]
        writes = [self._nk(w) for w in writes]
        sk, v = ev
        for r in reads:
            d = self.res_r.setdefault(r, {})
            if d.get(sk, 0) < v:
                d[sk] = v
        for w in writes:
            if acc:
                d = self.res_wacc.setdefault(w, {})
                d[sk] = max(d.get(sk, 0), v)
            else:
                self.res_w[w] = {sk: v}
                self.res_wacc[w] = {}
                self.res_r[w] = {}

    def op(self, e, fn, reads=(), writes=()):
        pr = [r for r in reads if self._nk(r) in self.psum_keys]
        if pr:
            reads = [r for r in reads if self._nk(r) not in self.psum_keys]
            writes = list(writes) + [r for r in pr if self._nk(r) not in [self._nk(w) for w in writes]]
        self._deps(e, reads, writes)
        ins = fn(self.eng[e])
        self.cnt[e] += 1
        ins.then_inc(self.sem[e], 1)
        self._commit((e, self.cnt[e]), reads, writes)
        self.n_ins += 1
        return ins

    def dma(self, out, in_, reads=(), writes=(), q='sp', acc=False, **kw):
        self._deps(q, reads, writes, acc=acc)
        i = self.dq[q][self.dnext[q] % len(self.dq[q])]
        self.dnext[q] += 1
        if self.dval[i] > 0 and self.waited[q].get(i, 0) < self.dval[i]:
            self.eng[q].wait_ge(self.dsem[i], self.dval[i])
            self.waited[q][i] = self.dval[i]
        ins = self.eng[q].dma_start(out=out, in_=in_, **kw)
        self.dval[i] += 16
        ins.then_inc(self.dsem[i], 16)
        self._commit((i, self.dval[i]), reads, writes, acc=acc)
        self.n_ins += 1
        return ins

    def dma_ind(self, out, in_, idx_ap, reads=(), writes=()):
        q = 'pool'
        self._deps(q, reads, writes)
        i = self.dq[q][self.dnext[q] % len(self.dq[q])]
        self.dnext[q] += 1
        if self.dval[i] > 0 and self.waited[q].get(i, 0) < self.dval[i]:
            self.eng[q].wait_ge(self.dsem[i], self.dval[i])
            self.waited[q][i] = self.dval[i]
        ins = self.nc.gpsimd.indirect_dma_start(out, None, in_, bass.IndirectOffsetOnAxis(ap=idx_ap, axis=0))
        self.dval[i] += 16
        ins.then_inc(self.dsem[i], 16)
        self._commit((i, self.dval[i]), reads, writes)
        self.n_ins += 1
        return ins

    def reg_psum(self, t):
        self.psum_keys.add(self._nk(t))
        return t

    def barrier(self):
        for e in self.eng:
            evs = {k: c for k, c in self.cnt.items() if c > 0 and k != e}
            for i, v in enumerate(self.dval):
                if v > 0:
                    evs[i] = v
            for sk, v in evs.items():
                if self.waited[e].get(sk, 0) >= v:
                    continue
                self.eng[e].wait_ge(self._semobj(sk), v)
                self.waited[e][sk] = v
        self.res_w = {}
        self.res_wacc = {}
        self.res_r = {}

    def finish(self):
        self.barrier()


class Ctx:
    pass


class StopBuild(Exception):
    pass


STOP_AFTER = None


STOPPED = [False]


def checkpoint(name):
    if STOP_AFTER == name:
        STOPPED[0] = True
    return STOPPED[0]


def rr(lst, state=[0]):
    state[0] += 1
    return lst[state[0] % len(lst)]


def rstd_op(k, C, out_ap, in_ap, scale, nt, key):
    k.op('act', lambda en: en.activation(out_ap, in_ap, AF.Sqrt, bias=C.epsc[0:nt, 0:1], scale=scale),
         reads=[key, 'epsc'], writes=[key])
    k.op('dve', lambda en: en.reciprocal(out_ap, out_ap), reads=[key], writes=[key])


def stage_inproj(k, nc, C, x_ap, T, w_ap, ncol, g_col_ap, plan, tag):
    with ExitStack() as es:
        sb = lambda n, s, d=F32: es.enter_context(nc.sbuf_tensor(tag + n, s, d))
        ps = lambda n, s, d=F32: k.reg_psum(es.enter_context(nc.psum_tensor(tag + n, s, d)))
        wb = sb('wb', [128, 8, ncol], BF16)
        wst = [sb('wst%d' % i, [128, 8, 512], F32) for i in range(2)]
        gT = sb('gT', [128, 8], F32)
        xts = [sb('xt%d' % i, [128, D], F32) for i in range(2)]
        xn = [sb('xn%d' % i, [128, D], F32) for i in range(2)]
        junk = sb('junk', [128, D], F32)
        st = [sb('st%d' % i, [128, 4], F32) for i in range(2)]
        hT = [sb('hT%d' % i, [128, 8, 512], BF16) for i in range(2)]
        ob = [sb('ob%d' % i, [128, 512], F32) for i in range(4)]
        ptr = [ps('ptr%d' % i, [128, 512], F32) for i in range(2)]
        pg = [ps('pg%d' % i, [128, 512], F32) for i in range(4)]

        k.dma(gT[:], g_col_ap, writes=[gT], allow_slow_non_contiguous=True)
        w_v = w_ap.rearrange("(kc p) c -> p kc c", p=128)
        ci = 0
        for cb in range(0, ncol, 512):
            n = min(512, ncol - cb)
            s = wst[ci % 2]
            k.dma(s[:, :, 0:n], w_v[:, :, cb:cb + n], writes=[s])
            e = ['act', 'dve', 'pool'][ci % 3]
            if e == 'act':
                k.op('act', lambda en: en.copy(wb[:, :, cb:cb + n], s[:, :, 0:n]), reads=[s], writes=[(wb, cb)])
            else:
                k.op(e, lambda en: en.tensor_copy(wb[:, :, cb:cb + n], s[:, :, 0:n]), reads=[s], writes=[(wb, cb)])
            ci += 1
        wkeys = [(wb, cb) for cb in range(0, ncol, 512)]

        xi = 0
        oi = 0
        gi = 0
        for si, t0 in enumerate(range(0, T, 512)):
            nts = min(512, T - t0)
            h = hT[si % 2]
            for j0 in range(0, nts, 128):
                nt = min(128, nts - j0)
                xt = xts[xi % 2]
                xnn = xn[xi % 2]
                stt = st[xi % 2]
                xi += 1
                k.dma(xt[0:nt, :], x_ap[t0 + j0:t0 + j0 + nt, :], writes=[xt])
                k.op('act', lambda en: en.activation(junk[0:nt, :], xt[0:nt, :], AF.Square, accum_out=stt[0:nt, 0:1]),
                     reads=[xt], writes=[junk, stt])
                rstd_op(k, C, stt[0:nt, 2:3], stt[0:nt, 0:1], 1.0 / D, nt, stt)
                k.op('dve', lambda en: en.tensor_scalar(xnn[0:nt, :], xt[0:nt, :], stt[0:nt, 2:3], None, ALU.mult),
                     reads=[xt, stt], writes=[xnn])
                for kc in range(8):
                    p = ptr[kc % 2]
                    k.op('pe', lambda en: en.transpose(p[:, 0:nt], xnn[0:nt, kc * 128:(kc + 1) * 128], C.ident[0:nt, 0:nt]),
                         reads=[xnn, 'ident'], writes=[p])
                    e = 'dve' if kc % 2 == 0 else 'pool'
                    if e == 'pool':
                        k.op('act', lambda en: en.activation(h[:, kc, j0:j0 + nt], p[:, 0:nt], AF.Copy, scale=gT[:, kc:kc + 1]),
                             reads=[p, gT], writes=[h])
                    else:
                        k.op('dve', lambda en: en.tensor_scalar(h[:, kc, j0:j0 + nt], p[:, 0:nt], gT[:, kc:kc + 1], None, ALU.mult),
                             reads=[p, gT], writes=[h])
            for pl in plan:
                c0, n, mode = pl['c0'], pl['n'], pl['mode']
                blk = pl.get('blk', 128)
                if mode == 'FM':
                    for b0 in range(0, n, blk):
                        nb = min(blk, n - b0)
                        pp = pg[gi % 4]
                        gi += 1
                        for kc in range(8):
                            k.op('pe', lambda en: en.matmul(pp[0:nb, 0:nts], wb[:, kc, c0 + b0:c0 + b0 + nb], h[:, kc, 0:nts],
                                                            start=(kc == 0), stop=(kc == 7)),
                                 reads=[h] + wkeys, writes=[pp])
                        o = ob[oi % 4]
                        oi += 1
                        if oi % 2 == 0:
                            k.op('act', lambda en: en.copy(o[0:nb, 0:nts], pp[0:nb, 0:nts]), reads=[pp], writes=[o])
                        else:
                            k.op('dve', lambda en: en.tensor_copy(o[0:nb, 0:nts], pp[0:nb, 0:nts]), reads=[pp], writes=[o])
                        k.dma(pl['dst'](b0, nb, t0, nts), o[0:nb, 0:nts], reads=[o], writes=[pl['key']], q='pool', acc=True)
                else:
                    for j0 in range(0, nts, 128):
                        nt = min(128, nts - j0)
                        if pl.get('tok_filter') is not None and not pl['tok_filter'](t0 + j0, nt):
                            continue
                        for b0 in range(0, n, 512):
                            nb = min(512, n - b0)
                            pp = pg[gi % 4]
                            gi += 1
                            for kc in range(8):
                                k.op('pe', lambda en: en.matmul(pp[0:nt, 0:nb], h[:, kc, j0:j0 + nt], wb[:, kc, c0 + b0:c0 + b0 + nb],
                                                                start=(kc == 0), stop=(kc == 7)),
                                     reads=[h] + wkeys, writes=[pp])
                            o = ob[oi % 4]
                            oi += 1
                            if oi % 2 == 0:
                                k.op('act', lambda en: en.copy(o[0:nt, 0:nb], pp[0:nt, 0:nb]), reads=[pp], writes=[o])
                            else:
                                k.op('dve', lambda en: en.tensor_copy(o[0:nt, 0:nb], pp[0:nt, 0:nb]), reads=[pp], writes=[o])
                            if pl.get('post') is not None:
                                pl['post'](o, nt, nb)
                            for ent in pl['dst'](b0, nb, t0 + j0, nt):
                                r0, r1 = (ent[2], ent[3]) if len(ent) == 4 else (0, nt)
                                k.dma(ent[0], o[r0:r1, 0:nb], reads=[o], writes=[ent[1]], q='pool', acc=True)
    k.barrier()


NEG = -30000.0
TINY = 1e-30


def host_tables(T, t_base):
    import ml_dtypes
    bf = ml_dtypes.bfloat16
    NT = (T + 127) // 128
    slopes = np.exp2(-8.0 * np.arange(1, 17, dtype=np.float32) / 16).astype(np.float32)

    def split2(v):
        hi = v.astype(bf)
        lo = (v - hi.astype(np.float32)).astype(bf)
        return hi, lo
    s_hi, s_lo = split2(slopes)
    t = (t_base + np.arange(NT * 128)).astype(np.float32)
    v = -(slopes[None, :] * t[:, None]).astype(np.float32)
    v_hi, v_lo = split2(v)
    qaug = np.zeros((6, NT, 16, 128), dtype=bf)
    qaug[0] = s_hi[None, :, None]
    qaug[1] = s_lo[None, :, None]
    qaug[2] = s_hi[None, :, None]
    qaug[3] = s_lo[None, :, None]
    qaug[4] = v_hi.reshape(NT, 128, 16).transpose(0, 2, 1)
    qaug[5] = v_lo.reshape(NT, 128, 16).transpose(0, 2, 1)

    def kaug_of(pos):
        ka = np.zeros((6, pos.shape[0]), dtype=bf)
        hi = (128 * (pos // 128)).astype(np.float32)
        lo = (pos % 128).astype(np.float32)
        ka[0] = hi
        ka[1] = hi
        ka[2] = lo
        ka[3] = lo
        ka[4] = 1.0
        ka[5] = 1.0
        return ka
    res = dict(qaug=qaug)
    kr = np.arange(128)[:, None]
    qr = np.arange(128)[None, :]
    caus = np.where(kr <= qr, 0.0, NEG).astype(np.float32)
    wlow = np.where(kr > qr, 0.0, NEG).astype(np.float32)
    res['caus'] = np.repeat(caus[:, None, :], 4, axis=1).astype(bf)
    res['wlow'] = np.repeat(wlow[:, None, :], 4, axis=1).astype(bf)
    cm = np.zeros((128, 17, 4, 128), dtype=np.float32)
    for d in range(17):
        m = np.where(16 * kr + 31 <= 128 * d + qr, 0.0, NEG)
        cm[:, d, :, :] = m[:, None, :]
    res['cm'] = cm.astype(bf)
    n = np.arange(512)
    poolm = np.zeros((128, 4, 128), dtype=np.float32)
    for kt in range(4):
        for nr in range(128):
            poolm[nr, kt, (128 * kt + nr) // 4] = 1.0
    res['poolm'] = poolm.astype(bf)
    bonus = np.zeros((128, 256), dtype=np.float32)
    for q in range(128):
        jq = 1 if q >= 64 else 0
        for c in range(256):
            jj = c - 126
            if jj > jq:
                bonus[q, c] = -1e30
            elif jj == jq or jj == jq - 1:
                bonus[q, c] = 1e4
    res['bonus'] = bonus
    sel = np.zeros((48, 48, 64), dtype=np.float32)
    for r in range(48):
        sel[r, r, :] = 1.0
    res['sel'] = sel
    res['kaug'] = kaug_of(np.arange(8192 + 128))
    res['kaugc'] = kaug_of(16 * np.arange(512) + 31)
    e = np.zeros((128, 65, 128), dtype=np.float32)
    for kt in range(65):
        for key in range(128):
            jb = 2 * kt + key // 64
            if jb < 128:
                e[jb, kt, key] = 1.0
    res['emat'] = e.astype(bf)
    res['ident'] = np.eye(128, dtype=np.float32)
    bs = np.zeros((128, 128), dtype=np.float32)
    bs[:, 127] = 1e4
    res['bonus_s'] = bs
    res['pcol'] = np.arange(128, dtype=np.float32).reshape(128, 1)
    res['pm4'] = (np.arange(128) < 4).astype(np.float32).reshape(128, 1)
    res['identb'] = np.eye(128, dtype=np.float32).astype(bf)
    res['onesf'] = np.ones((128, 128), dtype=np.float32)
    row = np.arange(128)[:, None]
    colx = np.arange(128)[None, :]
    res['tri'] = (row <= colx).astype(np.float32)
    res['slt'] = (row > colx).astype(np.float32)
    res['mstrict'] = np.where(row > colx, 0.0, NEG).astype(np.float32)
    res['minclT'] = np.where(colx >= row, 0.0, NEG).astype(np.float32)
    return res


TABLE_SPECS = dict(
    caus=([128, 4, 128], BF16), wlow=([128, 4, 128], BF16), cm=([128, 17, 4, 128], BF16),
    poolm=([128, 4, 128], BF16), bonus=([128, 256], F32), sel=([48, 48, 64], F32),
    kaug=([6, 8192 + 128], BF16), kaugc=([6, 512], BF16), emat=([128, 65, 128], BF16),
    ident=([128, 128], F32), identb=([128, 128], BF16), onesf=([128, 128], F32),
    tri=([128, 128], F32), slt=([128, 128], F32), mstrict=([128, 128], F32), minclT=([128, 128], F32),
    bonus_s=([128, 128], F32), pcol=([128, 1], F32), pm4=([128, 1], F32),
)


def fm_norm(k, C, W, src, n, dst_ap, dst_key, gain_col, eps_col, scale, P=64, src_ap=None, src_key=None):
    sq, ps, r = W['sq'], W['psn'], W['r']
    sap = src[0:P, 0:n] if src_ap is None else src_ap
    skey = src if src_key is None else src_key
    k.op('act', lambda en: en.activation(sq[0:P, 0:n], sap, AF.Square), reads=[skey], writes=[sq])
    k.op('pe', lambda en: en.matmul(ps[0:P, 0:n], C.onesf[0:P, 0:P], sq[0:P, 0:n], start=True, stop=True),
         reads=[sq, 'consts'], writes=[ps])
    k.op('act', lambda en: en.activation(r[0:P, 0:n], ps[0:P, 0:n], AF.Sqrt, bias=eps_col, scale=scale),
         reads=[ps, 'consts'], writes=[r])
    k.op('dve', lambda en: en.reciprocal(r[0:P, 0:n], r[0:P, 0:n]), reads=[r], writes=[r])
    k.op('dve', lambda en: en.scalar_tensor_tensor(dst_ap, sap, gain_col, r[0:P, 0:n], ALU.mult, ALU.mult),
         reads=[skey, r, 'consts'], writes=[dst_key])


def nsa_group(k, nc, C, U, j, g):
    L, T = U.L, U.T
    NKT = (L + 127) // 128
    with ExitStack() as es:
        pfx = '%s%d%d_' % (U.tag, j, g)
        sb = lambda n, s, d=F32: es.enter_context(nc.sbuf_tensor(pfx + 'g_' + n, s, d))
        psm = lambda n: k.reg_psum(es.enter_context(nc.psum_tensor(pfx + 'gp_' + n, [128, 512], F32)))
        KsT = sb('KsT', [70, NKT * 128], BF16)
        KwT = sb('KwT', [70, NKT * 128], BF16)
        Vs = sb('Vs', [128, NKT, 65], BF16)
        Vw = sb('Vw', [128, NKT, 65], BF16)
        KcT = sb('KcT', [70, 512], BF16)
        Vc = sb('Vc', [128, 4, 65], BF16)
        W = dict(sq=sb('sq', [64, 512]), r=sb('r', [64, 512]), psn=psm('psn'))
        S = [psm('S0'), psm('S1')]
        Ocmp, Osel, Owin = psm('Ocmp'), psm('Osel'), psm('Owin')
        M0, M1 = psm('M0'), psm('M1')
        gk = C.gains[0:64, :]
        jc = j * 8

        with ExitStack() as es2:
            sb2 = lambda n, s, d=F32: es2.enter_context(nc.sbuf_tensor(pfx + 'c_' + n, s, d))
            Wc = sb2('Wc', [64, 64, 64])
            PEm = sb2('PEm', [64, 64])
            Lc = U.Lc
            nblk = Lc // 16 - 1
            X = sb2('X', [64, Lc])
            comp = sb2('comp', [64, 512])
            bcol = sb2('bcol', [64, 2])
            k.dma(Wc[:], U.a_cmp_w[j].rearrange("c (hl d) e -> d (c hl) e", d=64), writes=[Wc])
            k.dma(PEm[:], U.a_cmp_pos[j].rearrange("c hl d -> d (c hl)"), writes=[PEm], allow_slow_non_contiguous=True)
            for c in range(2):
                k.dma(X[:, :], U.ct_src(c, g), writes=[X])
                for hl in range(32):
                    k.op('pe', lambda en: en.matmul(M0[0:64, 0:1], Wc[:, c * 32 + hl, :], PEm[:, c * 32 + hl:c * 32 + hl + 1],
                                                    start=(hl == 0), stop=(hl == 31)), reads=[Wc, PEm], writes=[M0])
                k.op('dve', lambda en: en.tensor_copy(bcol[:, c:c + 1], M0[0:64, 0:1]), reads=[M0], writes=[bcol])
                for hl in range(32):
                    half, l = hl // 16, hl % 16
                    st0 = 16 * half + l
                    k.op('pe', lambda en: en.matmul(M1[0:64, 0:nblk], Wc[:, c * 32 + hl, :], X[:, bass.ds(st0, nblk, 16)],
                                                    start=(hl == 0), stop=(hl == 31)), reads=[Wc, X], writes=[M1])
                k.op('pool', lambda en: en.memset(comp[:, nblk:512], 0.0), writes=[comp])
                k.op('act', lambda en: en.activation(comp[:, 0:nblk], M1[0:64, 0:nblk], AF.Identity, bias=bcol[:, c:c + 1]),
                     reads=[M1, bcol], writes=[comp])
                if c == 0:
                    fm_norm(k, C, W, comp, 512, KcT[0:64, 0:512], KcT, gk[:, jc + 1:jc + 2], gk[:, 16:17], 1.0 / 64)
                else:
                    for kt in range(4):
                        k.op('pe', lambda en: en.transpose(M0[0:128, 0:64], comp[:, kt * 128:(kt + 1) * 128], C.ident[0:64, 0:64]),
                             reads=[comp, 'consts'], writes=[M0])
                        k.op('dve', lambda en: en.tensor_copy(Vc[:, kt, 0:64], M0[0:128, 0:64]), reads=[M0], writes=[Vc])
                    k.op('pool', lambda en: en.memset(Vc[:, :, 64:65], 1.0), writes=[Vc])
            k.dma(KcT[64:70, :], U.tab['kaugc'][:, :], writes=[KcT])
            k.barrier()

        with ExitStack() as es2:
            sb2 = lambda n, s, d=F32: es2.enter_context(nc.sbuf_tensor(pfx + 'k_' + n, s, d))
            xs_ = [sb2('x%d' % i, [64, 512]) for i in range(2)]
            vst = sb2('vst', [128, NKT, 64])
            k.op('pool', lambda en: en.memset(vst[:], 0.0), writes=[vst])
            for (KT, V, ksrc, vsrc, gcol) in [(KsT, Vs, U.ks_src, U.vs_src, jc + 2), (KwT, Vw, U.kw_src, U.vw_src, jc + 3)]:
                ci = 0
                for c0 in range(0, L, 512):
                    n = min(512, L - c0)
                    x = xs_[ci % 2]
                    ci += 1
                    pieces = ksrc(g, c0, n)
                    if not pieces:
                        ci -= 1
                        continue
                    for pi_, (ap, lo, hi) in enumerate(pieces):
                        k.dma(x[:, lo:hi], ap, writes=[x], acc=(pi_ > 0), allow_slow_non_contiguous=True)
                    if U.k_prenormed:
                        k.op('dve', lambda en: en.tensor_copy(KT[0:64, c0:c0 + n], x[:, 0:n]), reads=[x], writes=[KT])
                    else:
                        fm_norm(k, C, W, x, n, KT[0:64, c0:c0 + n], KT, gk[:, gcol:gcol + 1], gk[:, 16:17], 1.0 / 64)
                k.dma(KT[64:70, 0:L], U.tab['kaug'][:, 0:L], writes=[KT])
                for (ap, dst) in vsrc(g, vst):
                    k.dma(dst, ap, writes=[vst], acc=True)
                NF = L // 128
                k.op('dve', lambda en: en.tensor_copy(V[:, 0:NF, 0:64], vst[:, 0:NF, :]), reads=[vst], writes=[V])
                if L % 128:
                    rem = L % 128
                    k.op('pool', lambda en: en.memset(V[:, NF, :], 0.0), reads=[], writes=[V])
                    k.op('dve', lambda en: en.tensor_copy(V[0:rem, NF, 0:64], vst[0:rem, NF, :]), reads=[vst], writes=[V])
                k.op('pool', lambda en: en.memset(V[:, :, 64:65], 1.0), writes=[V])
            k.barrier()

        with ExitStack() as es2:
            sb2 = lambda n, s, d=F32: es2.enter_context(nc.sbuf_tensor(pfx + 'q_' + n, s, d))
            qraw = [sb2('qraw%d' % i, [64, 512]) for i in range(2)]
            QA = [sb2('QA%d' % i, [70, 512], BF16) for i in range(2)]
            eT = [sb2('eT%d' % i, [128, 512], BF16) for i in range(4)]
            P = [sb2('P%d' % i, [128, 512], BF16) for i in range(3)]
            rrow = sb2('rrow', [128, 512])
            rb = sb2('rb', [128, 512])
            oc = sb2('oc', [64, 512])
            t1 = sb2('t1', [64, 512])
            acc = sb2('acc', [64, 512])
            score = sb2('score', [128, 128])
            stmp = sb2('stmp', [128, 128])
            m8 = sb2('m8', [128, 16])
            NM = sb2('NM', [128, 128])
            NMT = sb2('NMT', [128, 512], BF16)
            gsb = sb2('gsb', [48, 128])
            zsb = sb2('zsb', [64, 512])
            si = 0
            pi = 0
            for qi, (q0, nq) in enumerate(U.qtiles):
                HQ = 4 * nq
                i = (U.past + q0) // 128
                qr, qa = qraw[qi % 2], QA[qi % 2]
                k.dma(qr[:, 0:HQ].rearrange("d (h q) -> d h q", h=4),
                      U.QT[256 * g:256 * g + 256, q0:q0 + nq].rearrange("(h d) q -> d h q", d=64), writes=[qr])
                fm_norm(k, C, W, qr, HQ, qa[0:64, 0:HQ], qa, gk[:, jc:jc + 1], gk[:, 17:18], 1.0)
                k.dma(qa[64:70, 0:HQ].rearrange("r (h q) -> r h q", h=4), U.tab['qaug'][:, qi, 4 * g:4 * g + 4, 0:nq],
                      writes=[qa], acc=True)
                tiles = [kt for kt in range(4) if i - 16 * kt >= 0]
                for idx, kt in enumerate(tiles):
                    Sp = S[si % 2]
                    si += 1
                    d = i - 16 * kt
                    msk = (d <= 16) if U.past == 0 else (kt == 3)
                    k.op('pe', lambda en: en.matmul(Sp[0:128, 0:HQ], KcT[:, kt * 128:(kt + 1) * 128], qa[0:70, 0:HQ],
                                                    start=True, stop=not msk), reads=[KcT, qa], writes=[Sp])
                    if msk:
                        dd = min(d, 16)
                        k.op('pe', lambda en: en.matmul(Sp[0:128, 0:HQ], C.identb[:, :], C.cm[:, dd, :, 0:nq],
                                                        start=False, stop=True), reads=['consts'], writes=[Sp])
                    e = eT[idx]
                    k.op('act', lambda en: en.activation(e[:, 0:HQ], Sp[0:128, 0:HQ], AF.Exp), reads=[Sp], writes=[e])
                    k.op('pe', lambda en: en.matmul(Ocmp[0:65, 0:HQ], Vc[:, kt, :], e[:, 0:HQ],
                                                    start=(idx == 0), stop=(idx == len(tiles) - 1)), reads=[Vc, e], writes=[Ocmp])
                k.op('dve', lambda en: en.tensor_scalar_add(rrow[64:65, 0:HQ], Ocmp[64:65, 0:HQ], TINY), reads=[Ocmp], writes=[rrow])
                k.op('dve', lambda en: en.reciprocal(rrow[64:65, 0:HQ], rrow[64:65, 0:HQ]), reads=[rrow], writes=[rrow])
                k.op('pe', lambda en: en.matmul(M0[0:128, 0:HQ], C.onesf[64:65, 0:128], rrow[64:65, 0:HQ], start=True, stop=True),
                     reads=[rrow, 'consts'], writes=[M0])
                k.op('act', lambda en: en.copy(rb[:, 0:HQ], M0[0:128, 0:HQ]), reads=[M0], writes=[rb])
                k.op('dve', lambda en: en.tensor_tensor(oc[:, 0:HQ], Ocmp[0:64, 0:HQ], rb[0:64, 0:HQ], ALU.mult),
                     reads=[Ocmp, rb], writes=[oc])
                nmm = 4 * len(tiles)
                cnt = 0
                for idx, kt in enumerate(tiles):
                    e = eT[idx]
                    k.op('dve', lambda en: en.tensor_tensor(e[:, 0:HQ], e[:, 0:HQ], rb[:, 0:HQ], ALU.mult), reads=[e, rb], writes=[e])
                    for h in range(4):
                        k.op('pe', lambda en: en.matmul(M1[0:nq, 0:128], e[:, h * nq:(h + 1) * nq], C.poolm[:, kt, :],
                                                        start=(cnt == 0), stop=(cnt == nmm - 1)), reads=[e, 'consts'], writes=[M1])
                        cnt += 1
                bon = U.bonus_ap(C, qi, nq)
                k.op('dve', lambda en: en.tensor_tensor(score[0:nq, :], M1[0:nq, 0:128], bon, ALU.add),
                     reads=[M1, 'consts'], writes=[score])
                k.op('dve', lambda en: en.tensor_scalar_add(score[0:nq, 0:1], score[0:nq, 0:1], 1e4), reads=[score], writes=[score])
                k.op('dve', lambda en: en.max(m8[0:nq, 0:8], score[0:nq, :]), reads=[score], writes=[m8])
                k.op('dve', lambda en: en.match_replace(stmp[0:nq, :], m8[0:nq, 0:8], score[0:nq, :], -3e38),
                     reads=[score, m8], writes=[stmp])
                k.op('dve', lambda en: en.max(m8[0:nq, 8:16], stmp[0:nq, :]), reads=[stmp], writes=[m8])
                tc_ = U.thr_col
                k.op('dve', lambda en: en.tensor_scalar(NM[0:nq, :], score[0:nq, :], m8[0:nq, tc_:tc_ + 1], NEG, ALU.is_lt, ALU.mult),
                     reads=[score, m8], writes=[NM])
                k.op('pe', lambda en: en.transpose(M0[0:128, 0:nq], NM[0:nq, :], C.ident[0:nq, 0:nq]), reads=[NM, 'consts'], writes=[M0])
                k.op('act', lambda en: en.copy(NMT[:, 0:HQ].rearrange("p (h q) -> p h q", h=4),
                                               M0[0:128, 0:nq].unsqueeze(1).to_broadcast([128, 4, nq])), reads=[M0], writes=[NMT])
                last = NKT - 1 if U.past > 0 else i
                for (Ops, KT, V, kts, use_blk) in [(Osel, KsT, Vs, list(range(0, last + 1)), True),
                                                   (Owin, KwT, Vw, list(range(max(0, i - 4), last + 1)), False)]:
                    for n_i, kt in enumerate(kts):
                        nk = min(128, L - kt * 128)
                        Sp = S[si % 2]
                        si += 1
                        extra = []
                        if use_blk and kt < 64:
                            extra.append((C.emat[:, kt, 0:nk], NMT[:, 0:HQ], NMT))
                        if kt == last:
                            extra.append((C.identb[0:nk, 0:nk], C.caus[0:nk, :, 0:nq], 'consts'))
                        if (not use_blk) and kt == i - 4:
                            extra.append((C.identb[0:nk, 0:nk], C.wlow[0:nk, :, 0:nq], 'consts'))
                        k.op('pe', lambda en: en.matmul(Sp[0:nk, 0:HQ], KT[:, kt * 128:kt * 128 + nk], qa[0:70, 0:HQ],
                                                        start=True, stop=(len(extra) == 0)), reads=[KT, qa], writes=[Sp])
                        for xi_, (l_, r_, key_) in enumerate(extra):
                            k.op('pe', lambda en: en.matmul(Sp[0:nk, 0:HQ], l_, r_, start=False, stop=(xi_ == len(extra) - 1)),
                                 reads=[key_, 'consts'], writes=[Sp])
                        p = P[pi % 3]
                        pi += 1
                        k.op('act', lambda en: en.activation(p[0:nk, 0:HQ], Sp[0:nk, 0:HQ], AF.Exp), reads=[Sp], writes=[p])
                        k.op('pe', lambda en: en.matmul(Ops[0:65, 0:HQ], V[0:nk, kt, :], p[0:nk, 0:HQ],
                                                        start=(n_i == 0), stop=(n_i == len(kts) - 1)), reads=[V, p], writes=[Ops])
                k.dma(gsb[:, 0:nq], U.GT[:, q0:q0 + nq], writes=[gsb])
                k.op('act', lambda en: en.activation(gsb[:, 0:nq], gsb[:, 0:nq], AF.Sigmoid, bias=C.gateb[:, j:j + 1]),
                     reads=[gsb, 'consts'], writes=[gsb])
                for br, Ops in enumerate([None, Osel, Owin]):
                    for h in range(4):
                        rsel = br * 16 + 4 * g + h
                        k.op('pe', lambda en: en.matmul(M1[0:64, h * nq:(h + 1) * nq], C.sel[:, rsel, :], gsb[:, 0:nq],
                                                        start=True, stop=True), reads=[gsb, 'consts'], writes=[M1])
                    if br == 0:
                        k.op('dve', lambda en: en.tensor_tensor(acc[:, 0:HQ], oc[:, 0:HQ], M1[0:64, 0:HQ], ALU.mult),
                             reads=[oc, M1], writes=[acc])
                        continue
                    k.op('dve', lambda en: en.tensor_scalar_add(rrow[64:65, 0:HQ], Ops[64:65, 0:HQ], TINY), reads=[Ops], writes=[rrow])
                    k.op('dve', lambda en: en.reciprocal(rrow[64:65, 0:HQ], rrow[64:65, 0:HQ]), reads=[rrow], writes=[rrow])
                    k.op('pe', lambda en: en.matmul(M0[0:64, 0:HQ], C.onesf[64:65, 0:64], rrow[64:65, 0:HQ], start=True, stop=True),
                         reads=[rrow, 'consts'], writes=[M0])
                    k.op('act', lambda en: en.copy(rb[0:64, 0:HQ], M0[0:64, 0:HQ]), reads=[M0], writes=[rb])
                    k.op('dve', lambda en: en.tensor_tensor(t1[:, 0:HQ], Ops[0:64, 0:HQ], rb[0:64, 0:HQ], ALU.mult),
                         reads=[Ops, rb], writes=[t1])
                    k.op('dve', lambda en: en.tensor_tensor(t1[:, 0:HQ], t1[:, 0:HQ], M1[0:64, 0:HQ], ALU.mult),
                         reads=[t1, M1], writes=[t1])
                    k.op('pool', lambda en: en.tensor_tensor(acc[:, 0:HQ], acc[:, 0:HQ], t1[:, 0:HQ], ALU.add),
                         reads=[acc, t1], writes=[acc])
                zv = U.ZT[256 * g:256 * g + 256, q0:q0 + nq].rearrange("(h d) q -> d h q", d=64)
                k.dma(zsb[:, 0:HQ].rearrange("d (h q) -> d h q", h=4), zv, writes=[zsb])
                k.op('act', lambda en: en.activation(zsb[:, 0:HQ], zsb[:, 0:HQ], AF.Silu), reads=[zsb], writes=[zsb])
                k.op('dve', lambda en: en.tensor_tensor(zsb[:, 0:HQ], zsb[:, 0:HQ], acc[:, 0:HQ], ALU.mult), reads=[zsb, acc], writes=[zsb])
                k.dma(U.MT[256 * g:256 * g + 256, q0:q0 + nq].rearrange("(h d) q -> d h q", d=64),
                      zsb[:, 0:HQ].rearrange("d (h q) -> d h q", h=4), reads=[zsb], writes=[('MT', U.tag)], q='pool', acc=True)
            k.barrier()


G3_STEPS = None


def run_interleaved(gens):
    gens = list(gens)
    rounds = 0
    while gens:
        if G3_STEPS is not None and rounds >= G3_STEPS:
            break
        rounds += 1
        nxt = []
        for g_ in gens:
            try:
                next(g_)
                nxt.append(g_)
            except StopIteration:
                pass
        gens = nxt


def gdn_layer(k, nc, C, U, jb, P_):
    T = U.T
    NCH = (T + 127) // 128
    NS = NCH * 8
    pfx = 'gd%s_' % U.tag
    with ExitStack() as es:
        sb = lambda n, s, d=F32: es.enter_context(nc.sbuf_tensor(pfx + 'a_' + n, s, d))
        cw = sb('cw', [128, 24, 4])
        xh = [sb('xh%d' % i, [128, 515]) for i in range(2)]
        y = [sb('y%d' % i, [128, 512]) for i in range(2)]
        W = dict(sq=sb('sq', [128, 512]), r=sb('r', [128, 512]),
                 psn=k.reg_psum(es.enter_context(nc.psum_tensor(pfx + 'a_psn', [128, 512], F32))))
        for ft in range(24):
            k.dma(cw[:, ft, :], P_['b_conv_w'][jb][:, ft * 128:(ft + 1) * 128].rearrange("j p -> p j"), writes=[cw], acc=True,
                  allow_slow_non_contiguous=True)
        ci = 0
        for ft in range(24):
            for c0 in range(0, T, 512):
                n = min(512, T - c0)
                x_, y_ = xh[ci % 2], y[ci % 2]
                ci += 1
                rows = U.GQKV[ft * 128:(ft + 1) * 128, :]
                if c0 == 0:
                    U.conv_hist(k, x_, ft)
                    k.dma(x_[:, 3:3 + n], rows[:, 0:n], writes=[x_], acc=True)
                else:
                    k.dma(x_[:, 0:3 + n], rows[:, c0 - 3:c0 + n], writes=[x_])
                k.op('dve', lambda en: en.tensor_scalar(y_[:, 0:n], x_[:, 3:3 + n], cw[:, ft, 3:4], None, ALU.mult), reads=[x_, cw], writes=[y_])
                for jj in (2, 1, 0):
                    k.op('dve', lambda en: en.scalar_tensor_tensor(y_[:, 0:n], x_[:, jj:jj + n], cw[:, ft, jj:jj + 1], y_[:, 0:n],
                                                                   ALU.mult, ALU.add), reads=[x_, cw, y_], writes=[y_])
                k.op('act', lambda en: en.activation(y_[:, 0:n], y_[:, 0:n], AF.Silu), reads=[y_], writes=[y_])
                if ft < 16:
                    gcol = C.gains[:, 18:19] if ft < 8 else C.gains[:, 19:20]
                    fm_norm(k, C, W, y_, n, y_[:, 0:n], y_, gcol, C.epsc[:, 0:1], 1.0, P=128)
                k.dma(U.GN[ft * 128:(ft + 1) * 128, c0:c0 + n], y_[:, 0:n], reads=[y_], writes=['GN'], q='pool', acc=True)
    k.barrier()
    if checkpoint('g1'):
        return
    with ExitStack() as es:
        sb = lambda n, s, d=F32: es.enter_context(nc.sbuf_tensor(pfx + 'b_' + n, s, d))
        psb = lambda n: k.reg_psum(es.enter_context(nc.psum_tensor(pfx + 'b_' + n, [128, 512], F32)))
        bd = sb('bd', [128, NCH, 16])
        Beta, nBeta, G = sb('Beta', [128, NS]), sb('nBeta', [128, NS]), sb('G', [128, NS])
        t_a, t_b = sb('ta', [128, NS]), sb('tb', [128, NS])
        eG, eGl, eD, bE = sb('eG', [128, NS]), sb('eGl', [128, NS]), sb('eD', [128, NS]), sb('bE', [128, NS])
        dtb, nal = sb('dtb', [128, 8]), sb('nal', [128, 8])
        pa, pb = psb('pa'), psb('pb')
        k.op('pool', lambda en: en.memset(bd[:], 0.0), writes=[bd])
        NF = T // 128
        if NF:
            k.dma(bd[:, 0:NF, :], U.GBD[0:NF * 128, :].rearrange("(c p) f -> p c f", p=128), writes=[bd])
        if T % 128:
            k.dma(bd[0:T % 128, NF, :], U.GBD[NF * 128:T, :], writes=[bd])
        k.dma(dtb[:], P_['b_dt_bias'][jb].partition_broadcast(128), writes=[dtb])
        k.dma(nal[:], P_['b_a_log'][jb].partition_broadcast(128), writes=[nal])
        k.op('act', lambda en: en.activation(nal[:], nal[:], AF.Exp), reads=[nal], writes=[nal])
        k.op('dve', lambda en: en.tensor_scalar(nal[:], nal[:], -1.0, None, ALU.mult), reads=[nal], writes=[nal])
        v3 = lambda t_: t_[:, 0:NS].rearrange("p (c h) -> p c h", h=8)
        k.op('act', lambda en: en.activation(v3(Beta), bd[:, :, 0:8], AF.Sigmoid), reads=[bd], writes=[Beta])
        k.op('dve', lambda en: en.tensor_scalar(nBeta[:], Beta[:], -1.0, None, ALU.mult), reads=[Beta], writes=[nBeta])
        k.op('dve', lambda en: en.tensor_tensor(v3(t_a), bd[:, :, 8:16], dtb[:].unsqueeze(1).to_broadcast([128, NCH, 8]), ALU.add),
             reads=[bd, dtb], writes=[t_a])
        k.op('dve', lambda en: en.tensor_scalar_max(eD[:], t_a[:], 0.0), reads=[t_a], writes=[eD])
        k.op('dve', lambda en: en.scalar_tensor_tensor(t_b[:], eD[:], -2.0, t_a[:], ALU.mult, ALU.add), reads=[t_a, eD], writes=[t_b])
        k.op('act', lambda en: en.activation(t_b[:], t_b[:], AF.Exp), reads=[t_b], writes=[t_b])
        k.op('act', lambda en: en.activation(t_b[:], t_b[:], AF.Ln, bias=C.onesf[:, 0:1], scale=1.0), reads=[t_b, 'consts'], writes=[t_b])
        k.op('dve', lambda en: en.tensor_tensor(t_a[:], eD[:], t_b[:], ALU.add), reads=[eD, t_b], writes=[t_a])
        k.op('dve', lambda en: en.tensor_tensor(v3(G), v3(t_a), nal[:].unsqueeze(1).to_broadcast([128, NCH, 8]), ALU.mult),
             reads=[t_a, nal], writes=[G])
        if T % 128:
            assert T % 128 == 4
            for t_ in (G, Beta, nBeta):
                k.op('dve', lambda en: en.tensor_scalar(t_[:, (NCH - 1) * 8:NCH * 8], t_[:, (NCH - 1) * 8:NCH * 8], C.pm4[:, 0:1], None, ALU.mult),
                     reads=[t_, 'consts'], writes=[t_])
        k.op('pe', lambda en: en.matmul(pa[:, 0:NS], C.tri[:, :], G[:, 0:NS], start=True, stop=True), reads=[G, 'consts'], writes=[pa])
        k.op('pe', lambda en: en.matmul(pb[:, 0:NS], C.onesf[:, :], G[:, 0:NS], start=True, stop=True), reads=[G, 'consts'], writes=[pb])
        k.op('act', lambda en: en.activation(eG[:], pa[:, 0:NS], AF.Exp), reads=[pa], writes=[eG])
        k.op('act', lambda en: en.activation(eGl[:], pb[:, 0:NS], AF.Exp), reads=[pb], writes=[eGl])
        k.op('dve', lambda en: en.tensor_copy(t_a[:], pa[:, 0:NS]), reads=[pa], writes=[t_a])
        k.op('dve', lambda en: en.tensor_tensor(t_b[:], pb[:, 0:NS], t_a[:], ALU.subtract), reads=[pb, t_a], writes=[t_b])
        k.op('act', lambda en: en.activation(eD[:], t_b[:], AF.Exp), reads=[t_b], writes=[eD])
        k.op('dve', lambda en: en.tensor_tensor(bE[:], Beta[:], eG[:], ALU.mult), reads=[Beta, eG], writes=[bE])
        k.barrier()
        if checkpoint('g2'):
            return
        with ExitStack() as es3:
            sb3 = lambda n, s, d=F32: es3.enter_context(nc.sbuf_tensor(pfx + 'c_' + n, s, d))
            H = []
            for h in range(8):
                hb = Ctx()
                hb.ps = k.reg_psum(es3.enter_context(nc.psum_tensor(pfx + 'c_ps%d' % h, [128, 512], F32))) if h < 6 else None
                for nm in ['k_tm', 'v_tm', 'Gp', 'dec', 'decT', 'YT', 'Y', 'Tt', 'wT', 'u', 'qkT', 'qeT', 'kd', 'kbe', 'vb', 'vnew', 'S', 'dg']:
                    setattr(hb, nm, sb3('%s%d' % (nm, h), [128, 128]))
                hb.qkv = sb3('qkv%d' % h, [128, 3, 128])
                hb.oT = sb3('oT%d' % h, [128, 512])
                hb.z = sb3('z%d' % h, [128, 512])
                H.append(hb)
            H[6].ps = pa
            H[7].ps = pb
            Wn = dict(sq=sb3('nsq', [128, 512]), r=sb3('nr', [128, 512]), psn=None)
            for h in range(8):
                U.state_init(k, H[h].S, h)

            def slot(hb, i, w=1):
                return hb.ps[:, i * 128:(i + w) * 128]

            def steps(h, c):
                hb = H[h]
                c0 = c * 128
                n = min(128, T - c0)
                col = c * 8 + h
                K_ = lambda i: hb.ps
                gn = U.GN.rearrange("(s hh p) t -> p s hh t", s=3, hh=8, p=128)
                if n < 128:
                    k.op('pool', lambda en: en.memset(hb.qkv[:], 0.0), writes=[hb.qkv])
                k.dma(hb.qkv[:, :, 0:n], gn[:, :, h, c0:c0 + n], writes=[hb.qkv], acc=(n < 128))
                qT, kT, vT = hb.qkv[:, 0, :], hb.qkv[:, 1, :], hb.qkv[:, 2, :]
                yield
                k.op('pe', lambda en: en.transpose(slot(hb, 0), kT, C.ident[:, :]), reads=[hb.qkv, 'consts'], writes=[K_(0)])
                k.op('pe', lambda en: en.transpose(slot(hb, 1), vT, C.ident[:, :]), reads=[hb.qkv, 'consts'], writes=[K_(1)])
                k.op('dve', lambda en: en.tensor_scalar(hb.Gp[:], C.slt[:, :], G[:, col:col + 1], None, ALU.mult), reads=['consts', G], writes=[hb.Gp])
                yield
                k.op('act', lambda en: en.copy(hb.k_tm[:], slot(hb, 0)), reads=[K_(0)], writes=[hb.k_tm])
                k.op('act', lambda en: en.copy(hb.v_tm[:], slot(hb, 1)), reads=[K_(1)], writes=[hb.v_tm])
                k.op('pe', lambda en: en.matmul(slot(hb, 2), C.tri[:, :], hb.Gp[:], start=True, stop=False), reads=[hb.Gp, 'consts'], writes=[K_(2)])
                k.op('pe', lambda en: en.matmul(slot(hb, 2), C.ident[:, :], C.mstrict[:, :], start=False, stop=True), reads=['consts'], writes=[K_(2)])
                k.op('pe', lambda en: en.matmul(slot(hb, 3), hb.Gp[:], C.tri[:, :], start=True, stop=False), reads=[hb.Gp, 'consts'], writes=[K_(3)])
                k.op('pe', lambda en: en.matmul(slot(hb, 3), C.ident[:, :], C.minclT[:, :], start=False, stop=True), reads=['consts'], writes=[K_(3)])
                yield
                k.op('act', lambda en: en.activation(hb.dec[:], slot(hb, 2), AF.Exp), reads=[K_(2)], writes=[hb.dec])
                k.op('act', lambda en: en.activation(hb.decT[:], slot(hb, 3), AF.Exp), reads=[K_(3)], writes=[hb.decT])
                k.op('pe', lambda en: en.matmul(slot(hb, 0), kT, kT, start=True, stop=True), reads=[hb.qkv], writes=[K_(0)])
                k.op('pe', lambda en: en.matmul(slot(hb, 1), kT, qT, start=True, stop=True), reads=[hb.qkv], writes=[K_(1)])
                k.op('dve', lambda en: en.tensor_scalar(hb.kbe[:], hb.k_tm[:], bE[:, col:col + 1], None, ALU.mult), reads=[hb.k_tm, bE], writes=[hb.kbe])
                k.op('dve', lambda en: en.tensor_scalar(hb.vb[:], hb.v_tm[:], Beta[:, col:col + 1], None, ALU.mult), reads=[hb.v_tm, Beta], writes=[hb.vb])
                k.op('act', lambda en: en.activation(hb.kd[:], hb.k_tm[:], AF.Copy, scale=eD[:, col:col + 1]), reads=[hb.k_tm, eD], writes=[hb.kd])
                k.op('act', lambda en: en.activation(hb.dg[:], C.ident[:, :], AF.Copy, scale=eG[:, col:col + 1]), reads=['consts', eG], writes=[hb.dg])
                yield
                k.op('dve', lambda en: en.scalar_tensor_tensor(hb.YT[:], slot(hb, 0), nBeta[:, col:col + 1], hb.dec[:], ALU.mult, ALU.mult),
                     reads=[K_(0), nBeta, hb.dec], writes=[hb.YT])
                k.op('dve', lambda en: en.tensor_tensor(hb.qkT[:], slot(hb, 1), hb.decT[:], ALU.mult), reads=[K_(1), hb.decT], writes=[hb.qkT])
                yield
                k.op('pe', lambda en: en.transpose(slot(hb, 2), hb.YT[:], C.ident[:, :]), reads=[hb.YT, 'consts'], writes=[K_(2)])
                k.op('pe', lambda en: en.matmul(slot(hb, 3), C.onesf[:, :], hb.dg[:], start=True, stop=True), reads=[hb.dg, 'consts'], writes=[K_(3)])
                yield
                k.op('act', lambda en: en.copy(hb.Y[:], slot(hb, 2)), reads=[K_(2)], writes=[hb.Y])
                k.op('dve', lambda en: en.tensor_tensor(hb.Tt[:], slot(hb, 2), C.ident[:, :], ALU.add), reads=[K_(2), 'consts'], writes=[hb.Tt])
                k.op('dve', lambda en: en.tensor_tensor(hb.qeT[:], slot(hb, 3), qT, ALU.mult), reads=[K_(3), hb.qkv], writes=[hb.qeT])
                yield
                k.op('pe', lambda en: en.matmul(slot(hb, 0), hb.YT[:], hb.Y[:], start=True, stop=True), reads=[hb.YT, hb.Y], writes=[K_(0)])
                k.op('pe', lambda en: en.matmul(slot(hb, 1), hb.Y[:], hb.YT[:], start=True, stop=True), reads=[hb.YT, hb.Y], writes=[K_(1)])
                yield
                k.op('act', lambda en: en.copy(hb.Y[:], slot(hb, 0)), reads=[K_(0)], writes=[hb.Y])
                k.op('dve', lambda en: en.tensor_copy(hb.YT[:], slot(hb, 1)), reads=[K_(1)], writes=[hb.YT])
                yield
                for kk in range(1, 7):
                    lastk = (kk == 6)
                    k.op('pe', lambda en: en.matmul(slot(hb, 2), hb.YT[:], hb.Tt[:], start=True, stop=True), reads=[hb.YT, hb.Tt], writes=[K_(2)])
                    if not lastk:
                        k.op('pe', lambda en: en.matmul(slot(hb, 0), hb.YT[:], hb.Y[:], start=True, stop=True), reads=[hb.YT, hb.Y], writes=[K_(0)])
                        k.op('pe', lambda en: en.matmul(slot(hb, 1), hb.Y[:], hb.YT[:], start=True, stop=True), reads=[hb.YT, hb.Y], writes=[K_(1)])
                    yield
                    k.op('dve', lambda en: en.tensor_tensor(hb.Tt[:], hb.Tt[:], slot(hb, 2), ALU.add), reads=[K_(2), hb.Tt], writes=[hb.Tt])
                    if not lastk:
                        k.op('act', lambda en: en.copy(hb.Y[:], slot(hb, 0)), reads=[K_(0)], writes=[hb.Y])
                        k.op('act', lambda en: en.copy(hb.YT[:], slot(hb, 1)), reads=[K_(1)], writes=[hb.YT])
                    yield
                k.op('pe', lambda en: en.matmul(slot(hb, 0), hb.kbe[:], hb.Tt[:], start=True, stop=True), reads=[hb.kbe, hb.Tt], writes=[K_(0)])
                k.op('pe', lambda en: en.matmul(slot(hb, 1), hb.Tt[:], hb.vb[:], start=True, stop=True), reads=[hb.vb, hb.Tt], writes=[K_(1)])
                yield
                k.op('act', lambda en: en.copy(hb.wT[:], slot(hb, 0)), reads=[K_(0)], writes=[hb.wT])
                k.op('act', lambda en: en.copy(hb.u[:], slot(hb, 1)), reads=[K_(1)], writes=[hb.u])
                yield
                k.op('pe', lambda en: en.matmul(slot(hb, 2), hb.wT[:], hb.S[:], start=True, stop=True), reads=[hb.wT, hb.S], writes=[K_(2)])
                yield
                k.op('dve', lambda en: en.tensor_tensor(hb.vnew[:], hb.u[:], slot(hb, 2), ALU.subtract), reads=[K_(2), hb.u], writes=[hb.vnew])
                yield
                k.op('pe', lambda en: en.matmul(slot(hb, 3), hb.S[:], hb.qeT[:], start=True, stop=False), reads=[hb.qeT, hb.S], writes=[K_(3)])
                k.op('pe', lambda en: en.matmul(slot(hb, 3), hb.vnew[:], hb.qkT[:], start=False, stop=True), reads=[hb.vnew, hb.qkT], writes=[K_(3)])
                k.op('pe', lambda en: en.matmul(slot(hb, 0), hb.kd[:], hb.vnew[:], start=True, stop=True), reads=[hb.kd, hb.vnew], writes=[K_(0)])
                yield
                cc = c % 4
                k.op('act', lambda en: en.copy(hb.oT[:, cc * 128:(cc + 1) * 128], slot(hb, 3)), reads=[K_(3)], writes=[hb.oT])
                k.op('dve', lambda en: en.scalar_tensor_tensor(hb.S[:], hb.S[:], eGl[:, col:col + 1], slot(hb, 0), ALU.mult, ALU.add),
                     reads=[K_(0), hb.S, eGl], writes=[hb.S])
                yield
                if cc == 3 or c == NCH - 1:
                    t0 = (c - cc) * 128
                    nn = min(T, c0 + 128) - t0
                    Wn['psn'] = hb.ps
                    k.dma(hb.z[:, 0:nn], U.GZ[h * 128:(h + 1) * 128, t0:t0 + nn], writes=[hb.z])
                    k.op('act', lambda en: en.activation(hb.z[:, 0:nn], hb.z[:, 0:nn], AF.Silu), reads=[hb.z], writes=[hb.z])
                    sq, r = Wn['sq'], Wn['r']
                    k.op('act', lambda en: en.activation(sq[:, 0:nn], hb.oT[:, 0:nn], AF.Square), reads=[hb.oT], writes=[sq])
                    k.op('pe', lambda en: en.matmul(hb.ps[:, 0:nn], C.onesf[:, :], sq[:, 0:nn], start=True, stop=True),
                         reads=[sq, 'consts'], writes=[K_(0), K_(1), K_(2), K_(3)])
                    k.op('act', lambda en: en.activation(r[:, 0:nn], hb.ps[:, 0:nn], AF.Sqrt, bias=C.epsc[:, 0:1], scale=1.0 / 128),
                         reads=[K_(0), K_(1), K_(2), K_(3)], writes=[r])
                    k.op('dve', lambda en: en.reciprocal(r[:, 0:nn], r[:, 0:nn]), reads=[r], writes=[r])
                    k.op('dve', lambda en: en.scalar_tensor_tensor(hb.oT[:, 0:nn], hb.oT[:, 0:nn], C.ogain[:, jb:jb + 1], r[:, 0:nn], ALU.mult, ALU.mult),
                         reads=[hb.oT, r, 'consts'], writes=[hb.oT])
                    k.op('dve', lambda en: en.tensor_tensor(hb.oT[:, 0:nn], hb.oT[:, 0:nn], hb.z[:, 0:nn], ALU.mult), reads=[hb.oT, hb.z], writes=[hb.oT])
                    k.dma(U.MT[h * 128:(h + 1) * 128, t0:t0 + nn], hb.oT[:, 0:nn], reads=[hb.oT], writes=[('MT', U.tag)], q='pool', acc=True)
                yield

            for c in range(NCH if G3_STEPS is None else 1):
                run_interleaved([steps(h, c) for h in range(8)])
            for h in range(8):
                U.state_out(k, H[h].S, h)
    k.barrier()


def conf_layer(k, nc, C, U, jc, P_):
    T = U.T
    pfx = 'cf%s_' % U.tag
    with ExitStack() as es:
        sb = lambda n, s, d=F32: es.enter_context(nc.sbuf_tensor(pfx + n, s, d))
        psb = lambda n: k.reg_psum(es.enter_context(nc.psum_tensor(pfx + n, [128, 512], F32)))
        cw = sb('cw', [128, 8, 31])
        cb_ = sb('cbias', [128, 8])
        lg, lb = sb('lg', [128, 8]), sb('lb', [128, 8])
        a_ = [sb('a%d' % i, [128, 512]) for i in range(2)]
        b_ = [sb('b%d' % i, [128, 512]) for i in range(2)]
        uh = [sb('uh%d' % i, [128, 542]) for i in range(2)]
        cbuf = sb('cbuf', [128, 8, 512])
        sq = sb('sq', [128, 512])
        mean, rstd, tmp = sb('mean', [128, 512]), sb('rstd', [128, 512]), sb('tmp', [128, 512])
        zt = [sb('z%d' % i, [128, 512]) for i in range(2)]
        ps1, ps2 = psb('ps1'), psb('ps2')
        for ft in range(8):
            k.dma(cw[:, ft, :], P_['c_conv_w'][jc][:, ft * 128:(ft + 1) * 128].rearrange("j p -> p j"), writes=[cw], acc=True,
                  allow_slow_non_contiguous=True)
        k.dma(cb_[:], P_['c_conv_b'][jc].rearrange("(ft p) -> p ft", p=128), writes=[cb_], allow_slow_non_contiguous=True)
        k.dma(lg[:], P_['c_ln_g'][jc].rearrange("(ft p) -> p ft", p=128), writes=[lg], allow_slow_non_contiguous=True)
        k.dma(lb[:], P_['c_ln_b'][jc].rearrange("(ft p) -> p ft", p=128), writes=[lb], allow_slow_non_contiguous=True)
        ci = 0
        for ft in range(8):
            for c0 in range(0, T, 512):
                n = min(512, T - c0)
                a, b = a_[ci % 2], b_[ci % 2]
                ci += 1
                k.dma(a[:, 0:n], U.CA[ft * 128:(ft + 1) * 128, c0:c0 + n], writes=[a])
                k.dma(b[:, 0:n], U.CB[ft * 128:(ft + 1) * 128, c0:c0 + n], writes=[b])
                k.op('act', lambda en: en.activation(b[:, 0:n], b[:, 0:n], AF.Sigmoid), reads=[b], writes=[b])
                k.op('dve', lambda en: en.tensor_tensor(a[:, 0:n], a[:, 0:n], b[:, 0:n], ALU.mult), reads=[a, b], writes=[a])
                k.dma(U.CU[ft * 128:(ft + 1) * 128, c0:c0 + n], a[:, 0:n], reads=[a], writes=['CU'], q='pool', acc=True)
        k.barrier()
        if checkpoint('c1'):
            return
        ui = 0
        zi = 0
        for c0 in range(0, T, 512):
            n = min(512, T - c0)
            for ft in range(8):
                u = uh[ui % 2]
                ui += 1
                rows = U.CU[ft * 128:(ft + 1) * 128, :]
                if c0 == 0:
                    U.cconv_hist(k, u, ft)
                    k.dma(u[:, 30:30 + n], rows[:, 0:n], writes=[u], acc=True)
                else:
                    k.dma(u[:, 0:30 + n], rows[:, c0 - 30:c0 + n], writes=[u])
                y = cbuf[:, ft, 0:n]
                k.op('dve', lambda en: en.tensor_scalar(y, u[:, 30:30 + n], cw[:, ft, 30:31], cb_[:, ft:ft + 1], ALU.mult, ALU.add),
                     reads=[u, cw, cb_], writes=[(cbuf, ft)])
                for jj in range(30):
                    k.op('dve', lambda en: en.scalar_tensor_tensor(y, u[:, jj:jj + n], cw[:, ft, jj:jj + 1], y, ALU.mult, ALU.add),
                         reads=[u, cw, (cbuf, ft)], writes=[(cbuf, ft)])
                k.op('pe', lambda en: en.matmul(ps1[:, 0:n], C.onesf[:, :], y, start=(ft == 0), stop=(ft == 7)),
                     reads=[(cbuf, ft), 'consts'], writes=[ps1])
                k.op('act', lambda en: en.activation(sq[:, 0:n], y, AF.Square), reads=[(cbuf, ft)], writes=[sq])
                k.op('pe', lambda en: en.matmul(ps2[:, 0:n], C.onesf[:, :], sq[:, 0:n], start=(ft == 0), stop=(ft == 7)),
                     reads=[sq, 'consts'], writes=[ps2])
            k.op('act', lambda en: en.activation(mean[:, 0:n], ps1[:, 0:n], AF.Copy, scale=1.0 / 1024), reads=[ps1], writes=[mean])
            k.op('dve', lambda en: en.tensor_tensor(tmp[:, 0:n], mean[:, 0:n], mean[:, 0:n], ALU.mult), reads=[mean], writes=[tmp])
            k.op('dve', lambda en: en.scalar_tensor_tensor(tmp[:, 0:n], ps2[:, 0:n], 1.0 / 1024, tmp[:, 0:n], ALU.mult, ALU.subtract),
                 reads=[ps2, tmp], writes=[tmp])
            k.op('act', lambda en: en.activation(rstd[:, 0:n], tmp[:, 0:n], AF.Sqrt, bias=C.epsc[:, 0:1], scale=1.0), reads=[tmp, 'epsc'], writes=[rstd])
            k.op('dve', lambda en: en.reciprocal(rstd[:, 0:n], rstd[:, 0:n]), reads=[rstd], writes=[rstd])
            for ft in range(8):
                y = cbuf[:, ft, 0:n]
                z = zt[zi % 2]
                zi += 1
                k.dma(z[:, 0:n], U.CZ[ft * 128:(ft + 1) * 128, c0:c0 + n], writes=[z])
                k.op('act', lambda en: en.activation(z[:, 0:n], z[:, 0:n], AF.Silu), reads=[z], writes=[z])
                k.op('dve', lambda en: en.tensor_tensor(y, y, mean[:, 0:n], ALU.subtract), reads=[(cbuf, ft), mean], writes=[(cbuf, ft)])
                k.op('dve', lambda en: en.tensor_tensor(y, y, rstd[:, 0:n], ALU.mult), reads=[(cbuf, ft), rstd], writes=[(cbuf, ft)])
                k.op('act', lambda en: en.activation(y, y, AF.Silu, bias=lb[:, ft:ft + 1], scale=lg[:, ft:ft + 1]),
                     reads=[(cbuf, ft), lg, lb], writes=[(cbuf, ft)])
                k.op('dve', lambda en: en.tensor_tensor(z[:, 0:n], z[:, 0:n], y, ALU.mult), reads=[(cbuf, ft), z], writes=[z])
                k.dma(U.MT[ft * 128:(ft + 1) * 128, c0:c0 + n], z[:, 0:n], reads=[z], writes=[('MT', U.tag)], q='pool', acc=True)
        k.barrier()
        if checkpoint('c2'):
            return
        U.cconv_out(k, nc, C, ps1, ps2)
    k.barrier()
    checkpoint('c3')


def stage_outproj(k, nc, C, mt_ap, x_ap, y_ap, w_ap, T, tag):
    with ExitStack() as es:
        sb = lambda n, s, d=F32: es.enter_context(nc.sbuf_tensor(tag + n, s, d))
        ps = lambda n: k.reg_psum(es.enter_context(nc.psum_tensor(tag + n, [128, 512], F32)))
        wb = sb('wb', [128, 8, 1024], BF16)
        wst = [sb('wst%d' % i, [128, 8, 512], F32) for i in range(2)]
        mst = [sb('mst%d' % i, [128, 8, 128], F32) for i in range(2)]
        mb = [sb('mb%d' % i, [128, 8, 128], BF16) for i in range(2)]
        xt = [sb('xt%d' % i, [128, D], F32) for i in range(2)]
        yt = [sb('yt%d' % i, [128, D], F32) for i in range(2)]
        pp = [ps('p%d' % i) for i in range(4)]
        w_v = w_ap.rearrange("(kc p) c -> p kc c", p=128)
        for ci in range(2):
            k.dma(wst[ci][:], w_v[:, :, ci * 512:(ci + 1) * 512], writes=[wst[ci]])
            k.op(['dve', 'pool'][ci], lambda en: en.tensor_copy(wb[:, :, ci * 512:(ci + 1) * 512], wst[ci][:]),
                 reads=[wst[ci]], writes=[(wb, ci)])
        m_v = mt_ap.rearrange("(kc p) t -> p kc t", p=128)
        gi = 0
        for ti, t0 in enumerate(range(0, T, 128)):
            nt = min(128, T - t0)
            ms, mbb, x, y = mst[ti % 2], mb[ti % 2], xt[ti % 2], yt[ti % 2]
            k.dma(ms[:, :, 0:nt], m_v[:, :, t0:t0 + nt], writes=[ms])
            k.dma(x[0:nt, :], x_ap[t0:t0 + nt, :], writes=[x])
            k.op('pool', lambda en: en.tensor_copy(mbb[:, :, 0:nt], ms[:, :, 0:nt]), reads=[ms], writes=[mbb])
            for half in range(2):
                p = pp[gi % 4]
                gi += 1
                for kc in range(8):
                    k.op('pe', lambda en: en.matmul(p[0:nt, 0:512], mbb[:, kc, 0:nt], wb[:, kc, half * 512:(half + 1) * 512],
                                                    start=(kc == 0), stop=(kc == 7)), reads=[mbb, (wb, 0), (wb, 1)], writes=[p])
                k.op('dve', lambda en: en.tensor_tensor(y[0:nt, half * 512:(half + 1) * 512], p[0:nt, 0:512],
                                                        x[0:nt, half * 512:(half + 1) * 512], ALU.add), reads=[p, x], writes=[y])
            k.dma(y_ap[t0:t0 + nt, :], y[0:nt, :], reads=[y], writes=[('Y', tag)], q='pool', acc=True)
    k.barrier()


N_SEQ = 4
N_POOL = 2560
RUN_PROMPT = True
RUN_SAMPLE = True


def sample_setup(nc, dr, scr, out, tab):
    S = Ctx()
    TS = 4 * N_SEQ
    S.TS = TS
    S.pt = dr("pt", [N_SEQ, 64], I32)
    S.cache_cmp = [dr("cache_cmp%d" % i, [N_POOL * 128, 512]) for i in range(2)]
    S.cache_slc = [dr("cache_slc%d" % i, [N_POOL * 128, 512]) for i in range(2)]
    S.win_in = dr("win_in", [2, N_SEQ, 512, 512])
    S.sd_in = dr("sd_in", [N_SEQ, 8, 128, 128])
    S.sdc_in = dr("sdc_in", [N_SEQ, 3, 3072])
    S.sc_in = dr("sc_in", [N_SEQ, 30, 1024])
    S.qaug = dr("t_qaug_s", [6, 1, 16, 128], BF16)
    S.o_ys = out("o_ys", [TS, D])
    S.o_cmp = out("o_cmp_s", [2, TS, 512])
    S.o_slc = out("o_slc_s", [2, TS, 512])
    S.o_win = out("o_win_s", [2, N_SEQ, 512, 512])
    S.o_dst = out("o_dst_s", [N_SEQ, 8, 128, 128])
    S.o_dcv = out("o_dcv_s", [N_SEQ, 3, 3072])
    S.o_ccv = out("o_ccv_s", [N_SEQ, 30, 1024])
    S.BIG = scr("BIG_s", [4112, TS])
    S.GN, S.CU, S.GBD = scr("GN_s", [3072, TS]), scr("CU_s", [1024, TS]), scr("GBD_s", [TS, 16])
    S.MT = scr("MT_s", [1024, TS])
    S.Y = [scr("Y0_s", [TS, D]), scr("Y1_s", [TS, D])]
    S.kvw = scr("kvw_s", [TS, 512])
    S.CT = [scr("CTs%d" % i, [512, 8192]) for i in range(N_SEQ)]
    S.KST = [scr("KSTs%d" % i, [256, 8192]) for i in range(N_SEQ)]
    S.VS = [scr("VSs%d" % i, [8192, 256]) for i in range(N_SEQ)]
    S.KWT = [scr("KWTs%d" % i, [256, 512]) for i in range(N_SEQ)]
    return S


def sample_prepass(k, nc, C, S, j, s):
    with ExitStack() as es:
        pfx = 'pp%d%d_' % (j, s)
        pg = [es.enter_context(nc.sbuf_tensor(pfx + 'pg%d' % i, [128, 512], F32)) for i in range(3)]
        tsb = [es.enter_context(nc.sbuf_tensor(pfx + 'tsb%d' % i, [128, 512], F32)) for i in range(2)]
        tps = [k.reg_psum(es.enter_context(nc.psum_tensor(pfx + 'tps%d' % i, [128, 512], F32))) for i in range(2)]
        it = 0
        for (cache, nb, dstT, dstV) in [(S.cache_cmp[j], 4, S.CT[s], None), (S.cache_slc[j], 2, S.KST[s], S.VS[s]),
                                        (None, 2, S.KWT[s], None)]:
            ntile = 64 if cache is not None else 4
            for p in range(ntile):
                g_, t_, ps_ = pg[it % 3], tsb[it % 2], tps[it % 2]
                it += 1
                if cache is not None:
                    k.dma_ind(g_[:, :], cache, C.idx[:, s * 64 + p:s * 64 + p + 1], reads=['idx'], writes=[g_])
                else:
                    k.dma(g_[:, :], S.win_in[j, s, p * 128:(p + 1) * 128, :], writes=[g_])
                for b in range(nb):
                    k.op('pe', lambda en: en.transpose(ps_[:, b * 128:(b + 1) * 128], g_[:, b * 128:(b + 1) * 128], C.ident[:, :]),
                         reads=[g_, 'consts'], writes=[ps_])
                if it % 2:
                    k.op('act', lambda en: en.copy(t_[:, 0:nb * 128], ps_[:, 0:nb * 128]), reads=[ps_], writes=[t_])
                else:
                    k.op('dve', lambda en: en.tensor_copy(t_[:, 0:nb * 128], ps_[:, 0:nb * 128]), reads=[ps_], writes=[t_])
                k.dma(dstT.rearrange("(b q) t -> q b t", q=128)[:, :, p * 128:(p + 1) * 128],
                      t_[:, 0:nb * 128].rearrange("q (b t) -> q b t", b=nb), reads=[t_], writes=[('ppT', s)], acc=True)
                if dstV is not None:
                    k.dma(dstV[p * 128:(p + 1) * 128, :], g_[:, 256:512], reads=[g_], writes=[('ppV', s)], acc=True)
    k.barrier()


def sample_unit(S, s, tab):
    U = Ctx()
    U.tag = 's%d' % s
    U.T, U.past, U.L, U.Lc = 4, 8192, 8196, 8192
    c = slice(4 * s, 4 * s + 4)
    big = S.BIG
    U.QT, U.GT, U.ZT = big[0:1024, c], big[2048:2096, c], big[2096:3120, c]
    U.GQKV, U.GZ = big[0:3072, c], big[3072:4096, c]
    U.CA, U.CB, U.CZ = big[0:1024, c], big[1024:2048, c], big[2048:3072, c]
    U.GN, U.CU, U.GBD, U.MT = S.GN[:, c], S.CU[:, c], S.GBD[c, :], S.MT[:, c]
    U.qtiles = [(0, 4)]
    U.tab = dict(tab)
    U.tab['qaug'] = S.qaug
    U.k_prenormed = True
    U.thr_col = 14
    U.bonus_ap = lambda C_, qi, nq: C_.bonus_s[0:nq, :]
    return U


def load_nsa_consts(k, nc, C, tab, esn, pfx):
    cn = lambda n, s, d=F32: esn.enter_context(nc.sbuf_tensor(pfx + n, s, d))
    C.caus, C.wlow = cn("caus", [128, 4, 128], BF16), cn("wlow", [128, 4, 128], BF16)
    C.cm = cn("cm", [128, 17, 4, 128], BF16)
    C.poolm = cn("poolm", [128, 4, 128], BF16)
    C.bonus = cn("bonus", [128, 256])
    C.bonus_s = cn("bonus_s", [128, 128])
    C.sel = cn("sel", [48, 48, 64])
    C.emat = cn("emat", [128, 65, 128], BF16)
    for n_ in ['caus', 'wlow', 'cm', 'poolm', 'bonus', 'bonus_s', 'sel', 'emat']:
        k.dma(getattr(C, n_)[:], tab[n_], writes=['consts'], acc=True)
    k.barrier()


def sample_layer(k, nc, C, S, P_, tab, layer, x_cur, x_next, knorm_post):
    kind, j = layer % 3, layer // 3
    TS = S.TS
    gcol = P_['norm_g'][layer].rearrange("(kc p) -> p kc", p=128)
    fm = lambda ap: (lambda b0, nb, t0, nts: ap[b0:b0 + nb, t0:t0 + nts])
    if kind == 0:
        plan = [
            dict(c0=0, n=1024, mode='FM', key='QTs', dst=fm(S.BIG[0:1024])),
            dict(c0=2560, n=48, mode='FM', key='GTs', dst=fm(S.BIG[2048:2096])),
            dict(c0=2608, n=1024, mode='FM', key='ZTs', dst=fm(S.BIG[2096:3120])),
            dict(c0=1024, n=512, mode='TM', key='o_cmp_s',
                 dst=lambda b0, nb, t0, nt: [(S.o_cmp[j, t0:t0 + nt, b0:b0 + nb], 'o_cmp_s')]),
            dict(c0=1536, n=512, mode='TM', key='o_slc_s', post=knorm_post(j, 1),
                 dst=lambda b0, nb, t0, nt: [(S.o_slc[j, t0:t0 + nt, b0:b0 + nb], 'o_slc_s')]),
            dict(c0=2048, n=512, mode='TM', key='kvw_s', post=knorm_post(j, 2),
                 dst=lambda b0, nb, t0, nt: [(S.kvw[t0:t0 + nt, b0:b0 + nb], 'kvw_s')]),
        ]
        stage_inproj(k, nc, C, x_cur, TS, P_['a_w_in'][j], A_IN, gcol, plan, 'A%ds' % layer)
        esn = ExitStack()
        load_nsa_consts(k, nc, C, tab, esn, "cn%ds_" % layer)
        for s in range(N_SEQ):
            sample_prepass(k, nc, C, S, j, s)
            U = sample_unit(S, s, tab)
            U.a_cmp_w, U.a_cmp_pos = P_['a_cmp_w'], P_['a_cmp_pos']
            c4 = slice(4 * s, 4 * s + 4)
            U.ct_src = lambda c, g, s=s: S.CT[s][c * 256 + g * 64:c * 256 + g * 64 + 64, :]

            def ks_src(g, c0, n, s=s, c4=c4):
                res = []
                m = min(c0 + n, 8192) - c0
                if m > 0:
                    res.append((S.KST[s][g * 64:g * 64 + 64, c0:c0 + m], 0, m))
                if c0 + n > 8192:
                    res.append((S.o_slc[j, c4, g * 64:g * 64 + 64].rearrange("t d -> d t"), max(m, 0), max(m, 0) + 4))
                return res

            def kw_src(g, c0, n, s=s, c4=c4):
                if c0 == 7680:
                    return [(S.KWT[s][g * 64:g * 64 + 64, 0:512], 0, 512)]
                if c0 == 8192:
                    return [(S.kvw[c4, g * 64:g * 64 + 64].rearrange("t d -> d t"), 0, 4)]
                return []

            def vs_src(g, vst, s=s, c4=c4):
                res = []
                v = S.VS[s][:, g * 64:g * 64 + 64].rearrange("(kt p) d -> p kt d", p=128)
                for a in range(0, 64, 16):
                    res.append((v[:, a:a + 16, :], vst[:, a:a + 16, :]))
                res.append((S.o_slc[j, c4, 256 + g * 64:256 + g * 64 + 64], vst[0:4, 64, :]))
                return res

            def vw_src(g, vst, s=s, c4=c4):
                v = S.win_in[j, s][:, 256 + g * 64:256 + g * 64 + 64].rearrange("(kt p) d -> p kt d", p=128)
                return [(v, vst[:, 60:64, :]), (S.kvw[c4, 256 + g * 64:256 + g * 64 + 64], vst[0:4, 64, :])]
            U.ks_src, U.kw_src, U.vs_src, U.vw_src = ks_src, kw_src, vs_src, vw_src
            for g in range(N_GROUPS_RUN):
                nsa_group(k, nc, C, U, j, g)
            k.dma(S.o_win[j, s, 0:508, :], S.win_in[j, s, 4:512, :], writes=['o_win_s'], q='pool', acc=True)
            k.dma(S.o_win[j, s, 508:512, :], S.kvw[c4, :], reads=['kvw_s'], writes=['o_win_s'], q='pool', acc=True)
        k.barrier()
        esn.close()
        w_out = P_['a_w_out'][j]
    elif kind == 1:
        def dcv_dst(b0, nb, t0, nt):
            return [(S.o_dcv[s, :, b0:b0 + nb], 'o_dcv_s', 4 * s + 1, 4 * s + 4) for s in range(N_SEQ)]
        plan = [
            dict(c0=0, n=3072, mode='FM', key='GQKVs', dst=fm(S.BIG[0:3072])),
            dict(c0=3072, n=1024, mode='FM', key='GZs', dst=fm(S.BIG[3072:4096])),
            dict(c0=4096, n=16, mode='TM', key='GBDs', dst=lambda b0, nb, t0, nt: [(S.GBD[t0:t0 + nt, b0:b0 + nb], 'GBDs')]),
            dict(c0=0, n=3072, mode='TM', key='o_dcv_s', dst=dcv_dst),
        ]
        stage_inproj(k, nc, C, x_cur, TS, P_['b_w_in'][j], DN_IN, gcol, plan, 'A%ds' % layer)
        for s in range(N_SEQ):
            U = sample_unit(S, s, tab)
            U.conv_hist = lambda k_, x_, ft, s=s: k_.dma(x_[:, 0:3], S.sdc_in[s][:, ft * 128:(ft + 1) * 128].rearrange("j p -> p j"),
                                                         writes=[x_], allow_slow_non_contiguous=True)
            U.state_init = lambda k_, St, h, s=s: k_.dma(St[:], S.sd_in[s, h], writes=[St])
            U.state_out = lambda k_, St, h, s=s: k_.dma(S.o_dst[s, h], St[:], reads=[St], writes=['o_dst_s'], q='pool', acc=True)
            gdn_layer(k, nc, C, U, j, P_)
        w_out = P_['b_w_out'][j]
    else:
        plan = [
            dict(c0=0, n=1024, mode='FM', key='CAs', dst=fm(S.BIG[0:1024])),
            dict(c0=1024, n=1024, mode='FM', key='CBs', dst=fm(S.BIG[1024:2048])),
            dict(c0=2048, n=1024, mode='FM', key='CZs', dst=fm(S.BIG[2048:3072])),
        ]
        stage_inproj(k, nc, C, x_cur, TS, P_['c_w_in'][j], CONF_IN, gcol, plan, 'A%ds' % layer)
        for s in range(N_SEQ):
            U = sample_unit(S, s, tab)
            U.cconv_hist = lambda k_, u, ft, s=s: k_.dma(u[:, 0:30], S.sc_in[s][:, ft * 128:(ft + 1) * 128].rearrange("j p -> p j"),
                                                         writes=[u], allow_slow_non_contiguous=True)

            def cconv_out(k_, nc_, C_, ps1, ps2, s=s):
                k_.dma(S.o_ccv[s, 0:26, :], S.sc_in[s, 4:30, :], writes=['o_ccv_s'], q='pool', acc=True)
                with ExitStack() as es_:
                    ul = es_.enter_context(nc_.sbuf_tensor("ccvs_ul%d" % s, [128, 8, 4], F32))
                    ot = es_.enter_context(nc_.sbuf_tensor("ccvs_ot%d" % s, [4, 1024], F32))
                    k_.dma(ul[:], S.CU[:, 4 * s:4 * s + 4].rearrange("(ft p) t -> p ft t", p=128), reads=['CU'], writes=[ul])
                    for ft in range(8):
                        pp = ps1 if ft < 4 else ps2
                        k_.op('pe', lambda en: en.transpose(pp[0:4, (ft % 4) * 128:(ft % 4 + 1) * 128], ul[:, ft, :], C_.ident[:, :]),
                              reads=[ul, 'consts'], writes=[pp])
                    k_.op('dve', lambda en: en.tensor_copy(ot[:, 0:512], ps1[0:4, 0:512]), reads=[ps1], writes=[ot])
                    k_.op('dve', lambda en: en.tensor_copy(ot[:, 512:1024], ps2[0:4, 0:512]), reads=[ps2], writes=[ot])
                    k_.dma(S.o_ccv[s, 26:30, :], ot[:], reads=[ot], writes=['o_ccv_s'], q='pool', acc=True)
                    k_.barrier()
            U.cconv_out = cconv_out
            conf_layer(k, nc, C, U, j, P_)
        w_out = P_['c_w_out'][j]
    stage_outproj(k, nc, C, S.MT, x_cur, x_next, w_out, TS, 'O%ds' % layer)


def build_program():
    nc = bass.Bass("TRN2", target_bir_lowering=False)
    C = Ctx()
    TP = T_P
    NTP = TP // 128
    dr = lambda n, s, d=F32, kind="ExternalInput": nc.dram_tensor(n, s, d, kind=kind).ap()
    xp = dr("xp", [TP, D])
    xs = dr("xs", [4 * N_SEQ, D])
    P_ = {}
    for n_, sh in PARAM_SHAPES.items():
        P_[n_] = dr(n_, sh)
    tab = {n: dr("t_" + n, sh, dt) for n, (sh, dt) in TABLE_SPECS.items()}
    tab['qaug'] = dr("t_qaug_p", [6, NTP, 16, 128], BF16)
    out = lambda n, s: dr(n, s, kind="ExternalOutput")
    o_yp = out("o_yp", [TP, D])
    o_cmp_p = out("o_cmp_p", [2, TP, 512])
    o_slc_p = out("o_slc_p", [2, TP, 512])
    o_win_p = out("o_win_p", [2, 512, 512])
    o_dst_p = out("o_dst_p", [1, 8, 128, 128])
    o_dcv_p = out("o_dcv_p", [1, 3, 3072])
    o_ccv_p = out("o_ccv_p", [1, 30, 1024])
    scr = lambda n, s: dr(n, s, kind="Internal")
    kvw_p = scr("kvw_p", [TP, 512])
    S = sample_setup(nc, dr, scr, out, tab) if RUN_SAMPLE else None
    UP = Ctx()
    UP.tag = 'p'
    UP.T, UP.past, UP.L = TP, 0, TP
    UP.Lc = TP
    big = scr("BIG_p", [4112, TP])
    UP.QT, UP.CT, UP.KST, UP.KWT = big[0:1024], big[1024:1536], big[1536:1792], big[1792:2048]
    UP.GT, UP.ZT = big[2048:2096], big[2096:3120]
    UP.GQKV, UP.GZ = big[0:3072], big[3072:4096]
    UP.CA, UP.CB, UP.CZ = big[0:1024], big[1024:2048], big[2048:3072]
    UP.GN = scr("GN_p", [3072, TP])
    UP.CU = scr("CU_p", [1024, TP])
    UP.GBD = scr("GBD_p", [TP, 16])
    UP.MT = scr("MT_p", [1024, TP])
    UP.Y = [scr("Y0_p", [TP, D]), scr("Y1_p", [TP, D])]
    UP.qtiles = [(q0, 128) for q0 in range(0, TP, 128)]
    UP.tab = tab
    UP.a_cmp_w, UP.a_cmp_pos = P_['a_cmp_w'], P_['a_cmp_pos']
    UP.k_prenormed = False
    UP.thr_col = 15
    UP.bonus_ap = lambda C_, qi, nq: C_.bonus[0:nq, 126 - 2 * qi:254 - 2 * qi]

    k = KB(nc)
    UP.conv_hist = lambda k_, x_, ft: k_.op('pool', lambda en: en.memset(x_[:, 0:3], 0.0), writes=[x_])
    UP.cconv_hist = lambda k_, u, ft: k_.op('pool', lambda en: en.memset(u[:, 0:30], 0.0), writes=[u])
    UP.state_init = lambda k_, S, h: k_.op('pool', lambda en: en.memset(S[:], 0.0), writes=[S])
    UP.state_out = lambda k_, S, h: k_.dma(o_dst_p[0, h], S[:], reads=[S], writes=['o_dst'], q='pool', acc=True)

    def cconv_out_p(k_, nc_, C_, ps1, ps2):
        with ExitStack() as es_:
            ul = es_.enter_context(nc_.sbuf_tensor("ccv_ul", [128, 8, 30], F32))
            ot = es_.enter_context(nc_.sbuf_tensor("ccv_ot", [30, 1024], F32))
            k_.dma(ul[:], UP.CU[:, TP - 30:TP].rearrange("(ft p) t -> p ft t", p=128), writes=[ul])
            for ft in range(8):
                pp = ps1 if ft < 4 else ps2
                k_.op('pe', lambda en: en.transpose(pp[0:30, (ft % 4) * 128:(ft % 4 + 1) * 128], ul[:, ft, :], C_.ident[:, :]),
                      reads=[ul, 'consts'], writes=[pp])
            k_.op('dve', lambda en: en.tensor_copy(ot[:, 0:512], ps1[0:30, 0:512]), reads=[ps1], writes=[ot])
            k_.op('dve', lambda en: en.tensor_copy(ot[:, 512:1024], ps2[0:30, 0:512]), reads=[ps2], writes=[ot])
            k_.dma(o_ccv_p[0], ot[:], reads=[ot], writes=['o_ccv'], q='pool')
            k_.barrier()
    UP.cconv_out = cconv_out_p

    with ExitStack() as es0:
        cs = lambda n, s, d=F32: es0.enter_context(nc.sbuf_tensor("c_" + n, s, d))
        C.ident = cs("ident", [128, 128])
        C.identb = cs("identb", [128, 128], BF16)
        C.onesf = cs("onesf", [128, 128])
        C.tri, C.slt = cs("tri", [128, 128]), cs("slt", [128, 128])
        C.mstrict, C.minclT = cs("mstrict", [128, 128]), cs("minclT", [128, 128])
        C.gains = cs("gains", [128, 20])
        C.gateb = cs("gateb", [48, 2])
        C.ogain = cs("ogain", [128, 1])
        C.gainb = cs("gainb", [128, 2, 3, 64])
        C.tmp = cs("ctmp", [128, 512])
        C.st4 = cs("cst4", [128, 8])
        C.epsc = cs("epsc", [128, 1])
        k.op('pool', lambda en: en.memset(C.epsc[:], EPS), writes=['epsc'])
        k.op('pool', lambda en: en.memset(C.gains[:, 16:17], EPS), writes=['consts'])
        k.op('pool', lambda en: en.memset(C.gains[:, 17:18], 64 * EPS), writes=['consts'])
        k.op('pool', lambda en: en.memset(C.gains[:, 18:19], 128 ** -0.5), writes=['consts'])
        k.op('pool', lambda en: en.memset(C.gains[:, 19:20], 1.0), writes=['consts'])
        C.pm4 = cs("pm4", [128, 1])
        for n_ in ['ident', 'identb', 'onesf', 'tri', 'slt', 'mstrict', 'minclT', 'pm4']:
            k.dma(getattr(C, n_)[:], tab[n_], writes=['consts'], acc=True)
        for jj in range(2):
            k.dma(C.gains[0:64, jj * 8:jj * 8 + 1], P_['a_q_gain'][jj].rearrange("(d o) -> d o", o=1), writes=['consts'], acc=True)
            k.dma(C.gains[0:64, jj * 8 + 1:jj * 8 + 4], P_['a_k_gain'][jj].rearrange("b d -> d b"), writes=['consts'], acc=True,
                  allow_slow_non_contiguous=True)
            k.dma(C.gateb[:, jj:jj + 1], P_['a_gate_b'][jj].rearrange("(d o) -> d o", o=1), writes=['consts'], acc=True)
        k.dma(C.ogain[:], P_['b_o_gain'][0].rearrange("(d o) -> d o", o=1), writes=['consts'], acc=True)
        k.dma(C.gainb[:].rearrange("p a b c -> p (a b c)"),
              P_['a_k_gain'].rearrange("a b c -> (a b c)").partition_broadcast(128), writes=['gainb'])
        if RUN_SAMPLE:
            C.idx = cs("idx", [128, N_SEQ * 64], I32)
            pti = cs("pti", [128, N_SEQ * 64], I32)
            ptf = cs("ptf", [128, N_SEQ * 64])
            pcol = cs("pcol", [128, 1])
            k.dma(pcol[:], tab['pcol'], writes=['pcol'])
            k.dma(pti[:], S.pt.rearrange("s p -> (s p)").partition_broadcast(128), writes=['pti'])
            k.op('dve', lambda en: en.tensor_copy(ptf[:], pti[:]), reads=['pti'], writes=['ptf'])
            k.op('dve', lambda en: en.tensor_scalar(ptf[:], ptf[:], 128.0, pcol[:, 0:1], ALU.mult, ALU.add), reads=['ptf', 'pcol'], writes=['ptf'])
            k.op('dve', lambda en: en.tensor_copy(C.idx[:], ptf[:]), reads=['ptf'], writes=['idx'])
        k.barrier()

        def knorm_post(j, which):
            def post(o, nt, nb):
                kv = o[0:nt, 0:256]
                k.op('act', lambda en: en.activation(C.tmp[0:nt, 0:256], kv, AF.Square), reads=[o], writes=['ctmp'])
                k.op('dve', lambda en: en.tensor_reduce(C.st4[0:nt, 0:4], C.tmp[0:nt, 0:256].rearrange("p (g d) -> p g d", g=4),
                                                        AX.X, ALU.add), reads=['ctmp'], writes=['cst4'])
                rstd_op(k, C, C.st4[0:nt, 4:8], C.st4[0:nt, 0:4], 1.0 / 64, nt, 'cst4')
                kv3 = kv.rearrange("p (g d) -> p g d", g=4)
                k.op('dve', lambda en: en.tensor_tensor(kv3, kv3, C.st4[0:nt, 4:8].unsqueeze(2).to_broadcast([nt, 4, 64]), ALU.mult),
                     reads=[o, 'cst4'], writes=[o])
                k.op('dve', lambda en: en.tensor_tensor(kv3, kv3, C.gainb[0:nt, j, which:which + 1, :].to_broadcast([nt, 4, 64]), ALU.mult),
                     reads=[o, 'gainb'], writes=[o])
            return post

        fm = lambda ap: (lambda b0, nb, t0, nts: ap[b0:b0 + nb, t0:t0 + nts])
        U = UP
        x_cur = xp
        layers = list(LAYERS)
        xs_cur = xs
        for li, layer in enumerate(layers):
          if RUN_SAMPLE:
              xs_next = S.o_ys if li == len(layers) - 1 else S.Y[li % 2]
              sample_layer(k, nc, C, S, P_, tab, layer, xs_cur, xs_next, knorm_post)
              xs_cur = xs_next
          if not RUN_PROMPT:
              continue
          try:
            kind, j = layer % 3, layer // 3
            x_next = o_yp if li == len(layers) - 1 else U.Y[li % 2]
            gcol = P_['norm_g'][layer].rearrange("(kc p) -> p kc", p=128)
            if kind == 0:
                def win_dst(b0, nb, t0, nt, j=j):
                    res = [(kvw_p[t0:t0 + nt, b0:b0 + nb], ('kvw', 'p'))]
                    if t0 >= TP - 512:
                        res.append((o_win_p[j, t0 - (TP - 512):t0 - (TP - 512) + nt, b0:b0 + nb], 'o_win_p'))
                    return res
                plan = [
                    dict(c0=0, n=1024, mode='FM', key='QT', dst=fm(U.QT)),
                    dict(c0=1024, n=512, mode='FM', key='CT', dst=fm(U.CT)),
                    dict(c0=1536, n=256, mode='FM', key='KST', dst=fm(U.KST)),
                    dict(c0=2048, n=256, mode='FM', key='KWT', dst=fm(U.KWT)),
                    dict(c0=2560, n=48, mode='FM', key='GT', dst=fm(U.GT)),
                    dict(c0=2608, n=1024, mode='FM', key='ZT', dst=fm(U.ZT)),
                    dict(c0=1024, n=512, mode='TM', key='o_cmp',
                         dst=lambda b0, nb, t0, nt, j=j: [(o_cmp_p[j, t0:t0 + nt, b0:b0 + nb], 'o_cmp')]),
                    dict(c0=1536, n=512, mode='TM', key='o_slc', post=knorm_post(j, 1),
                         dst=lambda b0, nb, t0, nt, j=j: [(o_slc_p[j, t0:t0 + nt, b0:b0 + nb], 'o_slc')]),
                    dict(c0=2048, n=512, mode='TM', key='kvw', post=knorm_post(j, 2), dst=win_dst),
                ]
                stage_inproj(k, nc, C, x_cur, TP, P_['a_w_in'][j], A_IN, gcol, plan, 'A%dp' % layer)
                U.ct_src = lambda c, g: U.CT[c * 256 + g * 64:c * 256 + g * 64 + 64, :]
                U.ks_src = lambda g, c0, n: [(U.KST[g * 64:g * 64 + 64, c0:c0 + n], 0, n)]
                U.kw_src = lambda g, c0, n: [(U.KWT[g * 64:g * 64 + 64, c0:c0 + n], 0, n)]

                def vsrc_of(src):
                    def f(g, vst):
                        res = []
                        v = src[:, 256 + g * 64:256 + g * 64 + 64].rearrange("(kt p) d -> p kt d", p=128)
                        for a in range(0, NTP, 16):
                            b = min(NTP, a + 16)
                            res.append((v[:, a:b, :], vst[:, a:b, :]))
                        return res
                    return f
                U.vs_src = vsrc_of(o_slc_p[j])
                U.vw_src = vsrc_of(kvw_p)
                with ExitStack() as esn:
                    load_nsa_consts(k, nc, C, tab, esn, "cn%dp_" % layer)
                    for g in range(N_GROUPS_RUN):
                        nsa_group(k, nc, C, U, j, g)
                w_out = P_['a_w_out'][j]
            elif kind == 1:
                T3 = TP - 3

                def dcv_dst(b0, nb, t0, nt):
                    r0 = max(T3 - t0, 0)
                    return [(o_dcv_p[0, t0 + r0 - T3:t0 + nt - T3, b0:b0 + nb], 'o_dcv', r0, nt)]
                plan = [
                    dict(c0=0, n=3072, mode='FM', key='GQKV', dst=fm(U.GQKV)),
                    dict(c0=3072, n=1024, mode='FM', key='GZ', dst=fm(U.GZ)),
                    dict(c0=4096, n=16, mode='TM', key='GBD', dst=lambda b0, nb, t0, nt: [(U.GBD[t0:t0 + nt, b0:b0 + nb], 'GBD')]),
                    dict(c0=0, n=3072, mode='TM', key='o_dcv', dst=dcv_dst, tok_filter=lambda t0, nt: t0 + nt > T3),
                ]
                stage_inproj(k, nc, C, x_cur, TP, P_['b_w_in'][j], DN_IN, gcol, plan, 'A%dp' % layer)
                if not checkpoint('inproj'):
                    gdn_layer(k, nc, C, U, j, P_)
                w_out = P_['b_w_out'][j]
            else:
                plan = [
                    dict(c0=0, n=1024, mode='FM', key='CA', dst=fm(U.CA)),
                    dict(c0=1024, n=1024, mode='FM', key='CB', dst=fm(U.CB)),
                    dict(c0=2048, n=1024, mode='FM', key='CZ', dst=fm(U.CZ)),
                ]
                stage_inproj(k, nc, C, x_cur, TP, P_['c_w_in'][j], CONF_IN, gcol, plan, 'A%dp' % layer)
                if not checkpoint('inproj'):
                    conf_layer(k, nc, C, U, j, P_)
                w_out = P_['c_w_out'][j]
            if STOPPED[0]:
                break
            stage_outproj(k, nc, C, U.MT, x_cur, x_next, w_out, TP, 'O%dp' % layer)
            x_cur = x_next
          except StopBuild:
            k.barrier()
            break
        k.finish()
    print("instructions:", k.n_ins, k.cnt)
    return nc


LAYERS = [0, 1, 2, 3]
PARAM_SHAPES = dict(
    norm_g=[4, D], a_w_in=[2, D, A_IN], a_w_out=[2, D, D], a_k_gain=[2, 3, 64], a_q_gain=[2, 64], a_gate_b=[2, 48],
    a_cmp_w=[2, 2, 2048, 64], a_cmp_pos=[2, 2, 32, 64],
    b_w_in=[1, D, DN_IN], b_conv_w=[1, 4, 3072], b_a_log=[1, 8], b_dt_bias=[1, 8], b_o_gain=[1, 128], b_w_out=[1, D, D],
    c_w_in=[1, D, CONF_IN], c_conv_w=[1, 31, 1024], c_conv_b=[1, 1024], c_ln_g=[1, 1024], c_ln_b=[1, 1024], c_w_out=[1, D, D],
)
N_GROUPS_RUN = 4
_CACHE = {}


def extra_inputs(T=None):
    tb = host_tables(T or T_P, 0)
    out = {"t_" + n: tb[n] for n in TABLE_SPECS}
    out["t_qaug_p"] = tb['qaug']
    out["t_qaug_s"] = host_tables(128, 8192)['qaug']
    return out


def kernel(**inputs):
    f32 = lambda a: np.ascontiguousarray(np.asarray(a, dtype=np.float32))
    if 'nc' not in _CACHE:
        _CACHE['nc'] = build_program()
    nc = _CACHE['nc']
    x_prompt = f32(inputs['x_prompt'])
    x_sample = f32(inputs['x_sample'])
    tabs = extra_inputs()
    shared = {n: f32(inputs[n]) for n in PARAM_SHAPES}
    if RUN_SAMPLE:
        ccmp = f32(inputs['cache_cmp_kv']).reshape(2, N_POOL * 128, 512)
        cslc = f32(inputs['cache_slc_kv']).reshape(2, N_POOL * 128, 512)
        cwin = f32(inputs['cache_win_kv']).reshape(2, 32, 512, 512)
        pt = np.ascontiguousarray(np.asarray(inputs['page_table'], dtype=np.int32))
        sd = f32(inputs['state_delta'])[0]
        sdc = f32(inputs['state_delta_conv'])[0]
        sc = f32(inputs['state_conv'])[0]
    in_maps = []
    for c in range(N_CORES):
        b = c % 2
        m = {"xp": x_prompt[b], "xs": x_sample[4 * c:4 * c + 4].reshape(16, D)}
        m.update(shared)
        m.update(tabs)
        if RUN_SAMPLE:
            sl = slice(4 * c, 4 * c + 4)
            m.update({"pt": pt[sl], "cache_cmp0": ccmp[0], "cache_cmp1": ccmp[1], "cache_slc0": cslc[0], "cache_slc1": cslc[1], "win_in": np.ascontiguousarray(cwin[:, sl]),
                      "sd_in": np.ascontiguousarray(sd[sl]), "sdc_in": np.ascontiguousarray(sdc[sl]),
                      "sc_in": np.ascontiguousarray(sc[sl])})
        else:
            m.pop("t_qaug_s", None)
        in_maps.append(m)
    res = run_bass_kernel_spmd(nc, in_maps, core_ids=list(range(N_CORES)))
    R = res.results
    z = lambda *s: np.zeros(s, np.float32)
    y_prompt = np.stack([R[b]["o_yp"] for b in range(2)])
    cmp_p = np.stack([R[b]["o_cmp_p"] for b in range(2)], axis=1).reshape(2, 2, T_P, 2, 4, 64)
    slc_p = np.stack([R[b]["o_slc_p"] for b in range(2)], axis=1).reshape(2, 2, T_P, 2, 4, 64)
    win_p = np.stack([R[b]["o_win_p"] for b in range(2)], axis=1).reshape(2, 2, 512, 2, 4, 64)
    dst_p = np.stack([R[b]["o_dst_p"] for b in range(2)], axis=1)
    dcv_p = np.stack([R[b]["o_dcv_p"] for b in range(2)], axis=1)
    ccv_p = np.stack([R[b]["o_ccv_p"] for b in range(2)], axis=1)
    if RUN_SAMPLE:
        y_sample = np.concatenate([R[c]["o_ys"].reshape(4, 4, D) for c in range(8)], axis=0)
        cmp_s = np.concatenate([R[c]["o_cmp_s"].reshape(2, 4, 4, 2, 4, 64) for c in range(8)], axis=1)
        slc_s = np.concatenate([R[c]["o_slc_s"].reshape(2, 4, 4, 2, 4, 64) for c in range(8)], axis=1)
        win_s = np.concatenate([R[c]["o_win_s"].reshape(2, 4, 512, 2, 4, 64) for c in range(8)], axis=1)
        dst_s = np.concatenate([R[c]["o_dst_s"] for c in range(8)], axis=0)[None]
        dcv_s = np.concatenate([R[c]["o_dcv_s"] for c in range(8)], axis=0)[None]
        ccv_s = np.concatenate([R[c]["o_ccv_s"] for c in range(8)], axis=0)[None]
    else:
        y_sample, cmp_s, slc_s, win_s = z(32, 4, D), z(2, 32, 4, 2, 4, 64), z(2, 32, 4, 2, 4, 64), z(2, 32, 512, 2, 4, 64)
        dst_s, dcv_s, ccv_s = z(1, 32, 8, 128, 128), z(1, 32, 3, 3072), z(1, 32, 30, 1024)
    return (y_prompt, y_sample, cmp_p, cmp_s, slc_p, slc_s, win_p, win_s,
            dst_p, dst_s, dcv_p, dcv_s, ccv_p, ccv_s)
```

```python
import numpy as np
from contextlib import ExitStack
import concourse.bass as bass
import concourse.mybir as mybir
from concourse.bass_utils import run_bass_kernel_spmd

F32 = mybir.dt.float32
BF16 = mybir.dt.bfloat16
I32 = mybir.dt.int32
ALU = mybir.AluOpType
AF = mybir.ActivationFunctionType
AX = mybir.AxisListType

D = 1024
T_P = 8192
N_CORES = 8
EPS = 1e-6
A_IN = 3632
DN_IN = 4112
CONF_IN = 3072


class KB:
    def __init__(self, nc, n_dma_sems=32):
        self.nc = nc
        self.eng = {'pe': nc.tensor, 'act': nc.scalar, 'dve': nc.vector, 'pool': nc.gpsimd, 'sp': nc.sync}
        self.sem = {k: nc.alloc_semaphore('sem_' + k) for k in self.eng}
        self.cnt = {k: 0 for k in self.eng}
        self.dsem = [nc.alloc_semaphore('dsem%d' % i) for i in range(n_dma_sems)]
        self.dval = [0] * n_dma_sems
        self.dq = {'sp': list(range(0, 16)), 'pool': list(range(16, 28)), 'act': list(range(28, 32))}
        self.dnext = {'sp': 0, 'pool': 0, 'act': 0}
        self.waited = {k: {} for k in self.eng}
        self.res_w = {}
        self.res_wacc = {}
        self.res_r = {}
        self.psum_keys = set()
        self.n_ins = 0

    def _semobj(self, sk):
        return self.sem[sk] if isinstance(sk, str) else self.dsem[sk]

    def _need(self, e, evs):
        for sk, v in evs.items():
            if sk == e and e == 'pe':
                continue
            if self.waited[e].get(sk, 0) >= v:
                continue
            self.eng[e].wait_ge(self._semobj(sk), v)
            self.waited[e][sk] = v

    @staticmethod
    def _nk(x):
        if isinstance(x, (str, int)):
            return x
        if isinstance(x, tuple):
            return tuple(KB._nk(y) for y in x)
        return getattr(x, 'name', None) or id(x)

    def _deps(self, e, reads, writes, acc=False):
        reads = [self._nk(r) for r in reads]
        writes = [self._nk(w) for w in writes]
        evs = {}

        def add(d):
            for sk, v in d.items():
                if evs.get(sk, 0) < v:
                    evs[sk] = v
        for r in reads:
            add(self.res_w.get(r, {}))
            add(self.res_wacc.get(r, {}))
        for w in writes:
            add(self.res_w.get(w, {}))
            add(self.res_r.get(w, {}))
            if not acc:
                add(self.res_wacc.get(w, {}))
        self._need(e, evs)

    def _commit(self, ev, reads, writes, acc=False):
        reads = [self._nk(r) for r in reads]
        writes = [self._nk(w) for w in writes]
        sk, v = ev
        for r in reads:
            d = self.res_r.setdefault(r, {})
            if d.get(sk, 0) < v:
                d[sk] = v
        for w in writes:
            if acc:
                d = self.res_wacc.setdefault(w, {})
                d[sk] = max(d.get(sk, 0), v)
            else:
                self.res_w[w] = {sk: v}
                self.res_wacc[w] = {}
                self.res_r[w] = {}

    def op(self, e, fn, reads=(), writes=()):
        pr = [r for r in reads if self._nk(r) in self.psum_keys]
        if pr:
            reads = [r for r in reads if self._nk(r) not in self.psum_keys]
            writes = list(writes) + [r for r in pr if self._nk(r) not in [self._nk(w) for w in writes]]
        self._deps(e, reads, writes)
        ins = fn(self.eng[e])
        self.cnt[e] += 1
        ins.then_inc(self.sem[e], 1)
        self._commit((e, self.cnt[e]), reads, writes)
        self.n_ins += 1
        return ins

    def dma(self, out, in_, reads=(), writes=(), q='sp', acc=False, **kw):
        self._deps(q, reads, writes, acc=acc)
        i = self.dq[q][self.dnext[q] % len(self.dq[q])]
        self.dnext[q] += 1
        if self.dval[i] > 0 and self.waited[q].get(i, 0) < self.dval[i]:
            self.eng[q].wait_ge(self.dsem[i], self.dval[i])
            self.waited[q][i] = self.dval[i]
        ins = self.eng[q].dma_start(out=out, in_=in_, **kw)
        self.dval[i] += 16
        ins.then_inc(self.dsem[i], 16)
        self._commit((i, self.dval[i]), reads, writes, acc=acc)
        self.n_ins += 1
        return ins

    def dma_ind(self, out, in_, idx_ap, reads=(), writes=()):
        q = 'pool'
        self._deps(q, reads, writes)
        i = self.dq[q][self.dnext[q] % len(self.dq[q])]
        self.dnext[q] += 1
        if self.dval[i] > 0 and self.waited[q].get(i, 0) < self.dval[i]:
            self.eng[q].wait_ge(self.dsem[i], self.dval[i])
            self.waited[q][i] = self.dval[i]
        ins = self.nc.gpsimd.indirect_dma_start(out, None, in_, bass.IndirectOffsetOnAxis(ap=idx_ap, axis=0))
        self.dval[i] += 16
        ins.then_inc(self.dsem[i], 16)
        self._commit((i, self.dval[i]), reads, writes)
        self.n_ins += 1
        return ins

    def reg_psum(self, t):
        self.psum_keys.add(self._nk(t))
        return t

    def barrier(self):
        for e in self.eng:
            evs = {k: c for k, c in self.cnt.items() if c > 0 and k != e}
            for i, v in enumerate(self.dval):
                if v > 0:
                    evs[i] = v
            for sk, v in evs.items():
                if self.waited[e].get(sk, 0) >= v:
                    continue
                self.eng[e].wait_ge(self._semobj(sk), v)
                self.waited[e][sk] = v
        self.res_w = {}
        self.res_wacc = {}
        self.res_r = {}

    def finish(self):
        self.barrier()


class Ctx:
    pass


class StopBuild(Exception):
    pass


STOP_AFTER = None


STOPPED = [False]


def checkpoint(name):
    if STOP_AFTER == name:
        STOPPED[0] = True
    return STOPPED[0]


def rr(lst, state=[0]):
    state[0] += 1
    return lst[state[0] % len(lst)]


def rstd_op(k, C, out_ap, in_ap, scale, nt, key):
    k.op('act', lambda en: en.activation(out_ap, in_ap, AF.Sqrt, bias=C.epsc[0:nt, 0:1], scale=scale),
         reads=[key, 'epsc'], writes=[key])
    k.op('dve', lambda en: en.reciprocal(out_ap, out_ap), reads=[key], writes=[key])


def stage_inproj(k, nc, C, x_ap, T, w_ap, ncol, g_col_ap, plan, tag):
    with ExitStack() as es:
        sb = lambda n, s, d=F32: es.enter_context(nc.sbuf_tensor(tag + n, s, d))
        ps = lambda n, s, d=F32: k.reg_psum(es.enter_context(nc.psum_tensor(tag + n, s, d)))
        wb = sb('wb', [128, 8, ncol], BF16)
        wst = [sb('wst%d' % i, [128, 8, 512], F32) for i in range(2)]
        gT = sb('gT', [128, 8], F32)
        xts = [sb('xt%d' % i, [128, D], F32) for i in range(2)]
        xn = [sb('xn%d' % i, [128, D], F32) for i in range(2)]
        junk = sb('junk', [128, D], F32)
        st = [sb('st%d' % i, [128, 4], F32) for i in range(2)]
        hT = [sb('hT%d' % i, [128, 8, 512], BF16) for i in range(2)]
        ob = [sb('ob%d' % i, [128, 512], F32) for i in range(4)]
        ptr = [ps('ptr%d' % i, [128, 512], F32) for i in range(2)]
        pg = [ps('pg%d' % i, [128, 512], F32) for i in range(4)]

        k.dma(gT[:], g_col_ap, writes=[gT], allow_slow_non_contiguous=True)
        w_v = w_ap.rearrange("(kc p) c -> p kc c", p=128)
        ci = 0
        for cb in range(0, ncol, 512):
            n = min(512, ncol - cb)
            s = wst[ci % 2]
            k.dma(s[:, :, 0:n], w_v[:, :, cb:cb + n], writes=[s])
            e = ['act', 'dve', 'pool'][ci % 3]
            if e == 'act':
                k.op('act', lambda en: en.copy(wb[:, :, cb:cb + n], s[:, :, 0:n]), reads=[s], writes=[(wb, cb)])
            else:
                k.op(e, lambda en: en.tensor_copy(wb[:, :, cb:cb + n], s[:, :, 0:n]), reads=[s], writes=[(wb, cb)])
            ci += 1
        wkeys = [(wb, cb) for cb in range(0, ncol, 512)]

        xi = 0
        oi = 0
        gi = 0
        for si, t0 in enumerate(range(0, T, 512)):
            nts = min(512, T - t0)
            h = hT[si % 2]
            for j0 in range(0, nts, 128):
                nt = min(128, nts - j0)
                xt = xts[xi % 2]
                xnn = xn[xi % 2]
                stt = st[xi % 2]
                xi += 1
                k.dma(xt[0:nt, :], x_ap[t0 + j0:t0 + j0 + nt, :], writes=[xt])
                k.op('act', lambda en: en.activation(junk[0:nt, :], xt[0:nt, :], AF.Square, accum_out=stt[0:nt, 0:1]),
                     reads=[xt], writes=[junk, stt])
                rstd_op(k, C, stt[0:nt, 2:3], stt[0:nt, 0:1], 1.0 / D, nt, stt)
                k.op('dve', lambda en: en.tensor_scalar(xnn[0:nt, :], xt[0:nt, :], stt[0:nt, 2:3], None, ALU.mult),
                     reads=[xt, stt], writes=[xnn])
                for kc in range(8):
                    p = ptr[kc % 2]
                    k.op('pe', lambda en: en.transpose(p[:, 0:nt], xnn[0:nt, kc * 128:(kc + 1) * 128], C.ident[0:nt, 0:nt]),
                         reads=[xnn, 'ident'], writes=[p])
                    e = 'dve' if kc % 2 == 0 else 'pool'
                    if e == 'pool':
                        k.op('act', lambda en: en.activation(h[:, kc, j0:j0 + nt], p[:, 0:nt], AF.Copy, scale=gT[:, kc:kc + 1]),
                             reads=[p, gT], writes=[h])
                    else:
                        k.op('dve', lambda en: en.tensor_scalar(h[:, kc, j0:j0 + nt], p[:, 0:nt], gT[:, kc:kc + 1], None, ALU.mult),
                             reads=[p, gT], writes=[h])
            for pl in plan:
                c0, n, mode = pl['c0'], pl['n'], pl['mode']
                blk = pl.get('blk', 128)
                if mode == 'FM':
                    for b0 in range(0, n, blk):
                        nb = min(blk, n - b0)
                        pp = pg[gi % 4]
                        gi += 1
                        for kc in range(8):
                            k.op('pe', lambda en: en.matmul(pp[0:nb, 0:nts], wb[:, kc, c0 + b0:c0 + b0 + nb], h[:, kc, 0:nts],
                                                            start=(kc == 0), stop=(kc == 7)),
                                 reads=[h] + wkeys, writes=[pp])
                        o = ob[oi % 4]
                        oi += 1
                        if oi % 2 == 0:
                            k.op('act', lambda en: en.copy(o[0:nb, 0:nts], pp[0:nb, 0:nts]), reads=[pp], writes=[o])
                        else:
                            k.op('dve', lambda en: en.tensor_copy(o[0:nb, 0:nts], pp[0:nb, 0:nts]), reads=[pp], writes=[o])
                        k.dma(pl['dst'](b0, nb, t0, nts), o[0:nb, 0:nts], reads=[o], writes=[pl['key']], q='pool', acc=True)
                else:
                    for j0 in range(0, nts, 128):
                        nt = min(128, nts - j0)
                        if pl.get('tok_filter') is not None and not pl['tok_filter'](t0 + j0, nt):
                            continue
                        for b0 in range(0, n, 512):
                            nb = min(512, n - b0)
                            pp = pg[gi % 4]
                            gi += 1
                            for kc in range(8):
                                k.op('pe', lambda en: en.matmul(pp[0:nt, 0:nb], h[:, kc, j0:j0 + nt], wb[:, kc, c0 + b0:c0 + b0 + nb],
                                                                start=(kc == 0), stop=(kc == 7)),
                                     reads=[h] + wkeys, writes=[pp])
                            o = ob[oi % 4]
                            oi += 1
                            if oi % 2 == 0:
                                k.op('act', lambda en: en.copy(o[0:nt, 0:nb], pp[0:nt, 0:nb]), reads=[pp], writes=[o])
                            else:
                                k.op('dve', lambda en: en.tensor_copy(o[0:nt, 0:nb], pp[0:nt, 0:nb]), reads=[pp], writes=[o])
                            if pl.get('post') is not None:
                                pl['post'](o, nt, nb)
                            for ent in pl['dst'](b0, nb, t0 + j0, nt):
                                r0, r1 = (ent[2], ent[3]) if len(ent) == 4 else (0, nt)
                                k.dma(ent[0], o[r0:r1, 0:nb], reads=[o], writes=[ent[1]], q='pool', acc=True)
    k.barrier()


NEG = -30000.0
TINY = 1e-30


def host_tables(T, t_base):
    import ml_dtypes
    bf = ml_dtypes.bfloat16
    NT = (T + 127) // 128
    slopes = np.exp2(-8.0 * np.arange(1, 17, dtype=np.float32) / 16).astype(np.float32)

    def split2(v):
        hi = v.astype(bf)
        lo = (v - hi.astype(np.float32)).astype(bf)
        return hi, lo
    s_hi, s_lo = split2(slopes)
    t = (t_base + np.arange(NT * 128)).astype(np.float32)
    v = -(slopes[None, :] * t[:, None]).astype(np.float32)
    v_hi, v_lo = split2(v)
    qaug = np.zeros((6, NT, 16, 128), dtype=bf)
    qaug[0] = s_hi[None, :, None]
    qaug[1] = s_lo[None, :, None]
    qaug[2] = s_hi[None, :, None]
    qaug[3] = s_lo[None, :, None]
    qaug[4] = v_hi.reshape(NT, 128, 16).transpose(0, 2, 1)
    qaug[5] = v_lo.reshape(NT, 128, 16).transpose(0, 2, 1)

    def kaug_of(pos):
        ka = np.zeros((6, pos.shape[0]), dtype=bf)
        hi = (128 * (pos // 128)).astype(np.float32)
        lo = (pos % 128).astype(np.float32)
        ka[0] = hi
        ka[1] = hi
        ka[2] = lo
        ka[3] = lo
        ka[4] = 1.0
        ka[5] = 1.0
        return ka
    res = dict(qaug=qaug)
    kr = np.arange(128)[:, None]
    qr = np.arange(128)[None, :]
    caus = np.where(kr <= qr, 0.0, NEG).astype(np.float32)
    wlow = np.where(kr > qr, 0.0, NEG).astype(np.float32)
    res['caus'] = np.repeat(caus[:, None, :], 4, axis=1).astype(bf)
    res['wlow'] = np.repeat(wlow[:, None, :], 4, axis=1).astype(bf)
    cm = np.zeros((128, 17, 4, 128), dtype=np.float32)
    for d in range(17):
        m = np.where(16 * kr + 31 <= 128 * d + qr, 0.0, NEG)
        cm[:, d, :, :] = m[:, None, :]
    res['cm'] = cm.astype(bf)
    n = np.arange(512)
    poolm = np.zeros((128, 4, 128), dtype=np.float32)
    for kt in range(4):
        for nr in range(128):
            poolm[nr, kt, (128 * kt + nr) // 4] = 1.0
    res['poolm'] = poolm.astype(bf)
    bonus = np.zeros((128, 256), dtype=np.float32)
    for q in range(128):
        jq = 1 if q >= 64 else 0
        for c in range(256):
            jj = c - 126
            if jj > jq:
                bonus[q, c] = -1e30
            elif jj == jq or jj == jq - 1:
                bonus[q, c] = 1e4
    res['bonus'] = bonus
    sel = np.zeros((48, 48, 64), dtype=np.float32)
    for r in range(48):
        sel[r, r, :] = 1.0
    res['sel'] = sel
    res['kaug'] = kaug_of(np.arange(8192 + 128))
    res['kaugc'] = kaug_of(16 * np.arange(512) + 31)
    e = np.zeros((128, 65, 128), dtype=np.float32)
    for kt in range(65):
        for key in range(128):
            jb = 2 * kt + key // 64
            if jb < 128:
                e[jb, kt, key] = 1.0
    res['emat'] = e.astype(bf)
    res['ident'] = np.eye(128, dtype=np.float32)
    bs = np.zeros((128, 128), dtype=np.float32)
    bs[:, 127] = 1e4
    res['bonus_s'] = bs
    res['pcol'] = np.arange(128, dtype=np.float32).reshape(128, 1)
    res['pm4'] = (np.arange(128) < 4).astype(np.float32).reshape(128, 1)
    res['identb'] = np.eye(128, dtype=np.float32).astype(bf)
    res['onesf'] = np.ones((128, 128), dtype=np.float32)
    row = np.arange(128)[:, None]
    colx = np.arange(128)[None, :]
    res['tri'] = (row <= colx).astype(np.float32)
    res['slt'] = (row > colx).astype(np.float32)
    res['mstrict'] = np.where(row > colx, 0.0, NEG).astype(np.float32)
    res['minclT'] = np.where(colx >= row, 0.0, NEG).astype(np.float32)
    return res


TABLE_SPECS = dict(
    caus=([128, 4, 128], BF16), wlow=([128, 4, 128], BF16), cm=([128, 17, 4, 128], BF16),
    poolm=([128, 4, 128], BF16), bonus=([128, 256], F32), sel=([48, 48, 64], F32),
    kaug=([6, 8192 + 128], BF16), kaugc=([6, 512], BF16), emat=([128, 65, 128], BF16),
    ident=([128, 128], F32), identb=([128, 128], BF16), onesf=([128, 128], F32),
    tri=([128, 128], F32), slt=([128, 128], F32), mstrict=([128, 128], F32), minclT=([128, 128], F32),
    bonus_s=([128, 128], F32), pcol=([128, 1], F32), pm4=([128, 1], F32),
)


def fm_norm(k, C, W, src, n, dst_ap, dst_key, gain_col, eps_col, scale, P=64, src_ap=None, src_key=None):
    sq, ps, r = W['sq'], W['psn'], W['r']
    sap = src[0:P, 0:n] if src_ap is None else src_ap
    skey = src if src_key is None else src_key
    k.op('act', lambda en: en.activation(sq[0:P, 0:n], sap, AF.Square), reads=[skey], writes=[sq])
    k.op('pe', lambda en: en.matmul(ps[0:P, 0:n], C.onesf[0:P, 0:P], sq[0:P, 0:n], start=True, stop=True),
         reads=[sq, 'consts'], writes=[ps])
    k.op('act', lambda en: en.activation(r[0:P, 0:n], ps[0:P, 0:n], AF.Sqrt, bias=eps_col, scale=scale),
         reads=[ps, 'consts'], writes=[r])
    k.op('dve', lambda en: en.reciprocal(r[0:P, 0:n], r[0:P, 0:n]), reads=[r], writes=[r])
    k.op('dve', lambda en: en.scalar_tensor_tensor(dst_ap, sap, gain_col, r[0:P, 0:n], ALU.mult, ALU.mult),
         reads=[skey, r, 'consts'], writes=[dst_key])


def nsa_group(k, nc, C, U, j, g):
    L, T = U.L, U.T
    NKT = (L + 127) // 128
    with ExitStack() as es:
        pfx = '%s%d%d_' % (U.tag, j, g)
        sb = lambda n, s, d=F32: es.enter_context(nc.sbuf_tensor(pfx + 'g_' + n, s, d))
        psm = lambda n: k.reg_psum(es.enter_context(nc.psum_tensor(pfx + 'gp_' + n, [128, 512], F32)))
        KsT = sb('KsT', [70, NKT * 128], BF16)
        KwT = sb('KwT', [70, NKT * 128], BF16)
        Vs = sb('Vs', [128, NKT, 65], BF16)
        Vw = sb('Vw', [128, NKT, 65], BF16)
        KcT = sb('KcT', [70, 512], BF16)
        Vc = sb('Vc', [128, 4, 65], BF16)
        W = dict(sq=sb('sq', [64, 512]), r=sb('r', [64, 512]), psn=psm('psn'))
        S = [psm('S0'), psm('S1')]
        Ocmp, Osel, Owin = psm('Ocmp'), psm('Osel'), psm('Owin')
        M0, M1 = psm('M0'), psm('M1')
        gk = C.gains[0:64, :]
        jc = j * 8

        with ExitStack() as es2:
            sb2 = lambda n, s, d=F32: es2.enter_context(nc.sbuf_tensor(pfx + 'c_' + n, s, d))
            Wc = sb2('Wc', [64, 64, 64])
            PEm = sb2('PEm', [64, 64])
            Lc = U.Lc
            nblk = Lc // 16 - 1
            X = sb2('X', [64, Lc])
            comp = sb2('comp', [64, 512])
            bcol = sb2('bcol', [64, 2])
            k.dma(Wc[:], U.a_cmp_w[j].rearrange("c (hl d) e -> d (c hl) e", d=64), writes=[Wc])
            k.dma(PEm[:], U.a_cmp_pos[j].rearrange("c hl d -> d (c hl)"), writes=[PEm], allow_slow_non_contiguous=True)
            for c in range(2):
                k.dma(X[:, :], U.ct_src(c, g), writes=[X])
                for hl in range(32):
                    k.op('pe', lambda en: en.matmul(M0[0:64, 0:1], Wc[:, c * 32 + hl, :], PEm[:, c * 32 + hl:c * 32 + hl + 1],
                                                    start=(hl == 0), stop=(hl == 31)), reads=[Wc, PEm], writes=[M0])
                k.op('dve', lambda en: en.tensor_copy(bcol[:, c:c + 1], M0[0:64, 0:1]), reads=[M0], writes=[bcol])
                for hl in range(32):
                    half, l = hl // 16, hl % 16
                    st0 = 16 * half + l
                    k.op('pe', lambda en: en.matmul(M1[0:64, 0:nblk], Wc[:, c * 32 + hl, :], X[:, bass.ds(st0, nblk, 16)],
                                                    start=(hl == 0), stop=(hl == 31)), reads=[Wc, X], writes=[M1])
                k.op('pool', lambda en: en.memset(comp[:, nblk:512], 0.0), writes=[comp])
                k.op('act', lambda en: en.activation(comp[:, 0:nblk], M1[0:64, 0:nblk], AF.Identity, bias=bcol[:, c:c + 1]),
                     reads=[M1, bcol], writes=[comp])
                if c == 0:
                    fm_norm(k, C, W, comp, 512, KcT[0:64, 0:512], KcT, gk[:, jc + 1:jc + 2], gk[:, 16:17], 1.0 / 64)
                else:
                    for kt in range(4):
                        k.op('pe', lambda en: en.transpose(M0[0:128, 0:64], comp[:, kt * 128:(kt + 1) * 128], C.ident[0:64, 0:64]),
                             reads=[comp, 'consts'], writes=[M0])
                        k.op('dve', lambda en: en.tensor_copy(Vc[:, kt, 0:64], M0[0:128, 0:64]), reads=[M0], writes=[Vc])
                    k.op('pool', lambda en: en.memset(Vc[:, :, 64:65], 1.0), writes=[Vc])
            k.dma(KcT[64:70, :], U.tab['kaugc'][:, :], writes=[KcT])
            k.barrier()

        with ExitStack() as es2:
            sb2 = lambda n, s, d=F32: es2.enter_context(nc.sbuf_tensor(pfx + 'k_' + n, s, d))
            xs_ = [sb2('x%d' % i, [64, 512]) for i in range(2)]
            vst = sb2('vst', [128, NKT, 64])
            k.op('pool', lambda en: en.memset(vst[:], 0.0), writes=[vst])
            for (KT, V, ksrc, vsrc, gcol) in [(KsT, Vs, U.ks_src, U.vs_src, jc + 2), (KwT, Vw, U.kw_src, U.vw_src, jc + 3)]:
                ci = 0
                for c0 in range(0, L, 512):
                    n = min(512, L - c0)
                    x = xs_[ci % 2]
                    ci += 1
                    pieces = ksrc(g, c0, n)
                    if not pieces:
                        ci -= 1
                        continue
                    for pi_, (ap, lo, hi) in enumerate(pieces):
                        k.dma(x[:, lo:hi], ap, writes=[x], acc=(pi_ > 0), allow_slow_non_contiguous=True)
                    if U.k_prenormed:
                        k.op('dve', lambda en: en.tensor_copy(KT[0:64, c0:c0 + n], x[:, 0:n]), reads=[x], writes=[KT])
                    else:
                        fm_norm(k, C, W, x, n, KT[0:64, c0:c0 + n], KT, gk[:, gcol:gcol + 1], gk[:, 16:17], 1.0 / 64)
                k.dma(KT[64:70, 0:L], U.tab['kaug'][:, 0:L], writes=[KT])
                for (ap, dst) in vsrc(g, vst):
                    k.dma(dst, ap, writes=[vst], acc=True)
                NF = L // 128
                k.op('dve', lambda en: en.tensor_copy(V[:, 0:NF, 0:64], vst[:, 0:NF, :]), reads=[vst], writes=[V])
                if L % 128:
                    rem = L % 128
                    k.op('pool', lambda en: en.memset(V[:, NF, :], 0.0), reads=[], writes=[V])
                    k.op('dve', lambda en: en.tensor_copy(V[0:rem, NF, 0:64], vst[0:rem, NF, :]), reads=[vst], writes=[V])
                k.op('pool', lambda en: en.memset(V[:, :, 64:65], 1.0), writes=[V])
            k.barrier()

        with ExitStack() as es2:
            sb2 = lambda n, s, d=F32: es2.enter_context(nc.sbuf_tensor(pfx + 'q_' + n, s, d))
            qraw = [sb2('qraw%d' % i, [64, 512]) for i in range(2)]
            QA = [sb2('QA%d' % i, [70, 512], BF16) for i in range(2)]
            eT = [sb2('eT%d' % i, [128, 512], BF16) for i in range(4)]
            P = [sb2('P%d' % i, [128, 512], BF16) for i in range(3)]
            rrow = sb2('rrow', [128, 512])
            rb = sb2('rb', [128, 512])
            oc = sb2('oc', [64, 512])
            t1 = sb2('t1', [64, 512])
            acc = sb2('acc', [64, 512])
            score = sb2('score', [128, 128])
            stmp = sb2('stmp', [128, 128])
            m8 = sb2('m8', [128, 16])
            NM = sb2('NM', [128, 128])
            NMT = sb2('NMT', [128, 512], BF16)
            gsb = sb2('gsb', [48, 128])
            zsb = sb2('zsb', [64, 512])
            si = 0
            pi = 0
            for qi, (q0, nq) in enumerate(U.qtiles):
                HQ = 4 * nq
                i = (U.past + q0) // 128
                qr, qa = qraw[qi % 2], QA[qi % 2]
                k.dma(qr[:, 0:HQ].rearrange("d (h q) -> d h q", h=4),
                      U.QT[256 * g:256 * g + 256, q0:q0 + nq].rearrange("(h d) q -> d h q", d=64), writes=[qr])
                fm_norm(k, C, W, qr, HQ, qa[0:64, 0:HQ], qa, gk[:, jc:jc + 1], gk[:, 17:18], 1.0)
                k.dma(qa[64:70, 0:HQ].rearrange("r (h q) -> r h q", h=4), U.tab['qaug'][:, qi, 4 * g:4 * g + 4, 0:nq],
                      writes=[qa], acc=True)
                tiles = [kt for kt in range(4) if i - 16 * kt >= 0]
                for idx, kt in enumerate(tiles):
                    Sp = S[si % 2]
                    si += 1
                    d = i - 16 * kt
                    msk = (d <= 16) if U.past == 0 else (kt == 3)
                    k.op('pe', lambda en: en.matmul(Sp[0:128, 0:HQ], KcT[:, kt * 128:(kt + 1) * 128], qa[0:70, 0:HQ],
                                                    start=True, stop=not msk), reads=[KcT, qa], writes=[Sp])
                    if msk:
                        dd = min(d, 16)
                        k.op('pe', lambda en: en.matmul(Sp[0:128, 0:HQ], C.identb[:, :], C.cm[:, dd, :, 0:nq],
                                                        start=False, stop=True), reads=['consts'], writes=[Sp])
                    e = eT[idx]
                    k.op('act', lambda en: en.activation(e[:, 0:HQ], Sp[0:128, 0:HQ], AF.Exp), reads=[Sp], writes=[e])
                    k.op('pe', lambda en: en.matmul(Ocmp[0:65, 0:HQ], Vc[:, kt, :], e[:, 0:HQ],
                                                    start=(idx == 0), stop=(idx == len(tiles) - 1)), reads=[Vc, e], writes=[Ocmp])
                k.op('dve', lambda en: en.tensor_scalar_add(rrow[64:65, 0:HQ], Ocmp[64:65, 0:HQ], TINY), reads=[Ocmp], writes=[rrow])
                k.op('dve', lambda en: en.reciprocal(rrow[64:65, 0:HQ], rrow[64:65, 0:HQ]), reads=[rrow], writes=[rrow])
                k.op('pe', lambda en: en.matmul(M0[0:128, 0:HQ], C.onesf[64:65, 0:128], rrow[64:65, 0:HQ], start=True, stop=True),
                     reads=[rrow, 'consts'], writes=[M0])
                k.op('act', lambda en: en.copy(rb[:, 0:HQ], M0[0:128, 0:HQ]), reads=[M0], writes=[rb])
                k.op('dve', lambda en: en.tensor_tensor(oc[:, 0:HQ], Ocmp[0:64, 0:HQ], rb[0:64, 0:HQ], ALU.mult),
                     reads=[Ocmp, rb], writes=[oc])
                nmm = 4 * len(tiles)
                cnt = 0
                for idx, kt in enumerate(tiles):
                    e = eT[idx]
                    k.op('dve', lambda en: en.tensor_tensor(e[:, 0:HQ], e[:, 0:HQ], rb[:, 0:HQ], ALU.mult), reads=[e, rb], writes=[e])
                    for h in range(4):
                        k.op('pe', lambda en: en.matmul(M1[0:nq, 0:128], e[:, h * nq:(h + 1) * nq], C.poolm[:, kt, :],
                                                        start=(cnt == 0), stop=(cnt == nmm - 1)), reads=[e, 'consts'], writes=[M1])
                        cnt += 1
                bon = U.bonus_ap(C, qi, nq)
                k.op('dve', lambda en: en.tensor_tensor(score[0:nq, :], M1[0:nq, 0:128], bon, ALU.add),
                     reads=[M1, 'consts'], writes=[score])
                k.op('dve', lambda en: en.tensor_scalar_add(score[0:nq, 0:1], score[0:nq, 0:1], 1e4), reads=[score], writes=[score])
                k.op('dve', lambda en: en.max(m8[0:nq, 0:8], score[0:nq, :]), reads=[score], writes=[m8])
                k.op('dve', lambda en: en.match_replace(stmp[0:nq, :], m8[0:nq, 0:8], score[0:nq, :], -3e38),
                     reads=[score, m8], writes=[stmp])
                k.op('dve', lambda en: en.max(m8[0:nq, 8:16], stmp[0:nq, :]), reads=[stmp], writes=[m8])
                tc_ = U.thr_col
                k.op('dve', lambda en: en.tensor_scalar(NM[0:nq, :], score[0:nq, :], m8[0:nq, tc_:tc_ + 1], NEG, ALU.is_lt, ALU.mult),
                     reads=[score, m8], writes=[NM])
                k.op('pe', lambda en: en.transpose(M0[0:128, 0:nq], NM[0:nq, :], C.ident[0:nq, 0:nq]), reads=[NM, 'consts'], writes=[M0])
                k.op('act', lambda en: en.copy(NMT[:, 0:HQ].rearrange("p (h q) -> p h q", h=4),
                                               M0[0:128, 0:nq].unsqueeze(1).to_broadcast([128, 4, nq])), reads=[M0], writes=[NMT])
                last = NKT - 1 if U.past > 0 else i
                work = []
                for (Ops, KT, V, kts, use_blk) in [(Osel, KsT, Vs, list(range(0, last + 1)), True),
                                                   (Owin, KwT, Vw, list(range(max(0, i - 4), last + 1)), False)]:
                    for n_i, kt in enumerate(kts):
                        work.append((Ops, KT, V, kt, use_blk, n_i == 0, n_i == len(kts) - 1))
                pend = None

                def finish_tile(pd):
                    (Ops_, V_, kt_, nk_, Sp_, first_, last_) = pd
                    nonlocal_pi = P_idx[0]
                    p = P[nonlocal_pi % 3]
                    P_idx[0] += 1
                    k.op('act', lambda en: en.activation(p[0:nk_, 0:HQ], Sp_[0:nk_, 0:HQ], AF.Exp), reads=[Sp_], writes=[p])
                    k.op('pe', lambda en: en.matmul(Ops_[0:65, 0:HQ], V_[0:nk_, kt_, :], p[0:nk_, 0:HQ],
                                                    start=first_, stop=last_), reads=[V_, p], writes=[Ops_])
                P_idx = [pi]
                for (Ops, KT, V, kt, use_blk, first, lastf) in work:
                    nk = min(128, L - kt * 128)
                    Sp = S[si % 2]
                    si += 1
                    extra = []
                    if use_blk and kt < 64:
                        extra.append((C.emat[:, kt, 0:nk], NMT[:, 0:HQ], NMT))
                    if kt == last:
                        extra.append((C.identb[0:nk, 0:nk], C.caus[0:nk, :, 0:nq], 'consts'))
                    if (not use_blk) and kt == i - 4:
                        extra.append((C.identb[0:nk, 0:nk], C.wlow[0:nk, :, 0:nq], 'consts'))
                    k.op('pe', lambda en: en.matmul(Sp[0:nk, 0:HQ], KT[:, kt * 128:kt * 128 + nk], qa[0:70, 0:HQ],
                                                    start=True, stop=(len(extra) == 0)), reads=[KT, qa], writes=[Sp])
                    for xi_, (l_, r_, key_) in enumerate(extra):
                        k.op('pe', lambda en: en.matmul(Sp[0:nk, 0:HQ], l_, r_, start=False, stop=(xi_ == len(extra) - 1)),
                             reads=[key_, 'consts'], writes=[Sp])
                    if pend is not None:
                        finish_tile(pend)
                    pend = (Ops, V, kt, nk, Sp, first, lastf)
                if pend is not None:
                    finish_tile(pend)
                pi = P_idx[0]
                k.dma(gsb[:, 0:nq], U.GT[:, q0:q0 + nq], writes=[gsb])
                k.op('act', lambda en: en.activation(gsb[:, 0:nq], gsb[:, 0:nq], AF.Sigmoid, bias=C.gateb[:, j:j + 1]),
                     reads=[gsb, 'consts'], writes=[gsb])
                for br, Ops in enumerate([None, Osel, Owin]):
                    for h in range(4):
                        rsel = br * 16 + 4 * g + h
                        k.op('pe', lambda en: en.matmul(M1[0:64, h * nq:(h + 1) * nq], C.sel[:, rsel, :], gsb[:, 0:nq],
                                                        start=True, stop=True), reads=[gsb, 'consts'], writes=[M1])
                    if br == 0:
                        k.op('dve', lambda en: en.tensor_tensor(acc[:, 0:HQ], oc[:, 0:HQ], M1[0:64, 0:HQ], ALU.mult),
                             reads=[oc, M1], writes=[acc])
                        continue
                    k.op('dve', lambda en: en.tensor_scalar_add(rrow[64:65, 0:HQ], Ops[64:65, 0:HQ], TINY), reads=[Ops], writes=[rrow])
                    k.op('dve', lambda en: en.reciprocal(rrow[64:65, 0:HQ], rrow[64:65, 0:HQ]), reads=[rrow], writes=[rrow])
                    k.op('pe', lambda en: en.matmul(M0[0:64, 0:HQ], C.onesf[64:65, 0:64], rrow[64:65, 0:HQ], start=True, stop=True),
                         reads=[rrow, 'consts'], writes=[M0])
                    k.op('act', lambda en: en.copy(rb[0:64, 0:HQ], M0[0:64, 0:HQ]), reads=[M0], writes=[rb])
                    k.op('dve', lambda en: en.tensor_tensor(t1[:, 0:HQ], Ops[0:64, 0:HQ], rb[0:64, 0:HQ], ALU.mult),
                         reads=[Ops, rb], writes=[t1])
                    k.op('dve', lambda en: en.tensor_tensor(t1[:, 0:HQ], t1[:, 0:HQ], M1[0:64, 0:HQ], ALU.mult),
                         reads=[t1, M1], writes=[t1])
                    k.op('pool', lambda en: en.tensor_tensor(acc[:, 0:HQ], acc[:, 0:HQ], t1[:, 0:HQ], ALU.add),
                         reads=[acc, t1], writes=[acc])
                zv = U.ZT[256 * g:256 * g + 256, q0:q0 + nq].rearrange("(h d) q -> d h q", d=64)
                k.dma(zsb[:, 0:HQ].rearrange("d (h q) -> d h q", h=4), zv, writes=[zsb])
                k.op('act', lambda en: en.activation(zsb[:, 0:HQ], zsb[:, 0:HQ], AF.Silu), reads=[zsb], writes=[zsb])
                k.op('dve', lambda en: en.tensor_tensor(zsb[:, 0:HQ], zsb[:, 0:HQ], acc[:, 0:HQ], ALU.mult), reads=[zsb, acc], writes=[zsb])
                k.dma(U.MT[256 * g:256 * g + 256, q0:q0 + nq].rearrange("(h d) q -> d h q", d=64),
                      zsb[:, 0:HQ].rearrange("d (h q) -> d h q", h=4), reads=[zsb], writes=[('MT', U.tag)], q='pool', acc=True)
            k.barrier()


G3_STEPS = None


def run_interleaved(gens):
    gens = list(gens)
    rounds = 0
    while gens:
        if G3_STEPS is not None and rounds >= G3_STEPS:
            break
        rounds += 1
        nxt = []
        for g_ in gens:
            try:
                next(g_)
                nxt.append(g_)
            except StopIteration:
                pass
        gens = nxt


def gdn_layer(k, nc, C, U, jb, P_):
    T = U.T
    NCH = (T + 127) // 128
    NS = NCH * 8
    pfx = 'gd%s_' % U.tag
    with ExitStack() as es:
        sb = lambda n, s, d=F32: es.enter_context(nc.sbuf_tensor(pfx + 'a_' + n, s, d))
        cw = sb('cw', [128, 24, 4])
        xh = [sb('xh%d' % i, [128, 515]) for i in range(2)]
        y = [sb('y%d' % i, [128, 512]) for i in range(2)]
        W = dict(sq=sb('sq', [128, 512]), r=sb('r', [128, 512]),
                 psn=k.reg_psum(es.enter_context(nc.psum_tensor(pfx + 'a_psn', [128, 512], F32))))
        for ft in range(24):
            k.dma(cw[:, ft, :], P_['b_conv_w'][jb][:, ft * 128:(ft + 1) * 128].rearrange("j p -> p j"), writes=[cw], acc=True,
                  allow_slow_non_contiguous=True)
        ci = 0
        for ft in range(24):
            for c0 in range(0, T, 512):
                n = min(512, T - c0)
                x_, y_ = xh[ci % 2], y[ci % 2]
                ci += 1
                rows = U.GQKV[ft * 128:(ft + 1) * 128, :]
                if c0 == 0:
                    U.conv_hist(k, x_, ft)
                    k.dma(x_[:, 3:3 + n], rows[:, 0:n], writes=[x_], acc=True)
                else:
                    k.dma(x_[:, 0:3 + n], rows[:, c0 - 3:c0 + n], writes=[x_])
                k.op('dve', lambda en: en.tensor_scalar(y_[:, 0:n], x_[:, 3:3 + n], cw[:, ft, 3:4], None, ALU.mult), reads=[x_, cw], writes=[y_])
                for jj in (2, 1, 0):
                    k.op('dve', lambda en: en.scalar_tensor_tensor(y_[:, 0:n], x_[:, jj:jj + n], cw[:, ft, jj:jj + 1], y_[:, 0:n],
                                                                   ALU.mult, ALU.add), reads=[x_, cw, y_], writes=[y_])
                k.op('act', lambda en: en.activation(y_[:, 0:n], y_[:, 0:n], AF.Silu), reads=[y_], writes=[y_])
                if ft < 16:
                    gcol = C.gains[:, 18:19] if ft < 8 else C.gains[:, 19:20]
                    fm_norm(k, C, W, y_, n, y_[:, 0:n], y_, gcol, C.epsc[:, 0:1], 1.0, P=128)
                k.dma(U.GN[ft * 128:(ft + 1) * 128, c0:c0 + n], y_[:, 0:n], reads=[y_], writes=['GN'], q='pool', acc=True)
    k.barrier()
    if checkpoint('g1'):
        return
    with ExitStack() as es:
        sb = lambda n, s, d=F32: es.enter_context(nc.sbuf_tensor(pfx + 'b_' + n, s, d))
        psb = lambda n: k.reg_psum(es.enter_context(nc.psum_tensor(pfx + 'b_' + n, [128, 512], F32)))
        bd = sb('bd', [128, NCH, 16])
        Beta, nBeta, G = sb('Beta', [128, NS]), sb('nBeta', [128, NS]), sb('G', [128, NS])
        t_a, t_b = sb('ta', [128, NS]), sb('tb', [128, NS])
        eG, eGl, eD, bE = sb('eG', [128, NS]), sb('eGl', [128, NS]), sb('eD', [128, NS]), sb('bE', [128, NS])
        dtb, nal = sb('dtb', [128, 8]), sb('nal', [128, 8])
        pa, pb = psb('pa'), psb('pb')
        k.op('pool', lambda en: en.memset(bd[:], 0.0), writes=[bd])
        NF = T // 128
        if NF:
            k.dma(bd[:, 0:NF, :], U.GBD[0:NF * 128, :].rearrange("(c p) f -> p c f", p=128), writes=[bd])
        if T % 128:
            k.dma(bd[0:T % 128, NF, :], U.GBD[NF * 128:T, :], writes=[bd])
        k.dma(dtb[:], P_['b_dt_bias'][jb].partition_broadcast(128), writes=[dtb])
        k.dma(nal[:], P_['b_a_log'][jb].partition_broadcast(128), writes=[nal])
        k.op('act', lambda en: en.activation(nal[:], nal[:], AF.Exp), reads=[nal], writes=[nal])
        k.op('dve', lambda en: en.tensor_scalar(nal[:], nal[:], -1.0, None, ALU.mult), reads=[nal], writes=[nal])
        v3 = lambda t_: t_[:, 0:NS].rearrange("p (c h) -> p c h", h=8)
        k.op('act', lambda en: en.activation(v3(Beta), bd[:, :, 0:8], AF.Sigmoid), reads=[bd], writes=[Beta])
        k.op('dve', lambda en: en.tensor_scalar(nBeta[:], Beta[:], -1.0, None, ALU.mult), reads=[Beta], writes=[nBeta])
        k.op('dve', lambda en: en.tensor_tensor(v3(t_a), bd[:, :, 8:16], dtb[:].unsqueeze(1).to_broadcast([128, NCH, 8]), ALU.add),
             reads=[bd, dtb], writes=[t_a])
        k.op('dve', lambda en: en.tensor_scalar_max(eD[:], t_a[:], 0.0), reads=[t_a], writes=[eD])
        k.op('dve', lambda en: en.scalar_tensor_tensor(t_b[:], eD[:], -2.0, t_a[:], ALU.mult, ALU.add), reads=[t_a, eD], writes=[t_b])
        k.op('act', lambda en: en.activation(t_b[:], t_b[:], AF.Exp), reads=[t_b], writes=[t_b])
        k.op('act', lambda en: en.activation(t_b[:], t_b[:], AF.Ln, bias=C.onesf[:, 0:1], scale=1.0), reads=[t_b, 'consts'], writes=[t_b])
        k.op('dve', lambda en: en.tensor_tensor(t_a[:], eD[:], t_b[:], ALU.add), reads=[eD, t_b], writes=[t_a])
        k.op('dve', lambda en: en.tensor_tensor(v3(G), v3(t_a), nal[:].unsqueeze(1).to_broadcast([128, NCH, 8]), ALU.mult),
             reads=[t_a, nal], writes=[G])
        if T % 128:
            assert T % 128 == 4
            for t_ in (G, Beta, nBeta):
                k.op('dve', lambda en: en.tensor_scalar(t_[:, (NCH - 1) * 8:NCH * 8], t_[:, (NCH - 1) * 8:NCH * 8], C.pm4[:, 0:1], None, ALU.mult),
                     reads=[t_, 'consts'], writes=[t_])
        k.op('pe', lambda en: en.matmul(pa[:, 0:NS], C.tri[:, :], G[:, 0:NS], start=True, stop=True), reads=[G, 'consts'], writes=[pa])
        k.op('pe', lambda en: en.matmul(pb[:, 0:NS], C.onesf[:, :], G[:, 0:NS], start=True, stop=True), reads=[G, 'consts'], writes=[pb])
        k.op('act', lambda en: en.activation(eG[:], pa[:, 0:NS], AF.Exp), reads=[pa], writes=[eG])
        k.op('act', lambda en: en.activation(eGl[:], pb[:, 0:NS], AF.Exp), reads=[pb], writes=[eGl])
        k.op('dve', lambda en: en.tensor_copy(t_a[:], pa[:, 0:NS]), reads=[pa], writes=[t_a])
        k.op('dve', lambda en: en.tensor_tensor(t_b[:], pb[:, 0:NS], t_a[:], ALU.subtract), reads=[pb, t_a], writes=[t_b])
        k.op('act', lambda en: en.activation(eD[:], t_b[:], AF.Exp), reads=[t_b], writes=[eD])
        k.op('dve', lambda en: en.tensor_tensor(bE[:], Beta[:], eG[:], ALU.mult), reads=[Beta, eG], writes=[bE])
        k.barrier()
        if checkpoint('g2'):
            return
        with ExitStack() as es3:
            sb3 = lambda n, s, d=F32: es3.enter_context(nc.sbuf_tensor(pfx + 'c_' + n, s, d))
            H = []
            for h in range(8):
                hb = Ctx()
                hb.ps = k.reg_psum(es3.enter_context(nc.psum_tensor(pfx + 'c_ps%d' % h, [128, 512], F32))) if h < 6 else None
                for nm in ['k_tm', 'v_tm', 'Gp', 'dec', 'decT', 'YT', 'Y', 'Tt', 'wT', 'u', 'qkT', 'qeT', 'kd', 'kbe', 'vb', 'vnew', 'S', 'dg']:
                    setattr(hb, nm, sb3('%s%d' % (nm, h), [128, 128]))
                hb.qkv = sb3('qkv%d' % h, [128, 3, 128])
                hb.oT = sb3('oT%d' % h, [128, 512])
                hb.z = sb3('z%d' % h, [128, 512])
                H.append(hb)
            H[6].ps = pa
            H[7].ps = pb
            Wn = dict(sq=sb3('nsq', [128, 512]), r=sb3('nr', [128, 512]), psn=None)
            for h in range(8):
                U.state_init(k, H[h].S, h)

            def slot(hb, i, w=1):
                return hb.ps[:, i * 128:(i + w) * 128]

            def steps(h, c):
                hb = H[h]
                c0 = c * 128
                n = min(128, T - c0)
                col = c * 8 + h
                K_ = lambda i: hb.ps
                gn = U.GN.rearrange("(s hh p) t -> p s hh t", s=3, hh=8, p=128)
                if n < 128:
                    k.op('pool', lambda en: en.memset(hb.qkv[:], 0.0), writes=[hb.qkv])
                k.dma(hb.qkv[:, :, 0:n], gn[:, :, h, c0:c0 + n], writes=[hb.qkv], acc=(n < 128))
                qT, kT, vT = hb.qkv[:, 0, :], hb.qkv[:, 1, :], hb.qkv[:, 2, :]
                yield
                k.op('pe', lambda en: en.transpose(slot(hb, 0), kT, C.ident[:, :]), reads=[hb.qkv, 'consts'], writes=[K_(0)])
                k.op('pe', lambda en: en.transpose(slot(hb, 1), vT, C.ident[:, :]), reads=[hb.qkv, 'consts'], writes=[K_(1)])
                k.op('dve', lambda en: en.tensor_scalar(hb.Gp[:], C.slt[:, :], G[:, col:col + 1], None, ALU.mult), reads=['consts', G], writes=[hb.Gp])
                yield
                k.op('act', lambda en: en.copy(hb.k_tm[:], slot(hb, 0)), reads=[K_(0)], writes=[hb.k_tm])
                k.op('act', lambda en: en.copy(hb.v_tm[:], slot(hb, 1)), reads=[K_(1)], writes=[hb.v_tm])
                k.op('pe', lambda en: en.matmul(slot(hb, 2), C.tri[:, :], hb.Gp[:], start=True, stop=False), reads=[hb.Gp, 'consts'], writes=[K_(2)])
                k.op('pe', lambda en: en.matmul(slot(hb, 2), C.ident[:, :], C.mstrict[:, :], start=False, stop=True), reads=['consts'], writes=[K_(2)])
                k.op('pe', lambda en: en.matmul(slot(hb, 3), hb.Gp[:], C.tri[:, :], start=True, stop=False), reads=[hb.Gp, 'consts'], writes=[K_(3)])
                k.op('pe', lambda en: en.matmul(slot(hb, 3), C.ident[:, :], C.minclT[:, :], start=False, stop=True), reads=['consts'], writes=[K_(3)])
                yield
                k.op('act', lambda en: en.activation(hb.dec[:], slot(hb, 2), AF.Exp), reads=[K_(2)], writes=[hb.dec])
                k.op('act', lambda en: en.activation(hb.decT[:], slot(hb, 3), AF.Exp), reads=[K_(3)], writes=[hb.decT])
                k.op('pe', lambda en: en.matmul(slot(hb, 0), kT, kT, start=True, stop=True), reads=[hb.qkv], writes=[K_(0)])
                k.op('pe', lambda en: en.matmul(slot(hb, 1), kT, qT, start=True, stop=True), reads=[hb.qkv], writes=[K_(1)])
                k.op('dve', lambda en: en.tensor_scalar(hb.kbe[:], hb.k_tm[:], bE[:, col:col + 1], None, ALU.mult), reads=[hb.k_tm, bE], writes=[hb.kbe])
                k.op('dve', lambda en: en.tensor_scalar(hb.vb[:], hb.v_tm[:], Beta[:, col:col + 1], None, ALU.mult), reads=[hb.v_tm, Beta], writes=[hb.vb])
                k.op('act', lambda en: en.activation(hb.kd[:], hb.k_tm[:], AF.Copy, scale=eD[:, col:col + 1]), reads=[hb.k_tm, eD], writes=[hb.kd])
                k.op('act', lambda en: en.activation(hb.dg[:], C.ident[:, :], AF.Copy, scale=eG[:, col:col + 1]), reads=['consts', eG], writes=[hb.dg])
                yield
                k.op('dve', lambda en: en.scalar_tensor_tensor(hb.YT[:], slot(hb, 0), nBeta[:, col:col + 1], hb.dec[:], ALU.mult, ALU.mult),
                     reads=[K_(0), nBeta, hb.dec], writes=[hb.YT])
                k.op('dve', lambda en: en.tensor_tensor(hb.qkT[:], slot(hb, 1), hb.decT[:], ALU.mult), reads=[K_(1), hb.decT], writes=[hb.qkT])
                yield
                k.op('pe', lambda en: en.transpose(slot(hb, 2), hb.YT[:], C.ident[:, :]), reads=[hb.YT, 'consts'], writes=[K_(2)])
                k.op('pe', lambda en: en.matmul(slot(hb, 3), C.onesf[:, :], hb.dg[:], start=True, stop=True), reads=[hb.dg, 'consts'], writes=[K_(3)])
                yield
                k.op('act', lambda en: en.copy(hb.Y[:], slot(hb, 2)), reads=[K_(2)], writes=[hb.Y])
                k.op('dve', lambda en: en.tensor_tensor(hb.Tt[:], slot(hb, 2), C.ident[:, :], ALU.add), reads=[K_(2), 'consts'], writes=[hb.Tt])
                k.op('dve', lambda en: en.tensor_tensor(hb.qeT[:], slot(hb, 3), qT, ALU.mult), reads=[K_(3), hb.qkv], writes=[hb.qeT])
                yield
                k.op('pe', lambda en: en.matmul(slot(hb, 0), hb.YT[:], hb.Y[:], start=True, stop=True), reads=[hb.YT, hb.Y], writes=[K_(0)])
                k.op('pe', lambda en: en.matmul(slot(hb, 1), hb.Y[:], hb.YT[:], start=True, stop=True), reads=[hb.YT, hb.Y], writes=[K_(1)])
                yield
                k.op('act', lambda en: en.copy(hb.Y[:], slot(hb, 0)), reads=[K_(0)], writes=[hb.Y])
                k.op('dve', lambda en: en.tensor_copy(hb.YT[:], slot(hb, 1)), reads=[K_(1)], writes=[hb.YT])
                yield
                for kk in range(1, 7):
                    lastk = (kk == 6)
                    k.op('pe', lambda en: en.matmul(slot(hb, 2), hb.YT[:], hb.Tt[:], start=True, stop=True), reads=[hb.YT, hb.Tt], writes=[K_(2)])
                    if not lastk:
                        k.op('pe', lambda en: en.matmul(slot(hb, 0), hb.YT[:], hb.Y[:], start=True, stop=True), reads=[hb.YT, hb.Y], writes=[K_(0)])
                        k.op('pe', lambda en: en.matmul(slot(hb, 1), hb.Y[:], hb.YT[:], start=True, stop=True), reads=[hb.YT, hb.Y], writes=[K_(1)])
                    yield
                    k.op('dve', lambda en: en.tensor_tensor(hb.Tt[:], hb.Tt[:], slot(hb, 2), ALU.add), reads=[K_(2), hb.Tt], writes=[hb.Tt])
                    if not lastk:
                        k.op('act', lambda en: en.copy(hb.Y[:], slot(hb, 0)), reads=[K_(0)], writes=[hb.Y])
                        k.op('act', lambda en: en.copy(hb.YT[:], slot(hb, 1)), reads=[K_(1)], writes=[hb.YT])
                    yield
                k.op('pe', lambda en: en.matmul(slot(hb, 0), hb.kbe[:], hb.Tt[:], start=True, stop=True), reads=[hb.kbe, hb.Tt], writes=[K_(0)])
                k.op('pe', lambda en: en.matmul(slot(hb, 1), hb.Tt[:], hb.vb[:], start=True, stop=True), reads=[hb.vb, hb.Tt], writes=[K_(1)])
                yield
                k.op('act', lambda en: en.copy(hb.wT[:], slot(hb, 0)), reads=[K_(0)], writes=[hb.wT])
                k.op('act', lambda en: en.copy(hb.u[:], slot(hb, 1)), reads=[K_(1)], writes=[hb.u])
                yield
                k.op('pe', lambda en: en.matmul(slot(hb, 2), hb.wT[:], hb.S[:], start=True, stop=True), reads=[hb.wT, hb.S], writes=[K_(2)])
                yield
                k.op('dve', lambda en: en.tensor_tensor(hb.vnew[:], hb.u[:], slot(hb, 2), ALU.subtract), reads=[K_(2), hb.u], writes=[hb.vnew])
                yield
                k.op('pe', lambda en: en.matmul(slot(hb, 3), hb.S[:], hb.qeT[:], start=True, stop=False), reads=[hb.qeT, hb.S], writes=[K_(3)])
                k.op('pe', lambda en: en.matmul(slot(hb, 3), hb.vnew[:], hb.qkT[:], start=False, stop=True), reads=[hb.vnew, hb.qkT], writes=[K_(3)])
                k.op('pe', lambda en: en.matmul(slot(hb, 0), hb.kd[:], hb.vnew[:], start=True, stop=True), reads=[hb.kd, hb.vnew], writes=[K_(0)])
                yield
                cc = c % 4
                k.op('act', lambda en: en.copy(hb.oT[:, cc * 128:(cc + 1) * 128], slot(hb, 3)), reads=[K_(3)], writes=[hb.oT])
                k.op('dve', lambda en: en.scalar_tensor_tensor(hb.S[:], hb.S[:], eGl[:, col:col + 1], slot(hb, 0), ALU.mult, ALU.add),
                     reads=[K_(0), hb.S, eGl], writes=[hb.S])
                yield
                if cc == 3 or c == NCH - 1:
                    t0 = (c - cc) * 128
                    nn = min(T, c0 + 128) - t0
                    Wn['psn'] = hb.ps
                    k.dma(hb.z[:, 0:nn], U.GZ[h * 128:(h + 1) * 128, t0:t0 + nn], writes=[hb.z])
                    k.op('act', lambda en: en.activation(hb.z[:, 0:nn], hb.z[:, 0:nn], AF.Silu), reads=[hb.z], writes=[hb.z])
                    sq, r = Wn['sq'], Wn['r']
                    k.op('act', lambda en: en.activation(sq[:, 0:nn], hb.oT[:, 0:nn], AF.Square), reads=[hb.oT], writes=[sq])
                    k.op('pe', lambda en: en.matmul(hb.ps[:, 0:nn], C.onesf[:, :], sq[:, 0:nn], start=True, stop=True),
                         reads=[sq, 'consts'], writes=[K_(0), K_(1), K_(2), K_(3)])
                    k.op('act', lambda en: en.activation(r[:, 0:nn], hb.ps[:, 0:nn], AF.Sqrt, bias=C.epsc[:, 0:1], scale=1.0 / 128),
                         reads=[K_(0), K_(1), K_(2), K_(3)], writes=[r])
                    k.op('dve', lambda en: en.reciprocal(r[:, 0:nn], r[:, 0:nn]), reads=[r], writes=[r])
                    k.op('dve', lambda en: en.scalar_tensor_tensor(hb.oT[:, 0:nn], hb.oT[:, 0:nn], C.ogain[:, jb:jb + 1], r[:, 0:nn], ALU.mult, ALU.mult),
                         reads=[hb.oT, r, 'consts'], writes=[hb.oT])
                    k.op('dve', lambda en: en.tensor_tensor(hb.oT[:, 0:nn], hb.oT[:, 0:nn], hb.z[:, 0:nn], ALU.mult), reads=[hb.oT, hb.z], writes=[hb.oT])
                    k.dma(U.MT[h * 128:(h + 1) * 128, t0:t0 + nn], hb.oT[:, 0:nn], reads=[hb.oT], writes=[('MT', U.tag)], q='pool', acc=True)
                yield

            for c in range(NCH if G3_STEPS is None else 1):
                run_interleaved([steps(h, c) for h in range(8)])
            for h in range(8):
                U.state_out(k, H[h].S, h)
    k.barrier()


def conf_layer(k, nc, C, U, jc, P_):
    T = U.T
    pfx = 'cf%s_' % U.tag
    with ExitStack() as es:
        sb = lambda n, s, d=F32: es.enter_context(nc.sbuf_tensor(pfx + n, s, d))
        psb = lambda n: k.reg_psum(es.enter_context(nc.psum_tensor(pfx + n, [128, 512], F32)))
        cw = sb('cw', [128, 8, 31])
        cb_ = sb('cbias', [128, 8])
        lg, lb = sb('lg', [128, 8]), sb('lb', [128, 8])
        a_ = [sb('a%d' % i, [128, 512]) for i in range(2)]
        b_ = [sb('b%d' % i, [128, 512]) for i in range(2)]
        uh = [sb('uh%d' % i, [128, 542]) for i in range(2)]
        cbuf = sb('cbuf', [128, 8, 512])
        sq = sb('sq', [128, 512])
        mean, rstd, tmp = sb('mean', [128, 512]), sb('rstd', [128, 512]), sb('tmp', [128, 512])
        zt = [sb('z%d' % i, [128, 512]) for i in range(2)]
        ps1, ps2 = psb('ps1'), psb('ps2')
        for ft in range(8):
            k.dma(cw[:, ft, :], P_['c_conv_w'][jc][:, ft * 128:(ft + 1) * 128].rearrange("j p -> p j"), writes=[cw], acc=True,
                  allow_slow_non_contiguous=True)
        k.dma(cb_[:], P_['c_conv_b'][jc].rearrange("(ft p) -> p ft", p=128), writes=[cb_], allow_slow_non_contiguous=True)
        k.dma(lg[:], P_['c_ln_g'][jc].rearrange("(ft p) -> p ft", p=128), writes=[lg], allow_slow_non_contiguous=True)
        k.dma(lb[:], P_['c_ln_b'][jc].rearrange("(ft p) -> p ft", p=128), writes=[lb], allow_slow_non_contiguous=True)
        ci = 0
        for ft in range(8):
            for c0 in range(0, T, 512):
                n = min(512, T - c0)
                a, b = a_[ci % 2], b_[ci % 2]
                ci += 1
                k.dma(a[:, 0:n], U.CA[ft * 128:(ft + 1) * 128, c0:c0 + n], writes=[a])
                k.dma(b[:, 0:n], U.CB[ft * 128:(ft + 1) * 128, c0:c0 + n], writes=[b])
                k.op('act', lambda en: en.activation(b[:, 0:n], b[:, 0:n], AF.Sigmoid), reads=[b], writes=[b])
                k.op('dve', lambda en: en.tensor_tensor(a[:, 0:n], a[:, 0:n], b[:, 0:n], ALU.mult), reads=[a, b], writes=[a])
                k.dma(U.CU[ft * 128:(ft + 1) * 128, c0:c0 + n], a[:, 0:n], reads=[a], writes=['CU'], q='pool', acc=True)
        k.barrier()
        if checkpoint('c1'):
            return
        ui = 0
        zi = 0
        for c0 in range(0, T, 512):
            n = min(512, T - c0)
            for ft in range(8):
                u = uh[ui % 2]
                ui += 1
                rows = U.CU[ft * 128:(ft + 1) * 128, :]
                if c0 == 0:
                    U.cconv_hist(k, u, ft)
                    k.dma(u[:, 30:30 + n], rows[:, 0:n], writes=[u], acc=True)
                else:
                    k.dma(u[:, 0:30 + n], rows[:, c0 - 30:c0 + n], writes=[u])
                y = cbuf[:, ft, 0:n]
                k.op('dve', lambda en: en.tensor_scalar(y, u[:, 30:30 + n], cw[:, ft, 30:31], cb_[:, ft:ft + 1], ALU.mult, ALU.add),
                     reads=[u, cw, cb_], writes=[(cbuf, ft)])
                for jj in range(30):
                    k.op('dve', lambda en: en.scalar_tensor_tensor(y, u[:, jj:jj + n], cw[:, ft, jj:jj + 1], y, ALU.mult, ALU.add),
                         reads=[u, cw, (cbuf, ft)], writes=[(cbuf, ft)])
                k.op('pe', lambda en: en.matmul(ps1[:, 0:n], C.onesf[:, :], y, start=(ft == 0), stop=(ft == 7)),
                     reads=[(cbuf, ft), 'consts'], writes=[ps1])
                k.op('act', lambda en: en.activation(sq[:, 0:n], y, AF.Square), reads=[(cbuf, ft)], writes=[sq])
                k.op('pe', lambda en: en.matmul(ps2[:, 0:n], C.onesf[:, :], sq[:, 0:n], start=(ft == 0), stop=(ft == 7)),
                     reads=[sq, 'consts'], writes=[ps2])
            k.op('act', lambda en: en.activation(mean[:, 0:n], ps1[:, 0:n], AF.Copy, scale=1.0 / 1024), reads=[ps1], writes=[mean])
            k.op('dve', lambda en: en.tensor_tensor(tmp[:, 0:n], mean[:, 0:n], mean[:, 0:n], ALU.mult), reads=[mean], writes=[tmp])
            k.op('dve', lambda en: en.scalar_tensor_tensor(tmp[:, 0:n], ps2[:, 0:n], 1.0 / 1024, tmp[:, 0:n], ALU.mult, ALU.subtract),
                 reads=[ps2, tmp], writes=[tmp])
            k.op('act', lambda en: en.activation(rstd[:, 0:n], tmp[:, 0:n], AF.Sqrt, bias=C.epsc[:, 0:1], scale=1.0), reads=[tmp, 'epsc'], writes=[rstd])
            k.op('dve', lambda en: en.reciprocal(rstd[:, 0:n], rstd[:, 0:n]), reads=[rstd], writes=[rstd])
            for ft in range(8):
                y = cbuf[:, ft, 0:n]
                z = zt[zi % 2]
                zi += 1
                k.dma(z[:, 0:n], U.CZ[ft * 128:(ft + 1) * 128, c0:c0 + n], writes=[z])
                k.op('act', lambda en: en.activation(z[:, 0:n], z[:, 0:n], AF.Silu), reads=[z], writes=[z])
                k.op('dve', lambda en: en.tensor_tensor(y, y, mean[:, 0:n], ALU.subtract), reads=[(cbuf, ft), mean], writes=[(cbuf, ft)])
                k.op('dve', lambda en: en.tensor_tensor(y, y, rstd[:, 0:n], ALU.mult), reads=[(cbuf, ft), rstd], writes=[(cbuf, ft)])
                k.op('act', lambda en: en.activation(y, y, AF.Silu, bias=lb[:, ft:ft + 1], scale=lg[:, ft:ft + 1]),
                     reads=[(cbuf, ft), lg, lb], writes=[(cbuf, ft)])
                k.op('dve', lambda en: en.tensor_tensor(z[:, 0:n], z[:, 0:n], y, ALU.mult), reads=[(cbuf, ft), z], writes=[z])
                k.dma(U.MT[ft * 128:(ft + 1) * 128, c0:c0 + n], z[:, 0:n], reads=[z], writes=[('MT', U.tag)], q='pool', acc=True)
        k.barrier()
        if checkpoint('c2'):
            return
        U.cconv_out(k, nc, C, ps1, ps2)
    k.barrier()
    checkpoint('c3')


def stage_outproj(k, nc, C, mt_ap, x_ap, y_ap, w_ap, T, tag):
    with ExitStack() as es:
        sb = lambda n, s, d=F32: es.enter_context(nc.sbuf_tensor(tag + n, s, d))
        ps = lambda n: k.reg_psum(es.enter_context(nc.psum_tensor(tag + n, [128, 512], F32)))
        wb = sb('wb', [128, 8, 1024], BF16)
        wst = [sb('wst%d' % i, [128, 8, 512], F32) for i in range(2)]
        mst = [sb('mst%d' % i, [128, 8, 128], F32) for i in range(2)]
        mb = [sb('mb%d' % i, [128, 8, 128], BF16) for i in range(2)]
        xt = [sb('xt%d' % i, [128, D], F32) for i in range(2)]
        yt = [sb('yt%d' % i, [128, D], F32) for i in range(2)]
        pp = [ps('p%d' % i) for i in range(4)]
        w_v = w_ap.rearrange("(kc p) c -> p kc c", p=128)
        for ci in range(2):
            k.dma(wst[ci][:], w_v[:, :, ci * 512:(ci + 1) * 512], writes=[wst[ci]])
            k.op(['dve', 'pool'][ci], lambda en: en.tensor_copy(wb[:, :, ci * 512:(ci + 1) * 512], wst[ci][:]),
                 reads=[wst[ci]], writes=[(wb, ci)])
        m_v = mt_ap.rearrange("(kc p) t -> p kc t", p=128)
        gi = 0
        for ti, t0 in enumerate(range(0, T, 128)):
            nt = min(128, T - t0)
            ms, mbb, x, y = mst[ti % 2], mb[ti % 2], xt[ti % 2], yt[ti % 2]
            k.dma(ms[:, :, 0:nt], m_v[:, :, t0:t0 + nt], writes=[ms])
            k.dma(x[0:nt, :], x_ap[t0:t0 + nt, :], writes=[x])
            k.op('pool', lambda en: en.tensor_copy(mbb[:, :, 0:nt], ms[:, :, 0:nt]), reads=[ms], writes=[mbb])
            for half in range(2):
                p = pp[gi % 4]
                gi += 1
                for kc in range(8):
                    k.op('pe', lambda en: en.matmul(p[0:nt, 0:512], mbb[:, kc, 0:nt], wb[:, kc, half * 512:(half + 1) * 512],
                                                    start=(kc == 0), stop=(kc == 7)), reads=[mbb, (wb, 0), (wb, 1)], writes=[p])
                k.op('dve', lambda en: en.tensor_tensor(y[0:nt, half * 512:(half + 1) * 512], p[0:nt, 0:512],
                                                        x[0:nt, half * 512:(half + 1) * 512], ALU.add), reads=[p, x], writes=[y])
            k.dma(y_ap[t0:t0 + nt, :], y[0:nt, :], reads=[y], writes=[('Y', tag)], q='pool', acc=True)
    k.barrier()


N_SEQ = 4
N_POOL = 2560
RUN_PROMPT = True
RUN_SAMPLE = True


def sample_setup(nc, dr, scr, out, tab):
    S = Ctx()
    TS = 4 * N_SEQ
    S.TS = TS
    S.pt = dr("pt", [N_SEQ, 64], I32)
    S.cache_cmp = [dr("cache_cmp%d" % i, [N_POOL * 128, 512]) for i in range(2)]
    S.cache_slc = [dr("cache_slc%d" % i, [N_POOL * 128, 512]) for i in range(2)]
    S.win_in = dr("win_in", [2, N_SEQ, 512, 512])
    S.sd_in = dr("sd_in", [N_SEQ, 8, 128, 128])
    S.sdc_in = dr("sdc_in", [N_SEQ, 3, 3072])
    S.sc_in = dr("sc_in", [N_SEQ, 30, 1024])
    S.qaug = dr("t_qaug_s", [6, 1, 16, 128], BF16)
    S.o_ys = out("o_ys", [TS, D])
    S.o_cmp = out("o_cmp_s", [2, TS, 512])
    S.o_slc = out("o_slc_s", [2, TS, 512])
    S.o_win = out("o_win_s", [2, N_SEQ, 512, 512])
    S.o_dst = out("o_dst_s", [N_SEQ, 8, 128, 128])
    S.o_dcv = out("o_dcv_s", [N_SEQ, 3, 3072])
    S.o_ccv = out("o_ccv_s", [N_SEQ, 30, 1024])
    S.BIG = scr("BIG_s", [4112, TS])
    S.GN, S.CU, S.GBD = scr("GN_s", [3072, TS]), scr("CU_s", [1024, TS]), scr("GBD_s", [TS, 16])
    S.MT = scr("MT_s", [1024, TS])
    S.Y = [scr("Y0_s", [TS, D]), scr("Y1_s", [TS, D])]
    S.kvw = scr("kvw_s", [TS, 512])
    S.CT = [scr("CTs%d" % i, [512, 8192]) for i in range(N_SEQ)]
    S.KST = [scr("KSTs%d" % i, [256, 8192]) for i in range(N_SEQ)]
    S.VS = [scr("VSs%d" % i, [8192, 256]) for i in range(N_SEQ)]
    S.KWT = [scr("KWTs%d" % i, [256, 512]) for i in range(N_SEQ)]
    return S


def sample_prepass(k, nc, C, S, j, s):
    with ExitStack() as es:
        pfx = 'pp%d%d_' % (j, s)
        pg = [es.enter_context(nc.sbuf_tensor(pfx + 'pg%d' % i, [128, 512], F32)) for i in range(3)]
        tsb = [es.enter_context(nc.sbuf_tensor(pfx + 'tsb%d' % i, [128, 512], F32)) for i in range(2)]
        tps = [k.reg_psum(es.enter_context(nc.psum_tensor(pfx + 'tps%d' % i, [128, 512], F32))) for i in range(2)]
        it = 0
        for (cache, nb, dstT, dstV) in [(S.cache_cmp[j], 4, S.CT[s], None), (S.cache_slc[j], 2, S.KST[s], S.VS[s]),
                                        (None, 2, S.KWT[s], None)]:
            ntile = 64 if cache is not None else 4
            for p in range(ntile):
                g_, t_, ps_ = pg[it % 3], tsb[it % 2], tps[it % 2]
                it += 1
                if cache is not None:
                    k.dma_ind(g_[:, :], cache, C.idx[:, s * 64 + p:s * 64 + p + 1], reads=['idx'], writes=[g_])
                else:
                    k.dma(g_[:, :], S.win_in[j, s, p * 128:(p + 1) * 128, :], writes=[g_])
                for b in range(nb):
                    k.op('pe', lambda en: en.transpose(ps_[:, b * 128:(b + 1) * 128], g_[:, b * 128:(b + 1) * 128], C.ident[:, :]),
                         reads=[g_, 'consts'], writes=[ps_])
                if it % 2:
                    k.op('act', lambda en: en.copy(t_[:, 0:nb * 128], ps_[:, 0:nb * 128]), reads=[ps_], writes=[t_])
                else:
                    k.op('dve', lambda en: en.tensor_copy(t_[:, 0:nb * 128], ps_[:, 0:nb * 128]), reads=[ps_], writes=[t_])
                k.dma(dstT.rearrange("(b q) t -> q b t", q=128)[:, :, p * 128:(p + 1) * 128],
                      t_[:, 0:nb * 128].rearrange("q (b t) -> q b t", b=nb), reads=[t_], writes=[('ppT', s)], acc=True)
                if dstV is not None:
                    k.dma(dstV[p * 128:(p + 1) * 128, :], g_[:, 256:512], reads=[g_], writes=[('ppV', s)], acc=True)
    k.barrier()


def sample_unit(S, s, tab):
    U = Ctx()
    U.tag = 's%d' % s
    U.T, U.past, U.L, U.Lc = 4, 8192, 8196, 8192
    c = slice(4 * s, 4 * s + 4)
    big = S.BIG
    U.QT, U.GT, U.ZT = big[0:1024, c], big[2048:2096, c], big[2096:3120, c]
    U.GQKV, U.GZ = big[0:3072, c], big[3072:4096, c]
    U.CA, U.CB, U.CZ = big[0:1024, c], big[1024:2048, c], big[2048:3072, c]
    U.GN, U.CU, U.GBD, U.MT = S.GN[:, c], S.CU[:, c], S.GBD[c, :], S.MT[:, c]
    U.qtiles = [(0, 4)]
    U.tab = dict(tab)
    U.tab['qaug'] = S.qaug
    U.k_prenormed = True
    U.thr_col = 14
    U.bonus_ap = lambda C_, qi, nq: C_.bonus_s[0:nq, :]
    return U


def load_nsa_consts(k, nc, C, tab, esn, pfx):
    cn = lambda n, s, d=F32: esn.enter_context(nc.sbuf_tensor(pfx + n, s, d))
    C.caus, C.wlow = cn("caus", [128, 4, 128], BF16), cn("wlow", [128, 4, 128], BF16)
    C.cm = cn("cm", [128, 17, 4, 128], BF16)
    C.poolm = cn("poolm", [128, 4, 128], BF16)
    C.bonus = cn("bonus", [128, 256])
    C.bonus_s = cn("bonus_s", [128, 128])
    C.sel = cn("sel", [48, 48, 64])
    C.emat = cn("emat", [128, 65, 128], BF16)
    for n_ in ['caus', 'wlow', 'cm', 'poolm', 'bonus', 'bonus_s', 'sel', 'emat']:
        k.dma(getattr(C, n_)[:], tab[n_], writes=['consts'], acc=True)
    k.barrier()


def sample_layer(k, nc, C, S, P_, tab, layer, x_cur, x_next, knorm_post):
    kind, j = layer % 3, layer // 3
    TS = S.TS
    gcol = P_['norm_g'][layer].rearrange("(kc p) -> p kc", p=128)
    fm = lambda ap: (lambda b0, nb, t0, nts: ap[b0:b0 + nb, t0:t0 + nts])
    if kind == 0:
        plan = [
            dict(c0=0, n=1024, mode='FM', key='QTs', dst=fm(S.BIG[0:1024])),
            dict(c0=2560, n=48, mode='FM', key='GTs', dst=fm(S.BIG[2048:2096])),
            dict(c0=2608, n=1024, mode='FM', key='ZTs', dst=fm(S.BIG[2096:3120])),
            dict(c0=1024, n=512, mode='TM', key='o_cmp_s',
                 dst=lambda b0, nb, t0, nt: [(S.o_cmp[j, t0:t0 + nt, b0:b0 + nb], 'o_cmp_s')]),
            dict(c0=1536, n=512, mode='TM', key='o_slc_s', post=knorm_post(j, 1),
                 dst=lambda b0, nb, t0, nt: [(S.o_slc[j, t0:t0 + nt, b0:b0 + nb], 'o_slc_s')]),
            dict(c0=2048, n=512, mode='TM', key='kvw_s', post=knorm_post(j, 2),
                 dst=lambda b0, nb, t0, nt: [(S.kvw[t0:t0 + nt, b0:b0 + nb], 'kvw_s')]),
        ]
        stage_inproj(k, nc, C, x_cur, TS, P_['a_w_in'][j], A_IN, gcol, plan, 'A%ds' % layer)
        esn = ExitStack()
        load_nsa_consts(k, nc, C, tab, esn, "cn%ds_" % layer)
        for s in range(N_SEQ):
            sample_prepass(k, nc, C, S, j, s)
            U = sample_unit(S, s, tab)
            U.a_cmp_w, U.a_cmp_pos = P_['a_cmp_w'], P_['a_cmp_pos']
            c4 = slice(4 * s, 4 * s + 4)
            U.ct_src = lambda c, g, s=s: S.CT[s][c * 256 + g * 64:c * 256 + g * 64 + 64, :]

            def ks_src(g, c0, n, s=s, c4=c4):
                res = []
                m = min(c0 + n, 8192) - c0
                if m > 0:
                    res.append((S.KST[s][g * 64:g * 64 + 64, c0:c0 + m], 0, m))
                if c0 + n > 8192:
                    res.append((S.o_slc[j, c4, g * 64:g * 64 + 64].rearrange("t d -> d t"), max(m, 0), max(m, 0) + 4))
                return res

            def kw_src(g, c0, n, s=s, c4=c4):
                if c0 == 7680:
                    return [(S.KWT[s][g * 64:g * 64 + 64, 0:512], 0, 512)]
                if c0 == 8192:
                    return [(S.kvw[c4, g * 64:g * 64 + 64].rearrange("t d -> d t"), 0, 4)]
                return []

            def vs_src(g, vst, s=s, c4=c4):
                res = []
                v = S.VS[s][:, g * 64:g * 64 + 64].rearrange("(kt p) d -> p kt d", p=128)
                for a in range(0, 64, 16):
                    res.append((v[:, a:a + 16, :], vst[:, a:a + 16, :]))
                res.append((S.o_slc[j, c4, 256 + g * 64:256 + g * 64 + 64], vst[0:4, 64, :]))
                return res

            def vw_src(g, vst, s=s, c4=c4):
                v = S.win_in[j, s][:, 256 + g * 64:256 + g * 64 + 64].rearrange("(kt p) d -> p kt d", p=128)
                return [(v, vst[:, 60:64, :]), (S.kvw[c4, 256 + g * 64:256 + g * 64 + 64], vst[0:4, 64, :])]
            U.ks_src, U.kw_src, U.vs_src, U.vw_src = ks_src, kw_src, vs_src, vw_src
            for g in range(N_GROUPS_RUN):
                nsa_group(k, nc, C, U, j, g)
            k.dma(S.o_win[j, s, 0:508, :], S.win_in[j, s, 4:512, :], writes=['o_win_s'], q='pool', acc=True)
            k.dma(S.o_win[j, s, 508:512, :], S.kvw[c4, :], reads=['kvw_s'], writes=['o_win_s'], q='pool', acc=True)
        k.barrier()
        esn.close()
        w_out = P_['a_w_out'][j]
    elif kind == 1:
        def dcv_dst(b0, nb, t0, nt):
            return [(S.o_dcv[s, :, b0:b0 + nb], 'o_dcv_s', 4 * s + 1, 4 * s + 4) for s in range(N_SEQ)]
        plan = [
            dict(c0=0, n=3072, mode='FM', key='GQKVs', dst=fm(S.BIG[0:3072])),
            dict(c0=3072, n=1024, mode='FM', key='GZs', dst=fm(S.BIG[3072:4096])),
            dict(c0=4096, n=16, mode='TM', key='GBDs', dst=lambda b0, nb, t0, nt: [(S.GBD[t0:t0 + nt, b0:b0 + nb], 'GBDs')]),
            dict(c0=0, n=3072, mode='TM', key='o_dcv_s', dst=dcv_dst),
        ]
        stage_inproj(k, nc, C, x_cur, TS, P_['b_w_in'][j], DN_IN, gcol, plan, 'A%ds' % layer)
        for s in range(N_SEQ):
            U = sample_unit(S, s, tab)
            U.conv_hist = lambda k_, x_, ft, s=s: k_.dma(x_[:, 0:3], S.sdc_in[s][:, ft * 128:(ft + 1) * 128].rearrange("j p -> p j"),
                                                         writes=[x_], allow_slow_non_contiguous=True)
            U.state_init = lambda k_, St, h, s=s: k_.dma(St[:], S.sd_in[s, h], writes=[St])
            U.state_out = lambda k_, St, h, s=s: k_.dma(S.o_dst[s, h], St[:], reads=[St], writes=['o_dst_s'], q='pool', acc=True)
            gdn_layer(k, nc, C, U, j, P_)
        w_out = P_['b_w_out'][j]
    else:
        plan = [
            dict(c0=0, n=1024, mode='FM', key='CAs', dst=fm(S.BIG[0:1024])),
            dict(c0=1024, n=1024, mode='FM', key='CBs', dst=fm(S.BIG[1024:2048])),
            dict(c0=2048, n=1024, mode='FM', key='CZs', dst=fm(S.BIG[2048:3072])),
        ]
        stage_inproj(k, nc, C, x_cur, TS, P_['c_w_in'][j], CONF_IN, gcol, plan, 'A%ds' % layer)
        for s in range(N_SEQ):
            U = sample_unit(S, s, tab)
            U.cconv_hist = lambda k_, u, ft, s=s: k_.dma(u[:, 0:30], S.sc_in[s][:, ft * 128:(ft + 1) * 128].rearrange("j p -> p j"),
                                                         writes=[u], allow_slow_non_contiguous=True)

            def cconv_out(k_, nc_, C_, ps1, ps2, s=s):
                k_.dma(S.o_ccv[s, 0:26, :], S.sc_in[s, 4:30, :], writes=['o_ccv_s'], q='pool', acc=True)
                with ExitStack() as es_:
                    ul = es_.enter_context(nc_.sbuf_tensor("ccvs_ul%d" % s, [128, 8, 4], F32))
                    ot = es_.enter_context(nc_.sbuf_tensor("ccvs_ot%d" % s, [4, 1024], F32))
                    k_.dma(ul[:], S.CU[:, 4 * s:4 * s + 4].rearrange("(ft p) t -> p ft t", p=128), reads=['CU'], writes=[ul])
                    for ft in range(8):
                        pp = ps1 if ft < 4 else ps2
                        k_.op('pe', lambda en: en.transpose(pp[0:4, (ft % 4) * 128:(ft % 4 + 1) * 128], ul[:, ft, :], C_.ident[:, :]),
                              reads=[ul, 'consts'], writes=[pp])
                    k_.op('dve', lambda en: en.tensor_copy(ot[:, 0:512], ps1[0:4, 0:512]), reads=[ps1], writes=[ot])
                    k_.op('dve', lambda en: en.tensor_copy(ot[:, 512:1024], ps2[0:4, 0:512]), reads=[ps2], writes=[ot])
                    k_.dma(S.o_ccv[s, 26:30, :], ot[:], reads=[ot], writes=['o_ccv_s'], q='pool', acc=True)
                    k_.barrier()
            U.cconv_out = cconv_out
            conf_layer(k, nc, C, U, j, P_)
        w_out = P_['c_w_out'][j]
    stage_outproj(k, nc, C, S.MT, x_cur, x_next, w_out, TS, 'O%ds' % layer)


def build_program():
    nc = bass.Bass("TRN2", target_bir_lowering=False)
    C = Ctx()
    TP = T_P
    NTP = TP // 128
    dr = lambda n, s, d=F32, kind="ExternalInput": nc.dram_tensor(n, s, d, kind=kind).ap()
    xp = dr("xp", [TP, D])
    xs = dr("xs", [4 * N_SEQ, D])
    P_ = {}
    for n_, sh in PARAM_SHAPES.items():
        P_[n_] = dr(n_, sh)
    tab = {n: dr("t_" + n, sh, dt) for n, (sh, dt) in TABLE_SPECS.items()}
    tab['qaug'] = dr("t_qaug_p", [6, NTP, 16, 128], BF16)
    out = lambda n, s: dr(n, s, kind="ExternalOutput")
    o_yp = out("o_yp", [TP, D])
    o_cmp_p = out("o_cmp_p", [2, TP, 512])
    o_slc_p = out("o_slc_p", [2, TP, 512])
    o_win_p = out("o_win_p", [2, 512, 512])
    o_dst_p = out("o_dst_p", [1, 8, 128, 128])
    o_dcv_p = out("o_dcv_p", [1, 3, 3072])
    o_ccv_p = out("o_ccv_p", [1, 30, 1024])
    scr = lambda n, s: dr(n, s, kind="Internal")
    kvw_p = scr("kvw_p", [TP, 512])
    S = sample_setup(nc, dr, scr, out, tab) if RUN_SAMPLE else None
    UP = Ctx()
    UP.tag = 'p'
    UP.T, UP.past, UP.L = TP, 0, TP
    UP.Lc = TP
    big = scr("BIG_p", [4112, TP])
    UP.QT, UP.CT, UP.KST, UP.KWT = big[0:1024], big[1024:1536], big[1536:1792], big[1792:2048]
    UP.GT, UP.ZT = big[2048:2096], big[2096:3120]
    UP.GQKV, UP.GZ = big[0:3072], big[3072:4096]
    UP.CA, UP.CB, UP.CZ = big[0:1024], big[1024:2048], big[2048:3072]
    UP.GN = scr("GN_p", [3072, TP])
    UP.CU = scr("CU_p", [1024, TP])
    UP.GBD = scr("GBD_p", [TP, 16])
    UP.MT = scr("MT_p", [1024, TP])
    UP.Y = [scr("Y0_p", [TP, D]), scr("Y1_p", [TP, D])]
    UP.qtiles = [(q0, 128) for q0 in range(0, TP, 128)]
    UP.tab = tab
    UP.a_cmp_w, UP.a_cmp_pos = P_['a_cmp_w'], P_['a_cmp_pos']
    UP.k_prenormed = False
    UP.thr_col = 15
    UP.bonus_ap = lambda C_, qi, nq: C_.bonus[0:nq, 126 - 2 * qi:254 - 2 * qi]

    k = KB(nc)
    UP.conv_hist = lambda k_, x_, ft: k_.op('pool', lambda en: en.memset(x_[:, 0:3], 0.0), writes=[x_])
    UP.cconv_hist = lambda k_, u, ft: k_.op('pool', lambda en: en.memset(u[:, 0:30], 0.0), writes=[u])
    UP.state_init = lambda k_, S, h: k_.op('pool', lambda en: en.memset(S[:], 0.0), writes=[S])
    UP.state_out = lambda k_, S, h: k_.dma(o_dst_p[0, h], S[:], reads=[S], writes=['o_dst'], q='pool', acc=True)

    def cconv_out_p(k_, nc_, C_, ps1, ps2):
        with ExitStack() as es_:
            ul = es_.enter_context(nc_.sbuf_tensor("ccv_ul", [128, 8, 30], F32))
            ot = es_.enter_context(nc_.sbuf_tensor("ccv_ot", [30, 1024], F32))
            k_.dma(ul[:], UP.CU[:, TP - 30:TP].rearrange("(ft p) t -> p ft t", p=128), writes=[ul])
            for ft in range(8):
                pp = ps1 if ft < 4 else ps2
                k_.op('pe', lambda en: en.transpose(pp[0:30, (ft % 4) * 128:(ft % 4 + 1) * 128], ul[:, ft, :], C_.ident[:, :]),
                      reads=[ul, 'consts'], writes=[pp])
            k_.op('dve', lambda en: en.tensor_copy(ot[:, 0:512], ps1[0:30, 0:512]), reads=[ps1], writes=[ot])
            k_.op('dve', lambda en: en.tensor_copy(ot[:, 512:1024], ps2[0:30, 0:512]), reads=[ps2], writes=[ot])
            k_.dma(o_ccv_p[0], ot[:], reads=[ot], writes=['o_ccv'], q='pool')
            k_.barrier()
    UP.cconv_out = cconv_out_p

    with ExitStack() as es0:
        cs = lambda n, s, d=F32: es0.enter_context(nc.sbuf_tensor("c_" + n, s, d))
        C.ident = cs("ident", [128, 128])
        C.identb = cs("identb", [128, 128], BF16)
        C.onesf = cs("onesf", [128, 128])
        C.tri, C.slt = cs("tri", [128, 128]), cs("slt", [128, 128])
        C.mstrict, C.minclT = cs("mstrict", [128, 128]), cs("minclT", [128, 128])
        C.gains = cs("gains", [128, 20])
        C.gateb = cs("gateb", [48, 2])
        C.ogain = cs("ogain", [128, 1])
        C.gainb = cs("gainb", [128, 2, 3, 64])
        C.tmp = cs("ctmp", [128, 512])
        C.st4 = cs("cst4", [128, 8])
        C.epsc = cs("epsc", [128, 1])
        k.op('pool', lambda en: en.memset(C.epsc[:], EPS), writes=['epsc'])
        k.op('pool', lambda en: en.memset(C.gains[:, 16:17], EPS), writes=['consts'])
        k.op('pool', lambda en: en.memset(C.gains[:, 17:18], 64 * EPS), writes=['consts'])
        k.op('pool', lambda en: en.memset(C.gains[:, 18:19], 128 ** -0.5), writes=['consts'])
        k.op('pool', lambda en: en.memset(C.gains[:, 19:20], 1.0), writes=['consts'])
        C.pm4 = cs("pm4", [128, 1])
        for n_ in ['ident', 'identb', 'onesf', 'tri', 'slt', 'mstrict', 'minclT', 'pm4']:
            k.dma(getattr(C, n_)[:], tab[n_], writes=['consts'], acc=True)
        for jj in range(2):
            k.dma(C.gains[0:64, jj * 8:jj * 8 + 1], P_['a_q_gain'][jj].rearrange("(d o) -> d o", o=1), writes=['consts'], acc=True)
            k.dma(C.gains[0:64, jj * 8 + 1:jj * 8 + 4], P_['a_k_gain'][jj].rearrange("b d -> d b"), writes=['consts'], acc=True,
                  allow_slow_non_contiguous=True)
            k.dma(C.gateb[:, jj:jj + 1], P_['a_gate_b'][jj].rearrange("(d o) -> d o", o=1), writes=['consts'], acc=True)
        k.dma(C.ogain[:], P_['b_o_gain'][0].rearrange("(d o) -> d o", o=1), writes=['consts'], acc=True)
        k.dma(C.gainb[:].rearrange("p a b c -> p (a b c)"),
              P_['a_k_gain'].rearrange("a b c -> (a b c)").partition_broadcast(128), writes=['gainb'])
        if RUN_SAMPLE:
            C.idx = cs("idx", [128, N_SEQ * 64], I32)
            pti = cs("pti", [128, N_SEQ * 64], I32)
            ptf = cs("ptf", [128, N_SEQ * 64])
            pcol = cs("pcol", [128, 1])
            k.dma(pcol[:], tab['pcol'], writes=['pcol'])
            k.dma(pti[:], S.pt.rearrange("s p -> (s p)").partition_broadcast(128), writes=['pti'])
            k.op('dve', lambda en: en.tensor_copy(ptf[:], pti[:]), reads=['pti'], writes=['ptf'])
            k.op('dve', lambda en: en.tensor_scalar(ptf[:], ptf[:], 128.0, pcol[:, 0:1], ALU.mult, ALU.add), reads=['ptf', 'pcol'], writes=['ptf'])
            k.op('dve', lambda en: en.tensor_copy(C.idx[:], ptf[:]), reads=['ptf'], writes=['idx'])
        k.barrier()

        def knorm_post(j, which):
            def post(o, nt, nb):
                kv = o[0:nt, 0:256]
                k.op('act', lambda en: en.activation(C.tmp[0:nt, 0:256], kv, AF.Square), reads=[o], writes=['ctmp'])
                k.op('dve', lambda en: en.tensor_reduce(C.st4[0:nt, 0:4], C.tmp[0:nt, 0:256].rearrange("p (g d) -> p g d", g=4),
                                                        AX.X, ALU.add), reads=['ctmp'], writes=['cst4'])
                rstd_op(k, C, C.st4[0:nt, 4:8], C.st4[0:nt, 0:4], 1.0 / 64, nt, 'cst4')
                kv3 = kv.rearrange("p (g d) -> p g d", g=4)
                k.op('dve', lambda en: en.tensor_tensor(kv3, kv3, C.st4[0:nt, 4:8].unsqueeze(2).to_broadcast([nt, 4, 64]), ALU.mult),
                     reads=[o, 'cst4'], writes=[o])
                k.op('dve', lambda en: en.tensor_tensor(kv3, kv3, C.gainb[0:nt, j, which:which + 1, :].to_broadcast([nt, 4, 64]), ALU.mult),
                     reads=[o, 'gainb'], writes=[o])
            return post

        fm = lambda ap: (lambda b0, nb, t0, nts: ap[b0:b0 + nb, t0:t0 + nts])
        U = UP
        x_cur = xp
        layers = list(LAYERS)
        xs_cur = xs
        for li, layer in enumerate(layers):
          if RUN_SAMPLE:
              xs_next = S.o_ys if li == len(layers) - 1 else S.Y[li % 2]
              sample_layer(k, nc, C, S, P_, tab, layer, xs_cur, xs_next, knorm_post)
              xs_cur = xs_next
          if not RUN_PROMPT:
              continue
          try:
            kind, j = layer % 3, layer // 3
            x_next = o_yp if li == len(layers) - 1 else U.Y[li % 2]
            gcol = P_['norm_g'][layer].rearrange("(kc p) -> p kc", p=128)
            if kind == 0:
                def win_dst(b0, nb, t0, nt, j=j):
                    res = [(kvw_p[t0:t0 + nt, b0:b0 + nb], ('kvw', 'p'))]
                    if t0 >= TP - 512:
                        res.append((o_win_p[j, t0 - (TP - 512):t0 - (TP - 512) + nt, b0:b0 + nb], 'o_win_p'))
                    return res
                plan = [
                    dict(c0=0, n=1024, mode='FM', key='QT', dst=fm(U.QT)),
                    dict(c0=1024, n=512, mode='FM', key='CT', dst=fm(U.CT)),
                    dict(c0=1536, n=256, mode='FM', key='KST', dst=fm(U.KST)),
                    dict(c0=2048, n=256, mode='FM', key='KWT', dst=fm(U.KWT)),
                    dict(c0=2560, n=48, mode='FM', key='GT', dst=fm(U.GT)),
                    dict(c0=2608, n=1024, mode='FM', key='ZT', dst=fm(U.ZT)),
                    dict(c0=1024, n=512, mode='TM', key='o_cmp',
                         dst=lambda b0, nb, t0, nt, j=j: [(o_cmp_p[j, t0:t0 + nt, b0:b0 + nb], 'o_cmp')]),
                    dict(c0=1536, n=512, mode='TM', key='o_slc', post=knorm_post(j, 1),
                         dst=lambda b0, nb, t0, nt, j=j: [(o_slc_p[j, t0:t0 + nt, b0:b0 + nb], 'o_slc')]),
                    dict(c0=2048, n=512, mode='TM', key='kvw', post=knorm_post(j, 2), dst=win_dst),
                ]
                stage_inproj(k, nc, C, x_cur, TP, P_['a_w_in'][j], A_IN, gcol, plan, 'A%dp' % layer)
                U.ct_src = lambda c, g: U.CT[c * 256 + g * 64:c * 256 + g * 64 + 64, :]
                U.ks_src = lambda g, c0, n: [(U.KST[g * 64:g * 64 + 64, c0:c0 + n], 0, n)]
                U.kw_src = lambda g, c0, n: [(U.KWT[g * 64:g * 64 + 64, c0:c0 + n], 0, n)]

                def vsrc_of(src):
                    def f(g, vst):
                        res = []
                        v = src[:, 256 + g * 64:256 + g * 64 + 64].rearrange("(kt p) d -> p kt d", p=128)
                        for a in range(0, NTP, 16):
                            b = min(NTP, a + 16)
                            res.append((v[:, a:b, :], vst[:, a:b, :]))
                        return res
                    return f
                U.vs_src = vsrc_of(o_slc_p[j])
                U.vw_src = vsrc_of(kvw_p)
                with ExitStack() as esn:
                    load_nsa_consts(k, nc, C, tab, esn, "cn%dp_" % layer)
                    for g in range(N_GROUPS_RUN):
                        nsa_group(k, nc, C, U, j, g)
                w_out = P_['a_w_out'][j]
            elif kind == 1:
                T3 = TP - 3

                def dcv_dst(b0, nb, t0, nt):
                    r0 = max(T3 - t0, 0)
                    return [(o_dcv_p[0, t0 + r0 - T3:t0 + nt - T3, b0:b0 + nb], 'o_dcv', r0, nt)]
                plan = [
                    dict(c0=0, n=3072, mode='FM', key='GQKV', dst=fm(U.GQKV)),
                    dict(c0=3072, n=1024, mode='FM', key='GZ', dst=fm(U.GZ)),
                    dict(c0=4096, n=16, mode='TM', key='GBD', dst=lambda b0, nb, t0, nt: [(U.GBD[t0:t0 + nt, b0:b0 + nb], 'GBD')]),
                    dict(c0=0, n=3072, mode='TM', key='o_dcv', dst=dcv_dst, tok_filter=lambda t0, nt: t0 + nt > T3),
                ]
                stage_inproj(k, nc, C, x_cur, TP, P_['b_w_in'][j], DN_IN, gcol, plan, 'A%dp' % layer)
                if not checkpoint('inproj'):
                    gdn_layer(k, nc, C, U, j, P_)
                w_out = P_['b_w_out'][j]
            else:
                plan = [
                    dict(c0=0, n=1024, mode='FM', key='CA', dst=fm(U.CA)),
                    dict(c0=1024, n=1024, mode='FM', key='CB', dst=fm(U.CB)),
                    dict(c0=2048, n=1024, mode='FM', key='CZ', dst=fm(U.CZ)),
                ]
                stage_inproj(k, nc, C, x_cur, TP, P_['c_w_in'][j], CONF_IN, gcol, plan, 'A%dp' % layer)
                if not checkpoint('inproj'):
                    conf_layer(k, nc, C, U, j, P_)
                w_out = P_['c_w_out'][j]
            if STOPPED[0]:
                break
            stage_outproj(k, nc, C, U.MT, x_cur, x_next, w_out, TP, 'O%dp' % layer)
            x_cur = x_next
          except StopBuild:
            k.barrier()
            break
        k.finish()
    print("instructions:", k.n_ins, k.cnt)
    return nc


LAYERS = [0, 1, 2, 3]
PARAM_SHAPES = dict(
    norm_g=[4, D], a_w_in=[2, D, A_IN], a_w_out=[2, D, D], a_k_gain=[2, 3, 64], a_q_gain=[2, 64], a_gate_b=[2, 48],
    a_cmp_w=[2, 2, 2048, 64], a_cmp_pos=[2, 2, 32, 64],
    b_w_in=[1, D, DN_IN], b_conv_w=[1, 4, 3072], b_a_log=[1, 8], b_dt_bias=[1, 8], b_o_gain=[1, 128], b_w_out=[1, D, D],
    c_w_in=[1, D, CONF_IN], c_conv_w=[1, 31, 1024], c_conv_b=[1, 1024], c_ln_g=[1, 1024], c_ln_b=[1, 1024], c_w_out=[1, D, D],
)
N_GROUPS_RUN = 4
_CACHE = {}


def extra_inputs(T=None):
    tb = host_tables(T or T_P, 0)
    out = {"t_" + n: tb[n] for n in TABLE_SPECS}
    out["t_qaug_p"] = tb['qaug']
    out["t_qaug_s"] = host_tables(128, 8192)['qaug']
    return out


def kernel(**inputs):
    f32 = lambda a: np.ascontiguousarray(np.asarray(a, dtype=np.float32))
    if 'nc' not in _CACHE:
        _CACHE['nc'] = build_program()
    nc = _CACHE['nc']
    x_prompt = f32(inputs['x_prompt'])
    x_sample = f32(inputs['x_sample'])
    tabs = extra_inputs()
    shared = {n: f32(inputs[n]) for n in PARAM_SHAPES}
    if RUN_SAMPLE:
        ccmp = f32(inputs['cache_cmp_kv']).reshape(2, N_POOL * 128, 512)
        cslc = f32(inputs['cache_slc_kv']).reshape(2, N_POOL * 128, 512)
        cwin = f32(inputs['cache_win_kv']).reshape(2, 32, 512, 512)
        pt = np.ascontiguousarray(np.asarray(inputs['page_table'], dtype=np.int32))
        sd = f32(inputs['state_delta'])[0]
        sdc = f32(inputs['state_delta_conv'])[0]
        sc = f32(inputs['state_conv'])[0]
    in_maps = []
    for c in range(N_CORES):
        b = c % 2
        m = {"xp": x_prompt[b], "xs": x_sample[4 * c:4 * c + 4].reshape(16, D)}
        m.update(shared)
        m.update(tabs)
        if RUN_SAMPLE:
            sl = slice(4 * c, 4 * c + 4)
            m.update({"pt": pt[sl], "cache_cmp0": ccmp[0], "cache_cmp1": ccmp[1], "cache_slc0": cslc[0], "cache_slc1": cslc[1], "win_in": np.ascontiguousarray(cwin[:, sl]),
                      "sd_in": np.ascontiguousarray(sd[sl]), "sdc_in": np.ascontiguousarray(sdc[sl]),
                      "sc_in": np.ascontiguousarray(sc[sl])})
        else:
            m.pop("t_qaug_s", None)
        in_maps.append(m)
    res = run_bass_kernel_spmd(nc, in_maps, core_ids=list(range(N_CORES)))
    R = res.results
    z = lambda *s: np.zeros(s, np.float32)
    y_prompt = np.stack([R[b]["o_yp"] for b in range(2)])
    cmp_p = np.stack([R[b]["o_cmp_p"] for b in range(2)], axis=1).reshape(2, 2, T_P, 2, 4, 64)
    slc_p = np.stack([R[b]["o_slc_p"] for b in range(2)], axis=1).reshape(2, 2, T_P, 2, 4, 64)
    win_p = np.stack([R[b]["o_win_p"] for b in range(2)], axis=1).reshape(2, 2, 512, 2, 4, 64)
    dst_p = np.stack([R[b]["o_dst_p"] for b in range(2)], axis=1)
    dcv_p = np.stack([R[b]["o_dcv_p"] for b in range(2)], axis=1)
    ccv_p = np.stack([R[b]["o_ccv_p"] for b in range(2)], axis=1)
    if RUN_SAMPLE:
        y_sample = np.concatenate([R[c]["o_ys"].reshape(4, 4, D) for c in range(8)], axis=0)
        cmp_s = np.concatenate([R[c]["o_cmp_s"].reshape(2, 4, 4, 2, 4, 64) for c in range(8)], axis=1)
        slc_s = np.concatenate([R[c]["o_slc_s"].reshape(2, 4, 4, 2, 4, 64) for c in range(8)], axis=1)
        win_s = np.concatenate([R[c]["o_win_s"].reshape(2, 4, 512, 2, 4, 64) for c in range(8)], axis=1)
        dst_s = np.concatenate([R[c]["o_dst_s"] for c in range(8)], axis=0)[None]
        dcv_s = np.concatenate([R[c]["o_dcv_s"] for c in range(8)], axis=0)[None]
        ccv_s = np.concatenate([R[c]["o_ccv_s"] for c in range(8)], axis=0)[None]
    else:
        y_sample, cmp_s, slc_s, win_s = z(32, 4, D), z(2, 32, 4, 2, 4, 64), z(2, 32, 4, 2, 4, 64), z(2, 32, 512, 2, 4, 64)
        dst_s, dcv_s, ccv_s = z(1, 32, 8, 128, 128), z(1, 32, 3, 3072), z(1, 32, 30, 1024)
    return (y_prompt, y_sample, cmp_p, cmp_s, slc_p, slc_s, win_p, win_s,
            dst_p, dst_s, dcv_p, dcv_s, ccv_p, ccv_s)
```

```python
import numpy as np
from contextlib import ExitStack
import concourse.bass as bass
import concourse.mybir as mybir
from concourse.bass_utils import run_bass_kernel_spmd

F32 = mybir.dt.float32
BF16 = mybir.dt.bfloat16
I32 = mybir.dt.int32
ALU = mybir.AluOpType
AF = mybir.ActivationFunctionType
AX = mybir.AxisListType

D = 1024
T_P = 8192
N_CORES = 8
EPS = 1e-6
A_IN = 3632
DN_IN = 4112
CONF_IN = 3072


class KB:
    def __init__(self, nc, n_dma_sems=32):
        self.nc = nc
        self.eng = {'pe': nc.tensor, 'act': nc.scalar, 'dve': nc.vector, 'pool': nc.gpsimd, 'sp': nc.sync}
        self.sem = {k: nc.alloc_semaphore('sem_' + k) for k in self.eng}
        self.cnt = {k: 0 for k in self.eng}
        self.dsem = [nc.alloc_semaphore('dsem%d' % i) for i in range(n_dma_sems)]
        self.dval = [0] * n_dma_sems
        self.dq = {'sp': list(range(0, 16)), 'pool': list(range(16, 28)), 'act': list(range(28, 32))}
        self.dnext = {'sp': 0, 'pool': 0, 'act': 0}
        self.waited = {k: {} for k in self.eng}
        self.res_w = {}
        self.res_wacc = {}
        self.res_r = {}
        self.psum_keys = set()
        self.n_ins = 0

    def _semobj(self, sk):
        return self.sem[sk] if isinstance(sk, str) else self.dsem[sk]

    def _need(self, e, evs):
        for sk, v in evs.items():
            if sk == e and e == 'pe':
                continue
            if self.waited[e].get(sk, 0) >= v:
                continue
            self.eng[e].wait_ge(self._semobj(sk), v)
            self.waited[e][sk] = v

    @staticmethod
    def _nk(x):
        if isinstance(x, (str, int)):
            return x
        if isinstance(x, tuple):
            return tuple(KB._nk(y) for y in x)
        return getattr(x, 'name', None) or id(x)

    def _deps(self, e, reads, writes, acc=False):
        reads = [self._nk(r) for r in reads]
        writes = [self._nk(w) for w in writes]
        evs = {}

        def add(d):
            for sk, v in d.items():
                if evs.get(sk, 0) < v:
                    evs[sk] = v
        for r in reads:
            add(self.res_w.get(r, {}))
            add(self.res_wacc.get(r, {}))
        for w in writes:
            add(self.res_w.get(w, {}))
            add(self.res_r.get(w, {}))
            if not acc:
                add(self.res_wacc.get(w, {}))
        self._need(e, evs)

    def _commit(self, ev, reads, writes, acc=False):
        reads = [self._nk(r) for r in reads]
        writes = [self._nk(w) for w in writes]
        sk, v = ev
        for r in reads:
            d = self.res_r.setdefault(r, {})
            if d.get(sk, 0) < v:
                d[sk] = v
        for w in writes:
            if acc:
                d = self.res_wacc.setdefault(w, {})
                d[sk] = max(d.get(sk, 0), v)
            else:
                self.res_w[w] = {sk: v}
                self.res_wacc[w] = {}
                self.res_r[w] = {}

    def op(self, e, fn, reads=(), writes=()):
        pr = [r for r in reads if self._nk(r) in self.psum_keys]
        if pr:
            reads = [r for r in reads if self._nk(r) not in self.psum_keys]
            writes = list(writes) + [r for r in pr if self._nk(r) not in [self._nk(w) for w in writes]]
        self._deps(e, reads, writes)
        ins = fn(self.eng[e])
        self.cnt[e] += 1
        ins.then_inc(self.sem[e], 1)
        self._commit((e, self.cnt[e]), reads, writes)
        self.n_ins += 1
        return ins

    def dma(self, out, in_, reads=(), writes=(), q='sp', acc=False, **kw):
        self._deps(q, reads, writes, acc=acc)
        i = self.dq[q][self.dnext[q] % len(self.dq[q])]
        self.dnext[q] += 1
        if self.dval[i] > 0 and self.waited[q].get(i, 0) < self.dval[i]:
            self.eng[q].wait_ge(self.dsem[i], self.dval[i])
            self.waited[q][i] = self.dval[i]
        ins = self.eng[q].dma_start(out=out, in_=in_, **kw)
        self.dval[i] += 16
        ins.then_inc(self.dsem[i], 16)
        self._commit((i, self.dval[i]), reads, writes, acc=acc)
        self.n_ins += 1
        return ins

    def dma_ind(self, out, in_, idx_ap, reads=(), writes=()):
        q = 'pool'
        self._deps(q, reads, writes)
        i = self.dq[q][self.dnext[q] % len(self.dq[q])]
        self.dnext[q] += 1
        if self.dval[i] > 0 and self.waited[q].get(i, 0) < self.dval[i]:
            self.eng[q].wait_ge(self.dsem[i], self.dval[i])
            self.waited[q][i] = self.dval[i]
        ins = self.nc.gpsimd.indirect_dma_start(out, None, in_, bass.IndirectOffsetOnAxis(ap=idx_ap, axis=0))
        self.dval[i] += 16
        ins.then_inc(self.dsem[i], 16)
        self._commit((i, self.dval[i]), reads, writes)
        self.n_ins += 1
        return ins

    def reg_psum(self, t):
        self.psum_keys.add(self._nk(t))
        return t

    def barrier(self):
        for e in self.eng:
            evs = {k: c for k, c in self.cnt.items() if c > 0 and k != e}
            for i, v in enumerate(self.dval):
                if v > 0:
                    evs[i] = v
            for sk, v in evs.items():
                if self.waited[e].get(sk, 0) >= v:
                    continue
                self.eng[e].wait_ge(self._semobj(sk), v)
                self.waited[e][sk] = v
        self.res_w = {}
        self.res_wacc = {}
        self.res_r = {}

    def finish(self):
        self.barrier()


class Ctx:
    pass


class StopBuild(Exception):
    pass


STOP_AFTER = None


STOPPED = [False]


def checkpoint(name):
    if STOP_AFTER == name:
        STOPPED[0] = True
    return STOPPED[0]


def rr(lst, state=[0]):
    state[0] += 1
    return lst[state[0] % len(lst)]


def rstd_op(k, C, out_ap, in_ap, scale, nt, key):
    k.op('act', lambda en: en.activation(out_ap, in_ap, AF.Sqrt, bias=C.epsc[0:nt, 0:1], scale=scale),
         reads=[key, 'epsc'], writes=[key])
    k.op('dve', lambda en: en.reciprocal(out_ap, out_ap), reads=[key], writes=[key])


def stage_inproj(k, nc, C, x_ap, T, w_ap, ncol, g_col_ap, plan, tag):
    with ExitStack() as es:
        sb = lambda n, s, d=F32: es.enter_context(nc.sbuf_tensor(tag + n, s, d))
        ps = lambda n, s, d=F32: k.reg_psum(es.enter_context(nc.psum_tensor(tag + n, s, d)))
        wb = sb('wb', [128, 8, ncol], BF16)
        wst = [sb('wst%d' % i, [128, 8, 512], F32) for i in range(2)]
        gT = sb('gT', [128, 8], F32)
        xts = [sb('xt%d' % i, [128, D], F32) for i in range(2)]
        xn = [sb('xn%d' % i, [128, D], F32) for i in range(2)]
        junk = sb('junk', [128, D], F32)
        st = [sb('st%d' % i, [128, 4], F32) for i in range(2)]
        hT = [sb('hT%d' % i, [128, 8, 512], BF16) for i in range(2)]
        ob = [sb('ob%d' % i, [128, 512], F32) for i in range(4)]
        ptr = [ps('ptr%d' % i, [128, 512], F32) for i in range(2)]
        pg = [ps('pg%d' % i, [128, 512], F32) for i in range(4)]

        k.dma(gT[:], g_col_ap, writes=[gT], allow_slow_non_contiguous=True)
        w_v = w_ap.rearrange("(kc p) c -> p kc c", p=128)
        ci = 0
        for cb in range(0, ncol, 512):
            n = min(512, ncol - cb)
            s = wst[ci % 2]
            k.dma(s[:, :, 0:n], w_v[:, :, cb:cb + n], writes=[s])
            e = ['act', 'dve', 'pool'][ci % 3]
            if e == 'act':
                k.op('act', lambda en: en.copy(wb[:, :, cb:cb + n], s[:, :, 0:n]), reads=[s], writes=[(wb, cb)])
            else:
                k.op(e, lambda en: en.tensor_copy(wb[:, :, cb:cb + n], s[:, :, 0:n]), reads=[s], writes=[(wb, cb)])
            ci += 1
        wkeys = [(wb, cb) for cb in range(0, ncol, 512)]

        xi = 0
        oi = 0
        gi = 0
        def prep(si, t0):
            nts = min(512, T - t0)
            h = hT[si % 2]
            nonlocal xi
            for j0 in range(0, nts, 128):
                nt = min(128, nts - j0)
                xt = xts[xi % 2]
                xnn = xn[xi % 2]
                stt = st[xi % 2]
                xi += 1
                k.dma(xt[0:nt, :], x_ap[t0 + j0:t0 + j0 + nt, :], writes=[xt])
                k.op('act', lambda en: en.activation(junk[0:nt, :], xt[0:nt, :], AF.Square, accum_out=stt[0:nt, 0:1]),
                     reads=[xt], writes=[junk, stt])
                rstd_op(k, C, stt[0:nt, 2:3], stt[0:nt, 0:1], 1.0 / D, nt, stt)
                k.op('dve', lambda en: en.tensor_scalar(xnn[0:nt, :], xt[0:nt, :], stt[0:nt, 2:3], None, ALU.mult),
                     reads=[xt, stt], writes=[xnn])
                for kc in range(8):
                    p = ptr[kc % 2]
                    k.op('pe', lambda en: en.transpose(p[:, 0:nt], xnn[0:nt, kc * 128:(kc + 1) * 128], C.ident[0:nt, 0:nt]),
                         reads=[xnn, 'ident'], writes=[p])
                    e = 'dve' if kc % 2 == 0 else 'pool'
                    if e == 'pool':
                        k.op('act', lambda en: en.activation(h[:, kc, j0:j0 + nt], p[:, 0:nt], AF.Copy, scale=gT[:, kc:kc + 1]),
                             reads=[p, gT], writes=[h])
                    else:
                        k.op('dve', lambda en: en.tensor_scalar(h[:, kc, j0:j0 + nt], p[:, 0:nt], gT[:, kc:kc + 1], None, ALU.mult),
                             reads=[p, gT], writes=[h])
        st_list = list(enumerate(range(0, T, 512)))
        prep(*st_list[0])
        for idx_, (si, t0) in enumerate(st_list):
            nts = min(512, T - t0)
            h = hT[si % 2]
            if idx_ + 1 < len(st_list):
                prep(*st_list[idx_ + 1])
            for pl in plan:
                c0, n, mode = pl['c0'], pl['n'], pl['mode']
                blk = pl.get('blk', 128)
                if mode == 'FM':
                    for b0 in range(0, n, blk):
                        nb = min(blk, n - b0)
                        pp = pg[gi % 4]
                        gi += 1
                        for kc in range(8):
                            k.op('pe', lambda en: en.matmul(pp[0:nb, 0:nts], wb[:, kc, c0 + b0:c0 + b0 + nb], h[:, kc, 0:nts],
                                                            start=(kc == 0), stop=(kc == 7)),
                                 reads=[h] + wkeys, writes=[pp])
                        o = ob[oi % 4]
                        oi += 1
                        if oi % 2 == 0:
                            k.op('act', lambda en: en.copy(o[0:nb, 0:nts], pp[0:nb, 0:nts]), reads=[pp], writes=[o])
                        else:
                            k.op('dve', lambda en: en.tensor_copy(o[0:nb, 0:nts], pp[0:nb, 0:nts]), reads=[pp], writes=[o])
                        k.dma(pl['dst'](b0, nb, t0, nts), o[0:nb, 0:nts], reads=[o], writes=[pl['key']], q='pool', acc=True)
                else:
                    for j0 in range(0, nts, 128):
                        nt = min(128, nts - j0)
                        if pl.get('tok_filter') is not None and not pl['tok_filter'](t0 + j0, nt):
                            continue
                        for b0 in range(0, n, 512):
                            nb = min(512, n - b0)
                            pp = pg[gi % 4]
                            gi += 1
                            for kc in range(8):
                                k.op('pe', lambda en: en.matmul(pp[0:nt, 0:nb], h[:, kc, j0:j0 + nt], wb[:, kc, c0 + b0:c0 + b0 + nb],
                                                                start=(kc == 0), stop=(kc == 7)),
                                     reads=[h] + wkeys, writes=[pp])
                            o = ob[oi % 4]
                            oi += 1
                            if oi % 2 == 0:
                                k.op('act', lambda en: en.copy(o[0:nt, 0:nb], pp[0:nt, 0:nb]), reads=[pp], writes=[o])
                            else:
                                k.op('dve', lambda en: en.tensor_copy(o[0:nt, 0:nb], pp[0:nt, 0:nb]), reads=[pp], writes=[o])
                            if pl.get('post') is not None:
                                pl['post'](o, nt, nb)
                            for ent in pl['dst'](b0, nb, t0 + j0, nt):
                                r0, r1 = (ent[2], ent[3]) if len(ent) == 4 else (0, nt)
                                k.dma(ent[0], o[r0:r1, 0:nb], reads=[o], writes=[ent[1]], q='pool', acc=True)
    k.barrier()


NEG = -30000.0
TINY = 1e-30


def host_tables(T, t_base):
    import ml_dtypes
    bf = ml_dtypes.bfloat16
    NT = (T + 127) // 128
    slopes = np.exp2(-8.0 * np.arange(1, 17, dtype=np.float32) / 16).astype(np.float32)

    def split2(v):
        hi = v.astype(bf)
        lo = (v - hi.astype(np.float32)).astype(bf)
        return hi, lo
    s_hi, s_lo = split2(slopes)
    t = (t_base + np.arange(NT * 128)).astype(np.float32)
    v = -(slopes[None, :] * t[:, None]).astype(np.float32)
    v_hi, v_lo = split2(v)
    qaug = np.zeros((6, NT, 16, 128), dtype=bf)
    qaug[0] = s_hi[None, :, None]
    qaug[1] = s_lo[None, :, None]
    qaug[2] = s_hi[None, :, None]
    qaug[3] = s_lo[None, :, None]
    qaug[4] = v_hi.reshape(NT, 128, 16).transpose(0, 2, 1)
    qaug[5] = v_lo.reshape(NT, 128, 16).transpose(0, 2, 1)

    def kaug_of(pos):
        ka = np.zeros((6, pos.shape[0]), dtype=bf)
        hi = (128 * (pos // 128)).astype(np.float32)
        lo = (pos % 128).astype(np.float32)
        ka[0] = hi
        ka[1] = hi
        ka[2] = lo
        ka[3] = lo
        ka[4] = 1.0
        ka[5] = 1.0
        return ka
    res = dict(qaug=qaug)
    kr = np.arange(128)[:, None]
    qr = np.arange(128)[None, :]
    caus = np.where(kr <= qr, 0.0, NEG).astype(np.float32)
    wlow = np.where(kr > qr, 0.0, NEG).astype(np.float32)
    res['caus'] = np.repeat(caus[:, None, :], 4, axis=1).astype(bf)
    res['wlow'] = np.repeat(wlow[:, None, :], 4, axis=1).astype(bf)
    cm = np.zeros((128, 17, 4, 128), dtype=np.float32)
    for d in range(17):
        m = np.where(16 * kr + 31 <= 128 * d + qr, 0.0, NEG)
        cm[:, d, :, :] = m[:, None, :]
    res['cm'] = cm.astype(bf)
    n = np.arange(512)
    poolm = np.zeros((128, 4, 128), dtype=np.float32)
    for kt in range(4):
        for nr in range(128):
            poolm[nr, kt, (128 * kt + nr) // 4] = 1.0
    res['poolm'] = poolm.astype(bf)
    bonus = np.zeros((128, 256), dtype=np.float32)
    for q in range(128):
        jq = 1 if q >= 64 else 0
        for c in range(256):
            jj = c - 126
            if jj > jq:
                bonus[q, c] = -1e30
            elif jj == jq or jj == jq - 1:
                bonus[q, c] = 1e4
    res['bonus'] = bonus
    sel = np.zeros((48, 48, 64), dtype=np.float32)
    for r in range(48):
        sel[r, r, :] = 1.0
    res['sel'] = sel
    res['kaug'] = kaug_of(np.arange(8192 + 128))
    res['kaugc'] = kaug_of(16 * np.arange(512) + 31)
    e = np.zeros((128, 65, 128), dtype=np.float32)
    for kt in range(65):
        for key in range(128):
            jb = 2 * kt + key // 64
            if jb < 128:
                e[jb, kt, key] = 1.0
    res['emat'] = e.astype(bf)
    res['ident'] = np.eye(128, dtype=np.float32)
    bs = np.zeros((128, 128), dtype=np.float32)
    bs[:, 127] = 1e4
    res['bonus_s'] = bs
    res['pcol'] = np.arange(128, dtype=np.float32).reshape(128, 1)
    res['pm4'] = (np.arange(128) < 4).astype(np.float32).reshape(128, 1)
    res['identb'] = np.eye(128, dtype=np.float32).astype(bf)
    res['onesf'] = np.ones((128, 128), dtype=np.float32)
    row = np.arange(128)[:, None]
    colx = np.arange(128)[None, :]
    res['tri'] = (row <= colx).astype(np.float32)
    res['slt'] = (row > colx).astype(np.float32)
    res['mstrict'] = np.where(row > colx, 0.0, NEG).astype(np.float32)
    res['minclT'] = np.where(colx >= row, 0.0, NEG).astype(np.float32)
    return res


TABLE_SPECS = dict(
    caus=([128, 4, 128], BF16), wlow=([128, 4, 128], BF16), cm=([128, 17, 4, 128], BF16),
    poolm=([128, 4, 128], BF16), bonus=([128, 256], F32), sel=([48, 48, 64], F32),
    kaug=([6, 8192 + 128], BF16), kaugc=([6, 512], BF16), emat=([128, 65, 128], BF16),
    ident=([128, 128], F32), identb=([128, 128], BF16), onesf=([128, 128], F32),
    tri=([128, 128], F32), slt=([128, 128], F32), mstrict=([128, 128], F32), minclT=([128, 128], F32),
    bonus_s=([128, 128], F32), pcol=([128, 1], F32), pm4=([128, 1], F32),
)


def fm_norm(k, C, W, src, n, dst_ap, dst_key, gain_col, eps_col, scale, P=64, src_ap=None, src_key=None):
    sq, ps, r = W['sq'], W['psn'], W['r']
    sap = src[0:P, 0:n] if src_ap is None else src_ap
    skey = src if src_key is None else src_key
    k.op('act', lambda en: en.activation(sq[0:P, 0:n], sap, AF.Square), reads=[skey], writes=[sq])
    k.op('pe', lambda en: en.matmul(ps[0:P, 0:n], C.onesf[0:P, 0:P], sq[0:P, 0:n], start=True, stop=True),
         reads=[sq, 'consts'], writes=[ps])
    k.op('act', lambda en: en.activation(r[0:P, 0:n], ps[0:P, 0:n], AF.Sqrt, bias=eps_col, scale=scale),
         reads=[ps, 'consts'], writes=[r])
    k.op('dve', lambda en: en.reciprocal(r[0:P, 0:n], r[0:P, 0:n]), reads=[r], writes=[r])
    k.op('dve', lambda en: en.scalar_tensor_tensor(dst_ap, sap, gain_col, r[0:P, 0:n], ALU.mult, ALU.mult),
         reads=[skey, r, 'consts'], writes=[dst_key])


def nsa_group(k, nc, C, U, j, g):
    L, T = U.L, U.T
    NKT = (L + 127) // 128
    with ExitStack() as es:
        pfx = '%s%d%d_' % (U.tag, j, g)
        sb = lambda n, s, d=F32: es.enter_context(nc.sbuf_tensor(pfx + 'g_' + n, s, d))
        psm = lambda n: k.reg_psum(es.enter_context(nc.psum_tensor(pfx + 'gp_' + n, [128, 512], F32)))
        KsT = sb('KsT', [70, NKT * 128], BF16)
        KwT = sb('KwT', [70, NKT * 128], BF16)
        Vs = sb('Vs', [128, NKT, 65], BF16)
        Vw = sb('Vw', [128, NKT, 65], BF16)
        KcT = sb('KcT', [70, 512], BF16)
        Vc = sb('Vc', [128, 4, 65], BF16)
        W = dict(sq=sb('sq', [64, 512]), r=sb('r', [64, 512]), psn=psm('psn'))
        S = [psm('S0'), psm('S1')]
        Ocmp, Osel, Owin = psm('Ocmp'), psm('Osel'), psm('Owin')
        M0, M1 = psm('M0'), psm('M1')
        gk = C.gains[0:64, :]
        jc = j * 8

        with ExitStack() as es2:
            sb2 = lambda n, s, d=F32: es2.enter_context(nc.sbuf_tensor(pfx + 'c_' + n, s, d))
            Wc = sb2('Wc', [64, 64, 64])
            PEm = sb2('PEm', [64, 64])
            Lc = U.Lc
            nblk = Lc // 16 - 1
            X = sb2('X', [64, Lc])
            comp = sb2('comp', [64, 512])
            bcol = sb2('bcol', [64, 2])
            k.dma(Wc[:], U.a_cmp_w[j].rearrange("c (hl d) e -> d (c hl) e", d=64), writes=[Wc])
            k.dma(PEm[:], U.a_cmp_pos[j].rearrange("c hl d -> d (c hl)"), writes=[PEm], allow_slow_non_contiguous=True)
            for c in range(2):
                k.dma(X[:, :], U.ct_src(c, g), writes=[X])
                for hl in range(32):
                    k.op('pe', lambda en: en.matmul(M0[0:64, 0:1], Wc[:, c * 32 + hl, :], PEm[:, c * 32 + hl:c * 32 + hl + 1],
                                                    start=(hl == 0), stop=(hl == 31)), reads=[Wc, PEm], writes=[M0])
                k.op('dve', lambda en: en.tensor_copy(bcol[:, c:c + 1], M0[0:64, 0:1]), reads=[M0], writes=[bcol])
                for hl in range(32):
                    half, l = hl // 16, hl % 16
                    st0 = 16 * half + l
                    k.op('pe', lambda en: en.matmul(M1[0:64, 0:nblk], Wc[:, c * 32 + hl, :], X[:, bass.ds(st0, nblk, 16)],
                                                    start=(hl == 0), stop=(hl == 31)), reads=[Wc, X], writes=[M1])
                k.op('pool', lambda en: en.memset(comp[:, nblk:512], 0.0), writes=[comp])
                k.op('act', lambda en: en.activation(comp[:, 0:nblk], M1[0:64, 0:nblk], AF.Identity, bias=bcol[:, c:c + 1]),
                     reads=[M1, bcol], writes=[comp])
                if c == 0:
                    fm_norm(k, C, W, comp, 512, KcT[0:64, 0:512], KcT, gk[:, jc + 1:jc + 2], gk[:, 16:17], 1.0 / 64)
                else:
                    for kt in range(4):
                        k.op('pe', lambda en: en.transpose(M0[0:128, 0:64], comp[:, kt * 128:(kt + 1) * 128], C.ident[0:64, 0:64]),
                             reads=[comp, 'consts'], writes=[M0])
                        k.op('dve', lambda en: en.tensor_copy(Vc[:, kt, 0:64], M0[0:128, 0:64]), reads=[M0], writes=[Vc])
                    k.op('pool', lambda en: en.memset(Vc[:, :, 64:65], 1.0), writes=[Vc])
            k.dma(KcT[64:70, :], U.tab['kaugc'][:, :], writes=[KcT])
            k.barrier()

        with ExitStack() as es2:
            sb2 = lambda n, s, d=F32: es2.enter_context(nc.sbuf_tensor(pfx + 'k_' + n, s, d))
            xs_ = [sb2('x%d' % i, [64, 512]) for i in range(2)]
            vst = sb2('vst', [128, NKT, 64])
            k.op('pool', lambda en: en.memset(vst[:], 0.0), writes=[vst])
            for (KT, V, ksrc, vsrc, gcol) in [(KsT, Vs, U.ks_src, U.vs_src, jc + 2), (KwT, Vw, U.kw_src, U.vw_src, jc + 3)]:
                ci = 0
                for c0 in range(0, L, 512):
                    n = min(512, L - c0)
                    x = xs_[ci % 2]
                    ci += 1
                    pieces = ksrc(g, c0, n)
                    if not pieces:
                        ci -= 1
                        continue
                    for pi_, (ap, lo, hi) in enumerate(pieces):
                        k.dma(x[:, lo:hi], ap, writes=[x], acc=(pi_ > 0), allow_slow_non_contiguous=True)
                    if U.k_prenormed:
                        k.op('dve', lambda en: en.tensor_copy(KT[0:64, c0:c0 + n], x[:, 0:n]), reads=[x], writes=[KT])
                    else:
                        fm_norm(k, C, W, x, n, KT[0:64, c0:c0 + n], KT, gk[:, gcol:gcol + 1], gk[:, 16:17], 1.0 / 64)
                k.dma(KT[64:70, 0:L], U.tab['kaug'][:, 0:L], writes=[KT])
                for (ap, dst) in vsrc(g, vst):
                    k.dma(dst, ap, writes=[vst], acc=True)
                NF = L // 128
                k.op('dve', lambda en: en.tensor_copy(V[:, 0:NF, 0:64], vst[:, 0:NF, :]), reads=[vst], writes=[V])
                if L % 128:
                    rem = L % 128
                    k.op('pool', lambda en: en.memset(V[:, NF, :], 0.0), reads=[], writes=[V])
                    k.op('dve', lambda en: en.tensor_copy(V[0:rem, NF, 0:64], vst[0:rem, NF, :]), reads=[vst], writes=[V])
                k.op('pool', lambda en: en.memset(V[:, :, 64:65], 1.0), writes=[V])
            k.barrier()

        with ExitStack() as es2:
            sb2 = lambda n, s, d=F32: es2.enter_context(nc.sbuf_tensor(pfx + 'q_' + n, s, d))
            qraw = [sb2('qraw%d' % i, [64, 512]) for i in range(2)]
            QA = [sb2('QA%d' % i, [70, 512], BF16) for i in range(2)]
            eT = [sb2('eT%d' % i, [128, 512], BF16) for i in range(4)]
            P = [sb2('P%d' % i, [128, 512], BF16) for i in range(3)]
            rrow = sb2('rrow', [128, 512])
            rb = sb2('rb', [128, 512])
            oc = sb2('oc', [64, 512])
            t1 = sb2('t1', [64, 512])
            t2 = sb2('t2', [64, 512])
            rrow2 = sb2('rrow2', [128, 512])
            rb2 = sb2('rb2', [128, 512])
            acc = sb2('acc', [64, 512])
            score = sb2('score', [128, 128])
            stmp = sb2('stmp', [128, 128])
            m8 = sb2('m8', [128, 16])
            NM = sb2('NM', [128, 128])
            NMT = sb2('NMT', [128, 512], BF16)
            gsb = sb2('gsb', [48, 128])
            zsb = sb2('zsb', [64, 512])
            si = 0
            pi = 0
            for qi, (q0, nq) in enumerate(U.qtiles):
                HQ = 4 * nq
                i = (U.past + q0) // 128
                qr, qa = qraw[qi % 2], QA[qi % 2]
                k.dma(qr[:, 0:HQ].rearrange("d (h q) -> d h q", h=4),
                      U.QT[256 * g:256 * g + 256, q0:q0 + nq].rearrange("(h d) q -> d h q", d=64), writes=[qr])
                fm_norm(k, C, W, qr, HQ, qa[0:64, 0:HQ], qa, gk[:, jc:jc + 1], gk[:, 17:18], 1.0)
                k.dma(qa[64:70, 0:HQ].rearrange("r (h q) -> r h q", h=4), U.tab['qaug'][:, qi, 4 * g:4 * g + 4, 0:nq],
                      writes=[qa], acc=True)
                tiles = [kt for kt in range(4) if i - 16 * kt >= 0]
                pend_c = None

                def finish_c(pd):
                    (idx_, kt_, Sp_) = pd
                    e_ = eT[idx_]
                    k.op('act', lambda en: en.activation(e_[:, 0:HQ], Sp_[0:128, 0:HQ], AF.Exp), reads=[Sp_], writes=[e_])
                    k.op('pe', lambda en: en.matmul(Ocmp[0:65, 0:HQ], Vc[:, kt_, :], e_[:, 0:HQ],
                                                    start=(idx_ == 0), stop=(idx_ == len(tiles) - 1)), reads=[Vc, e_], writes=[Ocmp])
                for idx, kt in enumerate(tiles):
                    Sp = S[si % 2]
                    si += 1
                    d = i - 16 * kt
                    msk = (d <= 16) if U.past == 0 else (kt == 3)
                    k.op('pe', lambda en: en.matmul(Sp[0:128, 0:HQ], KcT[:, kt * 128:(kt + 1) * 128], qa[0:70, 0:HQ],
                                                    start=True, stop=not msk), reads=[KcT, qa], writes=[Sp])
                    if msk:
                        dd = min(d, 16)
                        k.op('pe', lambda en: en.matmul(Sp[0:128, 0:HQ], C.identb[:, :], C.cm[:, dd, :, 0:nq],
                                                        start=False, stop=True), reads=['consts'], writes=[Sp])
                    if pend_c is not None:
                        finish_c(pend_c)
                    pend_c = (idx, kt, Sp)
                if pend_c is not None:
                    finish_c(pend_c)
                k.op('dve', lambda en: en.tensor_scalar_add(rrow[64:65, 0:HQ], Ocmp[64:65, 0:HQ], TINY), reads=[Ocmp], writes=[rrow])
                k.op('dve', lambda en: en.reciprocal(rrow[64:65, 0:HQ], rrow[64:65, 0:HQ]), reads=[rrow], writes=[rrow])
                k.op('pe', lambda en: en.matmul(M0[0:128, 0:HQ], C.onesf[64:65, 0:128], rrow[64:65, 0:HQ], start=True, stop=True),
                     reads=[rrow, 'consts'], writes=[M0])
                k.op('act', lambda en: en.copy(rb[:, 0:HQ], M0[0:128, 0:HQ]), reads=[M0], writes=[rb])
                k.op('dve', lambda en: en.tensor_tensor(oc[:, 0:HQ], Ocmp[0:64, 0:HQ], rb[0:64, 0:HQ], ALU.mult),
                     reads=[Ocmp, rb], writes=[oc])
                nmm = 4 * len(tiles)
                cnt = 0
                for idx, kt in enumerate(tiles):
                    e = eT[idx]
                    k.op('dve', lambda en: en.tensor_tensor(e[:, 0:HQ], e[:, 0:HQ], rb[:, 0:HQ], ALU.mult), reads=[e, rb], writes=[e])
                    for h in range(4):
                        k.op('pe', lambda en: en.matmul(M1[0:nq, 0:128], e[:, h * nq:(h + 1) * nq], C.poolm[:, kt, :],
                                                        start=(cnt == 0), stop=(cnt == nmm - 1)), reads=[e, 'consts'], writes=[M1])
                        cnt += 1
                bon = U.bonus_ap(C, qi, nq)
                k.op('dve', lambda en: en.tensor_tensor(score[0:nq, :], M1[0:nq, 0:128], bon, ALU.add),
                     reads=[M1, 'consts'], writes=[score])
                k.op('dve', lambda en: en.tensor_scalar_add(score[0:nq, 0:1], score[0:nq, 0:1], 1e4), reads=[score], writes=[score])
                k.op('dve', lambda en: en.max(m8[0:nq, 0:8], score[0:nq, :]), reads=[score], writes=[m8])
                k.op('dve', lambda en: en.match_replace(stmp[0:nq, :], m8[0:nq, 0:8], score[0:nq, :], -3e38),
                     reads=[score, m8], writes=[stmp])
                k.op('dve', lambda en: en.max(m8[0:nq, 8:16], stmp[0:nq, :]), reads=[stmp], writes=[m8])
                tc_ = U.thr_col
                k.op('dve', lambda en: en.tensor_scalar(NM[0:nq, :], score[0:nq, :], m8[0:nq, tc_:tc_ + 1], NEG, ALU.is_lt, ALU.mult),
                     reads=[score, m8], writes=[NM])
                k.op('pe', lambda en: en.transpose(M0[0:128, 0:nq], NM[0:nq, :], C.ident[0:nq, 0:nq]), reads=[NM, 'consts'], writes=[M0])
                k.op('act', lambda en: en.copy(NMT[:, 0:HQ].rearrange("p (h q) -> p h q", h=4),
                                               M0[0:128, 0:nq].unsqueeze(1).to_broadcast([128, 4, nq])), reads=[M0], writes=[NMT])
                last = NKT - 1 if U.past > 0 else i
                work = []
                for (Ops, KT, V, kts, use_blk) in [(Osel, KsT, Vs, list(range(0, last + 1)), True),
                                                   (Owin, KwT, Vw, list(range(max(0, i - 4), last + 1)), False)]:
                    for n_i, kt in enumerate(kts):
                        work.append((Ops, KT, V, kt, use_blk, n_i == 0, n_i == len(kts) - 1))
                pend = None

                def finish_tile(pd):
                    (Ops_, V_, kt_, nk_, Sp_, first_, last_) = pd
                    nonlocal_pi = P_idx[0]
                    p = P[nonlocal_pi % 3]
                    P_idx[0] += 1
                    k.op('act', lambda en: en.activation(p[0:nk_, 0:HQ], Sp_[0:nk_, 0:HQ], AF.Exp), reads=[Sp_], writes=[p])
                    k.op('pe', lambda en: en.matmul(Ops_[0:65, 0:HQ], V_[0:nk_, kt_, :], p[0:nk_, 0:HQ],
                                                    start=first_, stop=last_), reads=[V_, p], writes=[Ops_])
                P_idx = [pi]
                for (Ops, KT, V, kt, use_blk, first, lastf) in work:
                    nk = min(128, L - kt * 128)
                    Sp = S[si % 2]
                    si += 1
                    extra = []
                    if use_blk and kt < 64:
                        extra.append((C.emat[:, kt, 0:nk], NMT[:, 0:HQ], NMT))
                    if kt == last:
                        extra.append((C.identb[0:nk, 0:nk], C.caus[0:nk, :, 0:nq], 'consts'))
                    if (not use_blk) and kt == i - 4:
                        extra.append((C.identb[0:nk, 0:nk], C.wlow[0:nk, :, 0:nq], 'consts'))
                    k.op('pe', lambda en: en.matmul(Sp[0:nk, 0:HQ], KT[:, kt * 128:kt * 128 + nk], qa[0:70, 0:HQ],
                                                    start=True, stop=(len(extra) == 0)), reads=[KT, qa], writes=[Sp])
                    for xi_, (l_, r_, key_) in enumerate(extra):
                        k.op('pe', lambda en: en.matmul(Sp[0:nk, 0:HQ], l_, r_, start=False, stop=(xi_ == len(extra) - 1)),
                             reads=[key_, 'consts'], writes=[Sp])
                    if pend is not None:
                        finish_tile(pend)
                    pend = (Ops, V, kt, nk, Sp, first, lastf)
                if pend is not None:
                    finish_tile(pend)
                pi = P_idx[0]
                k.dma(gsb[:, 0:nq], U.GT[:, q0:q0 + nq], writes=[gsb])
                k.op('act', lambda en: en.activation(gsb[:, 0:nq], gsb[:, 0:nq], AF.Sigmoid, bias=C.gateb[:, j:j + 1]),
                     reads=[gsb, 'consts'], writes=[gsb])
                gate_ps = [M1, S[0], S[1]]
                rden_ps = [None, M0, W['psn']]
                rrows = [None, rrow, rrow2]
                rbs = [None, rb, rb2]
                t1s = [None, t1, t2]
                brO = [None, Osel, Owin]
                for br in range(3):
                    for h in range(4):
                        rsel = br * 16 + 4 * g + h
                        k.op('pe', lambda en: en.matmul(gate_ps[br][0:64, h * nq:(h + 1) * nq], C.sel[:, rsel, :], gsb[:, 0:nq],
                                                        start=True, stop=True), reads=[gsb, 'consts'], writes=[gate_ps[br]])
                for br in (1, 2):
                    k.op('dve', lambda en: en.tensor_scalar_add(rrows[br][64:65, 0:HQ], brO[br][64:65, 0:HQ], TINY),
                         reads=[brO[br]], writes=[rrows[br]])
                    k.op('dve', lambda en: en.reciprocal(rrows[br][64:65, 0:HQ], rrows[br][64:65, 0:HQ]), reads=[rrows[br]], writes=[rrows[br]])
                for br in (1, 2):
                    k.op('pe', lambda en: en.matmul(rden_ps[br][0:64, 0:HQ], C.onesf[64:65, 0:64], rrows[br][64:65, 0:HQ], start=True, stop=True),
                         reads=[rrows[br], 'consts'], writes=[rden_ps[br]])
                for br in (1, 2):
                    k.op('act', lambda en: en.copy(rbs[br][0:64, 0:HQ], rden_ps[br][0:64, 0:HQ]), reads=[rden_ps[br]], writes=[rbs[br]])
                k.op('dve', lambda en: en.tensor_tensor(acc[:, 0:HQ], oc[:, 0:HQ], gate_ps[0][0:64, 0:HQ], ALU.mult),
                     reads=[oc, gate_ps[0]], writes=[acc])
                for br in (1, 2):
                    k.op('dve', lambda en: en.tensor_tensor(t1s[br][:, 0:HQ], brO[br][0:64, 0:HQ], rbs[br][0:64, 0:HQ], ALU.mult),
                         reads=[brO[br], rbs[br]], writes=[t1s[br]])
                    k.op('dve', lambda en: en.tensor_tensor(t1s[br][:, 0:HQ], t1s[br][:, 0:HQ], gate_ps[br][0:64, 0:HQ], ALU.mult),
                         reads=[t1s[br], gate_ps[br]], writes=[t1s[br]])
                for br in (1, 2):
                    k.op('pool', lambda en: en.tensor_tensor(acc[:, 0:HQ], acc[:, 0:HQ], t1s[br][:, 0:HQ], ALU.add),
                         reads=[acc, t1s[br]], writes=[acc])
                zv = U.ZT[256 * g:256 * g + 256, q0:q0 + nq].rearrange("(h d) q -> d h q", d=64)
                k.dma(zsb[:, 0:HQ].rearrange("d (h q) -> d h q", h=4), zv, writes=[zsb])
                k.op('act', lambda en: en.activation(zsb[:, 0:HQ], zsb[:, 0:HQ], AF.Silu), reads=[zsb], writes=[zsb])
                k.op('dve', lambda en: en.tensor_tensor(zsb[:, 0:HQ], zsb[:, 0:HQ], acc[:, 0:HQ], ALU.mult), reads=[zsb, acc], writes=[zsb])
                k.dma(U.MT[256 * g:256 * g + 256, q0:q0 + nq].rearrange("(h d) q -> d h q", d=64),
                      zsb[:, 0:HQ].rearrange("d (h q) -> d h q", h=4), reads=[zsb], writes=[('MT', U.tag)], q='pool', acc=True)
            k.barrier()


G3_STEPS = None


def run_interleaved(gens):
    gens = list(gens)
    rounds = 0
    while gens:
        if G3_STEPS is not None and rounds >= G3_STEPS:
            break
        rounds += 1
        nxt = []
        for g_ in gens:
            try:
                next(g_)
                nxt.append(g_)
            except StopIteration:
                pass
        gens = nxt


def gdn_layer(k, nc, C, U, jb, P_):
    T = U.T
    NCH = (T + 127) // 128
    NS = NCH * 8
    pfx = 'gd%s_' % U.tag
    with ExitStack() as es:
        sb = lambda n, s, d=F32: es.enter_context(nc.sbuf_tensor(pfx + 'a_' + n, s, d))
        cw = sb('cw', [128, 24, 4])
        xh = [sb('xh%d' % i, [128, 515]) for i in range(2)]
        y = [sb('y%d' % i, [128, 512]) for i in range(2)]
        W = dict(sq=sb('sq', [128, 512]), r=sb('r', [128, 512]),
                 psn=k.reg_psum(es.enter_context(nc.psum_tensor(pfx + 'a_psn', [128, 512], F32))))
        for ft in range(24):
            k.dma(cw[:, ft, :], P_['b_conv_w'][jb][:, ft * 128:(ft + 1) * 128].rearrange("j p -> p j"), writes=[cw], acc=True,
                  allow_slow_non_contiguous=True)
        ci = 0
        for ft in range(24):
            for c0 in range(0, T, 512):
                n = min(512, T - c0)
                x_, y_ = xh[ci % 2], y[ci % 2]
                ci += 1
                rows = U.GQKV[ft * 128:(ft + 1) * 128, :]
                if c0 == 0:
                    U.conv_hist(k, x_, ft)
                    k.dma(x_[:, 3:3 + n], rows[:, 0:n], writes=[x_], acc=True)
                else:
                    k.dma(x_[:, 0:3 + n], rows[:, c0 - 3:c0 + n], writes=[x_])
                k.op('dve', lambda en: en.tensor_scalar(y_[:, 0:n], x_[:, 3:3 + n], cw[:, ft, 3:4], None, ALU.mult), reads=[x_, cw], writes=[y_])
                for jj in (2, 1, 0):
                    k.op('dve', lambda en: en.scalar_tensor_tensor(y_[:, 0:n], x_[:, jj:jj + n], cw[:, ft, jj:jj + 1], y_[:, 0:n],
                                                                   ALU.mult, ALU.add), reads=[x_, cw, y_], writes=[y_])
                k.op('act', lambda en: en.activation(y_[:, 0:n], y_[:, 0:n], AF.Silu), reads=[y_], writes=[y_])
                if ft < 16:
                    gcol = C.gains[:, 18:19] if ft < 8 else C.gains[:, 19:20]
                    fm_norm(k, C, W, y_, n, y_[:, 0:n], y_, gcol, C.epsc[:, 0:1], 1.0, P=128)
                k.dma(U.GN[ft * 128:(ft + 1) * 128, c0:c0 + n], y_[:, 0:n], reads=[y_], writes=['GN'], q='pool', acc=True)
    k.barrier()
    if checkpoint('g1'):
        return
    with ExitStack() as es:
        sb = lambda n, s, d=F32: es.enter_context(nc.sbuf_tensor(pfx + 'b_' + n, s, d))
        psb = lambda n: k.reg_psum(es.enter_context(nc.psum_tensor(pfx + 'b_' + n, [128, 512], F32)))
        bd = sb('bd', [128, NCH, 16])
        Beta, nBeta, G = sb('Beta', [128, NS]), sb('nBeta', [128, NS]), sb('G', [128, NS])
        t_a, t_b = sb('ta', [128, NS]), sb('tb', [128, NS])
        eG, eGl, eD, bE = sb('eG', [128, NS]), sb('eGl', [128, NS]), sb('eD', [128, NS]), sb('bE', [128, NS])
        dtb, nal = sb('dtb', [128, 8]), sb('nal', [128, 8])
        pa, pb = psb('pa'), psb('pb')
        k.op('pool', lambda en: en.memset(bd[:], 0.0), writes=[bd])
        NF = T // 128
        if NF:
            k.dma(bd[:, 0:NF, :], U.GBD[0:NF * 128, :].rearrange("(c p) f -> p c f", p=128), writes=[bd])
        if T % 128:
            k.dma(bd[0:T % 128, NF, :], U.GBD[NF * 128:T, :], writes=[bd])
        k.dma(dtb[:], P_['b_dt_bias'][jb].partition_broadcast(128), writes=[dtb])
        k.dma(nal[:], P_['b_a_log'][jb].partition_broadcast(128), writes=[nal])
        k.op('act', lambda en: en.activation(nal[:], nal[:], AF.Exp), reads=[nal], writes=[nal])
        k.op('dve', lambda en: en.tensor_scalar(nal[:], nal[:], -1.0, None, ALU.mult), reads=[nal], writes=[nal])
        v3 = lambda t_: t_[:, 0:NS].rearrange("p (c h) -> p c h", h=8)
        k.op('act', lambda en: en.activation(v3(Beta), bd[:, :, 0:8], AF.Sigmoid), reads=[bd], writes=[Beta])
        k.op('dve', lambda en: en.tensor_scalar(nBeta[:], Beta[:], -1.0, None, ALU.mult), reads=[Beta], writes=[nBeta])
        k.op('dve', lambda en: en.tensor_tensor(v3(t_a), bd[:, :, 8:16], dtb[:].unsqueeze(1).to_broadcast([128, NCH, 8]), ALU.add),
             reads=[bd, dtb], writes=[t_a])
        k.op('dve', lambda en: en.tensor_scalar_max(eD[:], t_a[:], 0.0), reads=[t_a], writes=[eD])
        k.op('dve', lambda en: en.scalar_tensor_tensor(t_b[:], eD[:], -2.0, t_a[:], ALU.mult, ALU.add), reads=[t_a, eD], writes=[t_b])
        k.op('act', lambda en: en.activation(t_b[:], t_b[:], AF.Exp), reads=[t_b], writes=[t_b])
        k.op('act', lambda en: en.activation(t_b[:], t_b[:], AF.Ln, bias=C.onesf[:, 0:1], scale=1.0), reads=[t_b, 'consts'], writes=[t_b])
        k.op('dve', lambda en: en.tensor_tensor(t_a[:], eD[:], t_b[:], ALU.add), reads=[eD, t_b], writes=[t_a])
        k.op('dve', lambda en: en.tensor_tensor(v3(G), v3(t_a), nal[:].unsqueeze(1).to_broadcast([128, NCH, 8]), ALU.mult),
             reads=[t_a, nal], writes=[G])
        if T % 128:
            assert T % 128 == 4
            for t_ in (G, Beta, nBeta):
                k.op('dve', lambda en: en.tensor_scalar(t_[:, (NCH - 1) * 8:NCH * 8], t_[:, (NCH - 1) * 8:NCH * 8], C.pm4[:, 0:1], None, ALU.mult),
                     reads=[t_, 'consts'], writes=[t_])
        k.op('pe', lambda en: en.matmul(pa[:, 0:NS], C.tri[:, :], G[:, 0:NS], start=True, stop=True), reads=[G, 'consts'], writes=[pa])
        k.op('pe', lambda en: en.matmul(pb[:, 0:NS], C.onesf[:, :], G[:, 0:NS], start=True, stop=True), reads=[G, 'consts'], writes=[pb])
        k.op('act', lambda en: en.activation(eG[:], pa[:, 0:NS], AF.Exp), reads=[pa], writes=[eG])
        k.op('act', lambda en: en.activation(eGl[:], pb[:, 0:NS], AF.Exp), reads=[pb], writes=[eGl])
        k.op('dve', lambda en: en.tensor_copy(t_a[:], pa[:, 0:NS]), reads=[pa], writes=[t_a])
        k.op('dve', lambda en: en.tensor_tensor(t_b[:], pb[:, 0:NS], t_a[:], ALU.subtract), reads=[pb, t_a], writes=[t_b])
        k.op('act', lambda en: en.activation(eD[:], t_b[:], AF.Exp), reads=[t_b], writes=[eD])
        k.op('dve', lambda en: en.tensor_tensor(bE[:], Beta[:], eG[:], ALU.mult), reads=[Beta, eG], writes=[bE])
        k.barrier()
        if checkpoint('g2'):
            return
        with ExitStack() as es3:
            sb3 = lambda n, s, d=F32: es3.enter_context(nc.sbuf_tensor(pfx + 'c_' + n, s, d))
            H = []
            for h in range(8):
                hb = Ctx()
                hb.ps = k.reg_psum(es3.enter_context(nc.psum_tensor(pfx + 'c_ps%d' % h, [128, 512], F32))) if h < 6 else None
                for nm in ['k_tm', 'v_tm', 'Gp', 'dec', 'decT', 'YT', 'Y', 'Tt', 'wT', 'u', 'qkT', 'qeT', 'kd', 'kbe', 'vb', 'vnew', 'S', 'dg']:
                    setattr(hb, nm, sb3('%s%d' % (nm, h), [128, 128]))
                hb.qkv = sb3('qkv%d' % h, [128, 3, 128])
                hb.oT = sb3('oT%d' % h, [128, 512])
                hb.z = sb3('z%d' % h, [128, 512])
                H.append(hb)
            H[6].ps = pa
            H[7].ps = pb
            Wn = dict(sq=sb3('nsq', [128, 512]), r=sb3('nr', [128, 512]), psn=None)
            for h in range(8):
                U.state_init(k, H[h].S, h)

            def slot(hb, i, w=1):
                return hb.ps[:, i * 128:(i + w) * 128]

            def steps(h, c):
                hb = H[h]
                c0 = c * 128
                n = min(128, T - c0)
                col = c * 8 + h
                K_ = lambda i: hb.ps
                gn = U.GN.rearrange("(s hh p) t -> p s hh t", s=3, hh=8, p=128)
                if n < 128:
                    k.op('pool', lambda en: en.memset(hb.qkv[:], 0.0), writes=[hb.qkv])
                k.dma(hb.qkv[:, :, 0:n], gn[:, :, h, c0:c0 + n], writes=[hb.qkv], acc=(n < 128))
                qT, kT, vT = hb.qkv[:, 0, :], hb.qkv[:, 1, :], hb.qkv[:, 2, :]
                yield
                k.op('pe', lambda en: en.transpose(slot(hb, 0), kT, C.ident[:, :]), reads=[hb.qkv, 'consts'], writes=[K_(0)])
                k.op('pe', lambda en: en.transpose(slot(hb, 1), vT, C.ident[:, :]), reads=[hb.qkv, 'consts'], writes=[K_(1)])
                k.op('dve', lambda en: en.tensor_scalar(hb.Gp[:], C.slt[:, :], G[:, col:col + 1], None, ALU.mult), reads=['consts', G], writes=[hb.Gp])
                yield
                k.op('act', lambda en: en.copy(hb.k_tm[:], slot(hb, 0)), reads=[K_(0)], writes=[hb.k_tm])
                k.op('act', lambda en: en.copy(hb.v_tm[:], slot(hb, 1)), reads=[K_(1)], writes=[hb.v_tm])
                k.op('pe', lambda en: en.matmul(slot(hb, 2), C.tri[:, :], hb.Gp[:], start=True, stop=False), reads=[hb.Gp, 'consts'], writes=[K_(2)])
                k.op('pe', lambda en: en.matmul(slot(hb, 2), C.ident[:, :], C.mstrict[:, :], start=False, stop=True), reads=['consts'], writes=[K_(2)])
                k.op('pe', lambda en: en.matmul(slot(hb, 3), hb.Gp[:], C.tri[:, :], start=True, stop=False), reads=[hb.Gp, 'consts'], writes=[K_(3)])
                k.op('pe', lambda en: en.matmul(slot(hb, 3), C.ident[:, :], C.minclT[:, :], start=False, stop=True), reads=['consts'], writes=[K_(3)])
                yield
                k.op('act', lambda en: en.activation(hb.dec[:], slot(hb, 2), AF.Exp), reads=[K_(2)], writes=[hb.dec])
                k.op('act', lambda en: en.activation(hb.decT[:], slot(hb, 3), AF.Exp), reads=[K_(3)], writes=[hb.decT])
                k.op('pe', lambda en: en.matmul(slot(hb, 0), kT, kT, start=True, stop=True), reads=[hb.qkv], writes=[K_(0)])
                k.op('pe', lambda en: en.matmul(slot(hb, 1), kT, qT, start=True, stop=True), reads=[hb.qkv], writes=[K_(1)])
                k.op('dve', lambda en: en.tensor_scalar(hb.kbe[:], hb.k_tm[:], bE[:, col:col + 1], None, ALU.mult), reads=[hb.k_tm, bE], writes=[hb.kbe])
                k.op('dve', lambda en: en.tensor_scalar(hb.vb[:], hb.v_tm[:], Beta[:, col:col + 1], None, ALU.mult), reads=[hb.v_tm, Beta], writes=[hb.vb])
                k.op('act', lambda en: en.activation(hb.kd[:], hb.k_tm[:], AF.Copy, scale=eD[:, col:col + 1]), reads=[hb.k_tm, eD], writes=[hb.kd])
                k.op('act', lambda en: en.activation(hb.dg[:], C.ident[:, :], AF.Copy, scale=eG[:, col:col + 1]), reads=['consts', eG], writes=[hb.dg])
                yield
                k.op('dve', lambda en: en.scalar_tensor_tensor(hb.YT[:], slot(hb, 0), nBeta[:, col:col + 1], hb.dec[:], ALU.mult, ALU.mult),
                     reads=[K_(0), nBeta, hb.dec], writes=[hb.YT])
                k.op('dve', lambda en: en.tensor_tensor(hb.qkT[:], slot(hb, 1), hb.decT[:], ALU.mult), reads=[K_(1), hb.decT], writes=[hb.qkT])
                yield
                k.op('pe', lambda en: en.transpose(slot(hb, 2), hb.YT[:], C.ident[:, :]), reads=[hb.YT, 'consts'], writes=[K_(2)])
                k.op('pe', lambda en: en.matmul(slot(hb, 3), C.onesf[:, :], hb.dg[:], start=True, stop=True), reads=[hb.dg, 'consts'], writes=[K_(3)])
                yield
                k.op('act', lambda en: en.copy(hb.Y[:], slot(hb, 2)), reads=[K_(2)], writes=[hb.Y])
                k.op('dve', lambda en: en.tensor_tensor(hb.Tt[:], slot(hb, 2), C.ident[:, :], ALU.add), reads=[K_(2), 'consts'], writes=[hb.Tt])
                k.op('dve', lambda en: en.tensor_tensor(hb.qeT[:], slot(hb, 3), qT, ALU.mult), reads=[K_(3), hb.qkv], writes=[hb.qeT])
                yield
                k.op('pe', lambda en: en.matmul(slot(hb, 0), hb.YT[:], hb.Y[:], start=True, stop=True), reads=[hb.YT, hb.Y], writes=[K_(0)])
                k.op('pe', lambda en: en.matmul(slot(hb, 1), hb.Y[:], hb.YT[:], start=True, stop=True), reads=[hb.YT, hb.Y], writes=[K_(1)])
                yield
                k.op('act', lambda en: en.copy(hb.Y[:], slot(hb, 0)), reads=[K_(0)], writes=[hb.Y])
                k.op('dve', lambda en: en.tensor_copy(hb.YT[:], slot(hb, 1)), reads=[K_(1)], writes=[hb.YT])
                yield
                for kk in range(1, 7):
                    lastk = (kk == 6)
                    k.op('pe', lambda en: en.matmul(slot(hb, 2), hb.YT[:], hb.Tt[:], start=True, stop=True), reads=[hb.YT, hb.Tt], writes=[K_(2)])
                    if not lastk:
                        k.op('pe', lambda en: en.matmul(slot(hb, 0), hb.YT[:], hb.Y[:], start=True, stop=True), reads=[hb.YT, hb.Y], writes=[K_(0)])
                        k.op('pe', lambda en: en.matmul(slot(hb, 1), hb.Y[:], hb.YT[:], start=True, stop=True), reads=[hb.YT, hb.Y], writes=[K_(1)])
                    yield
                    k.op('dve', lambda en: en.tensor_tensor(hb.Tt[:], hb.Tt[:], slot(hb, 2), ALU.add), reads=[K_(2), hb.Tt], writes=[hb.Tt])
                    if not lastk:
                        k.op('act', lambda en: en.copy(hb.Y[:], slot(hb, 0)), reads=[K_(0)], writes=[hb.Y])
                        k.op('act', lambda en: en.copy(hb.YT[:], slot(hb, 1)), reads=[K_(1)], writes=[hb.YT])
                    yield
                k.op('pe', lambda en: en.matmul(slot(hb, 0), hb.kbe[:], hb.Tt[:], start=True, stop=True), reads=[hb.kbe, hb.Tt], writes=[K_(0)])
                k.op('pe', lambda en: en.matmul(slot(hb, 1), hb.Tt[:], hb.vb[:], start=True, stop=True), reads=[hb.vb, hb.Tt], writes=[K_(1)])
                yield
                k.op('act', lambda en: en.copy(hb.wT[:], slot(hb, 0)), reads=[K_(0)], writes=[hb.wT])
                k.op('act', lambda en: en.copy(hb.u[:], slot(hb, 1)), reads=[K_(1)], writes=[hb.u])
                yield
                k.op('pe', lambda en: en.matmul(slot(hb, 2), hb.wT[:], hb.S[:], start=True, stop=True), reads=[hb.wT, hb.S], writes=[K_(2)])
                yield
                k.op('dve', lambda en: en.tensor_tensor(hb.vnew[:], hb.u[:], slot(hb, 2), ALU.subtract), reads=[K_(2), hb.u], writes=[hb.vnew])
                yield
                k.op('pe', lambda en: en.matmul(slot(hb, 3), hb.S[:], hb.qeT[:], start=True, stop=False), reads=[hb.qeT, hb.S], writes=[K_(3)])
                k.op('pe', lambda en: en.matmul(slot(hb, 3), hb.vnew[:], hb.qkT[:], start=False, stop=True), reads=[hb.vnew, hb.qkT], writes=[K_(3)])
                k.op('pe', lambda en: en.matmul(slot(hb, 0), hb.kd[:], hb.vnew[:], start=True, stop=True), reads=[hb.kd, hb.vnew], writes=[K_(0)])
                yield
                cc = c % 4
                k.op('act', lambda en: en.copy(hb.oT[:, cc * 128:(cc + 1) * 128], slot(hb, 3)), reads=[K_(3)], writes=[hb.oT])
                k.op('dve', lambda en: en.scalar_tensor_tensor(hb.S[:], hb.S[:], eGl[:, col:col + 1], slot(hb, 0), ALU.mult, ALU.add),
                     reads=[K_(0), hb.S, eGl], writes=[hb.S])
                yield
                if cc == 3 or c == NCH - 1:
                    t0 = (c - cc) * 128
                    nn = min(T, c0 + 128) - t0
                    Wn['psn'] = hb.ps
                    k.dma(hb.z[:, 0:nn], U.GZ[h * 128:(h + 1) * 128, t0:t0 + nn], writes=[hb.z])
                    k.op('act', lambda en: en.activation(hb.z[:, 0:nn], hb.z[:, 0:nn], AF.Silu), reads=[hb.z], writes=[hb.z])
                    sq, r = Wn['sq'], Wn['r']
                    k.op('act', lambda en: en.activation(sq[:, 0:nn], hb.oT[:, 0:nn], AF.Square), reads=[hb.oT], writes=[sq])
                    k.op('pe', lambda en: en.matmul(hb.ps[:, 0:nn], C.onesf[:, :], sq[:, 0:nn], start=True, stop=True),
                         reads=[sq, 'consts'], writes=[K_(0), K_(1), K_(2), K_(3)])
                    k.op('act', lambda en: en.activation(r[:, 0:nn], hb.ps[:, 0:nn], AF.Sqrt, bias=C.epsc[:, 0:1], scale=1.0 / 128),
                         reads=[K_(0), K_(1), K_(2), K_(3)], writes=[r])
                    k.op('dve', lambda en: en.reciprocal(r[:, 0:nn], r[:, 0:nn]), reads=[r], writes=[r])
                    k.op('dve', lambda en: en.scalar_tensor_tensor(hb.oT[:, 0:nn], hb.oT[:, 0:nn], C.ogain[:, jb:jb + 1], r[:, 0:nn], ALU.mult, ALU.mult),
                         reads=[hb.oT, r, 'consts'], writes=[hb.oT])
                    k.op('dve', lambda en: en.tensor_tensor(hb.oT[:, 0:nn], hb.oT[:, 0:nn], hb.z[:, 0:nn], ALU.mult), reads=[hb.oT, hb.z], writes=[hb.oT])
                    k.dma(U.MT[h * 128:(h + 1) * 128, t0:t0 + nn], hb.oT[:, 0:nn], reads=[hb.oT], writes=[('MT', U.tag)], q='pool', acc=True)
                yield

            for c in range(NCH if G3_STEPS is None else 1):
                run_interleaved([steps(h, c) for h in range(8)])
            for h in range(8):
                U.state_out(k, H[h].S, h)
    k.barrier()


def conf_layer(k, nc, C, U, jc, P_):
    T = U.T
    pfx = 'cf%s_' % U.tag
    with ExitStack() as es:
        sb = lambda n, s, d=F32: es.enter_context(nc.sbuf_tensor(pfx + n, s, d))
        psb = lambda n: k.reg_psum(es.enter_context(nc.psum_tensor(pfx + n, [128, 512], F32)))
        cw = sb('cw', [128, 8, 31])
        cb_ = sb('cbias', [128, 8])
        lg, lb = sb('lg', [128, 8]), sb('lb', [128, 8])
        a_ = [sb('a%d' % i, [128, 512]) for i in range(2)]
        b_ = [sb('b%d' % i, [128, 512]) for i in range(2)]
        uh = [sb('uh%d' % i, [128, 542]) for i in range(2)]
        cbuf = sb('cbuf', [128, 8, 512])
        cb2 = [sb('cb2_%d' % i, [128, 512]) for i in range(2)]
        sq = sb('sq', [128, 512])
        mean, rstd, tmp = sb('mean', [128, 512]), sb('rstd', [128, 512]), sb('tmp', [128, 512])
        zt = [sb('z%d' % i, [128, 512]) for i in range(2)]
        ps1, ps2 = psb('ps1'), psb('ps2')
        for ft in range(8):
            k.dma(cw[:, ft, :], P_['c_conv_w'][jc][:, ft * 128:(ft + 1) * 128].rearrange("j p -> p j"), writes=[cw], acc=True,
                  allow_slow_non_contiguous=True)
        k.dma(cb_[:], P_['c_conv_b'][jc].rearrange("(ft p) -> p ft", p=128), writes=[cb_], allow_slow_non_contiguous=True)
        k.dma(lg[:], P_['c_ln_g'][jc].rearrange("(ft p) -> p ft", p=128), writes=[lg], allow_slow_non_contiguous=True)
        k.dma(lb[:], P_['c_ln_b'][jc].rearrange("(ft p) -> p ft", p=128), writes=[lb], allow_slow_non_contiguous=True)
        ci = 0
        for ft in range(8):
            for c0 in range(0, T, 512):
                n = min(512, T - c0)
                a, b = a_[ci % 2], b_[ci % 2]
                ci += 1
                k.dma(a[:, 0:n], U.CA[ft * 128:(ft + 1) * 128, c0:c0 + n], writes=[a])
                k.dma(b[:, 0:n], U.CB[ft * 128:(ft + 1) * 128, c0:c0 + n], writes=[b])
                k.op('act', lambda en: en.activation(b[:, 0:n], b[:, 0:n], AF.Sigmoid), reads=[b], writes=[b])
                k.op('dve', lambda en: en.tensor_tensor(a[:, 0:n], a[:, 0:n], b[:, 0:n], ALU.mult), reads=[a, b], writes=[a])
                k.dma(U.CU[ft * 128:(ft + 1) * 128, c0:c0 + n], a[:, 0:n], reads=[a], writes=['CU'], q='pool', acc=True)
        k.barrier()
        if checkpoint('c1'):
            return
        ui = 0
        zi = 0
        for c0 in range(0, T, 512):
            n = min(512, T - c0)
            for ft in range(8):
                u = uh[ui % 2]
                ui += 1
                rows = U.CU[ft * 128:(ft + 1) * 128, :]
                if c0 == 0:
                    U.cconv_hist(k, u, ft)
                    k.dma(u[:, 30:30 + n], rows[:, 0:n], writes=[u], acc=True)
                else:
                    k.dma(u[:, 0:30 + n], rows[:, c0 - 30:c0 + n], writes=[u])
                y = cbuf[:, ft, 0:n]
                k.op('dve', lambda en: en.tensor_scalar(y, u[:, 30:30 + n], cw[:, ft, 30:31], cb_[:, ft:ft + 1], ALU.mult, ALU.add),
                     reads=[u, cw, cb_], writes=[(cbuf, ft)])
                NPL = CONF_POOL_TAPS
                y2 = cb2[ui % 2][:, 0:n]
                for jj in range(30):
                    if jj < NPL:
                        if jj == 0:
                            k.op('pool', lambda en: en.tensor_scalar(y2, u[:, jj:jj + n], cw[:, ft, jj:jj + 1], None, ALU.mult),
                                 reads=[u, cw], writes=[cb2[ui % 2]])
                        else:
                            k.op('pool', lambda en: en.scalar_tensor_tensor(y2, u[:, jj:jj + n], cw[:, ft, jj:jj + 1], y2, ALU.mult, ALU.add),
                                 reads=[u, cw, cb2[ui % 2]], writes=[cb2[ui % 2]])
                    else:
                        k.op('dve', lambda en: en.scalar_tensor_tensor(y, u[:, jj:jj + n], cw[:, ft, jj:jj + 1], y, ALU.mult, ALU.add),
                             reads=[u, cw, (cbuf, ft)], writes=[(cbuf, ft)])
                if NPL:
                    k.op('dve', lambda en: en.tensor_tensor(y, y, y2, ALU.add), reads=[(cbuf, ft), cb2[ui % 2]], writes=[(cbuf, ft)])
                k.op('pe', lambda en: en.matmul(ps1[:, 0:n], C.onesf[:, :], y, start=(ft == 0), stop=(ft == 7)),
                     reads=[(cbuf, ft), 'consts'], writes=[ps1])
                k.op('act', lambda en: en.activation(sq[:, 0:n], y, AF.Square), reads=[(cbuf, ft)], writes=[sq])
                k.op('pe', lambda en: en.matmul(ps2[:, 0:n], C.onesf[:, :], sq[:, 0:n], start=(ft == 0), stop=(ft == 7)),
                     reads=[sq, 'consts'], writes=[ps2])
            k.op('act', lambda en: en.activation(mean[:, 0:n], ps1[:, 0:n], AF.Copy, scale=1.0 / 1024), reads=[ps1], writes=[mean])
            k.op('dve', lambda en: en.tensor_tensor(tmp[:, 0:n], mean[:, 0:n], mean[:, 0:n], ALU.mult), reads=[mean], writes=[tmp])
            k.op('dve', lambda en: en.scalar_tensor_tensor(tmp[:, 0:n], ps2[:, 0:n], 1.0 / 1024, tmp[:, 0:n], ALU.mult, ALU.subtract),
                 reads=[ps2, tmp], writes=[tmp])
            k.op('act', lambda en: en.activation(rstd[:, 0:n], tmp[:, 0:n], AF.Sqrt, bias=C.epsc[:, 0:1], scale=1.0), reads=[tmp, 'epsc'], writes=[rstd])
            k.op('dve', lambda en: en.reciprocal(rstd[:, 0:n], rstd[:, 0:n]), reads=[rstd], writes=[rstd])
            for ft in range(8):
                y = cbuf[:, ft, 0:n]
                z = zt[zi % 2]
                zi += 1
                k.dma(z[:, 0:n], U.CZ[ft * 128:(ft + 1) * 128, c0:c0 + n], writes=[z])
                k.op('act', lambda en: en.activation(z[:, 0:n], z[:, 0:n], AF.Silu), reads=[z], writes=[z])
                k.op('dve', lambda en: en.tensor_tensor(y, y, mean[:, 0:n], ALU.subtract), reads=[(cbuf, ft), mean], writes=[(cbuf, ft)])
                k.op('dve', lambda en: en.tensor_tensor(y, y, rstd[:, 0:n], ALU.mult), reads=[(cbuf, ft), rstd], writes=[(cbuf, ft)])
                k.op('act', lambda en: en.activation(y, y, AF.Silu, bias=lb[:, ft:ft + 1], scale=lg[:, ft:ft + 1]),
                     reads=[(cbuf, ft), lg, lb], writes=[(cbuf, ft)])
                k.op('dve', lambda en: en.tensor_tensor(z[:, 0:n], z[:, 0:n], y, ALU.mult), reads=[(cbuf, ft), z], writes=[z])
                k.dma(U.MT[ft * 128:(ft + 1) * 128, c0:c0 + n], z[:, 0:n], reads=[z], writes=[('MT', U.tag)], q='pool', acc=True)
        k.barrier()
        if checkpoint('c2'):
            return
        U.cconv_out(k, nc, C, ps1, ps2)
    k.barrier()
    checkpoint('c3')


def stage_outproj(k, nc, C, mt_ap, x_ap, y_ap, w_ap, T, tag):
    with ExitStack() as es:
        sb = lambda n, s, d=F32: es.enter_context(nc.sbuf_tensor(tag + n, s, d))
        ps = lambda n: k.reg_psum(es.enter_context(nc.psum_tensor(tag + n, [128, 512], F32)))
        wb = sb('wb', [128, 8, 1024], BF16)
        wst = [sb('wst%d' % i, [128, 8, 512], F32) for i in range(2)]
        mst = [sb('mst%d' % i, [128, 8, 128], F32) for i in range(2)]
        mb = [sb('mb%d' % i, [128, 8, 128], BF16) for i in range(2)]
        xt = [sb('xt%d' % i, [128, D], F32) for i in range(2)]
        yt = [sb('yt%d' % i, [128, D], F32) for i in range(2)]
        pp = [ps('p%d' % i) for i in range(4)]
        w_v = w_ap.rearrange("(kc p) c -> p kc c", p=128)
        for ci in range(2):
            k.dma(wst[ci][:], w_v[:, :, ci * 512:(ci + 1) * 512], writes=[wst[ci]])
            k.op(['dve', 'pool'][ci], lambda en: en.tensor_copy(wb[:, :, ci * 512:(ci + 1) * 512], wst[ci][:]),
                 reads=[wst[ci]], writes=[(wb, ci)])
        m_v = mt_ap.rearrange("(kc p) t -> p kc t", p=128)
        gi = 0
        for ti, t0 in enumerate(range(0, T, 128)):
            nt = min(128, T - t0)
            ms, mbb, x, y = mst[ti % 2], mb[ti % 2], xt[ti % 2], yt[ti % 2]
            k.dma(ms[:, :, 0:nt], m_v[:, :, t0:t0 + nt], writes=[ms])
            k.dma(x[0:nt, :], x_ap[t0:t0 + nt, :], writes=[x])
            k.op('pool', lambda en: en.tensor_copy(mbb[:, :, 0:nt], ms[:, :, 0:nt]), reads=[ms], writes=[mbb])
            for half in range(2):
                p = pp[gi % 4]
                gi += 1
                for kc in range(8):
                    k.op('pe', lambda en: en.matmul(p[0:nt, 0:512], mbb[:, kc, 0:nt], wb[:, kc, half * 512:(half + 1) * 512],
                                                    start=(kc == 0), stop=(kc == 7)), reads=[mbb, (wb, 0), (wb, 1)], writes=[p])
                k.op('dve', lambda en: en.tensor_tensor(y[0:nt, half * 512:(half + 1) * 512], p[0:nt, 0:512],
                                                        x[0:nt, half * 512:(half + 1) * 512], ALU.add), reads=[p, x], writes=[y])
            k.dma(y_ap[t0:t0 + nt, :], y[0:nt, :], reads=[y], writes=[('Y', tag)], q='pool', acc=True)
    k.barrier()


N_SEQ = 4
N_POOL = 2560
CONF_POOL_TAPS = 0
RUN_PROMPT = True
RUN_SAMPLE = True


def sample_setup(nc, dr, scr, out, tab):
    S = Ctx()
    TS = 4 * N_SEQ
    S.TS = TS
    S.pt = dr("pt", [N_SEQ, 64], I32)
    S.cache_cmp = [dr("cache_cmp%d" % i, [N_POOL * 128, 512]) for i in range(2)]
    S.cache_slc = [dr("cache_slc%d" % i, [N_POOL * 128, 512]) for i in range(2)]
    S.win_in = dr("win_in", [2, N_SEQ, 512, 512])
    S.sd_in = dr("sd_in", [N_SEQ, 8, 128, 128])
    S.sdc_in = dr("sdc_in", [N_SEQ, 3, 3072])
    S.sc_in = dr("sc_in", [N_SEQ, 30, 1024])
    S.qaug = dr("t_qaug_s", [6, 1, 16, 128], BF16)
    S.o_ys = out("o_ys", [TS, D])
    S.o_cmp = out("o_cmp_s", [2, TS, 512])
    S.o_slc = out("o_slc_s", [2, TS, 512])
    S.o_win = out("o_win_s", [2, N_SEQ, 512, 512])
    S.o_dst = out("o_dst_s", [N_SEQ, 8, 128, 128])
    S.o_dcv = out("o_dcv_s", [N_SEQ, 3, 3072])
    S.o_ccv = out("o_ccv_s", [N_SEQ, 30, 1024])
    S.BIG = scr("BIG_s", [4112, TS])
    S.GN, S.CU, S.GBD = scr("GN_s", [3072, TS]), scr("CU_s", [1024, TS]), scr("GBD_s", [TS, 16])
    S.MT = scr("MT_s", [1024, TS])
    S.Y = [scr("Y0_s", [TS, D]), scr("Y1_s", [TS, D])]
    S.kvw = scr("kvw_s", [TS, 512])
    S.CT = [scr("CTs%d" % i, [512, 8192]) for i in range(N_SEQ)]
    S.KST = [scr("KSTs%d" % i, [256, 8192]) for i in range(N_SEQ)]
    S.VS = [scr("VSs%d" % i, [8192, 256]) for i in range(N_SEQ)]
    S.KWT = [scr("KWTs%d" % i, [256, 512]) for i in range(N_SEQ)]
    return S


def sample_prepass(k, nc, C, S, j, s):
    with ExitStack() as es:
        pfx = 'pp%d%d_' % (j, s)
        pg = [es.enter_context(nc.sbuf_tensor(pfx + 'pg%d' % i, [128, 512], F32)) for i in range(3)]
        tsb = [es.enter_context(nc.sbuf_tensor(pfx + 'tsb%d' % i, [128, 512], F32)) for i in range(2)]
        tps = [k.reg_psum(es.enter_context(nc.psum_tensor(pfx + 'tps%d' % i, [128, 512], F32))) for i in range(2)]
        it = 0
        for (cache, nb, dstT, dstV) in [(S.cache_cmp[j], 4, S.CT[s], None), (S.cache_slc[j], 2, S.KST[s], S.VS[s]),
                                        (None, 2, S.KWT[s], None)]:
            ntile = 64 if cache is not None else 4
            for p in range(ntile):
                g_, t_, ps_ = pg[it % 3], tsb[it % 2], tps[it % 2]
                it += 1
                if cache is not None:
                    k.dma_ind(g_[:, :], cache, C.idx[:, s * 64 + p:s * 64 + p + 1], reads=['idx'], writes=[g_])
                else:
                    k.dma(g_[:, :], S.win_in[j, s, p * 128:(p + 1) * 128, :], writes=[g_])
                for b in range(nb):
                    k.op('pe', lambda en: en.transpose(ps_[:, b * 128:(b + 1) * 128], g_[:, b * 128:(b + 1) * 128], C.ident[:, :]),
                         reads=[g_, 'consts'], writes=[ps_])
                if it % 2:
                    k.op('act', lambda en: en.copy(t_[:, 0:nb * 128], ps_[:, 0:nb * 128]), reads=[ps_], writes=[t_])
                else:
                    k.op('dve', lambda en: en.tensor_copy(t_[:, 0:nb * 128], ps_[:, 0:nb * 128]), reads=[ps_], writes=[t_])
                k.dma(dstT.rearrange("(b q) t -> q b t", q=128)[:, :, p * 128:(p + 1) * 128],
                      t_[:, 0:nb * 128].rearrange("q (b t) -> q b t", b=nb), reads=[t_], writes=[('ppT', s)], acc=True)
                if dstV is not None:
                    k.dma(dstV[p * 128:(p + 1) * 128, :], g_[:, 256:512], reads=[g_], writes=[('ppV', s)], acc=True)
    k.barrier()


def sample_unit(S, s, tab):
    U = Ctx()
    U.tag = 's%d' % s
    U.T, U.past, U.L, U.Lc = 4, 8192, 8196, 8192
    c = slice(4 * s, 4 * s + 4)
    big = S.BIG
    U.QT, U.GT, U.ZT = big[0:1024, c], big[2048:2096, c], big[2096:3120, c]
    U.GQKV, U.GZ = big[0:3072, c], big[3072:4096, c]
    U.CA, U.CB, U.CZ = big[0:1024, c], big[1024:2048, c], big[2048:3072, c]
    U.GN, U.CU, U.GBD, U.MT = S.GN[:, c], S.CU[:, c], S.GBD[c, :], S.MT[:, c]
    U.qtiles = [(0, 4)]
    U.tab = dict(tab)
    U.tab['qaug'] = S.qaug
    U.k_prenormed = True
    U.thr_col = 14
    U.bonus_ap = lambda C_, qi, nq: C_.bonus_s[0:nq, :]
    return U


def load_nsa_consts(k, nc, C, tab, esn, pfx):
    cn = lambda n, s, d=F32: esn.enter_context(nc.sbuf_tensor(pfx + n, s, d))
    C.caus, C.wlow = cn("caus", [128, 4, 128], BF16), cn("wlow", [128, 4, 128], BF16)
    C.cm = cn("cm", [128, 17, 4, 128], BF16)
    C.poolm = cn("poolm", [128, 4, 128], BF16)
    C.bonus = cn("bonus", [128, 256])
    C.bonus_s = cn("bonus_s", [128, 128])
    C.sel = cn("sel", [48, 48, 64])
    C.emat = cn("emat", [128, 65, 128], BF16)
    for n_ in ['caus', 'wlow', 'cm', 'poolm', 'bonus', 'bonus_s', 'sel', 'emat']:
        k.dma(getattr(C, n_)[:], tab[n_], writes=['consts'], acc=True)
    k.barrier()


def sample_layer(k, nc, C, S, P_, tab, layer, x_cur, x_next, knorm_post):
    kind, j = layer % 3, layer // 3
    TS = S.TS
    gcol = P_['norm_g'][layer].rearrange("(kc p) -> p kc", p=128)
    fm = lambda ap: (lambda b0, nb, t0, nts: ap[b0:b0 + nb, t0:t0 + nts])
    if kind == 0:
        plan = [
            dict(c0=0, n=1024, mode='FM', key='QTs', dst=fm(S.BIG[0:1024])),
            dict(c0=2560, n=48, mode='FM', key='GTs', dst=fm(S.BIG[2048:2096])),
            dict(c0=2608, n=1024, mode='FM', key='ZTs', dst=fm(S.BIG[2096:3120])),
            dict(c0=1024, n=512, mode='TM', key='o_cmp_s',
                 dst=lambda b0, nb, t0, nt: [(S.o_cmp[j, t0:t0 + nt, b0:b0 + nb], 'o_cmp_s')]),
            dict(c0=1536, n=512, mode='TM', key='o_slc_s', post=knorm_post(j, 1),
                 dst=lambda b0, nb, t0, nt: [(S.o_slc[j, t0:t0 + nt, b0:b0 + nb], 'o_slc_s')]),
            dict(c0=2048, n=512, mode='TM', key='kvw_s', post=knorm_post(j, 2),
                 dst=lambda b0, nb, t0, nt: [(S.kvw[t0:t0 + nt, b0:b0 + nb], 'kvw_s')]),
        ]
        stage_inproj(k, nc, C, x_cur, TS, P_['a_w_in'][j], A_IN, gcol, plan, 'A%ds' % layer)
        esn = ExitStack()
        load_nsa_consts(k, nc, C, tab, esn, "cn%ds_" % layer)
        for s in range(N_SEQ):
            sample_prepass(k, nc, C, S, j, s)
            U = sample_unit(S, s, tab)
            U.a_cmp_w, U.a_cmp_pos = P_['a_cmp_w'], P_['a_cmp_pos']
            c4 = slice(4 * s, 4 * s + 4)
            U.ct_src = lambda c, g, s=s: S.CT[s][c * 256 + g * 64:c * 256 + g * 64 + 64, :]

            def ks_src(g, c0, n, s=s, c4=c4):
                res = []
                m = min(c0 + n, 8192) - c0
                if m > 0:
                    res.append((S.KST[s][g * 64:g * 64 + 64, c0:c0 + m], 0, m))
                if c0 + n > 8192:
                    res.append((S.o_slc[j, c4, g * 64:g * 64 + 64].rearrange("t d -> d t"), max(m, 0), max(m, 0) + 4))
                return res

            def kw_src(g, c0, n, s=s, c4=c4):
                if c0 == 7680:
                    return [(S.KWT[s][g * 64:g * 64 + 64, 0:512], 0, 512)]
                if c0 == 8192:
                    return [(S.kvw[c4, g * 64:g * 64 + 64].rearrange("t d -> d t"), 0, 4)]
                return []

            def vs_src(g, vst, s=s, c4=c4):
                res = []
                v = S.VS[s][:, g * 64:g * 64 + 64].rearrange("(kt p) d -> p kt d", p=128)
                for a in range(0, 64, 16):
                    res.append((v[:, a:a + 16, :], vst[:, a:a + 16, :]))
                res.append((S.o_slc[j, c4, 256 + g * 64:256 + g * 64 + 64], vst[0:4, 64, :]))
                return res

            def vw_src(g, vst, s=s, c4=c4):
                v = S.win_in[j, s][:, 256 + g * 64:256 + g * 64 + 64].rearrange("(kt p) d -> p kt d", p=128)
                return [(v, vst[:, 60:64, :]), (S.kvw[c4, 256 + g * 64:256 + g * 64 + 64], vst[0:4, 64, :])]
            U.ks_src, U.kw_src, U.vs_src, U.vw_src = ks_src, kw_src, vs_src, vw_src
            for g in range(N_GROUPS_RUN):
                nsa_group(k, nc, C, U, j, g)
            k.dma(S.o_win[j, s, 0:508, :], S.win_in[j, s, 4:512, :], writes=['o_win_s'], q='pool', acc=True)
            k.dma(S.o_win[j, s, 508:512, :], S.kvw[c4, :], reads=['kvw_s'], writes=['o_win_s'], q='pool', acc=True)
        k.barrier()
        esn.close()
        w_out = P_['a_w_out'][j]
    elif kind == 1:
        def dcv_dst(b0, nb, t0, nt):
            return [(S.o_dcv[s, :, b0:b0 + nb], 'o_dcv_s', 4 * s + 1, 4 * s + 4) for s in range(N_SEQ)]
        plan = [
            dict(c0=0, n=3072, mode='FM', key='GQKVs', dst=fm(S.BIG[0:3072])),
            dict(c0=3072, n=1024, mode='FM', key='GZs', dst=fm(S.BIG[3072:4096])),
            dict(c0=4096, n=16, mode='TM', key='GBDs', dst=lambda b0, nb, t0, nt: [(S.GBD[t0:t0 + nt, b0:b0 + nb], 'GBDs')]),
            dict(c0=0, n=3072, mode='TM', key='o_dcv_s', dst=dcv_dst),
        ]
        stage_inproj(k, nc, C, x_cur, TS, P_['b_w_in'][j], DN_IN, gcol, plan, 'A%ds' % layer)
        for s in range(N_SEQ):
            U = sample_unit(S, s, tab)
            U.conv_hist = lambda k_, x_, ft, s=s: k_.dma(x_[:, 0:3], S.sdc_in[s][:, ft * 128:(ft + 1) * 128].rearrange("j p -> p j"),
                                                         writes=[x_], allow_slow_non_contiguous=True)
            U.state_init = lambda k_, St, h, s=s: k_.dma(St[:], S.sd_in[s, h], writes=[St])
            U.state_out = lambda k_, St, h, s=s: k_.dma(S.o_dst[s, h], St[:], reads=[St], writes=['o_dst_s'], q='pool', acc=True)
            gdn_layer(k, nc, C, U, j, P_)
        w_out = P_['b_w_out'][j]
    else:
        plan = [
            dict(c0=0, n=1024, mode='FM', key='CAs', dst=fm(S.BIG[0:1024])),
            dict(c0=1024, n=1024, mode='FM', key='CBs', dst=fm(S.BIG[1024:2048])),
            dict(c0=2048, n=1024, mode='FM', key='CZs', dst=fm(S.BIG[2048:3072])),
        ]
        stage_inproj(k, nc, C, x_cur, TS, P_['c_w_in'][j], CONF_IN, gcol, plan, 'A%ds' % layer)
        for s in range(N_SEQ):
            U = sample_unit(S, s, tab)
            U.cconv_hist = lambda k_, u, ft, s=s: k_.dma(u[:, 0:30], S.sc_in[s][:, ft * 128:(ft + 1) * 128].rearrange("j p -> p j"),
                                                         writes=[u], allow_slow_non_contiguous=True)

            def cconv_out(k_, nc_, C_, ps1, ps2, s=s):
                k_.dma(S.o_ccv[s, 0:26, :], S.sc_in[s, 4:30, :], writes=['o_ccv_s'], q='pool', acc=True)
                with ExitStack() as es_:
                    ul = es_.enter_context(nc_.sbuf_tensor("ccvs_ul%d" % s, [128, 8, 4], F32))
                    ot = es_.enter_context(nc_.sbuf_tensor("ccvs_ot%d" % s, [4, 1024], F32))
                    k_.dma(ul[:], S.CU[:, 4 * s:4 * s + 4].rearrange("(ft p) t -> p ft t", p=128), reads=['CU'], writes=[ul])
                    for ft in range(8):
                        pp = ps1 if ft < 4 else ps2
                        k_.op('pe', lambda en: en.transpose(pp[0:4, (ft % 4) * 128:(ft % 4 + 1) * 128], ul[:, ft, :], C_.ident[:, :]),
                              reads=[ul, 'consts'], writes=[pp])
                    k_.op('dve', lambda en: en.tensor_copy(ot[:, 0:512], ps1[0:4, 0:512]), reads=[ps1], writes=[ot])
                    k_.op('dve', lambda en: en.tensor_copy(ot[:, 512:1024], ps2[0:4, 0:512]), reads=[ps2], writes=[ot])
                    k_.dma(S.o_ccv[s, 26:30, :], ot[:], reads=[ot], writes=['o_ccv_s'], q='pool', acc=True)
                    k_.barrier()
            U.cconv_out = cconv_out
            conf_layer(k, nc, C, U, j, P_)
        w_out = P_['c_w_out'][j]
    stage_outproj(k, nc, C, S.MT, x_cur, x_next, w_out, TS, 'O%ds' % layer)


def build_program():
    nc = bass.Bass("TRN2", target_bir_lowering=False)
    C = Ctx()
    TP = T_P
    NTP = TP // 128
    dr = lambda n, s, d=F32, kind="ExternalInput": nc.dram_tensor(n, s, d, kind=kind).ap()
    xp = dr("xp", [TP, D])
    xs = dr("xs", [4 * N_SEQ, D])
    P_ = {}
    for n_, sh in PARAM_SHAPES.items():
        P_[n_] = dr(n_, sh)
    tab = {n: dr("t_" + n, sh, dt) for n, (sh, dt) in TABLE_SPECS.items()}
    tab['qaug'] = dr("t_qaug_p", [6, NTP, 16, 128], BF16)
    out = lambda n, s: dr(n, s, kind="ExternalOutput")
    o_yp = out("o_yp", [TP, D])
    o_cmp_p = out("o_cmp_p", [2, TP, 512])
    o_slc_p = out("o_slc_p", [2, TP, 512])
    o_win_p = out("o_win_p", [2, 512, 512])
    o_dst_p = out("o_dst_p", [1, 8, 128, 128])
    o_dcv_p = out("o_dcv_p", [1, 3, 3072])
    o_ccv_p = out("o_ccv_p", [1, 30, 1024])
    scr = lambda n, s: dr(n, s, kind="Internal")
    kvw_p = scr("kvw_p", [TP, 512])
    S = sample_setup(nc, dr, scr, out, tab) if RUN_SAMPLE else None
    UP = Ctx()
    UP.tag = 'p'
    UP.T, UP.past, UP.L = TP, 0, TP
    UP.Lc = TP
    big = scr("BIG_p", [4112, TP])
    UP.QT, UP.CT, UP.KST, UP.KWT = big[0:1024], big[1024:1536], big[1536:1792], big[1792:2048]
    UP.GT, UP.ZT = big[2048:2096], big[2096:3120]
    UP.GQKV, UP.GZ = big[0:3072], big[3072:4096]
    UP.CA, UP.CB, UP.CZ = big[0:1024], big[1024:2048], big[2048:3072]
    UP.GN = scr("GN_p", [3072, TP])
    UP.CU = scr("CU_p", [1024, TP])
    UP.GBD = scr("GBD_p", [TP, 16])
    UP.MT = scr("MT_p", [1024, TP])
    UP.Y = [scr("Y0_p", [TP, D]), scr("Y1_p", [TP, D])]
    UP.qtiles = [(q0, 128) for q0 in range(0, TP, 128)]
    UP.tab = tab
    UP.a_cmp_w, UP.a_cmp_pos = P_['a_cmp_w'], P_['a_cmp_pos']
    UP.k_prenormed = False
    UP.thr_col = 15
    UP.bonus_ap = lambda C_, qi, nq: C_.bonus[0:nq, 126 - 2 * qi:254 - 2 * qi]

    k = KB(nc)
    UP.conv_hist = lambda k_, x_, ft: k_.op('pool', lambda en: en.memset(x_[:, 0:3], 0.0), writes=[x_])
    UP.cconv_hist = lambda k_, u, ft: k_.op('pool', lambda en: en.memset(u[:, 0:30], 0.0), writes=[u])
    UP.state_init = lambda k_, S, h: k_.op('pool', lambda en: en.memset(S[:], 0.0), writes=[S])
    UP.state_out = lambda k_, S, h: k_.dma(o_dst_p[0, h], S[:], reads=[S], writes=['o_dst'], q='pool', acc=True)

    def cconv_out_p(k_, nc_, C_, ps1, ps2):
        with ExitStack() as es_:
            ul = es_.enter_context(nc_.sbuf_tensor("ccv_ul", [128, 8, 30], F32))
            ot = es_.enter_context(nc_.sbuf_tensor("ccv_ot", [30, 1024], F32))
            k_.dma(ul[:], UP.CU[:, TP - 30:TP].rearrange("(ft p) t -> p ft t", p=128), writes=[ul])
            for ft in range(8):
                pp = ps1 if ft < 4 else ps2
                k_.op('pe', lambda en: en.transpose(pp[0:30, (ft % 4) * 128:(ft % 4 + 1) * 128], ul[:, ft, :], C_.ident[:, :]),
                      reads=[ul, 'consts'], writes=[pp])
            k_.op('dve', lambda en: en.tensor_copy(ot[:, 0:512], ps1[0:30, 0:512]), reads=[ps1], writes=[ot])
            k_.op('dve', lambda en: en.tensor_copy(ot[:, 512:1024], ps2[0:30, 0:512]), reads=[ps2], writes=[ot])
            k_.dma(o_ccv_p[0], ot[:], reads=[ot], writes=['o_ccv'], q='pool')
            k_.barrier()
    UP.cconv_out = cconv_out_p

    with ExitStack() as es0:
        cs = lambda n, s, d=F32: es0.enter_context(nc.sbuf_tensor("c_" + n, s, d))
        C.ident = cs("ident", [128, 128])
        C.identb = cs("identb", [128, 128], BF16)
        C.onesf = cs("onesf", [128, 128])
        C.tri, C.slt = cs("tri", [128, 128]), cs("slt", [128, 128])
        C.mstrict, C.minclT = cs("mstrict", [128, 128]), cs("minclT", [128, 128])
        C.gains = cs("gains", [128, 20])
        C.gateb = cs("gateb", [48, 2])
        C.ogain = cs("ogain", [128, 1])
        C.gainb = cs("gainb", [128, 2, 3, 64])
        C.tmp = cs("ctmp", [128, 512])
        C.st4 = cs("cst4", [128, 8])
        C.epsc = cs("epsc", [128, 1])
        k.op('pool', lambda en: en.memset(C.epsc[:], EPS), writes=['epsc'])
        k.op('pool', lambda en: en.memset(C.gains[:, 16:17], EPS), writes=['consts'])
        k.op('pool', lambda en: en.memset(C.gains[:, 17:18], 64 * EPS), writes=['consts'])
        k.op('pool', lambda en: en.memset(C.gains[:, 18:19], 128 ** -0.5), writes=['consts'])
        k.op('pool', lambda en: en.memset(C.gains[:, 19:20], 1.0), writes=['consts'])
        C.pm4 = cs("pm4", [128, 1])
        for n_ in ['ident', 'identb', 'onesf', 'tri', 'slt', 'mstrict', 'minclT', 'pm4']:
            k.dma(getattr(C, n_)[:], tab[n_], writes=['consts'], acc=True)
        for jj in range(2):
            k.dma(C.gains[0:64, jj * 8:jj * 8 + 1], P_['a_q_gain'][jj].rearrange("(d o) -> d o", o=1), writes=['consts'], acc=True)
            k.dma(C.gains[0:64, jj * 8 + 1:jj * 8 + 4], P_['a_k_gain'][jj].rearrange("b d -> d b"), writes=['consts'], acc=True,
                  allow_slow_non_contiguous=True)
            k.dma(C.gateb[:, jj:jj + 1], P_['a_gate_b'][jj].rearrange("(d o) -> d o", o=1), writes=['consts'], acc=True)
        k.dma(C.ogain[:], P_['b_o_gain'][0].rearrange("(d o) -> d o", o=1), writes=['consts'], acc=True)
        k.dma(C.gainb[:].rearrange("p a b c -> p (a b c)"),
              P_['a_k_gain'].rearrange("a b c -> (a b c)").partition_broadcast(128), writes=['gainb'])
        if RUN_SAMPLE:
            C.idx = cs("idx", [128, N_SEQ * 64], I32)
            pti = cs("pti", [128, N_SEQ * 64], I32)
            ptf = cs("ptf", [128, N_SEQ * 64])
            pcol = cs("pcol", [128, 1])
            k.dma(pcol[:], tab['pcol'], writes=['pcol'])
            k.dma(pti[:], S.pt.rearrange("s p -> (s p)").partition_broadcast(128), writes=['pti'])
            k.op('dve', lambda en: en.tensor_copy(ptf[:], pti[:]), reads=['pti'], writes=['ptf'])
            k.op('dve', lambda en: en.tensor_scalar(ptf[:], ptf[:], 128.0, pcol[:, 0:1], ALU.mult, ALU.add), reads=['ptf', 'pcol'], writes=['ptf'])
            k.op('dve', lambda en: en.tensor_copy(C.idx[:], ptf[:]), reads=['ptf'], writes=['idx'])
        k.barrier()

        def knorm_post(j, which):
            def post(o, nt, nb):
                kv = o[0:nt, 0:256]
                k.op('act', lambda en: en.activation(C.tmp[0:nt, 0:256], kv, AF.Square), reads=[o], writes=['ctmp'])
                k.op('dve', lambda en: en.tensor_reduce(C.st4[0:nt, 0:4], C.tmp[0:nt, 0:256].rearrange("p (g d) -> p g d", g=4),
                                                        AX.X, ALU.add), reads=['ctmp'], writes=['cst4'])
                rstd_op(k, C, C.st4[0:nt, 4:8], C.st4[0:nt, 0:4], 1.0 / 64, nt, 'cst4')
                kv3 = kv.rearrange("p (g d) -> p g d", g=4)
                k.op('dve', lambda en: en.tensor_tensor(kv3, kv3, C.st4[0:nt, 4:8].unsqueeze(2).to_broadcast([nt, 4, 64]), ALU.mult),
                     reads=[o, 'cst4'], writes=[o])
                k.op('dve', lambda en: en.tensor_tensor(kv3, kv3, C.gainb[0:nt, j, which:which + 1, :].to_broadcast([nt, 4, 64]), ALU.mult),
                     reads=[o, 'gainb'], writes=[o])
            return post

        fm = lambda ap: (lambda b0, nb, t0, nts: ap[b0:b0 + nb, t0:t0 + nts])
        U = UP
        x_cur = xp
        layers = list(LAYERS)
        xs_cur = xs
        for li, layer in enumerate(layers):
          if RUN_SAMPLE:
              xs_next = S.o_ys if li == len(layers) - 1 else S.Y[li % 2]
              sample_layer(k, nc, C, S, P_, tab, layer, xs_cur, xs_next, knorm_post)
              xs_cur = xs_next
          if not RUN_PROMPT:
              continue
          try:
            kind, j = layer % 3, layer // 3
            x_next = o_yp if li == len(layers) - 1 else U.Y[li % 2]
            gcol = P_['norm_g'][layer].rearrange("(kc p) -> p kc", p=128)
            if kind == 0:
                def win_dst(b0, nb, t0, nt, j=j):
                    res = [(kvw_p[t0:t0 + nt, b0:b0 + nb], ('kvw', 'p'))]
                    if t0 >= TP - 512:
                        res.append((o_win_p[j, t0 - (TP - 512):t0 - (TP - 512) + nt, b0:b0 + nb], 'o_win_p'))
                    return res
                plan = [
                    dict(c0=0, n=1024, mode='FM', key='QT', dst=fm(U.QT)),
                    dict(c0=1024, n=512, mode='FM', key='CT', dst=fm(U.CT)),
                    dict(c0=1536, n=256, mode='FM', key='KST', dst=fm(U.KST)),
                    dict(c0=2048, n=256, mode='FM', key='KWT', dst=fm(U.KWT)),
                    dict(c0=2560, n=48, mode='FM', key='GT', dst=fm(U.GT)),
                    dict(c0=2608, n=1024, mode='FM', key='ZT', dst=fm(U.ZT)),
                    dict(c0=1024, n=512, mode='TM', key='o_cmp',
                         dst=lambda b0, nb, t0, nt, j=j: [(o_cmp_p[j, t0:t0 + nt, b0:b0 + nb], 'o_cmp')]),
                    dict(c0=1536, n=512, mode='TM', key='o_slc', post=knorm_post(j, 1),
                         dst=lambda b0, nb, t0, nt, j=j: [(o_slc_p[j, t0:t0 + nt, b0:b0 + nb], 'o_slc')]),
                    dict(c0=2048, n=512, mode='TM', key='kvw', post=knorm_post(j, 2), dst=win_dst),
                ]
                stage_inproj(k, nc, C, x_cur, TP, P_['a_w_in'][j], A_IN, gcol, plan, 'A%dp' % layer)
                U.ct_src = lambda c, g: U.CT[c * 256 + g * 64:c * 256 + g * 64 + 64, :]
                U.ks_src = lambda g, c0, n: [(U.KST[g * 64:g * 64 + 64, c0:c0 + n], 0, n)]
                U.kw_src = lambda g, c0, n: [(U.KWT[g * 64:g * 64 + 64, c0:c0 + n], 0, n)]

                def vsrc_of(src):
                    def f(g, vst):
                        res = []
                        v = src[:, 256 + g * 64:256 + g * 64 + 64].rearrange("(kt p) d -> p kt d", p=128)
                        for a in range(0, NTP, 16):
                            b = min(NTP, a + 16)
                            res.append((v[:, a:b, :], vst[:, a:b, :]))
                        return res
                    return f
                U.vs_src = vsrc_of(o_slc_p[j])
                U.vw_src = vsrc_of(kvw_p)
                with ExitStack() as esn:
                    load_nsa_consts(k, nc, C, tab, esn, "cn%dp_" % layer)
                    for g in range(N_GROUPS_RUN):
                        nsa_group(k, nc, C, U, j, g)
                w_out = P_['a_w_out'][j]
            elif kind == 1:
                T3 = TP - 3

                def dcv_dst(b0, nb, t0, nt):
                    r0 = max(T3 - t0, 0)
                    return [(o_dcv_p[0, t0 + r0 - T3:t0 + nt - T3, b0:b0 + nb], 'o_dcv', r0, nt)]
                plan = [
                    dict(c0=0, n=3072, mode='FM', key='GQKV', dst=fm(U.GQKV)),
                    dict(c0=3072, n=1024, mode='FM', key='GZ', dst=fm(U.GZ)),
                    dict(c0=4096, n=16, mode='TM', key='GBD', dst=lambda b0, nb, t0, nt: [(U.GBD[t0:t0 + nt, b0:b0 + nb], 'GBD')]),
                    dict(c0=0, n=3072, mode='TM', key='o_dcv', dst=dcv_dst, tok_filter=lambda t0, nt: t0 + nt > T3),
                ]
                stage_inproj(k, nc, C, x_cur, TP, P_['b_w_in'][j], DN_IN, gcol, plan, 'A%dp' % layer)
                if not checkpoint('inproj'):
                    gdn_layer(k, nc, C, U, j, P_)
                w_out = P_['b_w_out'][j]
            else:
                plan = [
                    dict(c0=0, n=1024, mode='FM', key='CA', dst=fm(U.CA)),
                    dict(c0=1024, n=1024, mode='FM', key='CB', dst=fm(U.CB)),
                    dict(c0=2048, n=1024, mode='FM', key='CZ', dst=fm(U.CZ)),
                ]
                stage_inproj(k, nc, C, x_cur, TP, P_['c_w_in'][j], CONF_IN, gcol, plan, 'A%dp' % layer)
                if not checkpoint('inproj'):
                    conf_layer(k, nc, C, U, j, P_)
                w_out = P_['c_w_out'][j]
            if STOPPED[0]:
                break
            stage_outproj(k, nc, C, U.MT, x_cur, x_next, w_out, TP, 'O%dp' % layer)
            x_cur = x_next
          except StopBuild:
            k.barrier()
            break
        k.finish()
    print("instructions:", k.n_ins, k.cnt)
    return nc


LAYERS = [0, 1, 2, 3]
PARAM_SHAPES = dict(
    norm_g=[4, D], a_w_in=[2, D, A_IN], a_w_out=[2, D, D], a_k_gain=[2, 3, 64], a_q_gain=[2, 64], a_gate_b=[2, 48],
    a_cmp_w=[2, 2, 2048, 64], a_cmp_pos=[2, 2, 32, 64],
    b_w_in=[1, D, DN_IN], b_conv_w=[1, 4, 3072], b_a_log=[1, 8], b_dt_bias=[1, 8], b_o_gain=[1, 128], b_w_out=[1, D, D],
    c_w_in=[1, D, CONF_IN], c_conv_w=[1, 31, 1024], c_conv_b=[1, 1024], c_ln_g=[1, 1024], c_ln_b=[1, 1024], c_w_out=[1, D, D],
)
N_GROUPS_RUN = 4
_CACHE = {}


def extra_inputs(T=None):
    tb = host_tables(T or T_P, 0)
    out = {"t_" + n: tb[n] for n in TABLE_SPECS}
    out["t_qaug_p"] = tb['qaug']
    out["t_qaug_s"] = host_tables(128, 8192)['qaug']
    return out


def kernel(**inputs):
    f32 = lambda a: np.ascontiguousarray(np.asarray(a, dtype=np.float32))
    if 'nc' not in _CACHE:
        _CACHE['nc'] = build_program()
    nc = _CACHE['nc']
    x_prompt = f32(inputs['x_prompt'])
    x_sample = f32(inputs['x_sample'])
    tabs = extra_inputs()
    shared = {n: f32(inputs[n]) for n in PARAM_SHAPES}
    if RUN_SAMPLE:
        ccmp = f32(inputs['cache_cmp_kv']).reshape(2, N_POOL * 128, 512)
        cslc = f32(inputs['cache_slc_kv']).reshape(2, N_POOL * 128, 512)
        cwin = f32(inputs['cache_win_kv']).reshape(2, 32, 512, 512)
        pt = np.ascontiguousarray(np.asarray(inputs['page_table'], dtype=np.int32))
        sd = f32(inputs['state_delta'])[0]
        sdc = f32(inputs['state_delta_conv'])[0]
        sc = f32(inputs['state_conv'])[0]
    in_maps = []
    for c in range(N_CORES):
        b = c % 2
        m = {"xp": x_prompt[b], "xs": x_sample[4 * c:4 * c + 4].reshape(16, D)}
        m.update(shared)
        m.update(tabs)
        if RUN_SAMPLE:
            sl = slice(4 * c, 4 * c + 4)
            m.update({"pt": pt[sl], "cache_cmp0": ccmp[0], "cache_cmp1": ccmp[1], "cache_slc0": cslc[0], "cache_slc1": cslc[1], "win_in": np.ascontiguousarray(cwin[:, sl]),
                      "sd_in": np.ascontiguousarray(sd[sl]), "sdc_in": np.ascontiguousarray(sdc[sl]),
                      "sc_in": np.ascontiguousarray(sc[sl])})
        else:
            m.pop("t_qaug_s", None)
        in_maps.append(m)
    res = run_bass_kernel_spmd(nc, in_maps, core_ids=list(range(N_CORES)))
    R = res.results
    z = lambda *s: np.zeros(s, np.float32)
    y_prompt = np.stack([R[b]["o_yp"] for b in range(2)])
    cmp_p = np.stack([R[b]["o_cmp_p"] for b in range(2)], axis=1).reshape(2, 2, T_P, 2, 4, 64)
    slc_p = np.stack([R[b]["o_slc_p"] for b in range(2)], axis=1).reshape(2, 2, T_P, 2, 4, 64)
    win_p = np.stack([R[b]["o_win_p"] for b in range(2)], axis=1).reshape(2, 2, 512, 2, 4, 64)
    dst_p = np.stack([R[b]["o_dst_p"] for b in range(2)], axis=1)
    dcv_p = np.stack([R[b]["o_dcv_p"] for b in range(2)], axis=1)
    ccv_p = np.stack([R[b]["o_ccv_p"] for b in range(2)], axis=1)
    if RUN_SAMPLE:
        y_sample = np.concatenate([R[c]["o_ys"].reshape(4, 4, D) for c in range(8)], axis=0)
        cmp_s = np.concatenate([R[c]["o_cmp_s"].reshape(2, 4, 4, 2, 4, 64) for c in range(8)], axis=1)
        slc_s = np.concatenate([R[c]["o_slc_s"].reshape(2, 4, 4, 2, 4, 64) for c in range(8)], axis=1)
        win_s = np.concatenate([R[c]["o_win_s"].reshape(2, 4, 512, 2, 4, 64) for c in range(8)], axis=1)
        dst_s = np.concatenate([R[c]["o_dst_s"] for c in range(8)], axis=0)[None]
        dcv_s = np.concatenate([R[c]["o_dcv_s"] for c in range(8)], axis=0)[None]
        ccv_s = np.concatenate([R[c]["o_ccv_s"] for c in range(8)], axis=0)[None]
    else:
        y_sample, cmp_s, slc_s, win_s = z(32, 4, D), z(2, 32, 4, 2, 4, 64), z(2, 32, 4, 2, 4, 64), z(2, 32, 512, 2, 4, 64)
        dst_s, dcv_s, ccv_s = z(1, 32, 8, 128, 128), z(1, 32, 3, 3072), z(1, 32, 30, 1024)
    return (y_prompt, y_sample, cmp_p, cmp_s, slc_p, slc_s, win_p, win_s,
            dst_p, dst_s, dcv_p, dcv_s, ccv_p, ccv_s)
```

```python
import numpy as np
from contextlib import ExitStack
import concourse.bass as bass
import concourse.mybir as mybir
from concourse.bass_utils import run_bass_kernel_spmd

F32 = mybir.dt.float32
BF16 = mybir.dt.bfloat16
I32 = mybir.dt.int32
ALU = mybir.AluOpType
AF = mybir.ActivationFunctionType
AX = mybir.AxisListType

D = 1024
T_P = 8192
N_CORES = 8
EPS = 1e-6
A_IN = 3632
DN_IN = 4112
CONF_IN = 3072


class KB:
    def __init__(self, nc, n_dma_sems=32):
        self.nc = nc
        self.eng = {'pe': nc.tensor, 'act': nc.scalar, 'dve': nc.vector, 'pool': nc.gpsimd, 'sp': nc.sync}
        self.sem = {k: nc.alloc_semaphore('sem_' + k) for k in self.eng}
        self.cnt = {k: 0 for k in self.eng}
        self.dsem = [nc.alloc_semaphore('dsem%d' % i) for i in range(n_dma_sems)]
        self.dval = [0] * n_dma_sems
        self.dq = {'sp': list(range(0, 16)), 'pool': list(range(16, 28)), 'act': list(range(28, 32))}
        self.dnext = {'sp': 0, 'pool': 0, 'act': 0}
        self.waited = {k: {} for k in self.eng}
        self.res_w = {}
        self.res_wacc = {}
        self.res_r = {}
        self.psum_keys = set()
        self.n_ins = 0

    def _semobj(self, sk):
        return self.sem[sk] if isinstance(sk, str) else self.dsem[sk]

    def _need(self, e, evs):
        for sk, v in evs.items():
            if sk == e and e == 'pe':
                continue
            if self.waited[e].get(sk, 0) >= v:
                continue
            self.eng[e].wait_ge(self._semobj(sk), v)
            self.waited[e][sk] = v

    @staticmethod
    def _nk(x):
        if isinstance(x, (str, int)):
            return x
        if isinstance(x, tuple):
            return tuple(KB._nk(y) for y in x)
        return getattr(x, 'name', None) or id(x)

    def _deps(self, e, reads, writes, acc=False):
        reads = [self._nk(r) for r in reads]
        writes = [self._nk(w) for w in writes]
        evs = {}

        def add(d):
            for sk, v in d.items():
                if evs.get(sk, 0) < v:
                    evs[sk] = v
        for r in reads:
            add(self.res_w.get(r, {}))
            add(self.res_wacc.get(r, {}))
        for w in writes:
            add(self.res_w.get(w, {}))
            add(self.res_r.get(w, {}))
            if not acc:
                add(self.res_wacc.get(w, {}))
        self._need(e, evs)

    def _commit(self, ev, reads, writes, acc=False):
        reads = [self._nk(r) for r in reads]
        writes = [self._nk(w) for w in writes]
        sk, v = ev
        for r in reads:
            d = self.res_r.setdefault(r, {})
            if d.get(sk, 0) < v:
                d[sk] = v
        for w in writes:
            if acc:
                d = self.res_wacc.setdefault(w, {})
                d[sk] = max(d.get(sk, 0), v)
            else:
                self.res_w[w] = {sk: v}
                self.res_wacc[w] = {}
                self.res_r[w] = {}

    def op(self, e, fn, reads=(), writes=()):
        pr = [r for r in reads if self._nk(r) in self.psum_keys]
        if pr:
            reads = [r for r in reads if self._nk(r) not in self.psum_keys]
            writes = list(writes) + [r for r in pr if self._nk(r) not in [self._nk(w) for w in writes]]
        self._deps(e, reads, writes)
        ins = fn(self.eng[e])
        self.cnt[e] += 1
        ins.then_inc(self.sem[e], 1)
        self._commit((e, self.cnt[e]), reads, writes)
        self.n_ins += 1
        return ins

    def dma(self, out, in_, reads=(), writes=(), q='sp', acc=False, **kw):
        self._deps(q, reads, writes, acc=acc)
        i = self.dq[q][self.dnext[q] % len(self.dq[q])]
        self.dnext[q] += 1
        if self.dval[i] > 0 and self.waited[q].get(i, 0) < self.dval[i]:
            self.eng[q].wait_ge(self.dsem[i], self.dval[i])
            self.waited[q][i] = self.dval[i]
        ins = self.eng[q].dma_start(out=out, in_=in_, **kw)
        self.dval[i] += 16
        ins.then_inc(self.dsem[i], 16)
        self._commit((i, self.dval[i]), reads, writes, acc=acc)
        self.n_ins += 1
        return ins

    def dma_ind(self, out, in_, idx_ap, reads=(), writes=()):
        q = 'pool'
        self._deps(q, reads, writes)
        i = self.dq[q][self.dnext[q] % len(self.dq[q])]
        self.dnext[q] += 1
        if self.dval[i] > 0 and self.waited[q].get(i, 0) < self.dval[i]:
            self.eng[q].wait_ge(self.dsem[i], self.dval[i])
            self.waited[q][i] = self.dval[i]
        ins = self.nc.gpsimd.indirect_dma_start(out, None, in_, bass.IndirectOffsetOnAxis(ap=idx_ap, axis=0))
        self.dval[i] += 16
        ins.then_inc(self.dsem[i], 16)
        self._commit((i, self.dval[i]), reads, writes)
        self.n_ins += 1
        return ins

    def reg_psum(self, t):
        self.psum_keys.add(self._nk(t))
        return t

    def barrier(self):
        for e in self.eng:
            evs = {k: c for k, c in self.cnt.items() if c > 0 and k != e}
            for i, v in enumerate(self.dval):
                if v > 0:
                    evs[i] = v
            for sk, v in evs.items():
                if self.waited[e].get(sk, 0) >= v:
                    continue
                self.eng[e].wait_ge(self._semobj(sk), v)
                self.waited[e][sk] = v
        self.res_w = {}
        self.res_wacc = {}
        self.res_r = {}

    def finish(self):
        self.barrier()


class Ctx:
    pass


class StopBuild(Exception):
    pass


STOP_AFTER = None


STOPPED = [False]


def checkpoint(name):
    if STOP_AFTER == name:
        STOPPED[0] = True
    return STOPPED[0]


def rr(lst, state=[0]):
    state[0] += 1
    return lst[state[0] % len(lst)]


def rstd_op(k, C, out_ap, in_ap, scale, nt, key):
    k.op('act', lambda en: en.activation(out_ap, in_ap, AF.Sqrt, bias=C.epsc[0:nt, 0:1], scale=scale),
         reads=[key, 'epsc'], writes=[key])
    k.op('dve', lambda en: en.reciprocal(out_ap, out_ap), reads=[key], writes=[key])


def stage_inproj(k, nc, C, x_ap, T, w_ap, ncol, g_col_ap, plan, tag):
    with ExitStack() as es:
        sb = lambda n, s, d=F32: es.enter_context(nc.sbuf_tensor(tag + n, s, d))
        ps = lambda n, s, d=F32: k.reg_psum(es.enter_context(nc.psum_tensor(tag + n, s, d)))
        wb = sb('wb', [128, 8, ncol], BF16)
        wst = [sb('wst%d' % i, [128, 8, 512], F32) for i in range(2)]
        gT = sb('gT', [128, 8], F32)
        xts = [sb('xt%d' % i, [128, D], F32) for i in range(2)]
        xn = [sb('xn%d' % i, [128, D], F32) for i in range(2)]
        junk = sb('junk', [128, D], F32)
        st = [sb('st%d' % i, [128, 4], F32) for i in range(2)]
        hT = [sb('hT%d' % i, [128, 8, 512], BF16) for i in range(2)]
        ob = [sb('ob%d' % i, [128, 512], F32) for i in range(4)]
        ptr = [ps('ptr%d' % i, [128, 512], F32) for i in range(2)]
        pg = [ps('pg%d' % i, [128, 512], F32) for i in range(4)]

        k.dma(gT[:], g_col_ap, writes=[gT], allow_slow_non_contiguous=True)
        w_v = w_ap.rearrange("(kc p) c -> p kc c", p=128)
        ci = 0
        for cb in range(0, ncol, 512):
            n = min(512, ncol - cb)
            s = wst[ci % 2]
            k.dma(s[:, :, 0:n], w_v[:, :, cb:cb + n], writes=[s])
            e = ['act', 'dve', 'pool'][ci % 3]
            if e == 'act':
                k.op('act', lambda en: en.copy(wb[:, :, cb:cb + n], s[:, :, 0:n]), reads=[s], writes=[(wb, cb)])
            else:
                k.op(e, lambda en: en.tensor_copy(wb[:, :, cb:cb + n], s[:, :, 0:n]), reads=[s], writes=[(wb, cb)])
            ci += 1
        wkeys = [(wb, cb) for cb in range(0, ncol, 512)]

        xi = 0
        oi = 0
        gi = 0
        def prep(si, t0):
            nts = min(512, T - t0)
            h = hT[si % 2]
            nonlocal xi
            for j0 in range(0, nts, 128):
                nt = min(128, nts - j0)
                xt = xts[xi % 2]
                xnn = xn[xi % 2]
                stt = st[xi % 2]
                xi += 1
                k.dma(xt[0:nt, :], x_ap[t0 + j0:t0 + j0 + nt, :], writes=[xt])
                k.op('act', lambda en: en.activation(junk[0:nt, :], xt[0:nt, :], AF.Square, accum_out=stt[0:nt, 0:1]),
                     reads=[xt], writes=[junk, stt])
                rstd_op(k, C, stt[0:nt, 2:3], stt[0:nt, 0:1], 1.0 / D, nt, stt)
                k.op('dve', lambda en: en.tensor_scalar(xnn[0:nt, :], xt[0:nt, :], stt[0:nt, 2:3], None, ALU.mult),
                     reads=[xt, stt], writes=[xnn])
                for kc in range(8):
                    p = ptr[kc % 2]
                    k.op('pe', lambda en: en.transpose(p[:, 0:nt], xnn[0:nt, kc * 128:(kc + 1) * 128], C.ident[0:nt, 0:nt]),
                         reads=[xnn, 'ident'], writes=[p])
                    e = 'dve' if kc % 2 == 0 else 'pool'
                    if e == 'pool':
                        k.op('act', lambda en: en.activation(h[:, kc, j0:j0 + nt], p[:, 0:nt], AF.Copy, scale=gT[:, kc:kc + 1]),
                             reads=[p, gT], writes=[h])
                    else:
                        k.op('dve', lambda en: en.tensor_scalar(h[:, kc, j0:j0 + nt], p[:, 0:nt], gT[:, kc:kc + 1], None, ALU.mult),
                             reads=[p, gT], writes=[h])
        st_list = list(enumerate(range(0, T, 512)))
        prep(*st_list[0])
        for idx_, (si, t0) in enumerate(st_list):
            nts = min(512, T - t0)
            h = hT[si % 2]
            if idx_ + 1 < len(st_list):
                prep(*st_list[idx_ + 1])
            for pl in plan:
                c0, n, mode = pl['c0'], pl['n'], pl['mode']
                blk = pl.get('blk', 128)
                if mode == 'FM':
                    for b0 in range(0, n, blk):
                        nb = min(blk, n - b0)
                        pp = pg[gi % 4]
                        gi += 1
                        for kc in range(8):
                            k.op('pe', lambda en: en.matmul(pp[0:nb, 0:nts], wb[:, kc, c0 + b0:c0 + b0 + nb], h[:, kc, 0:nts],
                                                            start=(kc == 0), stop=(kc == 7)),
                                 reads=[h] + wkeys, writes=[pp])
                        o = ob[oi % 4]
                        oi += 1
                        if oi % 2 == 0:
                            k.op('act', lambda en: en.copy(o[0:nb, 0:nts], pp[0:nb, 0:nts]), reads=[pp], writes=[o])
                        else:
                            k.op('dve', lambda en: en.tensor_copy(o[0:nb, 0:nts], pp[0:nb, 0:nts]), reads=[pp], writes=[o])
                        k.dma(pl['dst'](b0, nb, t0, nts), o[0:nb, 0:nts], reads=[o], writes=[pl['key']], q='pool', acc=True)
                else:
                    for j0 in range(0, nts, 128):
                        nt = min(128, nts - j0)
                        if pl.get('tok_filter') is not None and not pl['tok_filter'](t0 + j0, nt):
                            continue
                        for b0 in range(0, n, 512):
                            nb = min(512, n - b0)
                            pp = pg[gi % 4]
                            gi += 1
                            for kc in range(8):
                                k.op('pe', lambda en: en.matmul(pp[0:nt, 0:nb], h[:, kc, j0:j0 + nt], wb[:, kc, c0 + b0:c0 + b0 + nb],
                                                                start=(kc == 0), stop=(kc == 7)),
                                     reads=[h] + wkeys, writes=[pp])
                            o = ob[oi % 4]
                            oi += 1
                            if oi % 2 == 0:
                                k.op('act', lambda en: en.copy(o[0:nt, 0:nb], pp[0:nt, 0:nb]), reads=[pp], writes=[o])
                            else:
                                k.op('dve', lambda en: en.tensor_copy(o[0:nt, 0:nb], pp[0:nt, 0:nb]), reads=[pp], writes=[o])
                            if pl.get('post') is not None:
                                pl['post'](o, nt, nb)
                            for ent in pl['dst'](b0, nb, t0 + j0, nt):
                                r0, r1 = (ent[2], ent[3]) if len(ent) == 4 else (0, nt)
                                k.dma(ent[0], o[r0:r1, 0:nb], reads=[o], writes=[ent[1]], q='pool', acc=True)
    k.barrier()


NEG = -30000.0
TINY = 1e-30


def host_tables(T, t_base):
    import ml_dtypes
    bf = ml_dtypes.bfloat16
    NT = (T + 127) // 128
    slopes = np.exp2(-8.0 * np.arange(1, 17, dtype=np.float32) / 16).astype(np.float32)

    def split2(v):
        hi = v.astype(bf)
        lo = (v - hi.astype(np.float32)).astype(bf)
        return hi, lo
    s_hi, s_lo = split2(slopes)
    t = (t_base + np.arange(NT * 128)).astype(np.float32)
    v = -(slopes[None, :] * t[:, None]).astype(np.float32)
    v_hi, v_lo = split2(v)
    qaug = np.zeros((6, NT, 16, 128), dtype=bf)
    qaug[0] = s_hi[None, :, None]
    qaug[1] = s_lo[None, :, None]
    qaug[2] = s_hi[None, :, None]
    qaug[3] = s_lo[None, :, None]
    qaug[4] = v_hi.reshape(NT, 128, 16).transpose(0, 2, 1)
    qaug[5] = v_lo.reshape(NT, 128, 16).transpose(0, 2, 1)

    def kaug_of(pos):
        ka = np.zeros((6, pos.shape[0]), dtype=bf)
        hi = (128 * (pos // 128)).astype(np.float32)
        lo = (pos % 128).astype(np.float32)
        ka[0] = hi
        ka[1] = hi
        ka[2] = lo
        ka[3] = lo
        ka[4] = 1.0
        ka[5] = 1.0
        return ka
    res = dict(qaug=qaug)
    kr = np.arange(128)[:, None]
    qr = np.arange(128)[None, :]
    caus = np.where(kr <= qr, 0.0, NEG).astype(np.float32)
    wlow = np.where(kr > qr, 0.0, NEG).astype(np.float32)
    res['caus'] = np.repeat(caus[:, None, :], 4, axis=1).astype(bf)
    res['wlow'] = np.repeat(wlow[:, None, :], 4, axis=1).astype(bf)
    cm = np.zeros((128, 17, 4, 128), dtype=np.float32)
    for d in range(17):
        m = np.where(16 * kr + 31 <= 128 * d + qr, 0.0, NEG)
        cm[:, d, :, :] = m[:, None, :]
    res['cm'] = cm.astype(bf)
    n = np.arange(512)
    poolm = np.zeros((128, 4, 128), dtype=np.float32)
    for kt in range(4):
        for nr in range(128):
            poolm[nr, kt, (128 * kt + nr) // 4] = 1.0
    res['poolm'] = poolm.astype(bf)
    bonus = np.zeros((128, 256), dtype=np.float32)
    for q in range(128):
        jq = 1 if q >= 64 else 0
        for c in range(256):
            jj = c - 126
            if jj > jq:
                bonus[q, c] = -1e30
            elif jj == jq or jj == jq - 1:
                bonus[q, c] = 1e4
    res['bonus'] = bonus
    sel = np.zeros((48, 48, 64), dtype=np.float32)
    for r in range(48):
        sel[r, r, :] = 1.0
    res['sel'] = sel
    res['kaug'] = kaug_of(np.arange(8192 + 128))
    res['kaugc'] = kaug_of(16 * np.arange(512) + 31)
    e = np.zeros((128, 65, 128), dtype=np.float32)
    for kt in range(65):
        for key in range(128):
            jb = 2 * kt + key // 64
            if jb < 128:
                e[jb, kt, key] = 1.0
    res['emat'] = e.astype(bf)
    res['ident'] = np.eye(128, dtype=np.float32)
    bs = np.zeros((128, 128), dtype=np.float32)
    bs[:, 127] = 1e4
    res['bonus_s'] = bs
    res['pcol'] = np.arange(128, dtype=np.float32).reshape(128, 1)
    res['pm4'] = (np.arange(128) < 4).astype(np.float32).reshape(128, 1)
    res['identb'] = np.eye(128, dtype=np.float32).astype(bf)
    res['onesf'] = np.ones((128, 128), dtype=np.float32)
    row = np.arange(128)[:, None]
    colx = np.arange(128)[None, :]
    res['tri'] = (row <= colx).astype(np.float32)
    res['slt'] = (row > colx).astype(np.float32)
    res['mstrict'] = np.where(row > colx, 0.0, NEG).astype(np.float32)
    res['minclT'] = np.where(colx >= row, 0.0, NEG).astype(np.float32)
    return res


TABLE_SPECS = dict(
    caus=([128, 4, 128], BF16), wlow=([128, 4, 128], BF16), cm=([128, 17, 4, 128], BF16),
    poolm=([128, 4, 128], BF16), bonus=([128, 256], F32), sel=([48, 48, 64], F32),
    kaug=([6, 8192 + 128], BF16), kaugc=([6, 512], BF16), emat=([128, 65, 128], BF16),
    ident=([128, 128], F32), identb=([128, 128], BF16), onesf=([128, 128], F32),
    tri=([128, 128], F32), slt=([128, 128], F32), mstrict=([128, 128], F32), minclT=([128, 128], F32),
    bonus_s=([128, 128], F32), pcol=([128, 1], F32), pm4=([128, 1], F32),
)


def fm_norm(k, C, W, src, n, dst_ap, dst_key, gain_col, eps_col, scale, P=64, src_ap=None, src_key=None):
    sq, ps, r = W['sq'], W['psn'], W['r']
    sap = src[0:P, 0:n] if src_ap is None else src_ap
    skey = src if src_key is None else src_key
    k.op('act', lambda en: en.activation(sq[0:P, 0:n], sap, AF.Square), reads=[skey], writes=[sq])
    k.op('pe', lambda en: en.matmul(ps[0:P, 0:n], C.onesf[0:P, 0:P], sq[0:P, 0:n], start=True, stop=True),
         reads=[sq, 'consts'], writes=[ps])
    k.op('act', lambda en: en.activation(r[0:P, 0:n], ps[0:P, 0:n], AF.Sqrt, bias=eps_col, scale=scale),
         reads=[ps, 'consts'], writes=[r])
    k.op('dve', lambda en: en.reciprocal(r[0:P, 0:n], r[0:P, 0:n]), reads=[r], writes=[r])
    k.op('dve', lambda en: en.scalar_tensor_tensor(dst_ap, sap, gain_col, r[0:P, 0:n], ALU.mult, ALU.mult),
         reads=[skey, r, 'consts'], writes=[dst_key])


def nsa_group(k, nc, C, U, j, g):
    L, T = U.L, U.T
    NKT = (L + 127) // 128
    with ExitStack() as es:
        pfx = '%s%d%d_' % (U.tag, j, g)
        sb = lambda n, s, d=F32: es.enter_context(nc.sbuf_tensor(pfx + 'g_' + n, s, d))
        psm = lambda n: k.reg_psum(es.enter_context(nc.psum_tensor(pfx + 'gp_' + n, [128, 512], F32)))
        KsT = sb('KsT', [70, NKT * 128], BF16)
        KwT = sb('KwT', [70, NKT * 128], BF16)
        Vs = sb('Vs', [128, NKT, 65], BF16)
        Vw = sb('Vw', [128, NKT, 65], BF16)
        KcT = sb('KcT', [70, 512], BF16)
        Vc = sb('Vc', [128, 4, 65], BF16)
        W = dict(sq=sb('sq', [64, 512]), r=sb('r', [64, 512]), psn=psm('psn'))
        S = [psm('S0'), psm('S1')]
        Ocmp, Osel, Owin = psm('Ocmp'), psm('Osel'), psm('Owin')
        M0, M1 = psm('M0'), psm('M1')
        gk = C.gains[0:64, :]
        jc = j * 8

        with ExitStack() as es2:
            sb2 = lambda n, s, d=F32: es2.enter_context(nc.sbuf_tensor(pfx + 'c_' + n, s, d))
            Wc = sb2('Wc', [64, 64, 64])
            PEm = sb2('PEm', [64, 64])
            Lc = U.Lc
            nblk = Lc // 16 - 1
            X = sb2('X', [64, Lc])
            comp = sb2('comp', [64, 512])
            bcol = sb2('bcol', [64, 2])
            k.dma(Wc[:], U.a_cmp_w[j].rearrange("c (hl d) e -> d (c hl) e", d=64), writes=[Wc])
            k.dma(PEm[:], U.a_cmp_pos[j].rearrange("c hl d -> d (c hl)"), writes=[PEm], allow_slow_non_contiguous=True)
            for c in range(2):
                k.dma(X[:, :], U.ct_src(c, g), writes=[X])
                for hl in range(32):
                    k.op('pe', lambda en: en.matmul(M0[0:64, 0:1], Wc[:, c * 32 + hl, :], PEm[:, c * 32 + hl:c * 32 + hl + 1],
                                                    start=(hl == 0), stop=(hl == 31)), reads=[Wc, PEm], writes=[M0])
                k.op('dve', lambda en: en.tensor_copy(bcol[:, c:c + 1], M0[0:64, 0:1]), reads=[M0], writes=[bcol])
                for hl in range(32):
                    half, l = hl // 16, hl % 16
                    st0 = 16 * half + l
                    k.op('pe', lambda en: en.matmul(M1[0:64, 0:nblk], Wc[:, c * 32 + hl, :], X[:, bass.ds(st0, nblk, 16)],
                                                    start=(hl == 0), stop=(hl == 31)), reads=[Wc, X], writes=[M1])
                k.op('pool', lambda en: en.memset(comp[:, nblk:512], 0.0), writes=[comp])
                k.op('act', lambda en: en.activation(comp[:, 0:nblk], M1[0:64, 0:nblk], AF.Identity, bias=bcol[:, c:c + 1]),
                     reads=[M1, bcol], writes=[comp])
                if c == 0:
                    fm_norm(k, C, W, comp, 512, KcT[0:64, 0:512], KcT, gk[:, jc + 1:jc + 2], gk[:, 16:17], 1.0 / 64)
                else:
                    for kt in range(4):
                        k.op('pe', lambda en: en.transpose(M0[0:128, 0:64], comp[:, kt * 128:(kt + 1) * 128], C.ident[0:64, 0:64]),
                             reads=[comp, 'consts'], writes=[M0])
                        k.op('dve', lambda en: en.tensor_copy(Vc[:, kt, 0:64], M0[0:128, 0:64]), reads=[M0], writes=[Vc])
                    k.op('pool', lambda en: en.memset(Vc[:, :, 64:65], 1.0), writes=[Vc])
            k.dma(KcT[64:70, :], U.tab['kaugc'][:, :], writes=[KcT])
            k.barrier()

        with ExitStack() as es2:
            sb2 = lambda n, s, d=F32: es2.enter_context(nc.sbuf_tensor(pfx + 'k_' + n, s, d))
            xs_ = [sb2('x%d' % i, [64, 512]) for i in range(2)]
            vst = sb2('vst', [128, NKT, 64])
            k.op('pool', lambda en: en.memset(vst[:], 0.0), writes=[vst])
            for (KT, V, ksrc, vsrc, gcol) in [(KsT, Vs, U.ks_src, U.vs_src, jc + 2), (KwT, Vw, U.kw_src, U.vw_src, jc + 3)]:
                ci = 0
                for c0 in range(0, L, 512):
                    n = min(512, L - c0)
                    x = xs_[ci % 2]
                    ci += 1
                    pieces = ksrc(g, c0, n)
                    if not pieces:
                        ci -= 1
                        continue
                    for pi_, (ap, lo, hi) in enumerate(pieces):
                        k.dma(x[:, lo:hi], ap, writes=[x], acc=(pi_ > 0), allow_slow_non_contiguous=True)
                    if U.k_prenormed:
                        k.op('dve', lambda en: en.tensor_copy(KT[0:64, c0:c0 + n], x[:, 0:n]), reads=[x], writes=[KT])
                    else:
                        fm_norm(k, C, W, x, n, KT[0:64, c0:c0 + n], KT, gk[:, gcol:gcol + 1], gk[:, 16:17], 1.0 / 64)
                k.dma(KT[64:70, 0:L], U.tab['kaug'][:, 0:L], writes=[KT])
                for (ap, dst) in vsrc(g, vst):
                    k.dma(dst, ap, writes=[vst], acc=True)
                NF = L // 128
                k.op('dve', lambda en: en.tensor_copy(V[:, 0:NF, 0:64], vst[:, 0:NF, :]), reads=[vst], writes=[V])
                if L % 128:
                    rem = L % 128
                    k.op('pool', lambda en: en.memset(V[:, NF, :], 0.0), reads=[], writes=[V])
                    k.op('dve', lambda en: en.tensor_copy(V[0:rem, NF, 0:64], vst[0:rem, NF, :]), reads=[vst], writes=[V])
                k.op('pool', lambda en: en.memset(V[:, :, 64:65], 1.0), writes=[V])
            k.barrier()

        with ExitStack() as es2:
            sb2 = lambda n, s, d=F32: es2.enter_context(nc.sbuf_tensor(pfx + 'q_' + n, s, d))
            qraw = [sb2('qraw%d' % i, [64, 512]) for i in range(2)]
            QA = [sb2('QA%d' % i, [70, 512], BF16) for i in range(2)]
            eT = [sb2('eT%d' % i, [128, 512], BF16) for i in range(4)]
            P = [sb2('P%d' % i, [128, 512], BF16) for i in range(3)]
            rrow = sb2('rrow', [128, 512])
            rb = sb2('rb', [128, 512])
            oc = sb2('oc', [64, 512])
            t1 = sb2('t1', [64, 512])
            t2 = sb2('t2', [64, 512])
            rrow2 = sb2('rrow2', [128, 512])
            rb2 = sb2('rb2', [128, 512])
            acc = sb2('acc', [64, 512])
            score = sb2('score', [128, 128])
            stmp = sb2('stmp', [128, 128])
            m8 = sb2('m8', [128, 16])
            NM = sb2('NM', [128, 128])
            NMT = sb2('NMT', [128, 512], BF16)
            gsb = sb2('gsb', [48, 128])
            zsb = sb2('zsb', [64, 512])
            si = 0
            pi = 0
            for qi, (q0, nq) in enumerate(U.qtiles):
                HQ = 4 * nq
                i = (U.past + q0) // 128
                qr, qa = qraw[qi % 2], QA[qi % 2]
                k.dma(qr[:, 0:HQ].rearrange("d (h q) -> d h q", h=4),
                      U.QT[256 * g:256 * g + 256, q0:q0 + nq].rearrange("(h d) q -> d h q", d=64), writes=[qr])
                fm_norm(k, C, W, qr, HQ, qa[0:64, 0:HQ], qa, gk[:, jc:jc + 1], gk[:, 17:18], 1.0)
                k.dma(qa[64:70, 0:HQ].rearrange("r (h q) -> r h q", h=4), U.tab['qaug'][:, qi, 4 * g:4 * g + 4, 0:nq],
                      writes=[qa], acc=True)
                tiles = [kt for kt in range(4) if i - 16 * kt >= 0]
                pend_c = None

                def finish_c(pd):
                    (idx_, kt_, Sp_) = pd
                    e_ = eT[idx_]
                    k.op('act', lambda en: en.activation(e_[:, 0:HQ], Sp_[0:128, 0:HQ], AF.Exp), reads=[Sp_], writes=[e_])
                    k.op('pe', lambda en: en.matmul(Ocmp[0:65, 0:HQ], Vc[:, kt_, :], e_[:, 0:HQ],
                                                    start=(idx_ == 0), stop=(idx_ == len(tiles) - 1)), reads=[Vc, e_], writes=[Ocmp])
                for idx, kt in enumerate(tiles):
                    Sp = S[si % 2]
                    si += 1
                    d = i - 16 * kt
                    msk = (d <= 16) if U.past == 0 else (kt == 3)
                    k.op('pe', lambda en: en.matmul(Sp[0:128, 0:HQ], KcT[:, kt * 128:(kt + 1) * 128], qa[0:70, 0:HQ],
                                                    start=True, stop=not msk), reads=[KcT, qa], writes=[Sp])
                    if msk:
                        dd = min(d, 16)
                        k.op('pe', lambda en: en.matmul(Sp[0:128, 0:HQ], C.identb[:, :], C.cm[:, dd, :, 0:nq],
                                                        start=False, stop=True), reads=['consts'], writes=[Sp])
                    if pend_c is not None:
                        finish_c(pend_c)
                    pend_c = (idx, kt, Sp)
                if pend_c is not None:
                    finish_c(pend_c)
                last = NKT - 1 if U.past > 0 else i
                def run_branches(branches, si, pi):
                    work = []
                    for (Ops, KT, V, kts, use_blk) in branches:
                        for n_i, kt in enumerate(kts):
                            work.append((Ops, KT, V, kt, use_blk, n_i == 0, n_i == len(kts) - 1))
                    pend = None

                    def finish_tile(pd):
                        (Ops_, V_, kt_, nk_, Sp_, first_, last_) = pd
                        nonlocal_pi = P_idx[0]
                        p = P[nonlocal_pi % 3]
                        P_idx[0] += 1
                        k.op('act', lambda en: en.activation(p[0:nk_, 0:HQ], Sp_[0:nk_, 0:HQ], AF.Exp), reads=[Sp_], writes=[p])
                        k.op('pe', lambda en: en.matmul(Ops_[0:65, 0:HQ], V_[0:nk_, kt_, :], p[0:nk_, 0:HQ],
                                                        start=first_, stop=last_), reads=[V_, p], writes=[Ops_])
                    P_idx = [pi]
                    for (Ops, KT, V, kt, use_blk, first, lastf) in work:
                        nk = min(128, L - kt * 128)
                        Sp = S[si % 2]
                        si += 1
                        extra = []
                        if use_blk and kt < 64:
                            extra.append((C.emat[:, kt, 0:nk], NMT[:, 0:HQ], NMT))
                        if kt == last:
                            extra.append((C.identb[0:nk, 0:nk], C.caus[0:nk, :, 0:nq], 'consts'))
                        if (not use_blk) and kt == i - 4:
                            extra.append((C.identb[0:nk, 0:nk], C.wlow[0:nk, :, 0:nq], 'consts'))
                        k.op('pe', lambda en: en.matmul(Sp[0:nk, 0:HQ], KT[:, kt * 128:kt * 128 + nk], qa[0:70, 0:HQ],
                                                        start=True, stop=(len(extra) == 0)), reads=[KT, qa], writes=[Sp])
                        for xi_, (l_, r_, key_) in enumerate(extra):
                            k.op('pe', lambda en: en.matmul(Sp[0:nk, 0:HQ], l_, r_, start=False, stop=(xi_ == len(extra) - 1)),
                                 reads=[key_, 'consts'], writes=[Sp])
                        if pend is not None:
                            finish_tile(pend)
                        pend = (Ops, V, kt, nk, Sp, first, lastf)
                    if pend is not None:
                        finish_tile(pend)
                    return si, P_idx[0]
                si, pi = run_branches([(Owin, KwT, Vw, list(range(max(0, i - 4), last + 1)), False)], si, pi)
                k.op('dve', lambda en: en.tensor_scalar_add(rrow[64:65, 0:HQ], Ocmp[64:65, 0:HQ], TINY), reads=[Ocmp], writes=[rrow])
                k.op('dve', lambda en: en.reciprocal(rrow[64:65, 0:HQ], rrow[64:65, 0:HQ]), reads=[rrow], writes=[rrow])
                k.op('pe', lambda en: en.matmul(M0[0:128, 0:HQ], C.onesf[64:65, 0:128], rrow[64:65, 0:HQ], start=True, stop=True),
                     reads=[rrow, 'consts'], writes=[M0])
                k.op('act', lambda en: en.copy(rb[:, 0:HQ], M0[0:128, 0:HQ]), reads=[M0], writes=[rb])
                k.op('dve', lambda en: en.tensor_tensor(oc[:, 0:HQ], Ocmp[0:64, 0:HQ], rb[0:64, 0:HQ], ALU.mult),
                     reads=[Ocmp, rb], writes=[oc])
                nmm = 4 * len(tiles)
                cnt = 0
                for idx, kt in enumerate(tiles):
                    e = eT[idx]
                    k.op('dve', lambda en: en.tensor_tensor(e[:, 0:HQ], e[:, 0:HQ], rb[:, 0:HQ], ALU.mult), reads=[e, rb], writes=[e])
                    for h in range(4):
                        k.op('pe', lambda en: en.matmul(M1[0:nq, 0:128], e[:, h * nq:(h + 1) * nq], C.poolm[:, kt, :],
                                                        start=(cnt == 0), stop=(cnt == nmm - 1)), reads=[e, 'consts'], writes=[M1])
                        cnt += 1
                bon = U.bonus_ap(C, qi, nq)
                k.op('dve', lambda en: en.tensor_tensor(score[0:nq, :], M1[0:nq, 0:128], bon, ALU.add),
                     reads=[M1, 'consts'], writes=[score])
                k.op('dve', lambda en: en.tensor_scalar_add(score[0:nq, 0:1], score[0:nq, 0:1], 1e4), reads=[score], writes=[score])
                k.op('dve', lambda en: en.max(m8[0:nq, 0:8], score[0:nq, :]), reads=[score], writes=[m8])
                k.op('dve', lambda en: en.match_replace(stmp[0:nq, :], m8[0:nq, 0:8], score[0:nq, :], -3e38),
                     reads=[score, m8], writes=[stmp])
                k.op('dve', lambda en: en.max(m8[0:nq, 8:16], stmp[0:nq, :]), reads=[stmp], writes=[m8])
                tc_ = U.thr_col
                k.op('dve', lambda en: en.tensor_scalar(NM[0:nq, :], score[0:nq, :], m8[0:nq, tc_:tc_ + 1], NEG, ALU.is_lt, ALU.mult),
                     reads=[score, m8], writes=[NM])
                k.op('pe', lambda en: en.transpose(M0[0:128, 0:nq], NM[0:nq, :], C.ident[0:nq, 0:nq]), reads=[NM, 'consts'], writes=[M0])
                k.op('act', lambda en: en.copy(NMT[:, 0:HQ].rearrange("p (h q) -> p h q", h=4),
                                               M0[0:128, 0:nq].unsqueeze(1).to_broadcast([128, 4, nq])), reads=[M0], writes=[NMT])
                si, pi = run_branches([(Osel, KsT, Vs, list(range(0, last + 1)), True)], si, pi)
                k.dma(gsb[:, 0:nq], U.GT[:, q0:q0 + nq], writes=[gsb])
                k.op('act', lambda en: en.activation(gsb[:, 0:nq], gsb[:, 0:nq], AF.Sigmoid, bias=C.gateb[:, j:j + 1]),
                     reads=[gsb, 'consts'], writes=[gsb])
                gate_ps = [M1, S[0], S[1]]
                rden_ps = [None, M0, W['psn']]
                rrows = [None, rrow, rrow2]
                rbs = [None, rb, rb2]
                t1s = [None, t1, t2]
                brO = [None, Osel, Owin]
                for br in range(3):
                    for h in range(4):
                        rsel = br * 16 + 4 * g + h
                        k.op('pe', lambda en: en.matmul(gate_ps[br][0:64, h * nq:(h + 1) * nq], C.sel[:, rsel, :], gsb[:, 0:nq],
                                                        start=True, stop=True), reads=[gsb, 'consts'], writes=[gate_ps[br]])
                for br in (1, 2):
                    k.op('dve', lambda en: en.tensor_scalar_add(rrows[br][64:65, 0:HQ], brO[br][64:65, 0:HQ], TINY),
                         reads=[brO[br]], writes=[rrows[br]])
                    k.op('dve', lambda en: en.reciprocal(rrows[br][64:65, 0:HQ], rrows[br][64:65, 0:HQ]), reads=[rrows[br]], writes=[rrows[br]])
                for br in (1, 2):
                    k.op('pe', lambda en: en.matmul(rden_ps[br][0:64, 0:HQ], C.onesf[64:65, 0:64], rrows[br][64:65, 0:HQ], start=True, stop=True),
                         reads=[rrows[br], 'consts'], writes=[rden_ps[br]])
                for br in (1, 2):
                    k.op('act', lambda en: en.copy(rbs[br][0:64, 0:HQ], rden_ps[br][0:64, 0:HQ]), reads=[rden_ps[br]], writes=[rbs[br]])
                k.op('dve', lambda en: en.tensor_tensor(acc[:, 0:HQ], oc[:, 0:HQ], gate_ps[0][0:64, 0:HQ], ALU.mult),
                     reads=[oc, gate_ps[0]], writes=[acc])
                for br in (1, 2):
                    k.op('dve', lambda en: en.tensor_tensor(t1s[br][:, 0:HQ], brO[br][0:64, 0:HQ], rbs[br][0:64, 0:HQ], ALU.mult),
                         reads=[brO[br], rbs[br]], writes=[t1s[br]])
                    k.op('dve', lambda en: en.tensor_tensor(t1s[br][:, 0:HQ], t1s[br][:, 0:HQ], gate_ps[br][0:64, 0:HQ], ALU.mult),
                         reads=[t1s[br], gate_ps[br]], writes=[t1s[br]])
                for br in (1, 2):
                    k.op('pool', lambda en: en.tensor_tensor(acc[:, 0:HQ], acc[:, 0:HQ], t1s[br][:, 0:HQ], ALU.add),
                         reads=[acc, t1s[br]], writes=[acc])
                zv = U.ZT[256 * g:256 * g + 256, q0:q0 + nq].rearrange("(h d) q -> d h q", d=64)
                k.dma(zsb[:, 0:HQ].rearrange("d (h q) -> d h q", h=4), zv, writes=[zsb])
                k.op('act', lambda en: en.activation(zsb[:, 0:HQ], zsb[:, 0:HQ], AF.Silu), reads=[zsb], writes=[zsb])
                k.op('dve', lambda en: en.tensor_tensor(zsb[:, 0:HQ], zsb[:, 0:HQ], acc[:, 0:HQ], ALU.mult), reads=[zsb, acc], writes=[zsb])
                k.dma(U.MT[256 * g:256 * g + 256, q0:q0 + nq].rearrange("(h d) q -> d h q", d=64),
                      zsb[:, 0:HQ].rearrange("d (h q) -> d h q", h=4), reads=[zsb], writes=[('MT', U.tag)], q='pool', acc=True)
            k.barrier()


G3_STEPS = None


def run_interleaved(gens):
    gens = list(gens)
    rounds = 0
    while gens:
        if G3_STEPS is not None and rounds >= G3_STEPS:
            break
        rounds += 1
        nxt = []
        for g_ in gens:
            try:
                next(g_)
                nxt.append(g_)
            except StopIteration:
                pass
        gens = nxt


def gdn_layer(k, nc, C, U, jb, P_):
    T = U.T
    NCH = (T + 127) // 128
    NS = NCH * 8
    pfx = 'gd%s_' % U.tag
    with ExitStack() as es:
        sb = lambda n, s, d=F32: es.enter_context(nc.sbuf_tensor(pfx + 'a_' + n, s, d))
        cw = sb('cw', [128, 24, 4])
        xh = [sb('xh%d' % i, [128, 515]) for i in range(2)]
        y = [sb('y%d' % i, [128, 512]) for i in range(2)]
        W = dict(sq=sb('sq', [128, 512]), r=sb('r', [128, 512]),
                 psn=k.reg_psum(es.enter_context(nc.psum_tensor(pfx + 'a_psn', [128, 512], F32))))
        for ft in range(24):
            k.dma(cw[:, ft, :], P_['b_conv_w'][jb][:, ft * 128:(ft + 1) * 128].rearrange("j p -> p j"), writes=[cw], acc=True,
                  allow_slow_non_contiguous=True)
        ci = 0
        for ft in range(24):
            for c0 in range(0, T, 512):
                n = min(512, T - c0)
                x_, y_ = xh[ci % 2], y[ci % 2]
                ci += 1
                rows = U.GQKV[ft * 128:(ft + 1) * 128, :]
                if c0 == 0:
                    U.conv_hist(k, x_, ft)
                    k.dma(x_[:, 3:3 + n], rows[:, 0:n], writes=[x_], acc=True)
                else:
                    k.dma(x_[:, 0:3 + n], rows[:, c0 - 3:c0 + n], writes=[x_])
                k.op('dve', lambda en: en.tensor_scalar(y_[:, 0:n], x_[:, 3:3 + n], cw[:, ft, 3:4], None, ALU.mult), reads=[x_, cw], writes=[y_])
                for jj in (2, 1, 0):
                    k.op('dve', lambda en: en.scalar_tensor_tensor(y_[:, 0:n], x_[:, jj:jj + n], cw[:, ft, jj:jj + 1], y_[:, 0:n],
                                                                   ALU.mult, ALU.add), reads=[x_, cw, y_], writes=[y_])
                k.op('act', lambda en: en.activation(y_[:, 0:n], y_[:, 0:n], AF.Silu), reads=[y_], writes=[y_])
                if ft < 16:
                    gcol = C.gains[:, 18:19] if ft < 8 else C.gains[:, 19:20]
                    fm_norm(k, C, W, y_, n, y_[:, 0:n], y_, gcol, C.epsc[:, 0:1], 1.0, P=128)
                k.dma(U.GN[ft * 128:(ft + 1) * 128, c0:c0 + n], y_[:, 0:n], reads=[y_], writes=['GN'], q='pool', acc=True)
    k.barrier()
    if checkpoint('g1'):
        return
    with ExitStack() as es:
        sb = lambda n, s, d=F32: es.enter_context(nc.sbuf_tensor(pfx + 'b_' + n, s, d))
        psb = lambda n: k.reg_psum(es.enter_context(nc.psum_tensor(pfx + 'b_' + n, [128, 512], F32)))
        bd = sb('bd', [128, NCH, 16])
        Beta, nBeta, G = sb('Beta', [128, NS]), sb('nBeta', [128, NS]), sb('G', [128, NS])
        t_a, t_b = sb('ta', [128, NS]), sb('tb', [128, NS])
        eG, eGl, eD, bE = sb('eG', [128, NS]), sb('eGl', [128, NS]), sb('eD', [128, NS]), sb('bE', [128, NS])
        dtb, nal = sb('dtb', [128, 8]), sb('nal', [128, 8])
        pa, pb = psb('pa'), psb('pb')
        k.op('pool', lambda en: en.memset(bd[:], 0.0), writes=[bd])
        NF = T // 128
        if NF:
            k.dma(bd[:, 0:NF, :], U.GBD[0:NF * 128, :].rearrange("(c p) f -> p c f", p=128), writes=[bd])
        if T % 128:
            k.dma(bd[0:T % 128, NF, :], U.GBD[NF * 128:T, :], writes=[bd])
        k.dma(dtb[:], P_['b_dt_bias'][jb].partition_broadcast(128), writes=[dtb])
        k.dma(nal[:], P_['b_a_log'][jb].partition_broadcast(128), writes=[nal])
        k.op('act', lambda en: en.activation(nal[:], nal[:], AF.Exp), reads=[nal], writes=[nal])
        k.op('dve', lambda en: en.tensor_scalar(nal[:], nal[:], -1.0, None, ALU.mult), reads=[nal], writes=[nal])
        v3 = lambda t_: t_[:, 0:NS].rearrange("p (c h) -> p c h", h=8)
        k.op('act', lambda en: en.activation(v3(Beta), bd[:, :, 0:8], AF.Sigmoid), reads=[bd], writes=[Beta])
        k.op('dve', lambda en: en.tensor_scalar(nBeta[:], Beta[:], -1.0, None, ALU.mult), reads=[Beta], writes=[nBeta])
        k.op('dve', lambda en: en.tensor_tensor(v3(t_a), bd[:, :, 8:16], dtb[:].unsqueeze(1).to_broadcast([128, NCH, 8]), ALU.add),
             reads=[bd, dtb], writes=[t_a])
        k.op('dve', lambda en: en.tensor_scalar_max(eD[:], t_a[:], 0.0), reads=[t_a], writes=[eD])
        k.op('dve', lambda en: en.scalar_tensor_tensor(t_b[:], eD[:], -2.0, t_a[:], ALU.mult, ALU.add), reads=[t_a, eD], writes=[t_b])
        k.op('act', lambda en: en.activation(t_b[:], t_b[:], AF.Exp), reads=[t_b], writes=[t_b])
        k.op('act', lambda en: en.activation(t_b[:], t_b[:], AF.Ln, bias=C.onesf[:, 0:1], scale=1.0), reads=[t_b, 'consts'], writes=[t_b])
        k.op('dve', lambda en: en.tensor_tensor(t_a[:], eD[:], t_b[:], ALU.add), reads=[eD, t_b], writes=[t_a])
        k.op('dve', lambda en: en.tensor_tensor(v3(G), v3(t_a), nal[:].unsqueeze(1).to_broadcast([128, NCH, 8]), ALU.mult),
             reads=[t_a, nal], writes=[G])
        if T % 128:
            assert T % 128 == 4
            for t_ in (G, Beta, nBeta):
                k.op('dve', lambda en: en.tensor_scalar(t_[:, (NCH - 1) * 8:NCH * 8], t_[:, (NCH - 1) * 8:NCH * 8], C.pm4[:, 0:1], None, ALU.mult),
                     reads=[t_, 'consts'], writes=[t_])
        k.op('pe', lambda en: en.matmul(pa[:, 0:NS], C.tri[:, :], G[:, 0:NS], start=True, stop=True), reads=[G, 'consts'], writes=[pa])
        k.op('pe', lambda en: en.matmul(pb[:, 0:NS], C.onesf[:, :], G[:, 0:NS], start=True, stop=True), reads=[G, 'consts'], writes=[pb])
        k.op('act', lambda en: en.activation(eG[:], pa[:, 0:NS], AF.Exp), reads=[pa], writes=[eG])
        k.op('act', lambda en: en.activation(eGl[:], pb[:, 0:NS], AF.Exp), reads=[pb], writes=[eGl])
        k.op('dve', lambda en: en.tensor_copy(t_a[:], pa[:, 0:NS]), reads=[pa], writes=[t_a])
        k.op('dve', lambda en: en.tensor_tensor(t_b[:], pb[:, 0:NS], t_a[:], ALU.subtract), reads=[pb, t_a], writes=[t_b])
        k.op('act', lambda en: en.activation(eD[:], t_b[:], AF.Exp), reads=[t_b], writes=[eD])
        k.op('dve', lambda en: en.tensor_tensor(bE[:], Beta[:], eG[:], ALU.mult), reads=[Beta, eG], writes=[bE])
        k.barrier()
        if checkpoint('g2'):
            return
        with ExitStack() as es3:
            sb3 = lambda n, s, d=F32: es3.enter_context(nc.sbuf_tensor(pfx + 'c_' + n, s, d))
            H = []
            for h in range(8):
                hb = Ctx()
                hb.ps = k.reg_psum(es3.enter_context(nc.psum_tensor(pfx + 'c_ps%d' % h, [128, 512], F32))) if h < 6 else None
                for nm in ['k_tm', 'v_tm', 'Gp', 'dec', 'decT', 'YT', 'Y', 'Tt', 'wT', 'u', 'qkT', 'qeT', 'kd', 'kbe', 'vb', 'vnew', 'S', 'dg']:
                    setattr(hb, nm, sb3('%s%d' % (nm, h), [128, 128]))
                hb.qkv = sb3('qkv%d' % h, [128, 3, 128])
                hb.oT = sb3('oT%d' % h, [128, 512])
                hb.z = sb3('z%d' % h, [128, 512])
                H.append(hb)
            H[6].ps = pa
            H[7].ps = pb
            Wn = dict(sq=sb3('nsq', [128, 512]), r=sb3('nr', [128, 512]), psn=None)
            for h in range(8):
                U.state_init(k, H[h].S, h)

            def slot(hb, i, w=1):
                return hb.ps[:, i * 128:(i + w) * 128]

            def steps(h, c):
                hb = H[h]
                c0 = c * 128
                n = min(128, T - c0)
                col = c * 8 + h
                K_ = lambda i: hb.ps
                gn = U.GN.rearrange("(s hh p) t -> p s hh t", s=3, hh=8, p=128)
                if n < 128:
                    k.op('pool', lambda en: en.memset(hb.qkv[:], 0.0), writes=[hb.qkv])
                k.dma(hb.qkv[:, :, 0:n], gn[:, :, h, c0:c0 + n], writes=[hb.qkv], acc=(n < 128))
                qT, kT, vT = hb.qkv[:, 0, :], hb.qkv[:, 1, :], hb.qkv[:, 2, :]
                yield
                k.op('pe', lambda en: en.transpose(slot(hb, 0), kT, C.ident[:, :]), reads=[hb.qkv, 'consts'], writes=[K_(0)])
                k.op('pe', lambda en: en.transpose(slot(hb, 1), vT, C.ident[:, :]), reads=[hb.qkv, 'consts'], writes=[K_(1)])
                k.op('dve', lambda en: en.tensor_scalar(hb.Gp[:], C.slt[:, :], G[:, col:col + 1], None, ALU.mult), reads=['consts', G], writes=[hb.Gp])
                yield
                k.op('act', lambda en: en.copy(hb.k_tm[:], slot(hb, 0)), reads=[K_(0)], writes=[hb.k_tm])
                k.op('act', lambda en: en.copy(hb.v_tm[:], slot(hb, 1)), reads=[K_(1)], writes=[hb.v_tm])
                k.op('pe', lambda en: en.matmul(slot(hb, 2), C.tri[:, :], hb.Gp[:], start=True, stop=False), reads=[hb.Gp, 'consts'], writes=[K_(2)])
                k.op('pe', lambda en: en.matmul(slot(hb, 2), C.ident[:, :], C.mstrict[:, :], start=False, stop=True), reads=['consts'], writes=[K_(2)])
                k.op('pe', lambda en: en.matmul(slot(hb, 3), hb.Gp[:], C.tri[:, :], start=True, stop=False), reads=[hb.Gp, 'consts'], writes=[K_(3)])
                k.op('pe', lambda en: en.matmul(slot(hb, 3), C.ident[:, :], C.minclT[:, :], start=False, stop=True), reads=['consts'], writes=[K_(3)])
                yield
                k.op('act', lambda en: en.activation(hb.dec[:], slot(hb, 2), AF.Exp), reads=[K_(2)], writes=[hb.dec])
                k.op('act', lambda en: en.activation(hb.decT[:], slot(hb, 3), AF.Exp), reads=[K_(3)], writes=[hb.decT])
                k.op('pe', lambda en: en.matmul(slot(hb, 0), kT, kT, start=True, stop=True), reads=[hb.qkv], writes=[K_(0)])
                k.op('pe', lambda en: en.matmul(slot(hb, 1), kT, qT, start=True, stop=True), reads=[hb.qkv], writes=[K_(1)])
                k.op('dve', lambda en: en.tensor_scalar(hb.kbe[:], hb.k_tm[:], bE[:, col:col + 1], None, ALU.mult), reads=[hb.k_tm, bE], writes=[hb.kbe])
                k.op('dve', lambda en: en.tensor_scalar(hb.vb[:], hb.v_tm[:], Beta[:, col:col + 1], None, ALU.mult), reads=[hb.v_tm, Beta], writes=[hb.vb])
                k.op('act', lambda en: en.activation(hb.kd[:], hb.k_tm[:], AF.Copy, scale=eD[:, col:col + 1]), reads=[hb.k_tm, eD], writes=[hb.kd])
                k.op('act', lambda en: en.activation(hb.dg[:], C.ident[:, :], AF.Copy, scale=eG[:, col:col + 1]), reads=['consts', eG], writes=[hb.dg])
                yield
                k.op('dve', lambda en: en.scalar_tensor_tensor(hb.YT[:], slot(hb, 0), nBeta[:, col:col + 1], hb.dec[:], ALU.mult, ALU.mult),
                     reads=[K_(0), nBeta, hb.dec], writes=[hb.YT])
                k.op('dve', lambda en: en.tensor_tensor(hb.qkT[:], slot(hb, 1), hb.decT[:], ALU.mult), reads=[K_(1), hb.decT], writes=[hb.qkT])
                yield
                k.op('pe', lambda en: en.transpose(slot(hb, 2), hb.YT[:], C.ident[:, :]), reads=[hb.YT, 'consts'], writes=[K_(2)])
                k.op('pe', lambda en: en.matmul(slot(hb, 3), C.onesf[:, :], hb.dg[:], start=True, stop=True), reads=[hb.dg, 'consts'], writes=[K_(3)])
                yield
                k.op('act', lambda en: en.copy(hb.Y[:], slot(hb, 2)), reads=[K_(2)], writes=[hb.Y])
                k.op('dve', lambda en: en.tensor_tensor(hb.Tt[:], slot(hb, 2), C.ident[:, :], ALU.add), reads=[K_(2), 'consts'], writes=[hb.Tt])
                k.op('dve', lambda en: en.tensor_tensor(hb.qeT[:], slot(hb, 3), qT, ALU.mult), reads=[K_(3), hb.qkv], writes=[hb.qeT])
                yield
                k.op('pe', lambda en: en.matmul(slot(hb, 0), hb.YT[:], hb.Y[:], start=True, stop=True), reads=[hb.YT, hb.Y], writes=[K_(0)])
                k.op('pe', lambda en: en.matmul(slot(hb, 1), hb.Y[:], hb.YT[:], start=True, stop=True), reads=[hb.YT, hb.Y], writes=[K_(1)])
                yield
                k.op('act', lambda en: en.copy(hb.Y[:], slot(hb, 0)), reads=[K_(0)], writes=[hb.Y])
                k.op('dve', lambda en: en.tensor_copy(hb.YT[:], slot(hb, 1)), reads=[K_(1)], writes=[hb.YT])
                yield
                for kk in range(1, 7):
                    lastk = (kk == 6)
                    k.op('pe', lambda en: en.matmul(slot(hb, 2), hb.YT[:], hb.Tt[:], start=True, stop=True), reads=[hb.YT, hb.Tt], writes=[K_(2)])
                    if not lastk:
                        k.op('pe', lambda en: en.matmul(slot(hb, 0), hb.YT[:], hb.Y[:], start=True, stop=True), reads=[hb.YT, hb.Y], writes=[K_(0)])
                        k.op('pe', lambda en: en.matmul(slot(hb, 1), hb.Y[:], hb.YT[:], start=True, stop=True), reads=[hb.YT, hb.Y], writes=[K_(1)])
                    yield
                    k.op('dve', lambda en: en.tensor_tensor(hb.Tt[:], hb.Tt[:], slot(hb, 2), ALU.add), reads=[K_(2), hb.Tt], writes=[hb.Tt])
                    if not lastk:
                        k.op('act', lambda en: en.copy(hb.Y[:], slot(hb, 0)), reads=[K_(0)], writes=[hb.Y])
                        k.op('act', lambda en: en.copy(hb.YT[:], slot(hb, 1)), reads=[K_(1)], writes=[hb.YT])
                    yield
                k.op('pe', lambda en: en.matmul(slot(hb, 0), hb.kbe[:], hb.Tt[:], start=True, stop=True), reads=[hb.kbe, hb.Tt], writes=[K_(0)])
                k.op('pe', lambda en: en.matmul(slot(hb, 1), hb.Tt[:], hb.vb[:], start=True, stop=True), reads=[hb.vb, hb.Tt], writes=[K_(1)])
                yield
                k.op('act', lambda en: en.copy(hb.wT[:], slot(hb, 0)), reads=[K_(0)], writes=[hb.wT])
                k.op('act', lambda en: en.copy(hb.u[:], slot(hb, 1)), reads=[K_(1)], writes=[hb.u])
                yield
                k.op('pe', lambda en: en.matmul(slot(hb, 2), hb.wT[:], hb.S[:], start=True, stop=True), reads=[hb.wT, hb.S], writes=[K_(2)])
                yield
                k.op('dve', lambda en: en.tensor_tensor(hb.vnew[:], hb.u[:], slot(hb, 2), ALU.subtract), reads=[K_(2), hb.u], writes=[hb.vnew])
                yield
                k.op('pe', lambda en: en.matmul(slot(hb, 3), hb.S[:], hb.qeT[:], start=True, stop=False), reads=[hb.qeT, hb.S], writes=[K_(3)])
                k.op('pe', lambda en: en.matmul(slot(hb, 3), hb.vnew[:], hb.qkT[:], start=False, stop=True), reads=[hb.vnew, hb.qkT], writes=[K_(3)])
                k.op('pe', lambda en: en.matmul(slot(hb, 0), hb.kd[:], hb.vnew[:], start=True, stop=True), reads=[hb.kd, hb.vnew], writes=[K_(0)])
                yield
                cc = c % 4
                k.op('act', lambda en: en.copy(hb.oT[:, cc * 128:(cc + 1) * 128], slot(hb, 3)), reads=[K_(3)], writes=[hb.oT])
                k.op('dve', lambda en: en.scalar_tensor_tensor(hb.S[:], hb.S[:], eGl[:, col:col + 1], slot(hb, 0), ALU.mult, ALU.add),
                     reads=[K_(0), hb.S, eGl], writes=[hb.S])
                yield
                if cc == 3 or c == NCH - 1:
                    t0 = (c - cc) * 128
                    nn = min(T, c0 + 128) - t0
                    Wn['psn'] = hb.ps
                    k.dma(hb.z[:, 0:nn], U.GZ[h * 128:(h + 1) * 128, t0:t0 + nn], writes=[hb.z])
                    k.op('act', lambda en: en.activation(hb.z[:, 0:nn], hb.z[:, 0:nn], AF.Silu), reads=[hb.z], writes=[hb.z])
                    sq, r = Wn['sq'], Wn['r']
                    k.op('act', lambda en: en.activation(sq[:, 0:nn], hb.oT[:, 0:nn], AF.Square), reads=[hb.oT], writes=[sq])
                    k.op('pe', lambda en: en.matmul(hb.ps[:, 0:nn], C.onesf[:, :], sq[:, 0:nn], start=True, stop=True),
                         reads=[sq, 'consts'], writes=[K_(0), K_(1), K_(2), K_(3)])
                    k.op('act', lambda en: en.activation(r[:, 0:nn], hb.ps[:, 0:nn], AF.Sqrt, bias=C.epsc[:, 0:1], scale=1.0 / 128),
                         reads=[K_(0), K_(1), K_(2), K_(3)], writes=[r])
                    k.op('dve', lambda en: en.reciprocal(r[:, 0:nn], r[:, 0:nn]), reads=[r], writes=[r])
                    k.op('dve', lambda en: en.scalar_tensor_tensor(hb.oT[:, 0:nn], hb.oT[:, 0:nn], C.ogain[:, jb:jb + 1], r[:, 0:nn], ALU.mult, ALU.mult),
                         reads=[hb.oT, r, 'consts'], writes=[hb.oT])
                    k.op('dve', lambda en: en.tensor_tensor(hb.oT[:, 0:nn], hb.oT[:, 0:nn], hb.z[:, 0:nn], ALU.mult), reads=[hb.oT, hb.z], writes=[hb.oT])
                    k.dma(U.MT[h * 128:(h + 1) * 128, t0:t0 + nn], hb.oT[:, 0:nn], reads=[hb.oT], writes=[('MT', U.tag)], q='pool', acc=True)
                yield

            for c in range(NCH if G3_STEPS is None else 1):
                run_interleaved([steps(h, c) for h in range(8)])
            for h in range(8):
                U.state_out(k, H[h].S, h)
    k.barrier()


def conf_layer(k, nc, C, U, jc, P_):
    T = U.T
    pfx = 'cf%s_' % U.tag
    with ExitStack() as es:
        sb = lambda n, s, d=F32: es.enter_context(nc.sbuf_tensor(pfx + n, s, d))
        psb = lambda n: k.reg_psum(es.enter_context(nc.psum_tensor(pfx + n, [128, 512], F32)))
        cw = sb('cw', [128, 8, 31])
        cb_ = sb('cbias', [128, 8])
        lg, lb = sb('lg', [128, 8]), sb('lb', [128, 8])
        a_ = [sb('a%d' % i, [128, 512]) for i in range(2)]
        b_ = [sb('b%d' % i, [128, 512]) for i in range(2)]
        uh = [sb('uh%d' % i, [128, 542]) for i in range(2)]
        cbuf = sb('cbuf', [128, 8, 512])
        cb2 = [sb('cb2_%d' % i, [128, 512]) for i in range(2)]
        sq = sb('sq', [128, 512])
        mean, rstd, tmp = sb('mean', [128, 512]), sb('rstd', [128, 512]), sb('tmp', [128, 512])
        zt = [sb('z%d' % i, [128, 512]) for i in range(2)]
        ps1, ps2 = psb('ps1'), psb('ps2')
        for ft in range(8):
            k.dma(cw[:, ft, :], P_['c_conv_w'][jc][:, ft * 128:(ft + 1) * 128].rearrange("j p -> p j"), writes=[cw], acc=True,
                  allow_slow_non_contiguous=True)
        k.dma(cb_[:], P_['c_conv_b'][jc].rearrange("(ft p) -> p ft", p=128), writes=[cb_], allow_slow_non_contiguous=True)
        k.dma(lg[:], P_['c_ln_g'][jc].rearrange("(ft p) -> p ft", p=128), writes=[lg], allow_slow_non_contiguous=True)
        k.dma(lb[:], P_['c_ln_b'][jc].rearrange("(ft p) -> p ft", p=128), writes=[lb], allow_slow_non_contiguous=True)
        ci = 0
        for ft in range(8):
            for c0 in range(0, T, 512):
                n = min(512, T - c0)
                a, b = a_[ci % 2], b_[ci % 2]
                ci += 1
                k.dma(a[:, 0:n], U.CA[ft * 128:(ft + 1) * 128, c0:c0 + n], writes=[a])
                k.dma(b[:, 0:n], U.CB[ft * 128:(ft + 1) * 128, c0:c0 + n], writes=[b])
                k.op('act', lambda en: en.activation(b[:, 0:n], b[:, 0:n], AF.Sigmoid), reads=[b], writes=[b])
                k.op('dve', lambda en: en.tensor_tensor(a[:, 0:n], a[:, 0:n], b[:, 0:n], ALU.mult), reads=[a, b], writes=[a])
                k.dma(U.CU[ft * 128:(ft + 1) * 128, c0:c0 + n], a[:, 0:n], reads=[a], writes=['CU'], q='pool', acc=True)
        k.barrier()
        if checkpoint('c1'):
            return
        ui = 0
        zi = 0
        for c0 in range(0, T, 512):
            n = min(512, T - c0)
            for ft in range(8):
                u = uh[ui % 2]
                ui += 1
                rows = U.CU[ft * 128:(ft + 1) * 128, :]
                if c0 == 0:
                    U.cconv_hist(k, u, ft)
                    k.dma(u[:, 30:30 + n], rows[:, 0:n], writes=[u], acc=True)
                else:
                    k.dma(u[:, 0:30 + n], rows[:, c0 - 30:c0 + n], writes=[u])
                y = cbuf[:, ft, 0:n]
                k.op('dve', lambda en: en.tensor_scalar(y, u[:, 30:30 + n], cw[:, ft, 30:31], cb_[:, ft:ft + 1], ALU.mult, ALU.add),
                     reads=[u, cw, cb_], writes=[(cbuf, ft)])
                NPL = CONF_POOL_TAPS
                y2 = cb2[ui % 2][:, 0:n]
                for jj in range(30):
                    if jj < NPL:
                        if jj == 0:
                            k.op('pool', lambda en: en.tensor_scalar(y2, u[:, jj:jj + n], cw[:, ft, jj:jj + 1], None, ALU.mult),
                                 reads=[u, cw], writes=[cb2[ui % 2]])
                        else:
                            k.op('pool', lambda en: en.scalar_tensor_tensor(y2, u[:, jj:jj + n], cw[:, ft, jj:jj + 1], y2, ALU.mult, ALU.add),
                                 reads=[u, cw, cb2[ui % 2]], writes=[cb2[ui % 2]])
                    else:
                        k.op('dve', lambda en: en.scalar_tensor_tensor(y, u[:, jj:jj + n], cw[:, ft, jj:jj + 1], y, ALU.mult, ALU.add),
                             reads=[u, cw, (cbuf, ft)], writes=[(cbuf, ft)])
                if NPL:
                    k.op('dve', lambda en: en.tensor_tensor(y, y, y2, ALU.add), reads=[(cbuf, ft), cb2[ui % 2]], writes=[(cbuf, ft)])
                k.op('pe', lambda en: en.matmul(ps1[:, 0:n], C.onesf[:, :], y, start=(ft == 0), stop=(ft == 7)),
                     reads=[(cbuf, ft), 'consts'], writes=[ps1])
                k.op('act', lambda en: en.activation(sq[:, 0:n], y, AF.Square), reads=[(cbuf, ft)], writes=[sq])
                k.op('pe', lambda en: en.matmul(ps2[:, 0:n], C.onesf[:, :], sq[:, 0:n], start=(ft == 0), stop=(ft == 7)),
                     reads=[sq, 'consts'], writes=[ps2])
            k.op('act', lambda en: en.activation(mean[:, 0:n], ps1[:, 0:n], AF.Copy, scale=1.0 / 1024), reads=[ps1], writes=[mean])
            k.op('dve', lambda en: en.tensor_tensor(tmp[:, 0:n], mean[:, 0:n], mean[:, 0:n], ALU.mult), reads=[mean], writes=[tmp])
            k.op('dve', lambda en: en.scalar_tensor_tensor(tmp[:, 0:n], ps2[:, 0:n], 1.0 / 1024, tmp[:, 0:n], ALU.mult, ALU.subtract),
                 reads=[ps2, tmp], writes=[tmp])
            k.op('act', lambda en: en.activation(rstd[:, 0:n], tmp[:, 0:n], AF.Sqrt, bias=C.epsc[:, 0:1], scale=1.0), reads=[tmp, 'epsc'], writes=[rstd])
            k.op('dve', lambda en: en.reciprocal(rstd[:, 0:n], rstd[:, 0:n]), reads=[rstd], writes=[rstd])
            for ft in range(8):
                y = cbuf[:, ft, 0:n]
                z = zt[zi % 2]
                zi += 1
                k.dma(z[:, 0:n], U.CZ[ft * 128:(ft + 1) * 128, c0:c0 + n], writes=[z])
                k.op('act', lambda en: en.activation(z[:, 0:n], z[:, 0:n], AF.Silu), reads=[z], writes=[z])
                k.op('dve', lambda en: en.tensor_tensor(y, y, mean[:, 0:n], ALU.subtract), reads=[(cbuf, ft), mean], writes=[(cbuf, ft)])
                k.op('dve', lambda en: en.tensor_tensor(y, y, rstd[:, 0:n], ALU.mult), reads=[(cbuf, ft), rstd], writes=[(cbuf, ft)])
                k.op('act', lambda en: en.activation(y, y, AF.Silu, bias=lb[:, ft:ft + 1], scale=lg[:, ft:ft + 1]),
                     reads=[(cbuf, ft), lg, lb], writes=[(cbuf, ft)])
                k.op('dve', lambda en: en.tensor_tensor(z[:, 0:n], z[:, 0:n], y, ALU.mult), reads=[(cbuf, ft), z], writes=[z])
                k.dma(U.MT[ft * 128:(ft + 1) * 128, c0:c0 + n], z[:, 0:n], reads=[z], writes=[('MT', U.tag)], q='pool', acc=True)
        k.barrier()
        if checkpoint('c2'):
            return
        U.cconv_out(k, nc, C, ps1, ps2)
    k.barrier()
    checkpoint('c3')


def stage_outproj(k, nc, C, mt_ap, x_ap, y_ap, w_ap, T, tag):
    with ExitStack() as es:
        sb = lambda n, s, d=F32: es.enter_context(nc.sbuf_tensor(tag + n, s, d))
        ps = lambda n: k.reg_psum(es.enter_context(nc.psum_tensor(tag + n, [128, 512], F32)))
        wb = sb('wb', [128, 8, 1024], BF16)
        wst = [sb('wst%d' % i, [128, 8, 512], F32) for i in range(2)]
        mst = [sb('mst%d' % i, [128, 8, 128], F32) for i in range(2)]
        mb = [sb('mb%d' % i, [128, 8, 128], BF16) for i in range(2)]
        xt = [sb('xt%d' % i, [128, D], F32) for i in range(2)]
        yt = [sb('yt%d' % i, [128, D], F32) for i in range(2)]
        pp = [ps('p%d' % i) for i in range(4)]
        w_v = w_ap.rearrange("(kc p) c -> p kc c", p=128)
        for ci in range(2):
            k.dma(wst[ci][:], w_v[:, :, ci * 512:(ci + 1) * 512], writes=[wst[ci]])
            k.op(['dve', 'pool'][ci], lambda en: en.tensor_copy(wb[:, :, ci * 512:(ci + 1) * 512], wst[ci][:]),
                 reads=[wst[ci]], writes=[(wb, ci)])
        m_v = mt_ap.rearrange("(kc p) t -> p kc t", p=128)
        gi = 0
        for ti, t0 in enumerate(range(0, T, 128)):
            nt = min(128, T - t0)
            ms, mbb, x, y = mst[ti % 2], mb[ti % 2], xt[ti % 2], yt[ti % 2]
            k.dma(ms[:, :, 0:nt], m_v[:, :, t0:t0 + nt], writes=[ms])
            k.dma(x[0:nt, :], x_ap[t0:t0 + nt, :], writes=[x])
            k.op('pool', lambda en: en.tensor_copy(mbb[:, :, 0:nt], ms[:, :, 0:nt]), reads=[ms], writes=[mbb])
            for half in range(2):
                p = pp[gi % 4]
                gi += 1
                for kc in range(8):
                    k.op('pe', lambda en: en.matmul(p[0:nt, 0:512], mbb[:, kc, 0:nt], wb[:, kc, half * 512:(half + 1) * 512],
                                                    start=(kc == 0), stop=(kc == 7)), reads=[mbb, (wb, 0), (wb, 1)], writes=[p])
                k.op('dve', lambda en: en.tensor_tensor(y[0:nt, half * 512:(half + 1) * 512], p[0:nt, 0:512],
                                                        x[0:nt, half * 512:(half + 1) * 512], ALU.add), reads=[p, x], writes=[y])
            k.dma(y_ap[t0:t0 + nt, :], y[0:nt, :], reads=[y], writes=[('Y', tag)], q='pool', acc=True)
    k.barrier()


N_SEQ = 4
N_POOL = 2560
CONF_POOL_TAPS = 0
RUN_PROMPT = True
RUN_SAMPLE = True


def sample_setup(nc, dr, scr, out, tab):
    S = Ctx()
    TS = 4 * N_SEQ
    S.TS = TS
    S.pt = dr("pt", [N_SEQ, 64], I32)
    S.cache_cmp = [dr("cache_cmp%d" % i, [N_POOL * 128, 512]) for i in range(2)]
    S.cache_slc = [dr("cache_slc%d" % i, [N_POOL * 128, 512]) for i in range(2)]
    S.win_in = dr("win_in", [2, N_SEQ, 512, 512])
    S.sd_in = dr("sd_in", [N_SEQ, 8, 128, 128])
    S.sdc_in = dr("sdc_in", [N_SEQ, 3, 3072])
    S.sc_in = dr("sc_in", [N_SEQ, 30, 1024])
    S.qaug = dr("t_qaug_s", [6, 1, 16, 128], BF16)
    S.o_ys = out("o_ys", [TS, D])
    S.o_cmp = out("o_cmp_s", [2, TS, 512])
    S.o_slc = out("o_slc_s", [2, TS, 512])
    S.o_win = out("o_win_s", [2, N_SEQ, 512, 512])
    S.o_dst = out("o_dst_s", [N_SEQ, 8, 128, 128])
    S.o_dcv = out("o_dcv_s", [N_SEQ, 3, 3072])
    S.o_ccv = out("o_ccv_s", [N_SEQ, 30, 1024])
    S.BIG = scr("BIG_s", [4112, TS])
    S.GN, S.CU, S.GBD = scr("GN_s", [3072, TS]), scr("CU_s", [1024, TS]), scr("GBD_s", [TS, 16])
    S.MT = scr("MT_s", [1024, TS])
    S.Y = [scr("Y0_s", [TS, D]), scr("Y1_s", [TS, D])]
    S.kvw = scr("kvw_s", [TS, 512])
    S.CT = [scr("CTs%d" % i, [512, 8192]) for i in range(N_SEQ)]
    S.KST = [scr("KSTs%d" % i, [256, 8192]) for i in range(N_SEQ)]
    S.VS = [scr("VSs%d" % i, [8192, 256]) for i in range(N_SEQ)]
    S.KWT = [scr("KWTs%d" % i, [256, 512]) for i in range(N_SEQ)]
    return S


def sample_prepass(k, nc, C, S, j, s):
    with ExitStack() as es:
        pfx = 'pp%d%d_' % (j, s)
        pg = [es.enter_context(nc.sbuf_tensor(pfx + 'pg%d' % i, [128, 512], F32)) for i in range(3)]
        tsb = [es.enter_context(nc.sbuf_tensor(pfx + 'tsb%d' % i, [128, 512], F32)) for i in range(2)]
        tps = [k.reg_psum(es.enter_context(nc.psum_tensor(pfx + 'tps%d' % i, [128, 512], F32))) for i in range(2)]
        it = 0
        for (cache, nb, dstT, dstV) in [(S.cache_cmp[j], 4, S.CT[s], None), (S.cache_slc[j], 2, S.KST[s], S.VS[s]),
                                        (None, 2, S.KWT[s], None)]:
            ntile = 64 if cache is not None else 4
            for p in range(ntile):
                g_, t_, ps_ = pg[it % 3], tsb[it % 2], tps[it % 2]
                it += 1
                if cache is not None:
                    k.dma_ind(g_[:, :], cache, C.idx[:, s * 64 + p:s * 64 + p + 1], reads=['idx'], writes=[g_])
                else:
                    k.dma(g_[:, :], S.win_in[j, s, p * 128:(p + 1) * 128, :], writes=[g_])
                for b in range(nb):
                    k.op('pe', lambda en: en.transpose(ps_[:, b * 128:(b + 1) * 128], g_[:, b * 128:(b + 1) * 128], C.ident[:, :]),
                         reads=[g_, 'consts'], writes=[ps_])
                if it % 2:
                    k.op('act', lambda en: en.copy(t_[:, 0:nb * 128], ps_[:, 0:nb * 128]), reads=[ps_], writes=[t_])
                else:
                    k.op('dve', lambda en: en.tensor_copy(t_[:, 0:nb * 128], ps_[:, 0:nb * 128]), reads=[ps_], writes=[t_])
                k.dma(dstT.rearrange("(b q) t -> q b t", q=128)[:, :, p * 128:(p + 1) * 128],
                      t_[:, 0:nb * 128].rearrange("q (b t) -> q b t", b=nb), reads=[t_], writes=[('ppT', s)], acc=True)
                if dstV is not None:
                    k.dma(dstV[p * 128:(p + 1) * 128, :], g_[:, 256:512], reads=[g_], writes=[('ppV', s)], acc=True)
    k.barrier()


def sample_unit(S, s, tab):
    U = Ctx()
    U.tag = 's%d' % s
    U.T, U.past, U.L, U.Lc = 4, 8192, 8196, 8192
    c = slice(4 * s, 4 * s + 4)
    big = S.BIG
    U.QT, U.GT, U.ZT = big[0:1024, c], big[2048:2096, c], big[2096:3120, c]
    U.GQKV, U.GZ = big[0:3072, c], big[3072:4096, c]
    U.CA, U.CB, U.CZ = big[0:1024, c], big[1024:2048, c], big[2048:3072, c]
    U.GN, U.CU, U.GBD, U.MT = S.GN[:, c], S.CU[:, c], S.GBD[c, :], S.MT[:, c]
    U.qtiles = [(0, 4)]
    U.tab = dict(tab)
    U.tab['qaug'] = S.qaug
    U.k_prenormed = True
    U.thr_col = 14
    U.bonus_ap = lambda C_, qi, nq: C_.bonus_s[0:nq, :]
    return U


def load_nsa_consts(k, nc, C, tab, esn, pfx):
    cn = lambda n, s, d=F32: esn.enter_context(nc.sbuf_tensor(pfx + n, s, d))
    C.caus, C.wlow = cn("caus", [128, 4, 128], BF16), cn("wlow", [128, 4, 128], BF16)
    C.cm = cn("cm", [128, 17, 4, 128], BF16)
    C.poolm = cn("poolm", [128, 4, 128], BF16)
    C.bonus = cn("bonus", [128, 256])
    C.bonus_s = cn("bonus_s", [128, 128])
    C.sel = cn("sel", [48, 48, 64])
    C.emat = cn("emat", [128, 65, 128], BF16)
    for n_ in ['caus', 'wlow', 'cm', 'poolm', 'bonus', 'bonus_s', 'sel', 'emat']:
        k.dma(getattr(C, n_)[:], tab[n_], writes=['consts'], acc=True)
    k.barrier()


def sample_layer(k, nc, C, S, P_, tab, layer, x_cur, x_next, knorm_post):
    kind, j = layer % 3, layer // 3
    TS = S.TS
    gcol = P_['norm_g'][layer].rearrange("(kc p) -> p kc", p=128)
    fm = lambda ap: (lambda b0, nb, t0, nts: ap[b0:b0 + nb, t0:t0 + nts])
    if kind == 0:
        plan = [
            dict(c0=0, n=1024, mode='FM', key='QTs', dst=fm(S.BIG[0:1024])),
            dict(c0=2560, n=48, mode='FM', key='GTs', dst=fm(S.BIG[2048:2096])),
            dict(c0=2608, n=1024, mode='FM', key='ZTs', dst=fm(S.BIG[2096:3120])),
            dict(c0=1024, n=512, mode='TM', key='o_cmp_s',
                 dst=lambda b0, nb, t0, nt: [(S.o_cmp[j, t0:t0 + nt, b0:b0 + nb], 'o_cmp_s')]),
            dict(c0=1536, n=512, mode='TM', key='o_slc_s', post=knorm_post(j, 1),
                 dst=lambda b0, nb, t0, nt: [(S.o_slc[j, t0:t0 + nt, b0:b0 + nb], 'o_slc_s')]),
            dict(c0=2048, n=512, mode='TM', key='kvw_s', post=knorm_post(j, 2),
                 dst=lambda b0, nb, t0, nt: [(S.kvw[t0:t0 + nt, b0:b0 + nb], 'kvw_s')]),
        ]
        stage_inproj(k, nc, C, x_cur, TS, P_['a_w_in'][j], A_IN, gcol, plan, 'A%ds' % layer)
        esn = ExitStack()
        load_nsa_consts(k, nc, C, tab, esn, "cn%ds_" % layer)
        for s in range(N_SEQ):
            sample_prepass(k, nc, C, S, j, s)
            U = sample_unit(S, s, tab)
            U.a_cmp_w, U.a_cmp_pos = P_['a_cmp_w'], P_['a_cmp_pos']
            c4 = slice(4 * s, 4 * s + 4)
            U.ct_src = lambda c, g, s=s: S.CT[s][c * 256 + g * 64:c * 256 + g * 64 + 64, :]

            def ks_src(g, c0, n, s=s, c4=c4):
                res = []
                m = min(c0 + n, 8192) - c0
                if m > 0:
                    res.append((S.KST[s][g * 64:g * 64 + 64, c0:c0 + m], 0, m))
                if c0 + n > 8192:
                    res.append((S.o_slc[j, c4, g * 64:g * 64 + 64].rearrange("t d -> d t"), max(m, 0), max(m, 0) + 4))
                return res

            def kw_src(g, c0, n, s=s, c4=c4):
                if c0 == 7680:
                    return [(S.KWT[s][g * 64:g * 64 + 64, 0:512], 0, 512)]
                if c0 == 8192:
                    return [(S.kvw[c4, g * 64:g * 64 + 64].rearrange("t d -> d t"), 0, 4)]
                return []

            def vs_src(g, vst, s=s, c4=c4):
                res = []
                v = S.VS[s][:, g * 64:g * 64 + 64].rearrange("(kt p) d -> p kt d", p=128)
                for a in range(0, 64, 16):
                    res.append((v[:, a:a + 16, :], vst[:, a:a + 16, :]))
                res.append((S.o_slc[j, c4, 256 + g * 64:256 + g * 64 + 64], vst[0:4, 64, :]))
                return res

            def vw_src(g, vst, s=s, c4=c4):
                v = S.win_in[j, s][:, 256 + g * 64:256 + g * 64 + 64].rearrange("(kt p) d -> p kt d", p=128)
                return [(v, vst[:, 60:64, :]), (S.kvw[c4, 256 + g * 64:256 + g * 64 + 64], vst[0:4, 64, :])]
            U.ks_src, U.kw_src, U.vs_src, U.vw_src = ks_src, kw_src, vs_src, vw_src
            for g in range(N_GROUPS_RUN):
                nsa_group(k, nc, C, U, j, g)
            k.dma(S.o_win[j, s, 0:508, :], S.win_in[j, s, 4:512, :], writes=['o_win_s'], q='pool', acc=True)
            k.dma(S.o_win[j, s, 508:512, :], S.kvw[c4, :], reads=['kvw_s'], writes=['o_win_s'], q='pool', acc=True)
        k.barrier()
        esn.close()
        w_out = P_['a_w_out'][j]
    elif kind == 1:
        def dcv_dst(b0, nb, t0, nt):
            return [(S.o_dcv[s, :, b0:b0 + nb], 'o_dcv_s', 4 * s + 1, 4 * s + 4) for s in range(N_SEQ)]
        plan = [
            dict(c0=0, n=3072, mode='FM', key='GQKVs', dst=fm(S.BIG[0:3072])),
            dict(c0=3072, n=1024, mode='FM', key='GZs', dst=fm(S.BIG[3072:4096])),
            dict(c0=4096, n=16, mode='TM', key='GBDs', dst=lambda b0, nb, t0, nt: [(S.GBD[t0:t0 + nt, b0:b0 + nb], 'GBDs')]),
            dict(c0=0, n=3072, mode='TM', key='o_dcv_s', dst=dcv_dst),
        ]
        stage_inproj(k, nc, C, x_cur, TS, P_['b_w_in'][j], DN_IN, gcol, plan, 'A%ds' % layer)
        for s in range(N_SEQ):
            U = sample_unit(S, s, tab)
            U.conv_hist = lambda k_, x_, ft, s=s: k_.dma(x_[:, 0:3], S.sdc_in[s][:, ft * 128:(ft + 1) * 128].rearrange("j p -> p j"),
                                                         writes=[x_], allow_slow_non_contiguous=True)
            U.state_init = lambda k_, St, h, s=s: k_.dma(St[:], S.sd_in[s, h], writes=[St])
            U.state_out = lambda k_, St, h, s=s: k_.dma(S.o_dst[s, h], St[:], reads=[St], writes=['o_dst_s'], q='pool', acc=True)
            gdn_layer(k, nc, C, U, j, P_)
        w_out = P_['b_w_out'][j]
    else:
        plan = [
            dict(c0=0, n=1024, mode='FM', key='CAs', dst=fm(S.BIG[0:1024])),
            dict(c0=1024, n=1024, mode='FM', key='CBs', dst=fm(S.BIG[1024:2048])),
            dict(c0=2048, n=1024, mode='FM', key='CZs', dst=fm(S.BIG[2048:3072])),
        ]
        stage_inproj(k, nc, C, x_cur, TS, P_['c_w_in'][j], CONF_IN, gcol, plan, 'A%ds' % layer)
        for s in range(N_SEQ):
            U = sample_unit(S, s, tab)
            U.cconv_hist = lambda k_, u, ft, s=s: k_.dma(u[:, 0:30], S.sc_in[s][:, ft * 128:(ft + 1) * 128].rearrange("j p -> p j"),
                                                         writes=[u], allow_slow_non_contiguous=True)

            def cconv_out(k_, nc_, C_, ps1, ps2, s=s):
                k_.dma(S.o_ccv[s, 0:26, :], S.sc_in[s, 4:30, :], writes=['o_ccv_s'], q='pool', acc=True)
                with ExitStack() as es_:
                    ul = es_.enter_context(nc_.sbuf_tensor("ccvs_ul%d" % s, [128, 8, 4], F32))
                    ot = es_.enter_context(nc_.sbuf_tensor("ccvs_ot%d" % s, [4, 1024], F32))
                    k_.dma(ul[:], S.CU[:, 4 * s:4 * s + 4].rearrange("(ft p) t -> p ft t", p=128), reads=['CU'], writes=[ul])
                    for ft in range(8):
                        pp = ps1 if ft < 4 else ps2
                        k_.op('pe', lambda en: en.transpose(pp[0:4, (ft % 4) * 128:(ft % 4 + 1) * 128], ul[:, ft, :], C_.ident[:, :]),
                              reads=[ul, 'consts'], writes=[pp])
                    k_.op('dve', lambda en: en.tensor_copy(ot[:, 0:512], ps1[0:4, 0:512]), reads=[ps1], writes=[ot])
                    k_.op('dve', lambda en: en.tensor_copy(ot[:, 512:1024], ps2[0:4, 0:512]), reads=[ps2], writes=[ot])
                    k_.dma(S.o_ccv[s, 26:30, :], ot[:], reads=[ot], writes=['o_ccv_s'], q='pool', acc=True)
                    k_.barrier()
            U.cconv_out = cconv_out
            conf_layer(k, nc, C, U, j, P_)
        w_out = P_['c_w_out'][j]
    stage_outproj(k, nc, C, S.MT, x_cur, x_next, w_out, TS, 'O%ds' % layer)


def build_program():
    nc = bass.Bass("TRN2", target_bir_lowering=False)
    C = Ctx()
    TP = T_P
    NTP = TP // 128
    dr = lambda n, s, d=F32, kind="ExternalInput": nc.dram_tensor(n, s, d, kind=kind).ap()
    xp = dr("xp", [TP, D])
    xs = dr("xs", [4 * N_SEQ, D])
    P_ = {}
    for n_, sh in PARAM_SHAPES.items():
        P_[n_] = dr(n_, sh)
    tab = {n: dr("t_" + n, sh, dt) for n, (sh, dt) in TABLE_SPECS.items()}
    tab['qaug'] = dr("t_qaug_p", [6, NTP, 16, 128], BF16)
    out = lambda n, s: dr(n, s, kind="ExternalOutput")
    o_yp = out("o_yp", [TP, D])
    o_cmp_p = out("o_cmp_p", [2, TP, 512])
    o_slc_p = out("o_slc_p", [2, TP, 512])
    o_win_p = out("o_win_p", [2, 512, 512])
    o_dst_p = out("o_dst_p", [1, 8, 128, 128])
    o_dcv_p = out("o_dcv_p", [1, 3, 3072])
    o_ccv_p = out("o_ccv_p", [1, 30, 1024])
    scr = lambda n, s: dr(n, s, kind="Internal")
    kvw_p = scr("kvw_p", [TP, 512])
    S = sample_setup(nc, dr, scr, out, tab) if RUN_SAMPLE else None
    UP = Ctx()
    UP.tag = 'p'
    UP.T, UP.past, UP.L = TP, 0, TP
    UP.Lc = TP
    big = scr("BIG_p", [4112, TP])
    UP.QT, UP.CT, UP.KST, UP.KWT = big[0:1024], big[1024:1536], big[1536:1792], big[1792:2048]
    UP.GT, UP.ZT = big[2048:2096], big[2096:3120]
    UP.GQKV, UP.GZ = big[0:3072], big[3072:4096]
    UP.CA, UP.CB, UP.CZ = big[0:1024], big[1024:2048], big[2048:3072]
    UP.GN = scr("GN_p", [3072, TP])
    UP.CU = scr("CU_p", [1024, TP])
    UP.GBD = scr("GBD_p", [TP, 16])
    UP.MT = scr("MT_p", [1024, TP])
    UP.Y = [scr("Y0_p", [TP, D]), scr("Y1_p", [TP, D])]
    UP.qtiles = [(q0, 128) for q0 in range(0, TP, 128)]
    UP.tab = tab
    UP.a_cmp_w, UP.a_cmp_pos = P_['a_cmp_w'], P_['a_cmp_pos']
    UP.k_prenormed = False
    UP.thr_col = 15
    UP.bonus_ap = lambda C_, qi, nq: C_.bonus[0:nq, 126 - 2 * qi:254 - 2 * qi]

    k = KB(nc)
    UP.conv_hist = lambda k_, x_, ft: k_.op('pool', lambda en: en.memset(x_[:, 0:3], 0.0), writes=[x_])
    UP.cconv_hist = lambda k_, u, ft: k_.op('pool', lambda en: en.memset(u[:, 0:30], 0.0), writes=[u])
    UP.state_init = lambda k_, S, h: k_.op('pool', lambda en: en.memset(S[:], 0.0), writes=[S])
    UP.state_out = lambda k_, S, h: k_.dma(o_dst_p[0, h], S[:], reads=[S], writes=['o_dst'], q='pool', acc=True)

    def cconv_out_p(k_, nc_, C_, ps1, ps2):
        with ExitStack() as es_:
            ul = es_.enter_context(nc_.sbuf_tensor("ccv_ul", [128, 8, 30], F32))
            ot = es_.enter_context(nc_.sbuf_tensor("ccv_ot", [30, 1024], F32))
            k_.dma(ul[:], UP.CU[:, TP - 30:TP].rearrange("(ft p) t -> p ft t", p=128), writes=[ul])
            for ft in range(8):
                pp = ps1 if ft < 4 else ps2
                k_.op('pe', lambda en: en.transpose(pp[0:30, (ft % 4) * 128:(ft % 4 + 1) * 128], ul[:, ft, :], C_.ident[:, :]),
                      reads=[ul, 'consts'], writes=[pp])
            k_.op('dve', lambda en: en.tensor_copy(ot[:, 0:512], ps1[0:30, 0:512]), reads=[ps1], writes=[ot])
            k_.op('dve', lambda en: en.tensor_copy(ot[:, 512:1024], ps2[0:30, 0:512]), reads=[ps2], writes=[ot])
            k_.dma(o_ccv_p[0], ot[:], reads=[ot], writes=['o_ccv'], q='pool')
            k_.barrier()
    UP.cconv_out = cconv_out_p

    with ExitStack() as es0:
        cs = lambda n, s, d=F32: es0.enter_context(nc.sbuf_tensor("c_" + n, s, d))
        C.ident = cs("ident", [128, 128])
        C.identb = cs("identb", [128, 128], BF16)
        C.onesf = cs("onesf", [128, 128])
        C.tri, C.slt = cs("tri", [128, 128]), cs("slt", [128, 128])
        C.mstrict, C.minclT = cs("mstrict", [128, 128]), cs("minclT", [128, 128])
        C.gains = cs("gains", [128, 20])
        C.gateb = cs("gateb", [48, 2])
        C.ogain = cs("ogain", [128, 1])
        C.gainb = cs("gainb", [128, 2, 3, 64])
        C.tmp = cs("ctmp", [128, 512])
        C.st4 = cs("cst4", [128, 8])
        C.epsc = cs("epsc", [128, 1])
        k.op('pool', lambda en: en.memset(C.epsc[:], EPS), writes=['epsc'])
        k.op('pool', lambda en: en.memset(C.gains[:, 16:17], EPS), writes=['consts'])
        k.op('pool', lambda en: en.memset(C.gains[:, 17:18], 64 * EPS), writes=['consts'])
        k.op('pool', lambda en: en.memset(C.gains[:, 18:19], 128 ** -0.5), writes=['consts'])
        k.op('pool', lambda en: en.memset(C.gains[:, 19:20], 1.0), writes=['consts'])
        C.pm4 = cs("pm4", [128, 1])
        for n_ in ['ident', 'identb', 'onesf', 'tri', 'slt', 'mstrict', 'minclT', 'pm4']:
            k.dma(getattr(C, n_)[:], tab[n_], writes=['consts'], acc=True)
        for jj in range(2):
            k.dma(C.gains[0:64, jj * 8:jj * 8 + 1], P_['a_q_gain'][jj].rearrange("(d o) -> d o", o=1), writes=['consts'], acc=True)
            k.dma(C.gains[0:64, jj * 8 + 1:jj * 8 + 4], P_['a_k_gain'][jj].rearrange("b d -> d b"), writes=['consts'], acc=True,
                  allow_slow_non_contiguous=True)
            k.dma(C.gateb[:, jj:jj + 1], P_['a_gate_b'][jj].rearrange("(d o) -> d o", o=1), writes=['consts'], acc=True)
        k.dma(C.ogain[:], P_['b_o_gain'][0].rearrange("(d o) -> d o", o=1), writes=['consts'], acc=True)
        k.dma(C.gainb[:].rearrange("p a b c -> p (a b c)"),
              P_['a_k_gain'].rearrange("a b c -> (a b c)").partition_broadcast(128), writes=['gainb'])
        if RUN_SAMPLE:
            C.idx = cs("idx", [128, N_SEQ * 64], I32)
            pti = cs("pti", [128, N_SEQ * 64], I32)
            ptf = cs("ptf", [128, N_SEQ * 64])
            pcol = cs("pcol", [128, 1])
            k.dma(pcol[:], tab['pcol'], writes=['pcol'])
            k.dma(pti[:], S.pt.rearrange("s p -> (s p)").partition_broadcast(128), writes=['pti'])
            k.op('dve', lambda en: en.tensor_copy(ptf[:], pti[:]), reads=['pti'], writes=['ptf'])
            k.op('dve', lambda en: en.tensor_scalar(ptf[:], ptf[:], 128.0, pcol[:, 0:1], ALU.mult, ALU.add), reads=['ptf', 'pcol'], writes=['ptf'])
            k.op('dve', lambda en: en.tensor_copy(C.idx[:], ptf[:]), reads=['ptf'], writes=['idx'])
        k.barrier()

        def knorm_post(j, which):
            def post(o, nt, nb):
                kv = o[0:nt, 0:256]
                k.op('act', lambda en: en.activation(C.tmp[0:nt, 0:256], kv, AF.Square), reads=[o], writes=['ctmp'])
                k.op('dve', lambda en: en.tensor_reduce(C.st4[0:nt, 0:4], C.tmp[0:nt, 0:256].rearrange("p (g d) -> p g d", g=4),
                                                        AX.X, ALU.add), reads=['ctmp'], writes=['cst4'])
                rstd_op(k, C, C.st4[0:nt, 4:8], C.st4[0:nt, 0:4], 1.0 / 64, nt, 'cst4')
                kv3 = kv.rearrange("p (g d) -> p g d", g=4)
                k.op('dve', lambda en: en.tensor_tensor(kv3, kv3, C.st4[0:nt, 4:8].unsqueeze(2).to_broadcast([nt, 4, 64]), ALU.mult),
                     reads=[o, 'cst4'], writes=[o])
                k.op('dve', lambda en: en.tensor_tensor(kv3, kv3, C.gainb[0:nt, j, which:which + 1, :].to_broadcast([nt, 4, 64]), ALU.mult),
                     reads=[o, 'gainb'], writes=[o])
            return post

        fm = lambda ap: (lambda b0, nb, t0, nts: ap[b0:b0 + nb, t0:t0 + nts])
        U = UP
        x_cur = xp
        layers = list(LAYERS)
        xs_cur = xs
        for li, layer in enumerate(layers):
          if RUN_SAMPLE:
              xs_next = S.o_ys if li == len(layers) - 1 else S.Y[li % 2]
              sample_layer(k, nc, C, S, P_, tab, layer, xs_cur, xs_next, knorm_post)
              xs_cur = xs_next
          if not RUN_PROMPT:
              continue
          try:
            kind, j = layer % 3, layer // 3
            x_next = o_yp if li == len(layers) - 1 else U.Y[li % 2]
            gcol = P_['norm_g'][layer].rearrange("(kc p) -> p kc", p=128)
            if kind == 0:
                def win_dst(b0, nb, t0, nt, j=j):
                    res = [(kvw_p[t0:t0 + nt, b0:b0 + nb], ('kvw', 'p'))]
                    if t0 >= TP - 512:
                        res.append((o_win_p[j, t0 - (TP - 512):t0 - (TP - 512) + nt, b0:b0 + nb], 'o_win_p'))
                    return res
                plan = [
                    dict(c0=0, n=1024, mode='FM', key='QT', dst=fm(U.QT)),
                    dict(c0=1024, n=512, mode='FM', key='CT', dst=fm(U.CT)),
                    dict(c0=1536, n=256, mode='FM', key='KST', dst=fm(U.KST)),
                    dict(c0=2048, n=256, mode='FM', key='KWT', dst=fm(U.KWT)),
                    dict(c0=2560, n=48, mode='FM', key='GT', dst=fm(U.GT)),
                    dict(c0=2608, n=1024, mode='FM', key='ZT', dst=fm(U.ZT)),
                    dict(c0=1024, n=512, mode='TM', key='o_cmp',
                         dst=lambda b0, nb, t0, nt, j=j: [(o_cmp_p[j, t0:t0 + nt, b0:b0 + nb], 'o_cmp')]),
                    dict(c0=1536, n=512, mode='TM', key='o_slc', post=knorm_post(j, 1),
                         dst=lambda b0, nb, t0, nt, j=j: [(o_slc_p[j, t0:t0 + nt, b0:b0 + nb], 'o_slc')]),
                    dict(c0=2048, n=512, mode='TM', key='kvw', post=knorm_post(j, 2), dst=win_dst),
                ]
                stage_inproj(k, nc, C, x_cur, TP, P_['a_w_in'][j], A_IN, gcol, plan, 'A%dp' % layer)
                U.ct_src = lambda c, g: U.CT[c * 256 + g * 64:c * 256 + g * 64 + 64, :]
                U.ks_src = lambda g, c0, n: [(U.KST[g * 64:g * 64 + 64, c0:c0 + n], 0, n)]
                U.kw_src = lambda g, c0, n: [(U.KWT[g * 64:g * 64 + 64, c0:c0 + n], 0, n)]

                def vsrc_of(src):
                    def f(g, vst):
                        res = []
                        v = src[:, 256 + g * 64:256 + g * 64 + 64].rearrange("(kt p) d -> p kt d", p=128)
                        for a in range(0, NTP, 16):
                            b = min(NTP, a + 16)
                            res.append((v[:, a:b, :], vst[:, a:b, :]))
                        return res
                    return f
                U.vs_src = vsrc_of(o_slc_p[j])
                U.vw_src = vsrc_of(kvw_p)
                with ExitStack() as esn:
                    load_nsa_consts(k, nc, C, tab, esn, "cn%dp_" % layer)
                    for g in range(N_GROUPS_RUN):
                        nsa_group(k, nc, C, U, j, g)
                w_out = P_['a_w_out'][j]
            elif kind == 1:
                T3 = TP - 3

                def dcv_dst(b0, nb, t0, nt):
                    r0 = max(T3 - t0, 0)
                    return [(o_dcv_p[0, t0 + r0 - T3:t0 + nt - T3, b0:b0 + nb], 'o_dcv', r0, nt)]
                plan = [
                    dict(c0=0, n=3072, mode='FM', key='GQKV', dst=fm(U.GQKV)),
                    dict(c0=3072, n=1024, mode='FM', key='GZ', dst=fm(U.GZ)),
                    dict(c0=4096, n=16, mode='TM', key='GBD', dst=lambda b0, nb, t0, nt: [(U.GBD[t0:t0 + nt, b0:b0 + nb], 'GBD')]),
                    dict(c0=0, n=3072, mode='TM', key='o_dcv', dst=dcv_dst, tok_filter=lambda t0, nt: t0 + nt > T3),
                ]
                stage_inproj(k, nc, C, x_cur, TP, P_['b_w_in'][j], DN_IN, gcol, plan, 'A%dp' % layer)
                if not checkpoint('inproj'):
                    gdn_layer(k, nc, C, U, j, P_)
                w_out = P_['b_w_out'][j]
            else:
                plan = [
                    dict(c0=0, n=1024, mode='FM', key='CA', dst=fm(U.CA)),
                    dict(c0=1024, n=1024, mode='FM', key='CB', dst=fm(U.CB)),
                    dict(c0=2048, n=1024, mode='FM', key='CZ', dst=fm(U.CZ)),
                ]
                stage_inproj(k, nc, C, x_cur, TP, P_['c_w_in'][j], CONF_IN, gcol, plan, 'A%dp' % layer)
                if not checkpoint('inproj'):
                    conf_layer(k, nc, C, U, j, P_)
                w_out = P_['c_w_out'][j]
            if STOPPED[0]:
                break
            stage_outproj(k, nc, C, U.MT, x_cur, x_next, w_out, TP, 'O%dp' % layer)
            x_cur = x_next
          except StopBuild:
            k.barrier()
            break
        k.finish()
    print("instructions:", k.n_ins, k.cnt)
    return nc


LAYERS = [0, 1, 2, 3]
PARAM_SHAPES = dict(
    norm_g=[4, D], a_w_in=[2, D, A_IN], a_w_out=[2, D, D], a_k_gain=[2, 3, 64], a_q_gain=[2, 64], a_gate_b=[2, 48],
    a_cmp_w=[2, 2, 2048, 64], a_cmp_pos=[2, 2, 32, 64],
    b_w_in=[1, D, DN_IN], b_conv_w=[1, 4, 3072], b_a_log=[1, 8], b_dt_bias=[1, 8], b_o_gain=[1, 128], b_w_out=[1, D, D],
    c_w_in=[1, D, CONF_IN], c_conv_w=[1, 31, 1024], c_conv_b=[1, 1024], c_ln_g=[1, 1024], c_ln_b=[1, 1024], c_w_out=[1, D, D],
)
N_GROUPS_RUN = 4
_CACHE = {}


def extra_inputs(T=None):
    tb = host_tables(T or T_P, 0)
    out = {"t_" + n: tb[n] for n in TABLE_SPECS}
    out["t_qaug_p"] = tb['qaug']
    out["t_qaug_s"] = host_tables(128, 8192)['qaug']
    return out


def kernel(**inputs):
    f32 = lambda a: np.ascontiguousarray(np.asarray(a, dtype=np.float32))
    if 'nc' not in _CACHE:
        _CACHE['nc'] = build_program()
    nc = _CACHE['nc']
    x_prompt = f32(inputs['x_prompt'])
    x_sample = f32(inputs['x_sample'])
    tabs = extra_inputs()
    shared = {n: f32(inputs[n]) for n in PARAM_SHAPES}
    if RUN_SAMPLE:
        ccmp = f32(inputs['cache_cmp_kv']).reshape(2, N_POOL * 128, 512)
        cslc = f32(inputs['cache_slc_kv']).reshape(2, N_POOL * 128, 512)
        cwin = f32(inputs['cache_win_kv']).reshape(2, 32, 512, 512)
        pt = np.ascontiguousarray(np.asarray(inputs['page_table'], dtype=np.int32))
        sd = f32(inputs['state_delta'])[0]
        sdc = f32(inputs['state_delta_conv'])[0]
        sc = f32(inputs['state_conv'])[0]
    in_maps = []
    for c in range(N_CORES):
        b = c % 2
        m = {"xp": x_prompt[b], "xs": x_sample[4 * c:4 * c + 4].reshape(16, D)}
        m.update(shared)
        m.update(tabs)
        if RUN_SAMPLE:
            sl = slice(4 * c, 4 * c + 4)
            m.update({"pt": pt[sl], "cache_cmp0": ccmp[0], "cache_cmp1": ccmp[1], "cache_slc0": cslc[0], "cache_slc1": cslc[1], "win_in": np.ascontiguousarray(cwin[:, sl]),
                      "sd_in": np.ascontiguousarray(sd[sl]), "sdc_in": np.ascontiguousarray(sdc[sl]),
                      "sc_in": np.ascontiguousarray(sc[sl])})
        else:
            m.pop("t_qaug_s", None)
        in_maps.append(m)
    res = run_bass_kernel_spmd(nc, in_maps, core_ids=list(range(N_CORES)))
    R = res.results
    z = lambda *s: np.zeros(s, np.float32)
    y_prompt = np.stack([R[b]["o_yp"] for b in range(2)])
    cmp_p = np.stack([R[b]["o_cmp_p"] for b in range(2)], axis=1).reshape(2, 2, T_P, 2, 4, 64)
    slc_p = np.stack([R[b]["o_slc_p"] for b in range(2)], axis=1).reshape(2, 2, T_P, 2, 4, 64)
    win_p = np.stack([R[b]["o_win_p"] for b in range(2)], axis=1).reshape(2, 2, 512, 2, 4, 64)
    dst_p = np.stack([R[b]["o_dst_p"] for b in range(2)], axis=1)
    dcv_p = np.stack([R[b]["o_dcv_p"] for b in range(2)], axis=1)
    ccv_p = np.stack([R[b]["o_ccv_p"] for b in range(2)], axis=1)
    if RUN_SAMPLE:
        y_sample = np.concatenate([R[c]["o_ys"].reshape(4, 4, D) for c in range(8)], axis=0)
        cmp_s = np.concatenate([R[c]["o_cmp_s"].reshape(2, 4, 4, 2, 4, 64) for c in range(8)], axis=1)
        slc_s = np.concatenate([R[c]["o_slc_s"].reshape(2, 4, 4, 2, 4, 64) for c in range(8)], axis=1)
        win_s = np.concatenate([R[c]["o_win_s"].reshape(2, 4, 512, 2, 4, 64) for c in range(8)], axis=1)
        dst_s = np.concatenate([R[c]["o_dst_s"] for c in range(8)], axis=0)[None]
        dcv_s = np.concatenate([R[c]["o_dcv_s"] for c in range(8)], axis=0)[None]
        ccv_s = np.concatenate([R[c]["o_ccv_s"] for c in range(8)], axis=0)[None]
    else:
        y_sample, cmp_s, slc_s, win_s = z(32, 4, D), z(2, 32, 4, 2, 4, 64), z(2, 32, 4, 2, 4, 64), z(2, 32, 512, 2, 4, 64)
        dst_s, dcv_s, ccv_s = z(1, 32, 8, 128, 128), z(1, 32, 3, 3072), z(1, 32, 30, 1024)
    return (y_prompt, y_sample, cmp_p, cmp_s, slc_p, slc_s, win_p, win_s,
            dst_p, dst_s, dcv_p, dcv_s, ccv_p, ccv_s)
```
